# Optimizing a Trainium2 kernel written in Bass

```python
import functools
import jax, jax.numpy as jnp
from jax import lax
import numpy as np

D_MODEL = 2048
BATCH = 8
SEQ = 2048
DEPTH = 1
DEC_BATCH = 32
DEC_SEQ = 64
PAST_LEN = 4096

CHUNK = 64
HEAD_DIM = 64
D_A = D_MODEL // 2
A_HEADS = D_A // HEAD_DIM
R_W = 64
R_A = 64
SHIFT_W = 3 * D_A + R_W + R_A
LNX_EPS = 64e-5
D_B = D_MODEL // 2
Q_HEADS = D_B // HEAD_DIM
KV_HEADS = 4
GROUP = Q_HEADS // KV_HEADS
KV_W = KV_HEADS * HEAD_DIM
WINDOW = 128
WIN_CHUNKS = WINDOW // CHUNK
CACHE_WIN = min(WINDOW, PAST_LEN)
IN_W = SHIFT_W + D_A + D_B + 2 * KV_W + D_B + 2 * D_MODEL
RMS_EPS = 1e-6
NEG_INF = -1e30

kernel_name = 'hybrid_rwkv7_swa_sink_streaming_step'


def rms_norm(x, g):
    x32 = x.astype(jnp.float32)
    y = x32 * lax.rsqrt(jnp.mean(x32 * x32, axis=-1, keepdims=True) + RMS_EPS)
    return (y * g.astype(jnp.float32)).astype(x.dtype)


def project(x, g_norm, w_in):
    h = rms_norm(x, g_norm)
    z = jnp.einsum('btd,de->bte', h, w_in)
    sizes = (SHIFT_W, D_A, D_B, KV_W, KV_W, D_B, D_MODEL, D_MODEL)
    offs = [int(o) for o in np.cumsum(sizes)[:-1]]
    return jnp.split(z, offs, axis=-1)


def rwkv7_branch(p, gate, shift_prev, wkv0, mu, w0, w_w_up, a0, w_a_up, k_k, k_a, r_k, lnx_w, lnx_b):
    bsz, t_len, _ = p.shape
    f32 = jnp.float32
    prev = jnp.concatenate([shift_prev[:, None, :].astype(p.dtype), p[:, :-1]], axis=1)
    xs = p + mu * (prev - p)
    r, k, v, wd, ad = jnp.split(xs, [D_A, 2 * D_A, 3 * D_A, 3 * D_A + R_W], axis=-1)
    w = -jax.nn.softplus(-(w0 + jnp.tanh(wd) @ w_w_up).astype(f32)) - 0.5
    decay = jnp.exp(-jnp.exp(w))
    a = jax.nn.sigmoid((a0 + ad @ w_a_up).astype(f32))
    heads = lambda u: u.astype(f32).reshape(bsz, t_len, A_HEADS, HEAD_DIM)
    r, k, v, decay, a = heads(r), heads(k), heads(v), heads(decay), heads(a)
    kk = k * k_k.astype(f32).reshape(A_HEADS, HEAD_DIM)
    kk = kk / jnp.maximum(jnp.sqrt(jnp.sum(kk * kk, axis=-1, keepdims=True)), 1e-12)
    k = k * (1.0 + (a - 1.0) * k_a.astype(f32).reshape(A_HEADS, HEAD_DIM))
    seq = tuple(jnp.moveaxis(u, 1, 0) for u in (r, decay, k, v, -kk, kk * a))

    def step(S, inp):
        r_t, w_t, k_t, v_t, a_t, b_t = inp
        sa = jnp.einsum('bhij,bhj->bhi', S, a_t)
        S = S * w_t[:, :, None, :] + sa[..., None] * b_t[:, :, None, :] + v_t[..., None] * k_t[:, :, None, :]
        return S, jnp.einsum('bhij,bhj->bhi', S, r_t)

    s_final, y = lax.scan(step, wkv0.astype(f32), seq)
    y = jnp.moveaxis(y, 0, 1)
    mean = jnp.mean(y, axis=-1, keepdims=True)
    var = jnp.mean(jnp.square(y - mean), axis=-1, keepdims=True)
    y = (y - mean) * lax.rsqrt(var + LNX_EPS) * lnx_w.astype(f32).reshape(A_HEADS, HEAD_DIM) \
        + lnx_b.astype(f32).reshape(A_HEADS, HEAD_DIM)
    y = y + jnp.sum(r * k * r_k.astype(f32), axis=-1, keepdims=True) * v
    y = y.reshape(bsz, t_len, D_A).astype(p.dtype) * jax.nn.silu(gate)
    return y, p[:, -1], s_final.astype(wkv0.dtype)


def sink_attention(q, k, v, q_pos, k_pos, sinks):
    s = jnp.einsum('...qhgd,...khd->...hgqk', q, k, preferred_element_type=jnp.float32) * (HEAD_DIM ** -0.5)
    dist = (q_pos[..., :, None] - k_pos[..., None, :]).astype(jnp.float32)
    dchunk = jnp.floor_divide(q_pos, CHUNK)[..., :, None] - jnp.floor_divide(k_pos, CHUNK)[..., None, :]
    visible = (dchunk >= 0) & (dchunk <= WIN_CHUNKS) & (k_pos >= 0)[..., None, :]
    slopes = (2.0 ** (-8.0 * jnp.arange(1, Q_HEADS + 1, dtype=jnp.float32) / Q_HEADS)).reshape(KV_HEADS, GROUP)
    s = s - slopes[:, :, None, None] * jnp.abs(dist)[..., None, None, :, :]
    s = jnp.where(visible[..., None, None, :, :], s, NEG_INF)
    sink = sinks.astype(jnp.float32).reshape(KV_HEADS, GROUP)[:, :, None, None]
    m = jnp.maximum(jnp.max(s, axis=-1, keepdims=True), sink)
    e = jnp.exp(s - m)
    p = e / (jnp.sum(e, axis=-1, keepdims=True) + jnp.exp(sink - m))
    return jnp.einsum('...hgqk,...khd->...qhgd', p.astype(v.dtype), v)


def attn_prompt(q, k, v, sinks):
    bsz, t_len, _ = q.shape
    n_c = t_len // CHUNK
    pad = WIN_CHUNKS * CHUNK
    q = q.reshape(bsz, n_c, CHUNK, KV_HEADS, GROUP, HEAD_DIM)
    k = k.reshape(bsz, t_len, KV_HEADS, HEAD_DIM)
    v = v.reshape(bsz, t_len, KV_HEADS, HEAD_DIM)

    def band(u):
        up = jnp.pad(u, ((0, 0), (pad, 0), (0, 0), (0, 0))).reshape(bsz, n_c + WIN_CHUNKS, CHUNK, KV_HEADS, HEAD_DIM)
        return jnp.concatenate([up[:, i:i + n_c] for i in range(WIN_CHUNKS + 1)], axis=2)

    kp = jnp.arange(-pad, t_len).reshape(n_c + WIN_CHUNKS, CHUNK)
    k_pos = jnp.concatenate([kp[i:i + n_c] for i in range(WIN_CHUNKS + 1)], axis=1)
    q_pos = jnp.arange(t_len).reshape(n_c, CHUNK)
    o = sink_attention(q, band(k), band(v), q_pos, k_pos, sinks)
    return o.reshape(bsz, t_len, D_B), k[:, -CACHE_WIN:], v[:, -CACHE_WIN:]


def attn_sample(q, k, v, sinks, cache_k, cache_v):
    bsz, t_len, _ = q.shape
    q = q.reshape(bsz, t_len, KV_HEADS, GROUP, HEAD_DIM)
    k_all = jnp.concatenate([cache_k.astype(k.dtype), k.reshape(bsz, t_len, KV_HEADS, HEAD_DIM)], axis=1)
    v_all = jnp.concatenate([cache_v.astype(v.dtype), v.reshape(bsz, t_len, KV_HEADS, HEAD_DIM)], axis=1)
    q_pos = PAST_LEN + jnp.arange(t_len)
    k_pos = jnp.concatenate([PAST_LEN - CACHE_WIN + jnp.arange(CACHE_WIN), q_pos])
    o = sink_attention(q, k_all, v_all, q_pos, k_pos, sinks)
    return o.reshape(bsz, t_len, D_B), k_all[:, -CACHE_WIN:], v_all[:, -CACHE_WIN:]


def mixer_layer(x, shift_prev, wkv0, attend, g_norm, w_in, mu, w0, w_w_up, a0, w_a_up, k_k, k_a, r_k,
                lnx_w, lnx_b, p_a, p_b, w_o):
    p_shift, gate_a, q, kb, vb, gate_b, m_a, m_b = project(x, g_norm, w_in)
    y_a, shift_last, wkv_new = rwkv7_branch(p_shift, gate_a, shift_prev, wkv0, mu, w0, w_w_up, a0, w_a_up,
                                            k_k, k_a, r_k, lnx_w, lnx_b)
    o_b, k_rows, v_rows = attend(q, kb, vb)
    y_b = o_b * jax.nn.silu(gate_b)
    merged = jax.nn.sigmoid(m_a) * (y_a @ p_a) + jax.nn.sigmoid(m_b) * (y_b @ p_b)
    return x + merged @ w_o, shift_last, wkv_new, k_rows, v_rows


def setup_inputs(seed: int = 0) -> dict:
    key = jax.random.key(seed)
    ks = jax.random.split(key, 24)
    n = jax.random.normal
    f32 = jnp.float32
    return {
        'x_prompt': n(ks[0], (BATCH, SEQ, D_MODEL), f32),
        'x_sample': n(ks[1], (DEC_BATCH, DEC_SEQ, D_MODEL), f32),
        'state_wkv': 0.5 * n(ks[2], (DEPTH, DEC_BATCH, A_HEADS, HEAD_DIM, HEAD_DIM), f32),
        'state_shift': n(ks[3], (DEPTH, DEC_BATCH, SHIFT_W), f32),
        'cache_k': n(ks[4], (DEPTH, DEC_BATCH, CACHE_WIN, KV_HEADS, HEAD_DIM), f32),
        'cache_v': n(ks[5], (DEPTH, DEC_BATCH, CACHE_WIN, KV_HEADS, HEAD_DIM), f32),
        'g_norm': 1.0 + 0.02 * n(ks[6], (DEPTH, D_MODEL), f32),
        'w_in': n(ks[7], (DEPTH, D_MODEL, IN_W), f32) * D_MODEL ** -0.5,
        'mu_shift': jax.random.uniform(ks[8], (DEPTH, SHIFT_W), f32, 0.2, 0.8),
        'w0': -0.5 + 0.5 * n(ks[9], (DEPTH, D_A), f32),
        'w_w_up': 0.1 * n(ks[10], (DEPTH, R_W, D_A), f32),
        'a0': 0.1 * n(ks[11], (DEPTH, D_A), f32),
        'w_a_up': 0.1 * n(ks[12], (DEPTH, R_A, D_A), f32),
        'k_k': 0.85 + 0.02 * n(ks[13], (DEPTH, D_A), f32),
        'k_a': 1.0 + 0.02 * n(ks[14], (DEPTH, D_A), f32),
        'r_k': 0.1 * n(ks[15], (DEPTH, A_HEADS, HEAD_DIM), f32),
        'lnx_w': 1.0 + 0.02 * n(ks[16], (DEPTH, D_A), f32),
        'lnx_b': 0.02 * n(ks[17], (DEPTH, D_A), f32),
        'sinks': n(ks[18], (DEPTH, Q_HEADS), f32),
        'p_a': n(ks[19], (DEPTH, D_A, D_MODEL), f32) * D_A ** -0.5,
        'p_b': n(ks[20], (DEPTH, D_B, D_MODEL), f32) * D_B ** -0.5,
        'w_o': n(ks[21], (DEPTH, D_MODEL, D_MODEL), f32) * D_MODEL ** -0.5,
        'g_final': 1.0 + 0.02 * n(ks[22], (D_MODEL,), f32),
    }


def reference(x_prompt, x_sample, state_wkv, state_shift, cache_k, cache_v, g_norm, w_in, mu_shift, w0,
              w_w_up, a0, w_a_up, k_k, k_a, r_k, lnx_w, lnx_b, sinks, p_a, p_b, w_o, g_final):
    xp, xs = x_prompt, x_sample
    zero_shift = jnp.zeros((xp.shape[0], SHIFT_W), xp.dtype)
    zero_wkv = jnp.zeros((xp.shape[0], A_HEADS, HEAD_DIM, HEAD_DIM), jnp.float32)
    wkv_p, shift_p, k_p, v_p = [], [], [], []
    wkv_s, shift_s, k_s, v_s = [], [], [], []
    for l in range(DEPTH):
        lw = (g_norm[l], w_in[l], mu_shift[l], w0[l], w_w_up[l], a0[l], w_a_up[l], k_k[l], k_a[l], r_k[l],
              lnx_w[l], lnx_b[l], p_a[l], p_b[l], w_o[l])
        xp, sh, wk, kr, vr = mixer_layer(xp, zero_shift, zero_wkv,
                                         functools.partial(attn_prompt, sinks=sinks[l]), *lw)
        shift_p.append(sh); wkv_p.append(wk); k_p.append(kr); v_p.append(vr)
        xs, sh, wk, kr, vr = mixer_layer(xs, state_shift[l], state_wkv[l],
                                         functools.partial(attn_sample, sinks=sinks[l], cache_k=cache_k[l],
                                                           cache_v=cache_v[l]), *lw)
        shift_s.append(sh); wkv_s.append(wk); k_s.append(kr); v_s.append(vr)
    y_prompt = rms_norm(xp, g_final)
    y_sample = rms_norm(xs, g_final)
    return (y_prompt, y_sample,
            jnp.stack(wkv_p), jnp.stack(shift_p), jnp.stack(k_p), jnp.stack(v_p),
            jnp.stack(wkv_s), jnp.stack(shift_s), jnp.stack(k_s), jnp.stack(v_s))
```

```python
import numpy as np
from contextlib import ExitStack
import concourse.bass as bass
import concourse.mybir as mybir
from concourse.bass_utils import run_bass_kernel_spmd

F32 = mybir.dt.float32
BF16 = mybir.dt.bfloat16
ALU = mybir.AluOpType
AF = mybir.ActivationFunctionType
AX = mybir.AxisListType

D = 2048
NTILES = 18
NT = 2
TB = NT * 128
NBLK = NTILES // NT
INW = 10880
RMS_EPS = 1e-6
LNX_EPS = 64e-5
C0 = float(np.exp(-0.5))
DBG = dict(nblk=NBLK, stage=99)
SLOPES = [float(2.0 ** (-(h + 1) / 2.0)) for h in range(16)]


class Prog:
    COMPUTE = ('pe', 'act', 'dve', 'pool')

    def __init__(self, nc):
        self.nc = nc
        self.ops = []
        self.last_w = {}
        self.readers = {}
        self.streams = None
        self.cur_stream = None

    def begin_streams(self, n):
        self.streams = [[] for _ in range(n)]
        self.cur_stream = None

    def set_stream(self, i):
        self.cur_stream = i

    def merge_streams(self):
        streams, self.streams, self.cur_stream = self.streams, None, None
        n = max(len(q) for q in streams)
        for k in range(n):
            for q in streams:
                if k < len(q):
                    a, kw, cb = q[k]
                    idx = self.op(*a, **kw)
                    if cb is not None:
                        cb.append(idx)

    def op(self, eng, fn, reads=(), writes=(), chan=None, cb=None):
        if getattr(self, 'cur_stream', None) is not None:
            self.streams[self.cur_stream].append(((eng, fn), dict(reads=reads, writes=writes, chan=chan), cb))
            return -1
        idx = len(self.ops)
        deps = {}
        for k in reads:
            d = self.last_w.get(k)
            if d is not None:
                deps[d] = True
        for k in writes:
            d = self.last_w.get(k)
            if d is not None:
                deps.setdefault(d, False)
            for r in self.readers.get(k, ()):
                deps.setdefault(r, False)
        deps.pop(idx, None)
        self.ops.append(dict(eng=eng, fn=fn, deps=deps, chan=chan))
        for k in reads:
            self.readers.setdefault(k, []).append(idx)
        for k in writes:
            self.last_w[k] = idx
            self.readers[k] = []
        return idx

    def wait_all(self, eng, idxs):
        idx = len(self.ops)
        self.ops.append(dict(eng=eng, fn=None, deps={d: True for d in idxs}, chan=None))
        return idx

    def _need_wait(self, x, d, raw):
        od, ox = self.ops[d], self.ops[x]
        if od['chan'] is not None:
            return True
        if od['eng'] != ox['eng']:
            return True
        if ox['chan'] is not None:
            return True
        if ox['eng'] == 'pe':
            return False
        return True

    def emit(self):
        nc = self.nc
        ops = self.ops
        needed = [False] * len(ops)
        for x, o in enumerate(ops):
            for d, raw in o['deps'].items():
                if self._need_wait(x, d, raw):
                    needed[d] = True
        chans = []
        for o in ops:
            if o['chan'] is not None and o['chan'] not in chans:
                chans.append(o['chan'])
        with ExitStack() as es:
            sems = {}
            for e in self.COMPUTE:
                sems[e] = es.enter_context(nc.semaphore('s_' + e))
            for c in chans:
                sems[('c', c)] = es.enter_context(nc.semaphore('c_' + str(c)))
            cnt = {k: 0 for k in sems}
            ev = [None] * len(ops)
            for x, o in enumerate(ops):
                if o['fn'] is None:
                    continue
                if o['chan'] is not None:
                    k = ('c', o['chan'])
                    cnt[k] += 16
                    ev[x] = (k, cnt[k])
                elif needed[x]:
                    k = o['eng']
                    cnt[k] += 1
                    ev[x] = (k, cnt[k])
            per_eng = {}
            for x, o in enumerate(ops):
                per_eng.setdefault(o['eng'], []).append(x)
            self.stats = {e: len(v) for e, v in per_eng.items()}
            self.stats['sem_max'] = dict((str(k), v) for k, v in cnt.items() if v > 30000)

            def run(e, ename):
                waited = {}
                for x in per_eng.get(ename, ()):
                    o = ops[x]
                    want = {}
                    for d, raw in o['deps'].items():
                        if not self._need_wait(x, d, raw):
                            continue
                        k, v = ev[d]
                        if v > want.get(k, 0):
                            want[k] = v
                    for k, v in want.items():
                        if v > waited.get(k, 0):
                            e.wait_ge(sems[k], v)
                            waited[k] = v
                    if o['fn'] is None:
                        continue
                    ins = o['fn'](e)
                    if ev[x] is not None:
                        k, v = ev[x]
                        ins.then_inc(sems[k], 16 if o['chan'] is not None else 1)

            with nc.Block() as block:
                @block.tensor
                def _(e):
                    run(e, 'pe')

                @block.scalar
                def _(e):
                    run(e, 'act')

                @block.vector
                def _(e):
                    run(e, 'dve')

                @block.gpsimd
                def _(e):
                    run(e, 'pool')

                @block.sync
                def _(e):
                    run(e, 'sp')


def build():
    nc = bass.Bass("TRN2", target_bir_lowering=False)

    def din(name, shape, dt=F32):
        return nc.dram_tensor(name, list(shape), dt, kind="ExternalInput").ap()

    def dout(name, shape, dt=F32):
        return nc.dram_tensor(name, list(shape), dt, kind="ExternalOutput").ap()

    x_d = din("x", [NTILES * 128, D])
    w_in_d = din("w_in", [D, INW])
    p_a_d = din("p_a", [1024, D])
    p_b_d = din("p_b", [1024, D])
    w_o_d = din("w_o", [D, D])
    cnames = dict(gbc=[128, D], gfbc=[128, D], lnxw=[128, 1024], lnxb=[128, 1024], muT=[128, 25],
                  w0c=[128, 8], a0c=[128, 8], kkc=[128, 8], kac=[128, 8], rkc=[128, 8],
                  sinks=[128, 16], identf=[128, 128], MU4=[128, 512], MLs=[128, 128], bones=[128, 128],
                  bo2=[128, 2], resetm=[128, TB], MU5=[128, 320], I2=[128, 64], DmP=[128, 256], DmP0=[128, 256], DmS=[128, 384],
                  sshT=[128, 25, 4])
    cd = {k: din(k, v) for k, v in cnames.items()}
    wup_d = din("wup", [128, 1024])
    swkv_d = din("swkv", [4, 128, 8, 64])
    ckT_d = din("ckT", [128, 4, 4, 128])
    cv_d = din("cv", [128, 4, 256])
    ck_raw = din("ck_raw", [4, 128, 256])
    cv_raw = din("cv_raw", [4, 128, 256])

    y_o = dout("y", [NTILES * 128, D])
    wkvp_o = dout("wkv_p", [128, 8, 64])
    wkvs_o = dout("wkv_s", [4, 128, 8, 64])
    shp_o = dout("shift_p", [128, 25])
    shs_o = dout("shift_s", [128, 25, 4])
    kp_o = dout("k_p", [128, 256])
    vp_o = dout("v_p", [128, 256])
    ks_o = dout("k_s", [4, 128, 256])
    vs_o = dout("v_s", [4, 128, 256])

    NUNITS = 0
    scr = nc.dram_tensor("wscr", [80, 128, 16 * 256], BF16).ap()

    es = ExitStack()
    with es:
        def sb(name, shape, dt=F32):
            return es.enter_context(nc.sbuf_tensor(name, list(shape), dt))

        def ps(name, shape, dt=F32):
            return es.enter_context(nc.psum_tensor(name, list(shape), dt))

        P = Prog(nc)
        ct = {k: sb("c_" + k, v) for k, v in cnames.items()}
        identb = sb("identb", [128, 128], BF16)
        wupb = sb("wupb", [128, 1024], BF16)
        kc = sb("kc", [128, 4, 4, 128], BF16)
        vc = sb("vc", [128, 4, 256], BF16)
        dummy = sb("dummy_t", [128, 8])
        small = sb("small", [128, 64])
        hT = sb("hT", [128, 16, TB], BF16)
        stage = [sb("stage%d" % i, [128, 8, 256]) for i in range(2)]
        wbf = [sb("wbf%d" % i, [128, 16, 256], BF16) for i in range(2)]
        xt = sb("xt", [128, D])
        hb = sb("hb", [128, D], BF16)
        plast = sb("plast", [128, 25])
        shs = sb("shs", [128, 25, 4])
        pT = sb("pT", [128, TB + 1])
        arenaA = sb("arenaA", [128, 23 * TB])
        xs = [arenaA[:, i * TB:(i + 1) * TB] for i in range(6)]
        tq = [[arenaA[:, (6 + s_ * 8 + i) * TB:(7 + s_ * 8 + i) * TB] for i in range(8)] for s_ in range(2)]
        tmpf = {11: arenaA[:, 22 * TB:23 * TB]}
        xr = arenaA[:, 0:NT * D].rearrange("p (t d) -> p t d", d=D)
        ta_all = arenaA[:, 0:16 * TB].rearrange("p (m t) -> p m t", t=TB)
        lora = sb("lora", [128, TB], BF16)
        xsB_t = sb("xsB", [128, 6 * TB])
        xsB = [xsB_t[:, i * TB:(i + 1) * TB] for i in range(6)]
        arenaC = sb("arenaC", [128, 4 * 8 * TB], BF16)
        opT = [arenaC[:, i * 8 * TB:(i + 1) * 8 * TB].rearrange("p (j t) -> p j t", t=TB) for i in range(4)]
        rT, aT, bT, kT = opT
        mergedT = arenaC[:, 0:16 * TB].rearrange("p (j t) -> p j t", t=TB)
        fin32 = arenaC[:, 16 * TB:32 * TB].bitcast(F32)
        sga = fin32[:, 0:2 * TB].rearrange("p (f t) -> p f t", t=TB)
        sgb = fin32[:, 2 * TB:4 * TB].rearrange("p (f t) -> p f t", t=TB)
        ta = fin32[:, 4 * TB:6 * TB].rearrange("p (f t) -> p f t", t=TB)
        gC = sb("gC", [128, 8, 2 * NT])
        vtok = sb("vtok", [128, NT, 1024], BF16)
        vtk = sb("vtk", [128, 8, 2 * NT, 64], BF16)
        I2b = sb("I2b", [128, 64], BF16)
        LNSS = [[sb("LNS%d_%d" % (k, i), [128, 192], BF16) for i in range(2)] for k in range(2)]
        bon = sb("bon", [128, NT, 16])
        kbtokS = [sb("kbtok%d" % i, [128, 128], BF16) for i in range(2)]
        MmS = [sb("Mm%d" % i, [128, 320], BF16) for i in range(2)]
        XUbS = [sb("XUb%d" % i, [128, 128], BF16) for i in range(2)]
        Pf = sb("Pf", [128, 8, 64])
        Pb = sb("Pb", [128, 8, 64], BF16)
        ysb = sb("ysb", [128, 1024])
        ysq = sb("ysq", [128, 1024])
        sa = sb("sa", [128, NT, 1024], BF16)
        sbg = sb("sbg", [128, NT, 1024], BF16)
        yab = sb("yab", [128, 1024], BF16)
        yaT = sb("yaT", [128, 8, TB], BF16)
        ybT = sb("ybT", [128, 8, TB], BF16)
        qT = sb("qT", [128, 8, TB], BF16)
        kTd = sb("kTd", [128, 4, 128 + TB], BF16)
        vat = sb("vat", [128, 1 + NT, 256], BF16)
        kvo = sb("kvo", [128, NT, 512])
        s_sbS = [sb("s_sb%d" % i, [128, 384]) for i in range(2)]
        e_sbS = [sb("e_sb%d" % i, [128, 384], BF16) for i in range(2)]
        eTS = [sb("eT%d" % i, [128, 384], BF16) for i in range(2)]
        smallS = [sb("smallS%d" % i, [128, 8]) for i in range(2)]
        ob = sb("ob", [128, 1024])
        rden = sb("rden", [128, 16])

        Aacc = [ps("A%d" % i, [128, 512]) for i in range(2)]
        A = [Aacc[i // 2][:, (i % 2) * 256:(i % 2) * 256 + 256] for i in range(4)]
        tp = ps("tp", [128, 1024], BF16)
        Mb = ps("Mb", [128, 512])
        Db = [ps("D%d" % i, [128, 512]) for i in range(2)]
        Cb = ps("Cb", [128, 512])
        Eb = ps("Eb", [128, 512])
        A = A + [Cb[:, 0:256], Cb[:, 256:512]]

        cidx = []
        for k in cnames:
            src = cd[k]
            cidx.append(P.op('pool', (lambda e, o=ct[k], s=src: e.dma_start(out=o[:], in_=s)), writes=[k], chan='const'))
        cidx.append(P.op('pool', lambda e: e.dma_start(out=wupb[:], in_=wup_d), writes=['wupb'], chan='const'))
        cidx.append(P.op('pool', lambda e: e.dma_start(out=kc[:], in_=ckT_d), writes=['kc'], chan='const'))
        cidx.append(P.op('pool', lambda e: e.dma_start(out=vc[:], in_=cv_d), writes=['vc'], chan='const'))
        for eng in ('pe', 'act', 'dve', 'pool'):
            P.wait_all(eng, cidx)
        P.op('dve', lambda e: e.tensor_copy(out=identb[:], in_=ct['identf'][:]), reads=['identf'], writes=['identb'])
        P.op('dve', lambda e: e.tensor_copy(out=I2b[:], in_=ct['I2'][:]), reads=['I2'], writes=['I2b'])
        P.op('dve', lambda e: e.memset(Pf[:], 0.0), writes=['Pf%d' % j for j in range(8)])
        P.op('dve', lambda e: e.memset(Pb[:], 0.0), writes=['Pb%d' % j for j in range(8)])
        P.op('dve', lambda e: e.memset(plast[:], 0.0), writes=['plast'])
        P.op('dve', lambda e: e.memset(kTd[:], 0.0), writes=['kTd'])
        P.op('dve', lambda e: e.memset(vat[:], 0.0), writes=['vat'])
        identf = ct['identf']

        st = dict(uid=0, sidx=0, acc=0, cast=0)
        out_idx = []
        ARENA_PREP = (['xs%d' % i for i in range(6)] + ['tq%d_%d' % (s_, i) for s_ in range(2) for i in range(8)] + ['tmp11']
                      + ['rT', 'aT', 'bT', 'kT'])
        ARENA_FIN = ['xr', 'mergedT', 'sga', 'sgb', 'ta', 'taall']

        def fence(after, before):
            P.op('pool', lambda e: e.memset(dummy[0:1, 0:1], 0.0), writes=list(after) + list(before))

        def load_unit(b, u, W, c0, n, KT, dup=False):
            slot = st['uid'] % 2
            st['uid'] += 1
            if st.get('fix') is not None:
                slot = st['fix']
            wk = 'wbf%d' % slot
            ncols = 256 if dup else n
            if b == 0:
                for half in range(KT // 8):
                    si = st['sidx'] % 2
                    st['sidx'] += 1
                    if st.get('fix') is not None:
                        si = st['fix']
                    src = W[half * 1024:(half + 1) * 1024, c0:c0 + n].rearrange("(k p) n -> p k n", p=128)
                    P.op('sp', (lambda e, si=si, src=src: e.dma_start(out=stage[si][:, :, 0:n], in_=src)),
                         writes=['stage%d' % si], chan='stg%d' % si)
                    ceng = ('pool', 'act')[st['cast'] % 2]
                    st['cast'] += 1
                    if not dup:
                        dst = wbf[slot][:, half * 8:(half + 1) * 8, 0:n]
                        srcs = stage[si][:, :, 0:n]
                        if ceng == 'act':
                            P.op('act', (lambda e, dst=dst, srcs=srcs: e.copy(out=dst, in_=srcs)),
                                 reads=['stage%d' % si], writes=[wk])
                        else:
                            P.op('pool', (lambda e, dst=dst, srcs=srcs: e.tensor_copy(out=dst, in_=srcs)),
                                 reads=['stage%d' % si], writes=[wk])
                    else:
                        for dd in range(2):
                            dst = wbf[slot][:, half * 8:(half + 1) * 8, :].rearrange(
                                "p k (g d c) -> p k g d c", g=2, d=2)[:, :, :, dd, :]
                            srcs = stage[si][:, :, 0:128].rearrange("p k (g c) -> p k g c", g=2)
                            P.op('pool', (lambda e, dst=dst, srcs=srcs: e.tensor_copy(out=dst, in_=srcs)),
                                 reads=['stage%d' % si], writes=[wk])
                P.op('pool', (lambda e, u=u, slot=slot: e.dma_start(
                    out=scr[u % DBG.get("umod", 80), :, 0:KT * ncols].rearrange("p (k n) -> p k n", n=ncols),
                    in_=wbf[slot][:, 0:KT, 0:ncols])),
                    reads=[wk], writes=['scr%d' % u], chan='wst%d' % slot)
            else:
                P.op('sp', (lambda e, u=u, slot=slot: e.dma_start(
                    out=wbf[slot][:, 0:KT, 0:ncols],
                    in_=scr[u % DBG.get("umod", 80), :, 0:KT * ncols].rearrange("p (k n) -> p k n", n=ncols))),
                    reads=['scr%d' % u], writes=[wk], chan='wld%d' % slot)
            return slot

        def nu():
            st['u'] += 1
            return st['u'] - 1

        def next_pair():
            if st.get('pb0'):
                return (0, 1)
            if st.get('pb') is not None:
                return (2 * st['pb'], 2 * st['pb'] + 1)
            k = st['acc'] % 2
            st['acc'] += 1
            return (2 * k, 2 * k + 1)

        def next_acc():
            return next_pair()[0]

        def akey(ai):
            return ('PB0', 'PB1', 'Cb')[ai // 2]

        def mm_fm(slot, KT, f, rhsT, rkey, ai, ncol=TB):
            def fn(e):
                ins = None
                for kt in range(KT):
                    ins = e.matmul(A[ai][:, 0:ncol], lhsT=wbf[slot][:, kt, f * 128:(f + 1) * 128],
                                   rhs=rhsT[:, kt, 0:ncol], start=(kt == 0), stop=(kt == KT - 1))
                return ins
            P.op('pe', fn, reads=['wbf%d' % slot, rkey], writes=[akey(ai)])

        def mm_tm(slot, KT, ti, lhs_tile, lkey, ai, n=256):
            def fn(e):
                ins = None
                for kt in range(KT):
                    ins = e.matmul(A[ai][:, 0:n], lhsT=lhs_tile[:, kt, ti * 128:(ti + 1) * 128],
                                   rhs=wbf[slot][:, kt, 0:n], start=(kt == 0), stop=(kt == KT - 1))
                return ins
            P.op('pe', fn, reads=['wbf%d' % slot, lkey], writes=[akey(ai)])

        def mmk(e, out, lhsT, rhs, kbase):
            if kbase == 0:
                return e.matmul(out, lhsT=lhsT, rhs=rhs, start=True, stop=True)
            e.matmul(out[0:64], lhsT=lhsT[:, 0:64], rhs=rhs, start=True, stop=True)
            return e.matmul(out[64:128], lhsT=lhsT[:, 64:128], rhs=rhs, start=True, stop=True)

        def chunk3(ap):
            return ap.rearrange("p (c t) -> p c t", t=64)

        for b in range(DBG['nblk']):
            sample = (b == NBLK - 1)
            st['u'] = 0
            for ti in range(NT):
                gt = b * NT + ti
                P.op('sp', (lambda e, gt=gt: e.dma_start(out=xt[:], in_=x_d[gt * 128:(gt + 1) * 128, :])),
                     writes=['xt'], chan='xt')
                if DBG.get('s1', 9) < 2:
                    continue
                P.op('dve', lambda e: e.memset(small[:, 0:1], 0.0), writes=['ssq'])
                P.op('act', lambda e: e.activation(out=hb[:], in_=xt[:], func=AF.Square, accum_out=small[:, 0:1]),
                     reads=['xt', 'ssq'], writes=['hb', 'ssq'])
                if DBG.get('s1', 9) < 3:
                    continue
                P.op('dve', lambda e: e.tensor_scalar(out=small[:, 1:2], in0=small[:, 0:1], scalar1=1.0 / D,
                                                      scalar2=RMS_EPS, op0=ALU.mult, op1=ALU.add),
                     reads=['ssq'], writes=['ms'])
                P.op('act', lambda e: e.activation(out=small[:, 2:3], in_=small[:, 1:2], func=AF.Sqrt),
                     reads=['ms'], writes=['sq'])
                P.op('dve', lambda e: e.reciprocal(out=small[:, 3:4], in_=small[:, 2:3]), reads=['sq'], writes=['rstd'])
                P.op('dve', lambda e: e.scalar_tensor_tensor(out=hb[:], in0=xt[:], scalar=small[:, 3:4],
                                                             in1=ct['gbc'][:], op0=ALU.mult, op1=ALU.mult),
                     reads=['xt', 'rstd', 'gbc'], writes=['hb'])
                if DBG.get('s1', 9) < 4:
                    continue
                for half in range(2):
                    def fn(e, half=half):
                        ins = None
                        for k in range(8):
                            kt = half * 8 + k
                            ins = e.transpose(out=tp[:, k * 128:(k + 1) * 128], in_=hb[:, kt * 128:(kt + 1) * 128],
                                              identity=identb[:])
                        return ins
                    P.op('pe', fn, reads=['hb', 'identb'], writes=['tp'])
                    dst = hT[:, half * 8:(half + 1) * 8, ti * 128:(ti + 1) * 128]
                    srcv = tp[:, :].rearrange("p (k t) -> p k t", t=128)
                    if half == 0:
                        P.op('act', (lambda e, dst=dst, srcv=srcv: e.copy(out=dst, in_=srcv)), writes=['hT', 'tp'])
                    else:
                        P.op('dve', (lambda e, dst=dst, srcv=srcv: e.tensor_copy(out=dst, in_=srcv)), writes=['hT', 'tp'])

            if DBG['stage'] <= 1:
                continue
            fence(ARENA_FIN, ARENA_PREP)

            def shift(f, ai, xs_ap, xkey):
                P.op('act', (lambda e: e.copy(out=pT[:, 1:TB + 1], in_=A[ai][:, 0:TB])), writes=['pT', akey(ai)])
                if not sample:
                    P.op('dve', (lambda e: e.tensor_copy(out=pT[:, 0:1], in_=plast[:, f:f + 1])), reads=['plast'], writes=['pT'])
                    P.op('dve', (lambda e: e.tensor_copy(out=plast[:, f:f + 1], in_=pT[:, TB:TB + 1])), reads=['pT'], writes=['plast'])
                else:
                    P.op('dve', (lambda e: e.memset(pT[:, 0:1], 0.0)), writes=['pT'])
                    P.op('dve', (lambda e: e.tensor_copy(out=shs[:, f, :], in_=chunk3(pT[:, 1:TB + 1])[:, :, 63])),
                         reads=['pT'], writes=['shs'])
                t0 = tmpf[11]
                P.op('pool', (lambda e: e.tensor_tensor(out=t0, in0=pT[:, 0:TB], in1=pT[:, 1:TB + 1], op=ALU.subtract)),
                     reads=['pT'], writes=['tmp11'])
                if sample:
                    P.op('dve', (lambda e: e.tensor_tensor(out=chunk3(t0)[:, :, 0], in0=ct['sshT'][:, f, :],
                                                           in1=chunk3(pT[:, 1:TB + 1])[:, :, 0], op=ALU.subtract)),
                         reads=['pT', 'sshT', 'tmp11'], writes=['tmp11'])
                P.op('dve', (lambda e: e.scalar_tensor_tensor(out=xs_ap, in0=t0, scalar=ct['muT'][:, f:f + 1],
                                                              in1=pT[:, 1:TB + 1], op0=ALU.mult, op1=ALU.add)),
                     reads=['tmp11', 'pT', 'muT'], writes=[xkey])

            slot = load_unit(b, nu(), w_in_d, 3072, 128, 16)
            ai = next_acc()
            mm_fm(slot, 16, 0, hT, 'hT', ai)
            shift(24, ai, xs[0], 'xs0')
            P.op('act', lambda e: e.activation(out=lora[0:64, :], in_=xs[0][0:64, :], func=AF.Tanh), reads=['xs0'], writes=['lora'])
            P.op('dve', lambda e: e.tensor_copy(out=lora[64:128, :], in_=xs[0][64:128, :]), reads=['xs0'], writes=['lora'])

            def aux_ga(i):
                slot = load_unit(b, nu(), w_in_d, 3200 + i * 256, 256, 16)
                pr = next_pair()
                for ti in range(NT):
                    mm_tm(slot, 16, ti, hT, 'hT', pr[ti])
                for ti in range(NT):
                    ai = pr[ti]
                    P.op('act', (lambda e, ai=ai, ti=ti, i=i: e.activation(out=sa[:, ti, i * 256:(i + 1) * 256], in_=A[ai][:, 0:256], func=AF.Silu)),
                         writes=['sa', akey(ai)])

            def aux_q(i):
                slot = load_unit(b, nu(), w_in_d, 4224 + i * 256, 256, 16)
                pr = next_pair()
                for f in range(2):
                    mm_fm(slot, 16, f, hT, 'hT', pr[f])
                for f in range(2):
                    ai = pr[f]
                    P.op('act', (lambda e, ai=ai, i=i, f=f: e.activation(out=qT[:, i * 2 + f, :], in_=A[ai][:, 0:TB], func=AF.Copy, scale=0.125)),
                         writes=['qT', akey(ai)])

            def aux_kd(i):
                slot = load_unit(b, nu(), w_in_d, 5248 + i * 128, 128, 16, dup=True)
                pr = next_pair()
                for f in range(2):
                    mm_fm(slot, 16, f, hT, 'hT', pr[f])
                for f in range(2):
                    ai = pr[f]
                    P.op('dve', (lambda e, ai=ai, i=i, f=f: e.tensor_copy(out=kTd[:, i * 2 + f, 128:128 + TB], in_=A[ai][:, 0:TB])),
                         writes=['kTd', akey(ai)])

            def aux_kv(i):
                slot = load_unit(b, nu(), w_in_d, 5248 + i * 256, 256, 16)
                pr = next_pair()
                for ti in range(NT):
                    mm_tm(slot, 16, ti, hT, 'hT', pr[ti])
                for ti in range(NT):
                    ai = pr[ti]
                    P.op('act', (lambda e, ai=ai, ti=ti, i=i: e.copy(out=kvo[:, ti, i * 256:(i + 1) * 256], in_=A[ai][:, 0:256])),
                         writes=['kvo', akey(ai)])
                    if i == 1:
                        P.op('act', (lambda e, ai=ai, ti=ti: e.copy(out=vat[:, 1 + ti, :], in_=A[ai][:, 0:256])),
                             writes=['vat', akey(ai)])

            def aux_gb(i):
                slot = load_unit(b, nu(), w_in_d, 5760 + i * 256, 256, 16)
                pr = next_pair()
                for ti in range(NT):
                    mm_tm(slot, 16, ti, hT, 'hT', pr[ti])
                for ti in range(NT):
                    ai = pr[ti]
                    P.op('act', (lambda e, ai=ai, ti=ti, i=i: e.activation(out=sbg[:, ti, i * 256:(i + 1) * 256], in_=A[ai][:, 0:256], func=AF.Silu)),
                         writes=['sbg', akey(ai)])

            def aux_carry():
                if 0 < b and not sample:
                    P.op('pool', lambda e: e.tensor_copy(out=kTd[:, :, 0:128], in_=kTd[:, :, TB:TB + 128]), reads=['kTd'], writes=['kTd'])
                    P.op('pool', lambda e: e.tensor_copy(out=vat[:, 0, :], in_=vat[:, NT, :]), reads=['vat'], writes=['vat'])

            AUX = [
                [lambda: aux_ga(0), lambda: aux_ga(1), lambda: aux_ga(2), lambda: aux_ga(3)],
                [lambda: aux_q(0), lambda: aux_q(1), lambda: aux_q(2), lambda: aux_q(3)],
                [aux_carry, lambda: aux_kd(0), lambda: aux_kd(1), lambda: aux_kv(0), lambda: aux_kv(1)],
                [lambda: aux_gb(0), lambda: aux_gb(1), lambda: aux_gb(2), lambda: aux_gb(3)],
            ]

            XSETS = [(xs, ['xs%d' % i for i in range(6)]), (xsB, ['xb%d' % i for i in range(6)])]

            def emit_proj(g2, XS, XK):
                for kind in range(3):
                    slot = load_unit(b, nu(), w_in_d, kind * 1024 + g2 * 256, 256, 16)
                    pr = next_pair()
                    for f in range(2):
                        mm_fm(slot, 16, f, hT, 'hT', pr[f])
                    for f in range(2):
                        shift(kind * 8 + g2 * 2 + f, pr[f], XS[kind * 2 + f], XK[kind * 2 + f])

            def prep_pair(j, sx, xr_, xk_, xv_, kr, kk_, kv):
                T = tq[sx]
                K = ['tq%d_%d' % (sx, q) for q in range(8)]
                sg, cs, eg, eig, alr, kk2, kkn, b32 = T
                k_sg, k_cs, k_eg, k_eig, k_alr, k_kk2, k_kkn, k_b32 = K
                egm, k_egm = cs, k_cs
                rn, k_rn = kk2, k_kk2
                t1, k_t1 = sg, k_sg
                jc = slice(j * 128, (j + 1) * 128)
                if sx == 0:
                    A1, A2, A3 = A[2][:, 0:TB], A[3][:, 0:TB], A[2][:, 0:TB]
                    ak = 'PB1'
                else:
                    A1, A2, A3 = Mb[:, 0:TB], Mb[:, 256:256 + TB], Mb[:, 0:TB]
                    ak = 'Mb'
                tv = sx * 128
                tk = 256 + sx * 256
                P.op('pe', (lambda e: e.matmul(A1, lhsT=wupb[0:64, jc], rhs=lora[0:64, :], start=True, stop=True)),
                     reads=['wupb', 'lora'], writes=[ak])
                P.op('pe', (lambda e: mmk(e, A2, wupb[64:128, jc], lora[64:128, :], 64)),
                     reads=['wupb', 'lora'], writes=[ak])
                P.op('act', (lambda e: e.activation(out=sg, in_=A1, func=AF.Sigmoid, bias=ct['w0c'][:, j:j + 1])),
                     reads=['w0c'], writes=[k_sg, ak])
                P.op('act', (lambda e: e.activation(out=alr, in_=A2, func=AF.Sigmoid, bias=ct['a0c'][:, j:j + 1])),
                     reads=['a0c'], writes=[k_alr, ak])
                P.op('dve', (lambda e: e.tensor_tensor_scan(out=cs, data0=ct['resetm'][:], data1=sg, initial=0.0, op0=ALU.mult, op1=ALU.add)),
                     reads=[k_sg, 'resetm'], writes=[k_cs])
                P.op('act', (lambda e: e.activation(out=eg, in_=cs, func=AF.Exp, scale=-C0)), reads=[k_cs], writes=[k_eg])
                P.op('act', (lambda e: e.activation(out=eig, in_=cs, func=AF.Exp, scale=C0)), reads=[k_cs], writes=[k_eig])
                P.op('dve', (lambda e: e.tensor_tensor(out=t1, in0=cs, in1=sg, op=ALU.subtract)), reads=[k_cs, k_sg], writes=[k_t1])
                P.op('act', (lambda e: e.activation(out=egm, in_=t1, func=AF.Exp, scale=-C0)), reads=[k_t1], writes=[k_egm])
                P.op('dve', (lambda e: e.tensor_copy(out=gC[:, j, :], in_=chunk3(eg)[:, :, 63])), reads=[k_eg], writes=['gC%d' % j])
                P.op('act', (lambda e: e.activation(out=kk2, in_=xk_, func=AF.Square, scale=ct['kkc'][:, j:j + 1])),
                     reads=[kk_, 'kkc'], writes=[k_kk2])
                P.op('pe', (lambda e: e.matmul(A3, lhsT=ct['bones'][:], rhs=kk2, start=True, stop=True)),
                     reads=['bones', k_kk2], writes=[ak])
                P.op('act', (lambda e: e.activation(out=rn, in_=A3, func=AF.Sqrt)), writes=[k_rn, ak])
                P.op('dve', (lambda e: e.tensor_scalar(out=rn, in0=rn, scalar1=1e-12, scalar2=None, op0=ALU.max)), reads=[k_rn], writes=[k_rn])
                P.op('dve', (lambda e: e.reciprocal(out=rn, in_=rn)), reads=[k_rn], writes=[k_rn])
                P.op('dve', (lambda e: e.scalar_tensor_tensor(out=kkn, in0=xk_, scalar=ct['kkc'][:, j:j + 1], in1=rn, op0=ALU.mult, op1=ALU.mult)),
                     reads=[kk_, 'kkc', k_rn], writes=[k_kkn])
                P.op('dve', (lambda e: e.tensor_scalar(out=t1, in0=alr, scalar1=-1.0, scalar2=ct['kac'][:, j:j + 1], op0=ALU.add, op1=ALU.mult)),
                     reads=[k_alr, 'kac'], writes=[k_t1])
                P.op('dve', (lambda e: e.scalar_tensor_tensor(out=t1, in0=t1, scalar=1.0, in1=xk_, op0=ALU.add, op1=ALU.mult)),
                     reads=[k_t1, kk_], writes=[k_t1])
                P.op('dve', (lambda e: e.tensor_tensor(out=rT[:, j, :], in0=xr_, in1=eg, op=ALU.mult)), reads=[kr, k_eg], writes=['rT'])
                P.op('dve', (lambda e: e.scalar_tensor_tensor(out=aT[:, j, :], in0=kkn, scalar=-1.0, in1=egm, op0=ALU.mult, op1=ALU.mult)),
                     reads=[k_kkn, k_egm], writes=['aT'])
                P.op('dve', (lambda e: e.tensor_tensor(out=b32, in0=kkn, in1=alr, op=ALU.mult)), reads=[k_kkn, k_alr], writes=[k_b32])
                P.op('dve', (lambda e: e.tensor_tensor(out=bT[:, j, :], in0=b32, in1=eig, op=ALU.mult)), reads=[k_b32, k_eig], writes=['bT'])
                P.op('dve', (lambda e: e.tensor_tensor(out=kT[:, j, :], in0=t1, in1=eig, op=ALU.mult)), reads=[k_t1, k_eig], writes=['kT'])
                P.op('dve', (lambda e: e.scalar_tensor_tensor(out=kk2, in0=xr_, scalar=ct['rkc'][:, j:j + 1], in1=t1, op0=ALU.mult, op1=ALU.mult)),
                     reads=[kr, 'rkc', k_t1], writes=[k_kk2])
                vb = b32.bitcast(BF16)[:, 0:TB]
                P.op('act', (lambda e: e.copy(out=vb, in_=xv_)), reads=[kv], writes=[k_b32])

                def fnVT(e):
                    ins = None
                    for ci_ in range(2 * NT):
                        for hp in (slice(0, 64), slice(64, 128)):
                            ins = e.transpose(out=tp[hp, tk + ci_ * 64:tk + ci_ * 64 + 64], in_=vb[hp, ci_ * 64:(ci_ + 1) * 64], identity=identb[hp, hp])
                    return ins
                P.op('pe', fnVT, reads=[k_b32, 'identb'], writes=['tp'])
                P.op('act', (lambda e: e.copy(out=vtk[:, j, :, :], in_=tp[:, tk:tk + 2 * NT * 64].rearrange("p (c v) -> p c v", v=64))),
                     writes=['vtk', 'tp'])
                for ti in range(NT):
                    tcs = slice(ti * 128, (ti + 1) * 128)
                    P.op('pe', (lambda e, ti=ti, tcs=tcs: e.matmul(Eb[:, ti * 16 + j * 2:ti * 16 + j * 2 + 2], lhsT=kk2[:, tcs], rhs=ct['bo2'][:], start=True, stop=True)),
                         reads=[k_kk2, 'bo2'], writes=['Eb'])
                    P.op('pe', (lambda e, tcs=tcs: e.transpose(out=tp[:, tv:tv + 128], in_=vb[:, tcs], identity=identb[:])),
                         reads=[k_b32, 'identb'], writes=['tp'])
                    P.op('act', (lambda e, ti=ti: e.copy(out=vtok[:, ti, jc], in_=tp[:, tv:tv + 128])), writes=['vtok', 'tp'])

            for it in range(5):
                P.begin_streams(4)
                if it < 4:
                    P.set_stream(3)
                    st['pb'] = 2
                    st['fix'] = 1
                    for task in AUX[it]:
                        task()
                    st['pb'] = None
                    st['fix'] = None
                if it < 4:
                    P.set_stream(0)
                    st['pb'] = 0
                    st['fix'] = 0
                    emit_proj(it, *XSETS[it % 2])
                    st['pb'] = None
                    st['fix'] = None
                if it > 0:
                    XS, XK = XSETS[(it - 1) % 2]
                    for jj in range(2):
                        P.set_stream(1 + jj)
                        prep_pair((it - 1) * 2 + jj, jj, XS[jj], XS[2 + jj], XS[4 + jj], XK[jj], XK[2 + jj], XK[4 + jj])
                P.merge_streams()
            P.op('dve', lambda e: e.tensor_copy(out=bon[:].rearrange("p t h -> p (t h)"), in_=Eb[:, 0:NT * 16]), writes=['bon', 'Eb'])

            if DBG['stage'] <= 3:
                continue
            H2 = (slice(0, 64), slice(64, 128))
            for ti in range(NT):
                gt = b * NT + ti
                tcs = slice(ti * 128, (ti + 1) * 128)
                for c in range(2):
                    cp = slice(c * 64, c * 64 + 64)
                    cc = slice(ti * 128 + c * 64, ti * 128 + c * 64 + 64)
                    ci = ti * 2 + c
                    P.begin_streams(2)
                    for j in range(8):
                        sx = (j % 2) if DBG.get('ss', 1) else 0
                        P.set_stream(sx)
                        kbtok, Mmx, LNS, XUb = kbtokS[sx], MmS[sx], LNSS[sx], XUbS[sx]
                        MC = Mb if sx == 0 else Cb
                        MCk = 'Mb' if sx == 0 else 'Cb'
                        DD = Db[0][:, 0:192] if sx == 0 else Eb[:, 192:384]
                        DDk = 'D0' if sx == 0 else 'Eb'
                        kX, kM, kL = 'X%d' % sx, 'Mm%d' % sx, 'LNS%d_' % sx
                        if sample:
                            seq = ti * 2 + c
                            P.op('sp', (lambda e, seq=seq, j=j: e.dma_start(out=Pf[:, j, :], in_=swkv_d[seq, :, j, :])),
                                 writes=['Pf%d' % j], chan='pst%d' % j)
                            P.op('dve', (lambda e, j=j: e.tensor_copy(out=Pb[:, j, :], in_=Pf[:, j, :])), reads=['Pf%d' % j], writes=['Pb%d' % j])
                        tpo = sx * 128

                        def fnT(e, j=j, cc=cc, tpo=tpo):
                            ins = None
                            for hp in H2:
                                e.transpose(out=tp[hp, tpo:tpo + 64], in_=kT[hp, j, cc], identity=identb[hp, hp])
                                ins = e.transpose(out=tp[hp, tpo + 64:tpo + 128], in_=bT[hp, j, cc], identity=identb[hp, hp])
                            return ins
                        P.op('pe', fnT, reads=['kT', 'bT', 'identb'], writes=['tp'])
                        P.op('act', (lambda e, kbtok=kbtok, tpo=tpo: e.copy(out=kbtok[:, 0:128], in_=tp[:, tpo:tpo + 128])), writes=['kbtok%d' % sx, 'tp'])

                        def fnM(e, j=j, cc=cc, MC=MC):
                            ins = None
                            for hp in H2:
                                e.matmul(MC[hp, 0:64], lhsT=bT[hp, j, cc], rhs=aT[hp, j, cc], start=True, stop=True)
                                e.matmul(MC[hp, 64:128], lhsT=kT[hp, j, cc], rhs=aT[hp, j, cc], start=True, stop=True)
                                e.matmul(MC[hp, 128:192], lhsT=bT[hp, j, cc], rhs=rT[hp, j, cc], start=True, stop=True)
                                e.matmul(MC[hp, 192:256], lhsT=kT[hp, j, cc], rhs=rT[hp, j, cc], start=True, stop=True)
                                ins = e.matmul(MC[hp, 256:320], lhsT=aT[hp, j, cc], rhs=bT[hp, j, cc], start=True, stop=True)
                            return ins
                        P.op('pe', fnM, reads=['aT', 'bT', 'kT', 'rT'], writes=[MCk])
                        P.op('dve', (lambda e, Mmx=Mmx, MC=MC: e.tensor_tensor(out=Mmx[:, 0:320], in0=MC[:, 0:320], in1=ct['MU5'][:], op=ALU.mult)),
                             reads=['MU5'], writes=[kM, MCk])
                        P.op('pool', (lambda e, LNS=LNS, Mmx=Mmx: e.tensor_tensor(out=LNS[0][:, 128:192], in0=Mmx[:, 0:64], in1=I2b[:], op=ALU.add)),
                             reads=[kM, 'I2b'], writes=[kL + '0'])
                        for lvl in range(1, 7):
                            cur, nxt = (lvl - 1) % 2, lvl % 2
                            if lvl == 1:
                                Lc, Nc = Mmx[:, 256:320], Mmx[:, 0:64]
                                rk = [kM, kL + '0']
                            else:
                                Lc, Nc = LNS[cur][:, 0:64], LNS[cur][:, 64:128]
                                rk = [kL + str(cur)]
                            Sc = LNS[cur][:, 128:192]

                            def fnD(e, Lc=Lc, Nc=Nc, Sc=Sc, lvl=lvl, DD=DD):
                                ins = None
                                for hp in H2:
                                    if lvl < 6:
                                        e.matmul(DD[hp, 0:64], lhsT=Nc[hp], rhs=Lc[hp], start=True, stop=True)
                                    if lvl < 5:
                                        e.matmul(DD[hp, 64:128], lhsT=Lc[hp], rhs=Nc[hp], start=True, stop=True)
                                    if lvl == 1:
                                        ins = e.matmul(DD[hp, 128:192], lhsT=I2b[hp], rhs=Sc[hp], start=True, stop=True)
                                    else:
                                        e.matmul(DD[hp, 128:192], lhsT=I2b[hp], rhs=Sc[hp], start=True, stop=False)
                                        ins = e.matmul(DD[hp, 128:192], lhsT=Lc[hp], rhs=Sc[hp], start=False, stop=True)
                                return ins
                            P.op('pe', fnD, reads=rk + ['I2b'], writes=[DDk])
                            lo = 0 if lvl < 6 else 128
                            if (lvl + sx) % 2 == 1:
                                P.op('dve', (lambda e, LNS=LNS, nxt=nxt, lo=lo, DD=DD: e.tensor_copy(out=LNS[nxt][:, lo:192], in_=DD[:, lo:192])),
                                     writes=[kL + str(nxt), DDk])
                            else:
                                P.op('act', (lambda e, LNS=LNS, nxt=nxt, lo=lo, DD=DD: e.copy(out=LNS[nxt][:, lo:192], in_=DD[:, lo:192])),
                                     writes=[kL + str(nxt), DDk])

                        def fnX(e, j=j, cc=cc, ci=ci, MC=MC, Mmx=Mmx):
                            ins = None
                            for hp in H2:
                                e.matmul(MC[hp, 320:384], lhsT=aT[hp, j, cc], rhs=Pb[hp, j, :], start=True, stop=False)
                                ins = e.matmul(MC[hp, 320:384], lhsT=Mmx[hp, 64:128], rhs=vtk[hp, j, ci, :], start=False, stop=True)
                            return ins
                        P.op('pe', fnX, reads=['aT', 'Pb%d' % j, kM, 'vtk'], writes=[MCk])
                        P.op('dve', (lambda e, XUb=XUb, MC=MC: e.tensor_copy(out=XUb[:, 0:64], in_=MC[:, 320:384])), writes=[kX + 'x', MCk])

                        def fnU(e, MC=MC, LNS=LNS, XUb=XUb):
                            ins = None
                            for hp in H2:
                                ins = e.matmul(MC[hp, 384:448], lhsT=LNS[0][hp, 128:192], rhs=XUb[hp, 0:64], start=True, stop=True)
                            return ins
                        P.op('pe', fnU, reads=[kX + 'x', kL + '0'], writes=[MCk])
                        P.op('act', (lambda e, XUb=XUb, MC=MC: e.copy(out=XUb[:, 64:128], in_=MC[:, 384:448])), writes=[kX + 'u', MCk])

                        def fnO(e, j=j, cp=cp, cc=cc, ci=ci, Mmx=Mmx, XUb=XUb):
                            ins = None
                            for hh, hp in enumerate(H2):
                                ob_ = Db[1][cp, j * 64:j * 64 + 64] if hh == 0 else Aacc[1][cp, j * 64:j * 64 + 64]
                                e.matmul(ob_, lhsT=rT[hp, j, cc], rhs=Pb[hp, j, :], start=True, stop=False)
                                e.matmul(ob_, lhsT=Mmx[hp, 128:192], rhs=XUb[hp, 64:128], start=False, stop=False)
                                ins = e.matmul(ob_, lhsT=Mmx[hp, 192:256], rhs=vtk[hp, j, ci, :], start=False, stop=True)
                            return ins
                        P.op('pe', fnO, reads=['rT', 'Pb%d' % j, kM, kX + 'u', 'vtk'], writes=['D1', 'PB1'])

                        def fnP(e, j=j, ci=ci, MC=MC, kbtok=kbtok, XUb=XUb):
                            ins = None
                            for hp in H2:
                                e.matmul(MC[hp, 448:512], lhsT=identf[hp, hp], rhs=Pf[hp, j, :], start=True, stop=False)
                                e.matmul(MC[hp, 448:512], lhsT=kbtok[hp, 64:128], rhs=XUb[hp, 64:128], start=False, stop=False)
                                ins = e.matmul(MC[hp, 448:512], lhsT=kbtok[hp, 0:64], rhs=vtk[hp, j, ci, :], start=False, stop=True)
                            return ins
                        P.op('pe', fnP, reads=['identf', 'Pf%d' % j, 'kbtok%d' % sx, kX + 'u', 'vtk'], writes=[MCk])
                        gcol = gC[:, j, ci:ci + 1]
                        P.op('act', (lambda e, j=j, gcol=gcol, MC=MC: e.activation(out=Pf[:, j, :], in_=MC[:, 448:512], func=AF.Copy, scale=gcol)),
                             reads=['gC%d' % j], writes=['Pf%d' % j, MCk])
                        P.op('dve', (lambda e, j=j, gcol=gcol, MC=MC: e.tensor_scalar(out=Pb[:, j, :], in0=MC[:, 448:512], scalar1=gcol, scalar2=None, op0=ALU.mult)),
                             reads=['gC%d' % j], writes=['Pb%d' % j, MCk])
                        if sample:
                            seq = ti * 2 + c
                            P.op('pool', (lambda e, seq=seq, j=j: e.dma_start(out=wkvs_o[seq, :, j, :], in_=Pf[:, j, :])),
                                 reads=['Pf%d' % j], chan='o_pf%d' % j, cb=out_idx)
                    P.merge_streams()
                y4 = ysb[:].rearrange("p (j h c) -> p j h c", h=2, c=64)
                if DBG.get('oe', 0) == 0:
                    P.op('dve', lambda e: e.tensor_copy(out=y4[:, :, 0, :], in_=Db[1][:, :].rearrange("p (j c) -> p j c", c=64)), writes=['ysb', 'D1'])
                    P.op('act', lambda e: e.copy(out=y4[:, :, 1, :], in_=Aacc[1][:, :].rearrange("p (j c) -> p j c", c=64)), writes=['ysb', 'PB1'])
                else:
                    for j in range(8):
                        P.op('dve', (lambda e, j=j: e.tensor_copy(out=ysb[:, j * 128:j * 128 + 64], in_=Db[1][:, j * 64:j * 64 + 64])), writes=['ysb', 'D1'])
                        P.op('act', (lambda e, j=j: e.copy(out=ysb[:, j * 128 + 64:j * 128 + 128], in_=Aacc[1][:, j * 64:j * 64 + 64])), writes=['ysb', 'PB1'])
                if gt == 15:
                    out_idx.append(P.op('pool', lambda e: e.dma_start(out=wkvp_o, in_=Pf[:]), reads=['Pf%d' % j for j in range(8)], chan='o_pfp'))

                if DBG.get('dump', 0):
                    out_idx.append(P.op('pool', (lambda e, gt=gt: e.dma_start(out=y_o[(gt + 4) * 128:(gt + 5) * 128, 0:1024], in_=ysb[:])), reads=['ysb'], chan='o_dbg'))
                y3 = ysb[:].rearrange("p (h c) -> p h c", c=64)
                q3 = ysq[:].rearrange("p (h c) -> p h c", c=64)
                P.op('dve', lambda e: e.tensor_reduce(out=small[:, 8:24], in_=y3, axis=AX.X, op=ALU.add), reads=['ysb'], writes=['gn_s1'])
                P.op('act', lambda e: e.activation(out=ysq[:], in_=ysb[:], func=AF.Square), reads=['ysb'], writes=['ysq'])
                P.op('dve', lambda e: e.tensor_reduce(out=small[:, 24:40], in_=q3, axis=AX.X, op=ALU.add), reads=['ysq'], writes=['gn_s2'])
                P.op('dve', lambda e: e.tensor_scalar(out=small[:, 40:56], in0=small[:, 8:24], scalar1=1.0 / 64, scalar2=None, op0=ALU.mult),
                     reads=['gn_s1'], writes=['gn_mean'])
                P.op('dve', lambda e: e.tensor_tensor(out=small[:, 8:24], in0=small[:, 40:56], in1=small[:, 40:56], op=ALU.mult),
                     reads=['gn_mean', 'gn_s1'], writes=['gn_s1'])
                P.op('dve', lambda e: e.scalar_tensor_tensor(out=small[:, 24:40], in0=small[:, 24:40], scalar=1.0 / 64, in1=small[:, 8:24], op0=ALU.mult, op1=ALU.subtract),
                     reads=['gn_s2', 'gn_s1'], writes=['gn_s2'])
                P.op('dve', lambda e: e.tensor_scalar(out=small[:, 24:40], in0=small[:, 24:40], scalar1=LNX_EPS, scalar2=None, op0=ALU.add),
                     reads=['gn_s2'], writes=['gn_s2'])
                P.op('act', lambda e: e.activation(out=small[:, 24:40], in_=small[:, 24:40], func=AF.Sqrt), reads=['gn_s2'], writes=['gn_s2'])
                P.op('dve', lambda e: e.reciprocal(out=small[:, 24:40], in_=small[:, 24:40]), reads=['gn_s2'], writes=['gn_s2'])
                P.op('dve', lambda e: e.tensor_tensor(out=y3, in0=y3, in1=small[:, 40:56].unsqueeze(2).to_broadcast([128, 16, 64]), op=ALU.subtract),
                     reads=['ysb', 'gn_mean'], writes=['ysb'])
                P.op('dve', lambda e: e.tensor_tensor(out=y3, in0=y3, in1=small[:, 24:40].unsqueeze(2).to_broadcast([128, 16, 64]), op=ALU.mult),
                     reads=['ysb', 'gn_s2'], writes=['ysb'])
                P.op('dve', lambda e: e.tensor_tensor(out=ysb[:], in0=ysb[:], in1=ct['lnxw'][:], op=ALU.mult), reads=['ysb', 'lnxw'], writes=['ysb'])
                P.op('dve', lambda e: e.tensor_tensor(out=ysb[:], in0=ysb[:], in1=ct['lnxb'][:], op=ALU.add), reads=['ysb', 'lnxb'], writes=['ysb'])
                P.op('dve', (lambda e, ti=ti: e.tensor_tensor(out=q3, in0=vtok[:, ti, :].rearrange("p (h c) -> p h c", c=64),
                                                              in1=bon[:, ti, :].unsqueeze(2).to_broadcast([128, 16, 64]), op=ALU.mult)),
                     reads=['vtok', 'bon', 'ysq'], writes=['ysq'])
                P.op('dve', lambda e: e.tensor_tensor(out=ysb[:], in0=ysb[:], in1=ysq[:], op=ALU.add), reads=['ysb', 'ysq'], writes=['ysb'])
                P.op('dve', (lambda e, ti=ti: e.tensor_tensor(out=yab[:], in0=ysb[:], in1=sa[:, ti, :], op=ALU.mult)), reads=['ysb', 'sa'], writes=['yab'])

                if DBG.get('dump', 0):
                    out_idx.append(P.op('pool', (lambda e, gt=gt: e.dma_start(out=y_o[(gt + 8) * 128:(gt + 9) * 128, 0:1024], in_=yab[:])), reads=['yab'], chan='o_dbg'))
                def fn(e):
                    ins = None
                    for k in range(8):
                        ins = e.transpose(out=tp[:, k * 128:(k + 1) * 128], in_=yab[:, k * 128:(k + 1) * 128], identity=identb[:])
                    return ins
                P.op('pe', fn, reads=['yab', 'identb'], writes=['tp'])
                P.op('act', (lambda e, tcs=tcs: e.copy(out=yaT[:, :, tcs], in_=tp[:, :].rearrange("p (k t) -> p k t", t=128))), writes=['yaT', 'tp'])

            if DBG['stage'] <= 4:
                continue
            for ti in range(NT):
                gt = b * NT + ti
                if gt == 15:
                    out_idx.append(P.op('pool', (lambda e, ti=ti: e.dma_start(out=kp_o, in_=kvo[:, ti, 0:256])), reads=['kvo'], chan='o_kv'))
                    out_idx.append(P.op('pool', (lambda e, ti=ti: e.dma_start(out=vp_o, in_=kvo[:, ti, 256:512])), reads=['kvo'], chan='o_kv'))
                if sample:
                    for c in range(2):
                        seq = ti * 2 + c
                        cp = slice(c * 64, c * 64 + 64)
                        out_idx.append(P.op('pool', (lambda e, ti=ti, seq=seq, cp=cp: e.dma_start(out=ks_o[seq, 64:128, :], in_=kvo[cp, ti, 0:256])), reads=['kvo'], chan='o_kv'))
                        out_idx.append(P.op('pool', (lambda e, ti=ti, seq=seq, cp=cp: e.dma_start(out=vs_o[seq, 64:128, :], in_=kvo[cp, ti, 256:512])), reads=['kvo'], chan='o_kv'))
                        out_idx.append(P.op('pool', (lambda e, seq=seq: e.dma_start(out=ks_o[seq, 0:64, :], in_=ck_raw[seq, 64:128, :])), chan='o_kv'))
                        out_idx.append(P.op('pool', (lambda e, seq=seq: e.dma_start(out=vs_o[seq, 0:64, :], in_=cv_raw[seq, 64:128, :])), chan='o_kv'))

            if DBG['stage'] <= 5:
                continue
            fence(ARENA_PREP, ARENA_FIN)

            def aux_ma(i):
                slot = load_unit(b, nu(), w_in_d, 6784 + i * 256, 256, 16)
                pr = next_pair()
                for f in range(2):
                    mm_fm(slot, 16, f, hT, 'hT', pr[f])
                for f in range(2):
                    ai = pr[f]
                    P.op('act', (lambda e, ai=ai, f=f: e.activation(out=sga[:, f, :], in_=A[ai][:, 0:TB], func=AF.Sigmoid)), writes=['sga', akey(ai)])
                slot = load_unit(b, nu(), p_a_d, i * 256, 256, 8)
                pr = next_pair()
                for f in range(2):
                    mm_fm(slot, 8, f, yaT, 'yaT', pr[f])
                for f in range(2):
                    ai = pr[f]
                    P.op('dve', (lambda e, ai=ai, f=f, i=i: e.tensor_tensor(out=ta_all[:, i * 2 + f, :], in0=sga[:, f, :], in1=A[ai][:, 0:TB], op=ALU.mult)),
                         reads=['sga'], writes=['taall', akey(ai)])
            for ti in range(NT):
                gt = b * NT + ti
                tcs = slice(ti * 128, (ti + 1) * 128)
                nkb = 3 if sample else 2
                nk = nkb * 128
                Dm = ct['DmS'] if sample else (ct['DmP0'] if gt == 0 else ct['DmP'])
                Dk = 'DmS' if sample else ('DmP0' if gt == 0 else 'DmP')
                P.begin_streams(3)
                P.set_stream(2)
                for i_ in range(ti * 4, ti * 4 + 4):
                    aux_ma(i_)
                for h in range(16):
                    sx = (h % 2) if DBG.get('as', 1) else 0
                    P.set_stream(sx)
                    g = h // 4
                    f = h // 2
                    hp = slice((h % 2) * 64, (h % 2) * 64 + 64)
                    SB = Mb if sx == 0 else Cb
                    SBk = 'Mb' if sx == 0 else 'Cb'
                    s_x, e_x, eT_x, sm = s_sbS[sx], e_sbS[sx], eTS[sx], smallS[sx]
                    ks = 'at%d_' % sx
                    tpo = sx * 512

                    def fnS(e, g=g, f=f, hp=hp, ti=ti, tcs=tcs, sample=sample, SB=SB):
                        kb_ = hp.start
                        if not sample:
                            return mmk(e, SB[:, 0:256], qT[hp, f, tcs], kTd[hp, g, ti * 128:ti * 128 + 256], kb_)
                        mmk(e, SB[:, 0:128], qT[hp, f, tcs], kc[hp, ti * 2, g, :], kb_)
                        mmk(e, SB[:, 128:256], qT[hp, f, tcs], kc[hp, ti * 2 + 1, g, :], kb_)
                        return mmk(e, SB[:, 256:384], qT[hp, f, tcs], kTd[hp, g, 128 + ti * 128:256 + ti * 128], kb_)
                    P.op('pe', fnS, reads=['qT', 'kTd', 'kc'], writes=[SBk])
                    P.op('dve', (lambda e, h=h, nk=nk, Dm=Dm, s_x=s_x, SB=SB: e.scalar_tensor_tensor(out=s_x[:, 0:nk], in0=Dm[:, 0:nk], scalar=SLOPES[h], in1=SB[:, 0:nk], op0=ALU.mult, op1=ALU.add)),
                         reads=[Dk], writes=[ks + 's', SBk])
                    P.op('dve', (lambda e, nk=nk, s_x=s_x, sm=sm: e.tensor_reduce(out=sm[:, 0:1], in_=s_x[:, 0:nk], axis=AX.X, op=ALU.max)), reads=[ks + 's'], writes=[ks + 'mx'])
                    P.op('dve', (lambda e, h=h, sm=sm: e.tensor_scalar(out=sm[:, 1:2], in0=sm[:, 0:1], scalar1=ct['sinks'][:, h:h + 1], scalar2=-1.0, op0=ALU.max, op1=ALU.mult)),
                         reads=[ks + 'mx', 'sinks'], writes=[ks + 'negm'])
                    P.op('dve', (lambda e, sm=sm: e.memset(sm[:, 2:3], 0.0)), writes=[ks + 'rs'])
                    P.op('act', (lambda e, nk=nk, s_x=s_x, e_x=e_x, sm=sm: e.activation(out=e_x[:, 0:nk], in_=s_x[:, 0:nk], func=AF.Exp, bias=sm[:, 1:2], accum_out=sm[:, 2:3])),
                         reads=[ks + 's', ks + 'negm', ks + 'rs'], writes=[ks + 'e', ks + 'rs'])
                    P.op('act', (lambda e, h=h, sm=sm: e.activation(out=sm[:, 3:4], in_=sm[:, 1:2], func=AF.Exp, bias=ct['sinks'][:, h:h + 1])),
                         reads=[ks + 'negm', 'sinks'], writes=[ks + 'es'])
                    P.op('dve', (lambda e, sm=sm: e.tensor_tensor(out=sm[:, 3:4], in0=sm[:, 3:4], in1=sm[:, 2:3], op=ALU.add)), reads=[ks + 'es', ks + 'rs'], writes=[ks + 'es'])
                    P.op('dve', (lambda e, h=h, sm=sm: e.reciprocal(out=rden[:, h:h + 1], in_=sm[:, 3:4])), reads=[ks + 'es'], writes=['rden%d' % h])

                    def fnT(e, nkb=nkb, e_x=e_x, tpo=tpo):
                        ins = None
                        for kb in range(nkb):
                            ins = e.transpose(out=tp[:, tpo + kb * 128:tpo + (kb + 1) * 128], in_=e_x[:, kb * 128:(kb + 1) * 128], identity=identb[:])
                        return ins
                    P.op('pe', fnT, reads=[ks + 'e', 'identb'], writes=['tp'])
                    P.op('act', (lambda e, nk=nk, eT_x=eT_x, tpo=tpo: e.copy(out=eT_x[:, 0:nk], in_=tp[:, tpo:tpo + nk])), writes=[ks + 'eT', 'tp'])

                    def fnV(e, g=g, ti=ti, nkb=nkb, sample=sample, SB=SB, eT_x=eT_x):
                        gs = slice(g * 64, g * 64 + 64)
                        po = SB[:, 448:512]
                        if not sample:
                            e.matmul(po, lhsT=eT_x[:, 0:128], rhs=vat[:, ti, gs], start=True, stop=False)
                            return e.matmul(po, lhsT=eT_x[:, 128:256], rhs=vat[:, ti + 1, gs], start=False, stop=True)
                        e.matmul(po, lhsT=eT_x[:, 0:128], rhs=vc[:, ti * 2, gs], start=True, stop=False)
                        e.matmul(po, lhsT=eT_x[:, 128:256], rhs=vc[:, ti * 2 + 1, gs], start=False, stop=False)
                        return e.matmul(po, lhsT=eT_x[:, 256:384], rhs=vat[:, ti + 1, gs], start=False, stop=True)
                    P.op('pe', fnV, reads=[ks + 'eT', 'vat', 'vc'], writes=[SBk])
                    P.op('dve', (lambda e, h=h, SB=SB: e.tensor_scalar(out=ob[:, h * 64:(h + 1) * 64], in0=SB[:, 448:512], scalar1=rden[:, h:h + 1], scalar2=None, op0=ALU.mult)),
                         reads=['rden%d' % h], writes=['ob%d' % h, SBk])
                P.merge_streams()
                if DBG.get('dump', 0):
                    out_idx.append(P.op('pool', (lambda e, gt=gt: e.dma_start(out=y_o[(gt + 4) * 128:(gt + 5) * 128, 1024:2048], in_=ob[:])), reads=['ob'] + ['ob%d' % h for h in range(16)], chan='o_dbg'))
                P.op('dve', (lambda e, ti=ti: e.tensor_tensor(out=yab[:], in0=ob[:], in1=sbg[:, ti, :], op=ALU.mult)), reads=['ob%d' % h for h in range(16)] + ['sbg'], writes=['yab'])

                if DBG.get('dump', 0):
                    out_idx.append(P.op('pool', (lambda e, gt=gt: e.dma_start(out=y_o[(gt + 8) * 128:(gt + 9) * 128, 1024:2048], in_=yab[:])), reads=['yab'], chan='o_dbg'))
                def fn(e):
                    ins = None
                    for k in range(8):
                        ins = e.transpose(out=tp[:, k * 128:(k + 1) * 128], in_=yab[:, k * 128:(k + 1) * 128], identity=identb[:])
                    return ins
                P.op('pe', fn, reads=['yab', 'identb'], writes=['tp'])
                P.op('act', (lambda e, tcs=tcs: e.copy(out=ybT[:, :, tcs], in_=tp[:, :].rearrange("p (k t) -> p k t", t=128))), writes=['ybT', 'tp'])


            if DBG.get('dump', 0):
                out_idx.append(P.op('pool', (lambda e: e.dma_start(out=y_o[12 * 128:13 * 128, :].rearrange('p (k t) -> p k t', t=TB), in_=yaT[:])), reads=['yaT'], chan='o_dbg'))
                out_idx.append(P.op('pool', (lambda e: e.dma_start(out=y_o[13 * 128:14 * 128, :].rearrange('p (k t) -> p k t', t=TB), in_=ybT[:])), reads=['ybT'], chan='o_dbg'))
                out_idx.append(P.op('pool', (lambda e: e.dma_start(out=y_o[14 * 128:15 * 128, :].rearrange('p (k t) -> p k t', t=TB), in_=hT[:, 0:8, :])), reads=['hT'], chan='o_dbg'))
            if DBG['stage'] <= 6:
                continue
            for i in range(8):
                slot = load_unit(b, nu(), w_in_d, 8832 + i * 256, 256, 16)
                pr = next_pair()
                for f in range(2):
                    mm_fm(slot, 16, f, hT, 'hT', pr[f])
                for f in range(2):
                    ai = pr[f]
                    P.op('act', (lambda e, ai=ai, f=f: e.activation(out=sgb[:, f, :], in_=A[ai][:, 0:TB], func=AF.Sigmoid)), writes=['sgb', akey(ai)])
                slot = load_unit(b, nu(), p_b_d, i * 256, 256, 8)
                pr = next_pair()
                for f in range(2):
                    mm_fm(slot, 8, f, ybT, 'ybT', pr[f])
                for f in range(2):
                    ai = pr[f]
                    P.op('dve', (lambda e, ai=ai, f=f: e.tensor_tensor(out=sgb[:, f, :], in0=sgb[:, f, :], in1=A[ai][:, 0:TB], op=ALU.mult)),
                         reads=['sgb'], writes=['sgb', akey(ai)])
                    P.op('pool', (lambda e, i=i, f=f: e.tensor_tensor(out=mergedT[:, i * 2 + f, :], in0=ta_all[:, i * 2 + f, :], in1=sgb[:, f, :], op=ALU.add)),
                         reads=['taall', 'sgb'], writes=['mergedT'])
            fence(['taall'], ['xr'])
            for ti in range(NT):
                gt = b * NT + ti
                P.op('sp', (lambda e, gt=gt, ti=ti: e.dma_start(out=xr[:, ti, :], in_=x_d[gt * 128:(gt + 1) * 128, :])),
                     writes=['xr'], chan='xr')
            for i in range(8):
                slot = load_unit(b, nu(), w_o_d, i * 256, 256, 16)
                pr = next_pair()
                for ti in range(NT):
                    mm_tm(slot, 16, ti, mergedT, 'mergedT', pr[ti])
                for ti in range(NT):
                    ai = pr[ti]
                    P.op('dve', (lambda e, ai=ai, ti=ti, i=i: e.tensor_tensor(out=xr[:, ti, i * 256:(i + 1) * 256], in0=xr[:, ti, i * 256:(i + 1) * 256], in1=A[ai][:, 0:256], op=ALU.add)),
                         reads=['xr'], writes=['xr', akey(ai)])
            for ti in range(NT):
                gt = b * NT + ti
                P.op('dve', lambda e: e.memset(small[:, 0:1], 0.0), writes=['ssq'])
                P.op('act', (lambda e, ti=ti: e.activation(out=hb[:], in_=xr[:, ti, :], func=AF.Square, accum_out=small[:, 0:1])),
                     reads=['xr', 'ssq'], writes=['hb', 'ssq'])
                P.op('dve', lambda e: e.tensor_scalar(out=small[:, 1:2], in0=small[:, 0:1], scalar1=1.0 / D, scalar2=RMS_EPS, op0=ALU.mult, op1=ALU.add),
                     reads=['ssq'], writes=['ms'])
                P.op('act', lambda e: e.activation(out=small[:, 2:3], in_=small[:, 1:2], func=AF.Sqrt), reads=['ms'], writes=['sq'])
                P.op('dve', lambda e: e.reciprocal(out=small[:, 3:4], in_=small[:, 2:3]), reads=['sq'], writes=['rstd'])
                P.op('dve', (lambda e, ti=ti: e.scalar_tensor_tensor(out=xr[:, ti, :], in0=xr[:, ti, :], scalar=small[:, 3:4], in1=ct['gfbc'][:], op0=ALU.mult, op1=ALU.mult)),
                     reads=['xr', 'rstd', 'gfbc'], writes=['xr'])
                out_idx.append(P.op('pool', (lambda e, gt=gt, ti=ti: e.dma_start(out=y_o[gt * 128:(gt + 1) * 128, :], in_=xr[:, ti, :])),
                                    reads=['xr'], writes=['xr_st'], chan='o_y'))
            if b == NBLK - 2:
                out_idx.append(P.op('pool', lambda e: e.dma_start(out=shp_o, in_=plast[:]), reads=['plast'], chan='o_sh'))
            if sample:
                out_idx.append(P.op('pool', lambda e: e.dma_start(out=shs_o, in_=shs[:]), reads=['shs'], chan='o_sh'))
            assert st['u'] <= 80, st['u']

        P.wait_all('pool', out_idx)
        P.emit()
        build.stats = P.stats
    return nc


_CACHE = {}


def _consts():
    c = {}
    c['identf'] = np.eye(128, dtype=np.float32)
    s = np.arange(128)[:, None]
    t = np.arange(128)[None, :]
    same = (s // 64) == (t // 64)
    MUs = (same & (s < t)).astype(np.float32)
    MUi = (same & (s <= t)).astype(np.float32)
    c['MU4'] = np.concatenate([MUs, MUs, MUi, MUi], axis=1)
    c['MLs'] = (same & (t < s)).astype(np.float32)
    c['bones'] = same.astype(np.float32)
    s6 = np.arange(64)[:, None]
    t6 = np.arange(64)[None, :]
    mus = (s6 < t6).astype(np.float32)
    mui = (s6 <= t6).astype(np.float32)
    mls = (t6 < s6).astype(np.float32)
    m5 = np.concatenate([mus, mus, mui, mui, mls], axis=1)
    c['MU5'] = np.concatenate([m5, m5], axis=0)
    c['I2'] = np.concatenate([np.eye(64, dtype=np.float32)] * 2, axis=0)
    bo2 = np.zeros((128, 2), np.float32)
    bo2[:64, 0] = 1
    bo2[64:, 1] = 1
    c['bo2'] = bo2
    rm = np.ones((128, TB), np.float32)
    rm[:, ::64] = 0
    c['resetm'] = rm
    NEG = -1e30
    i = np.arange(128)[:, None]
    k = np.arange(256)[None, :]
    dch = (2 + i // 64) - (k // 64)
    vis = (dch >= 0) & (dch <= 2)
    DmP = np.where(vis, -np.abs(128 + i - k).astype(np.float32), NEG).astype(np.float32)
    c['DmP'] = DmP
    DmP0 = DmP.copy()
    DmP0[:, :128] = NEG
    c['DmP0'] = DmP0
    DmS = np.full((128, 384), NEG, np.float32)
    tt = np.arange(64)[:, None]
    kk = np.arange(128)[None, :]
    t2 = np.arange(64)[None, :]
    for sq in range(2):
        rows = slice(sq * 64, sq * 64 + 64)
        DmS[rows, sq * 128:(sq + 1) * 128] = -(128 + tt - kk).astype(np.float32)
        DmS[rows, 256 + sq * 64:256 + sq * 64 + 64] = -np.abs(tt - t2).astype(np.float32)
    c['DmS'] = DmS
    return c


def kernel(x_prompt, x_sample, state_wkv, state_shift, cache_k, cache_v, g_norm, w_in, mu_shift, w0,
           w_w_up, a0, w_a_up, k_k, k_a, r_k, lnx_w, lnx_b, sinks, p_a, p_b, w_o, g_final):
    f32 = np.float32
    A_ = lambda v: np.ascontiguousarray(np.asarray(v, dtype=f32))
    if 'nc' not in _CACHE:
        _CACHE['nc'] = build()
    nc = _CACHE['nc']
    cst = _consts()
    col = lambda v: A_(np.asarray(v, f32).reshape(-1, 128).T)
    shared = dict(cst)
    shared.update(
        w_in=A_(w_in[0]), p_a=A_(p_a[0]), p_b=A_(p_b[0]), w_o=A_(w_o[0]),
        gbc=A_(np.broadcast_to(np.asarray(g_norm[0], f32)[None, :], (128, D))),
        gfbc=A_(np.broadcast_to(np.asarray(g_final, f32)[None, :], (128, D))),
        lnxw=A_(np.broadcast_to(np.asarray(lnx_w[0], f32)[None, :], (128, 1024))),
        lnxb=A_(np.broadcast_to(np.asarray(lnx_b[0], f32)[None, :], (128, 1024))),
        muT=col(mu_shift[0]), w0c=col(w0[0]), a0c=col(a0[0]), kkc=col(k_k[0]), kac=col(k_a[0]),
        rkc=col(np.asarray(r_k[0], f32).reshape(-1)),
        sinks=A_(np.broadcast_to(np.asarray(sinks[0], f32)[None, :], (128, 16))),
        wup=A_(np.concatenate([np.asarray(w_w_up[0], f32), np.asarray(w_a_up[0], f32)], axis=0)),
    )
    xp = np.asarray(x_prompt, f32)
    xs_ = np.asarray(x_sample, f32)
    swkv = np.asarray(state_wkv[0], f32)
    ssh = np.asarray(state_shift[0], f32)
    ck = np.asarray(cache_k[0], f32)
    cvv = np.asarray(cache_v[0], f32)
    in_maps = []
    for c in range(8):
        sl = slice(4 * c, 4 * c + 4)
        m = dict(shared)
        m['x'] = A_(np.concatenate([xp[c], xs_[sl].reshape(256, D)], axis=0))
        sw = swkv[sl].reshape(4, 8, 2, 64, 64)
        m['swkv'] = A_(sw.transpose(0, 2, 4, 1, 3).reshape(4, 128, 8, 64))
        m['sshT'] = A_(ssh[sl].reshape(4, 25, 128).transpose(2, 1, 0))
        ckc = ck[sl]
        kt_ = ckc.transpose(3, 0, 2, 1)
        m['ckT'] = A_(np.concatenate([kt_, kt_], axis=0))
        m['cv'] = A_(cvv[sl].reshape(4, 128, 256).transpose(1, 0, 2))
        m['ck_raw'] = A_(ckc.reshape(4, 128, 256))
        m['cv_raw'] = A_(cvv[sl].reshape(4, 128, 256))
        in_maps.append(m)
    res = run_bass_kernel_spmd(nc, in_maps, core_ids=list(range(8)))
    R = res.results
    y_prompt = np.stack([R[c]['y'][:2048] for c in range(8)]).astype(f32)
    y_sample = np.concatenate([R[c]['y'][2048:].reshape(4, 64, D) for c in range(8)]).astype(f32)

    def unP(a):
        return a.reshape(2, 64, 8, 64).transpose(2, 0, 3, 1).reshape(16, 64, 64)
    wkv_p = np.stack([unP(R[c]['wkv_p']) for c in range(8)])[None].astype(f32)
    wkv_s = np.stack([unP(R[c]['wkv_s'][s]) for c in range(8) for s in range(4)])[None].astype(f32)
    shift_p = np.stack([R[c]['shift_p'].T.reshape(3200) for c in range(8)])[None].astype(f32)
    shift_s = np.stack([R[c]['shift_s'][:, :, s].T.reshape(3200) for c in range(8) for s in range(4)])[None].astype(f32)
    k_p = np.stack([R[c]['k_p'].reshape(128, 4, 64) for c in range(8)])[None].astype(f32)
    v_p = np.stack([R[c]['v_p'].reshape(128, 4, 64) for c in range(8)])[None].astype(f32)
    k_s = np.concatenate([R[c]['k_s'].reshape(4, 128, 4, 64) for c in range(8)])[None].astype(f32)
    v_s = np.concatenate([R[c]['v_s'].reshape(4, 128, 4, 64) for c in range(8)])[None].astype(f32)
    return (y_prompt, y_sample, wkv_p, shift_p, k_p, v_p, wkv_s, shift_s, k_s, v_s)
```

```python
import numpy as np
from contextlib import ExitStack
import concourse.bass as bass
import concourse.mybir as mybir
from concourse.bass_utils import run_bass_kernel_spmd

F32 = mybir.dt.float32
BF16 = mybir.dt.bfloat16
ALU = mybir.AluOpType
AF = mybir.ActivationFunctionType
AX = mybir.AxisListType

D = 2048
NTILES = 18
NT = 2
TB = NT * 128
NBLK = NTILES // NT
INW = 10880
RMS_EPS = 1e-6
LNX_EPS = 64e-5
C0 = float(np.exp(-0.5))
DBG = dict(nblk=NBLK, stage=99)
SLOPES = [float(2.0 ** (-(h + 1) / 2.0)) for h in range(16)]


class Prog:
    COMPUTE = ('pe', 'act', 'dve', 'pool')

    def __init__(self, nc):
        self.nc = nc
        self.ops = []
        self.last_w = {}
        self.readers = {}
        self.streams = None
        self.cur_stream = None

    def begin_streams(self, n):
        self.streams = [[] for _ in range(n)]
        self.cur_stream = None

    def set_stream(self, i):
        self.cur_stream = i

    def merge_streams(self):
        streams, self.streams, self.cur_stream = self.streams, None, None
        n = max(len(q) for q in streams)
        for k in range(n):
            for q in streams:
                if k < len(q):
                    a, kw, cb = q[k]
                    idx = self.op(*a, **kw)
                    if cb is not None:
                        cb.append(idx)

    def op(self, eng, fn, reads=(), writes=(), chan=None, cb=None):
        if getattr(self, 'cur_stream', None) is not None:
            self.streams[self.cur_stream].append(((eng, fn), dict(reads=reads, writes=writes, chan=chan), cb))
            return -1
        idx = len(self.ops)
        deps = {}
        for k in reads:
            d = self.last_w.get(k)
            if d is not None:
                deps[d] = True
        for k in writes:
            d = self.last_w.get(k)
            if d is not None:
                deps.setdefault(d, False)
            for r in self.readers.get(k, ()):
                deps.setdefault(r, False)
        deps.pop(idx, None)
        self.ops.append(dict(eng=eng, fn=fn, deps=deps, chan=chan))
        for k in reads:
            self.readers.setdefault(k, []).append(idx)
        for k in writes:
            self.last_w[k] = idx
            self.readers[k] = []
        return idx

    def wait_all(self, eng, idxs):
        idx = len(self.ops)
        self.ops.append(dict(eng=eng, fn=None, deps={d: True for d in idxs}, chan=None))
        return idx

    def _need_wait(self, x, d, raw):
        od, ox = self.ops[d], self.ops[x]
        if od['chan'] is not None:
            return True
        if od['eng'] != ox['eng']:
            return True
        if ox['chan'] is not None:
            return True
        if ox['eng'] == 'pe':
            return False
        return True

    def emit(self):
        nc = self.nc
        ops = self.ops
        needed = [False] * len(ops)
        for x, o in enumerate(ops):
            for d, raw in o['deps'].items():
                if self._need_wait(x, d, raw):
                    needed[d] = True
        chans = []
        for o in ops:
            if o['chan'] is not None and o['chan'] not in chans:
                chans.append(o['chan'])
        with ExitStack() as es:
            sems = {}
            for e in self.COMPUTE:
                sems[e] = es.enter_context(nc.semaphore('s_' + e))
            for c in chans:
                sems[('c', c)] = es.enter_context(nc.semaphore('c_' + str(c)))
            cnt = {k: 0 for k in sems}
            ev = [None] * len(ops)
            for x, o in enumerate(ops):
                if o['fn'] is None:
                    continue
                if o['chan'] is not None:
                    k = ('c', o['chan'])
                    cnt[k] += 16
                    ev[x] = (k, cnt[k])
                elif needed[x]:
                    k = o['eng']
                    cnt[k] += 1
                    ev[x] = (k, cnt[k])
            per_eng = {}
            for x, o in enumerate(ops):
                per_eng.setdefault(o['eng'], []).append(x)
            self.stats = {e: len(v) for e, v in per_eng.items()}
            self.stats['sem_max'] = dict((str(k), v) for k, v in cnt.items() if v > 30000)

            def run(e, ename):
                waited = {}
                for x in per_eng.get(ename, ()):
                    o = ops[x]
                    want = {}
                    for d, raw in o['deps'].items():
                        if not self._need_wait(x, d, raw):
                            continue
                        k, v = ev[d]
                        if v > want.get(k, 0):
                            want[k] = v
                    for k, v in want.items():
                        if v > waited.get(k, 0):
                            e.wait_ge(sems[k], v)
                            waited[k] = v
                    if o['fn'] is None:
                        continue
                    ins = o['fn'](e)
                    if ev[x] is not None:
                        k, v = ev[x]
                        ins.then_inc(sems[k], 16 if o['chan'] is not None else 1)

            with nc.Block() as block:
                @block.tensor
                def _(e):
                    run(e, 'pe')

                @block.scalar
                def _(e):
                    run(e, 'act')

                @block.vector
                def _(e):
                    run(e, 'dve')

                @block.gpsimd
                def _(e):
                    run(e, 'pool')

                @block.sync
                def _(e):
                    run(e, 'sp')


def build():
    nc = bass.Bass("TRN2", target_bir_lowering=False)

    def din(name, shape, dt=F32):
        return nc.dram_tensor(name, list(shape), dt, kind="ExternalInput").ap()

    def dout(name, shape, dt=F32):
        return nc.dram_tensor(name, list(shape), dt, kind="ExternalOutput").ap()

    x_d = din("x", [NTILES * 128, D])
    w_in_d = din("w_in", [D, INW])
    p_a_d = din("p_a", [1024, D])
    p_b_d = din("p_b", [1024, D])
    w_o_d = din("w_o", [D, D])
    cnames = dict(gbc=[128, D], gfbc=[128, D], lnxw=[128, 1024], lnxb=[128, 1024], muT=[128, 25],
                  w0c=[128, 8], a0c=[128, 8], kkc=[128, 8], kac=[128, 8], rkc=[128, 8],
                  sinks=[128, 16], identf=[128, 128], MU4=[128, 512], MLs=[128, 128], bones=[128, 128],
                  bo2=[128, 2], resetm=[128, TB], MU5=[128, 320], I2=[128, 64], DmP=[128, 256], DmP0=[128, 256], DmS=[128, 384],
                  sshT=[128, 25, 4])
    cd = {k: din(k, v) for k, v in cnames.items()}
    wup_d = din("wup", [128, 1024])
    swkv_d = din("swkv", [4, 128, 8, 64])
    ckT_d = din("ckT", [128, 4, 4, 128])
    cv_d = din("cv", [128, 4, 256])
    ck_raw = din("ck_raw", [4, 128, 256])
    cv_raw = din("cv_raw", [4, 128, 256])

    y_o = dout("y", [NTILES * 128, D])
    wkvp_o = dout("wkv_p", [128, 8, 64])
    wkvs_o = dout("wkv_s", [4, 128, 8, 64])
    shp_o = dout("shift_p", [128, 25])
    shs_o = dout("shift_s", [128, 25, 4])
    kp_o = dout("k_p", [128, 256])
    vp_o = dout("v_p", [128, 256])
    ks_o = dout("k_s", [4, 128, 256])
    vs_o = dout("v_s", [4, 128, 256])

    NUNITS = 0
    scr = nc.dram_tensor("wscr", [80, 128, 16 * 256], BF16).ap()

    es = ExitStack()
    with es:
        def sb(name, shape, dt=F32):
            return es.enter_context(nc.sbuf_tensor(name, list(shape), dt))

        def ps(name, shape, dt=F32):
            return es.enter_context(nc.psum_tensor(name, list(shape), dt))

        P = Prog(nc)
        ct = {k: sb("c_" + k, v) for k, v in cnames.items()}
        identb = sb("identb", [128, 128], BF16)
        wupb = sb("wupb", [128, 1024], BF16)
        kc = sb("kc", [128, 4, 4, 128], BF16)
        vc = sb("vc", [128, 4, 256], BF16)
        dummy = sb("dummy_t", [128, 8])
        small = sb("small", [128, 64])
        hT = sb("hT", [128, 16, TB], BF16)
        stage = [sb("stage%d" % i, [128, 8, 256]) for i in range(2)]
        wbf = [sb("wbf%d" % i, [128, 16, 256], BF16) for i in range(2)]
        xt = sb("xt", [128, D])
        hb = sb("hb", [128, D], BF16)
        plast = sb("plast", [128, 25])
        shs = sb("shs", [128, 25, 4])
        pT = sb("pT", [128, TB + 1])
        arenaA = sb("arenaA", [128, 23 * TB])
        xs = [arenaA[:, i * TB:(i + 1) * TB] for i in range(6)]
        tq = [[arenaA[:, (6 + s_ * 8 + i) * TB:(7 + s_ * 8 + i) * TB] for i in range(8)] for s_ in range(2)]
        tmpf = {11: arenaA[:, 22 * TB:23 * TB]}
        xr = arenaA[:, 0:NT * D].rearrange("p (t d) -> p t d", d=D)
        ta_all = arenaA[:, 0:16 * TB].rearrange("p (m t) -> p m t", t=TB)
        lora = sb("lora", [128, TB], BF16)
        xsB_t = sb("xsB", [128, 6 * TB])
        xsB = [xsB_t[:, i * TB:(i + 1) * TB] for i in range(6)]
        arenaC = sb("arenaC", [128, 4 * 8 * TB], BF16)
        opT = [arenaC[:, i * 8 * TB:(i + 1) * 8 * TB].rearrange("p (j t) -> p j t", t=TB) for i in range(4)]
        rT, aT, bT, kT = opT
        mergedT = arenaC[:, 0:16 * TB].rearrange("p (j t) -> p j t", t=TB)
        fin32 = arenaC[:, 16 * TB:32 * TB].bitcast(F32)
        sga = fin32[:, 0:2 * TB].rearrange("p (f t) -> p f t", t=TB)
        sgb = fin32[:, 2 * TB:4 * TB].rearrange("p (f t) -> p f t", t=TB)
        ta = fin32[:, 4 * TB:6 * TB].rearrange("p (f t) -> p f t", t=TB)
        gC = sb("gC", [128, 8, 2 * NT])
        vtok = sb("vtok", [128, NT, 1024], BF16)
        vtk = sb("vtk", [128, 8, 2 * NT, 64], BF16)
        I2b = sb("I2b", [128, 64], BF16)
        LNSS = [[sb("LNS%d_%d" % (k, i), [128, 192], BF16) for i in range(2)] for k in range(2)]
        bon = sb("bon", [128, NT, 16])
        kbtokS = [sb("kbtok%d" % i, [128, 128], BF16) for i in range(2)]
        MmS = [sb("Mm%d" % i, [128, 320], BF16) for i in range(2)]
        XUbS = [sb("XUb%d" % i, [128, 128], BF16) for i in range(2)]
        Pf = sb("Pf", [128, 8, 64])
        Pb = sb("Pb", [128, 8, 64], BF16)
        ysb = sb("ysb", [128, 1024])
        ysq = sb("ysq", [128, 1024])
        sa = sb("sa", [128, NT, 1024], BF16)
        sbg = sb("sbg", [128, NT, 1024], BF16)
        yab = sb("yab", [128, 1024], BF16)
        yaT = sb("yaT", [128, 8, TB], BF16)
        ybT = sb("ybT", [128, 8, TB], BF16)
        qT = sb("qT", [128, 8, TB], BF16)
        kTd = sb("kTd", [128, 4, 128 + TB], BF16)
        vat = sb("vat", [128, 1 + NT, 256], BF16)
        kvo = sb("kvo", [128, NT, 512])
        s_sbS = [sb("s_sb%d" % i, [128, 384]) for i in range(2)]
        e_sbS = [sb("e_sb%d" % i, [128, 384], BF16) for i in range(2)]
        eTS = [sb("eT%d" % i, [128, 384], BF16) for i in range(2)]
        smallS = [sb("smallS%d" % i, [128, 8]) for i in range(2)]
        ob = sb("ob", [128, 1024])
        rden = sb("rden", [128, 16])

        Aacc = [ps("A%d" % i, [128, 512]) for i in range(2)]
        A = [Aacc[i // 2][:, (i % 2) * 256:(i % 2) * 256 + 256] for i in range(4)]
        tp = ps("tp", [128, 1024], BF16)
        Mb = ps("Mb", [128, 512])
        Db = [ps("D%d" % i, [128, 512]) for i in range(2)]
        Cb = ps("Cb", [128, 512])
        Eb = ps("Eb", [128, 512])

        cidx = []
        for k in cnames:
            src = cd[k]
            cidx.append(P.op('pool', (lambda e, o=ct[k], s=src: e.dma_start(out=o[:], in_=s)), writes=[k], chan='const'))
        cidx.append(P.op('pool', lambda e: e.dma_start(out=wupb[:], in_=wup_d), writes=['wupb'], chan='const'))
        cidx.append(P.op('pool', lambda e: e.dma_start(out=kc[:], in_=ckT_d), writes=['kc'], chan='const'))
        cidx.append(P.op('pool', lambda e: e.dma_start(out=vc[:], in_=cv_d), writes=['vc'], chan='const'))
        for eng in ('pe', 'act', 'dve', 'pool'):
            P.wait_all(eng, cidx)
        P.op('dve', lambda e: e.tensor_copy(out=identb[:], in_=ct['identf'][:]), reads=['identf'], writes=['identb'])
        P.op('dve', lambda e: e.tensor_copy(out=I2b[:], in_=ct['I2'][:]), reads=['I2'], writes=['I2b'])
        P.op('dve', lambda e: e.memset(Pf[:], 0.0), writes=['Pf%d' % j for j in range(8)])
        P.op('dve', lambda e: e.memset(Pb[:], 0.0), writes=['Pb%d' % j for j in range(8)])
        P.op('dve', lambda e: e.memset(plast[:], 0.0), writes=['plast'])
        P.op('dve', lambda e: e.memset(kTd[:], 0.0), writes=['kTd'])
        P.op('dve', lambda e: e.memset(vat[:], 0.0), writes=['vat'])
        identf = ct['identf']

        st = dict(uid=0, sidx=0, acc=0, cast=0)
        out_idx = []
        ARENA_PREP = (['xs%d' % i for i in range(6)] + ['tq%d_%d' % (s_, i) for s_ in range(2) for i in range(8)] + ['tmp11']
                      + ['rT', 'aT', 'bT', 'kT'])
        ARENA_FIN = ['xr', 'mergedT', 'sga', 'sgb', 'ta', 'taall']

        def fence(after, before):
            P.op('pool', lambda e: e.memset(dummy[0:1, 0:1], 0.0), writes=list(after) + list(before))

        def load_unit(b, u, W, c0, n, KT, dup=False):
            slot = st['uid'] % 2
            st['uid'] += 1
            wk = 'wbf%d' % slot
            ncols = 256 if dup else n
            if b == 0:
                for half in range(KT // 8):
                    si = st['sidx'] % 2
                    st['sidx'] += 1
                    src = W[half * 1024:(half + 1) * 1024, c0:c0 + n].rearrange("(k p) n -> p k n", p=128)
                    P.op('sp', (lambda e, si=si, src=src: e.dma_start(out=stage[si][:, :, 0:n], in_=src)),
                         writes=['stage%d' % si], chan='stg%d' % si)
                    ceng = ('dve', 'act')[st['cast'] % 2]
                    st['cast'] += 1
                    if not dup:
                        dst = wbf[slot][:, half * 8:(half + 1) * 8, 0:n]
                        srcs = stage[si][:, :, 0:n]
                        if ceng == 'act':
                            P.op('act', (lambda e, dst=dst, srcs=srcs: e.copy(out=dst, in_=srcs)),
                                 reads=['stage%d' % si], writes=[wk])
                        else:
                            P.op('dve', (lambda e, dst=dst, srcs=srcs: e.tensor_copy(out=dst, in_=srcs)),
                                 reads=['stage%d' % si], writes=[wk])
                    else:
                        for dd in range(2):
                            dst = wbf[slot][:, half * 8:(half + 1) * 8, :].rearrange(
                                "p k (g d c) -> p k g d c", g=2, d=2)[:, :, :, dd, :]
                            srcs = stage[si][:, :, 0:128].rearrange("p k (g c) -> p k g c", g=2)
                            P.op('pool', (lambda e, dst=dst, srcs=srcs: e.tensor_copy(out=dst, in_=srcs)),
                                 reads=['stage%d' % si], writes=[wk])
                P.op('pool', (lambda e, u=u, slot=slot: e.dma_start(
                    out=scr[u % DBG.get("umod", 80), :, 0:KT * ncols].rearrange("p (k n) -> p k n", n=ncols),
                    in_=wbf[slot][:, 0:KT, 0:ncols])),
                    reads=[wk], writes=['scr%d' % u], chan='wst%d' % slot)
            else:
                P.op('sp', (lambda e, u=u, slot=slot: e.dma_start(
                    out=wbf[slot][:, 0:KT, 0:ncols],
                    in_=scr[u % DBG.get("umod", 80), :, 0:KT * ncols].rearrange("p (k n) -> p k n", n=ncols))),
                    reads=['scr%d' % u], writes=[wk], chan='wld%d' % slot)
            return slot

        def nu():
            st['u'] += 1
            return st['u'] - 1

        def next_pair():
            if st.get('pb0'):
                return (0, 1)
            if st.get('pb') is not None:
                return (2 * st['pb'], 2 * st['pb'] + 1)
            k = st['acc'] % 2
            st['acc'] += 1
            return (2 * k, 2 * k + 1)

        def next_acc():
            return next_pair()[0]

        def akey(ai):
            return 'PB%d' % (ai // 2)

        def mm_fm(slot, KT, f, rhsT, rkey, ai, ncol=TB):
            def fn(e):
                ins = None
                for kt in range(KT):
                    ins = e.matmul(A[ai][:, 0:ncol], lhsT=wbf[slot][:, kt, f * 128:(f + 1) * 128],
                                   rhs=rhsT[:, kt, 0:ncol], start=(kt == 0), stop=(kt == KT - 1))
                return ins
            P.op('pe', fn, reads=['wbf%d' % slot, rkey], writes=[akey(ai)])

        def mm_tm(slot, KT, ti, lhs_tile, lkey, ai, n=256):
            def fn(e):
                ins = None
                for kt in range(KT):
                    ins = e.matmul(A[ai][:, 0:n], lhsT=lhs_tile[:, kt, ti * 128:(ti + 1) * 128],
                                   rhs=wbf[slot][:, kt, 0:n], start=(kt == 0), stop=(kt == KT - 1))
                return ins
            P.op('pe', fn, reads=['wbf%d' % slot, lkey], writes=[akey(ai)])

        def mmk(e, out, lhsT, rhs, kbase):
            if kbase == 0:
                return e.matmul(out, lhsT=lhsT, rhs=rhs, start=True, stop=True)
            e.matmul(out[0:64], lhsT=lhsT[:, 0:64], rhs=rhs, start=True, stop=True)
            return e.matmul(out[64:128], lhsT=lhsT[:, 64:128], rhs=rhs, start=True, stop=True)

        def chunk3(ap):
            return ap.rearrange("p (c t) -> p c t", t=64)

        for b in range(DBG['nblk']):
            sample = (b == NBLK - 1)
            st['u'] = 0
            for ti in range(NT):
                gt = b * NT + ti
                P.op('sp', (lambda e, gt=gt: e.dma_start(out=xt[:], in_=x_d[gt * 128:(gt + 1) * 128, :])),
                     writes=['xt'], chan='xt')
                if DBG.get('s1', 9) < 2:
                    continue
                P.op('dve', lambda e: e.memset(small[:, 0:1], 0.0), writes=['ssq'])
                P.op('act', lambda e: e.activation(out=hb[:], in_=xt[:], func=AF.Square, accum_out=small[:, 0:1]),
                     reads=['xt', 'ssq'], writes=['hb', 'ssq'])
                if DBG.get('s1', 9) < 3:
                    continue
                P.op('dve', lambda e: e.tensor_scalar(out=small[:, 1:2], in0=small[:, 0:1], scalar1=1.0 / D,
                                                      scalar2=RMS_EPS, op0=ALU.mult, op1=ALU.add),
                     reads=['ssq'], writes=['ms'])
                P.op('act', lambda e: e.activation(out=small[:, 2:3], in_=small[:, 1:2], func=AF.Sqrt),
                     reads=['ms'], writes=['sq'])
                P.op('dve', lambda e: e.reciprocal(out=small[:, 3:4], in_=small[:, 2:3]), reads=['sq'], writes=['rstd'])
                P.op('dve', lambda e: e.scalar_tensor_tensor(out=hb[:], in0=xt[:], scalar=small[:, 3:4],
                                                             in1=ct['gbc'][:], op0=ALU.mult, op1=ALU.mult),
                     reads=['xt', 'rstd', 'gbc'], writes=['hb'])
                if DBG.get('s1', 9) < 4:
                    continue
                for half in range(2):
                    def fn(e, half=half):
                        ins = None
                        for k in range(8):
                            kt = half * 8 + k
                            ins = e.transpose(out=tp[:, k * 128:(k + 1) * 128], in_=hb[:, kt * 128:(kt + 1) * 128],
                                              identity=identb[:])
                        return ins
                    P.op('pe', fn, reads=['hb', 'identb'], writes=['tp'])
                    dst = hT[:, half * 8:(half + 1) * 8, ti * 128:(ti + 1) * 128]
                    srcv = tp[:, :].rearrange("p (k t) -> p k t", t=128)
                    if half == 0:
                        P.op('act', (lambda e, dst=dst, srcv=srcv: e.copy(out=dst, in_=srcv)), writes=['hT', 'tp'])
                    else:
                        P.op('dve', (lambda e, dst=dst, srcv=srcv: e.tensor_copy(out=dst, in_=srcv)), writes=['hT', 'tp'])

            if DBG['stage'] <= 1:
                continue
            fence(ARENA_FIN, ARENA_PREP)

            def shift(f, ai, xs_ap, xkey):
                P.op('act', (lambda e: e.copy(out=pT[:, 1:TB + 1], in_=A[ai][:, 0:TB])), writes=['pT', akey(ai)])
                if not sample:
                    P.op('dve', (lambda e: e.tensor_copy(out=pT[:, 0:1], in_=plast[:, f:f + 1])), reads=['plast'], writes=['pT'])
                    P.op('dve', (lambda e: e.tensor_copy(out=plast[:, f:f + 1], in_=pT[:, TB:TB + 1])), reads=['pT'], writes=['plast'])
                else:
                    P.op('dve', (lambda e: e.memset(pT[:, 0:1], 0.0)), writes=['pT'])
                    P.op('dve', (lambda e: e.tensor_copy(out=shs[:, f, :], in_=chunk3(pT[:, 1:TB + 1])[:, :, 63])),
                         reads=['pT'], writes=['shs'])
                t0 = tmpf[11]
                P.op('pool', (lambda e: e.tensor_tensor(out=t0, in0=pT[:, 0:TB], in1=pT[:, 1:TB + 1], op=ALU.subtract)),
                     reads=['pT'], writes=['tmp11'])
                if sample:
                    P.op('dve', (lambda e: e.tensor_tensor(out=chunk3(t0)[:, :, 0], in0=ct['sshT'][:, f, :],
                                                           in1=chunk3(pT[:, 1:TB + 1])[:, :, 0], op=ALU.subtract)),
                         reads=['pT', 'sshT', 'tmp11'], writes=['tmp11'])
                P.op('dve', (lambda e: e.scalar_tensor_tensor(out=xs_ap, in0=t0, scalar=ct['muT'][:, f:f + 1],
                                                              in1=pT[:, 1:TB + 1], op0=ALU.mult, op1=ALU.add)),
                     reads=['tmp11', 'pT', 'muT'], writes=[xkey])

            slot = load_unit(b, nu(), w_in_d, 3072, 128, 16)
            ai = next_acc()
            mm_fm(slot, 16, 0, hT, 'hT', ai)
            shift(24, ai, xs[0], 'xs0')
            P.op('act', lambda e: e.activation(out=lora[0:64, :], in_=xs[0][0:64, :], func=AF.Tanh), reads=['xs0'], writes=['lora'])
            P.op('dve', lambda e: e.tensor_copy(out=lora[64:128, :], in_=xs[0][64:128, :]), reads=['xs0'], writes=['lora'])

            XSETS = [(xs, ['xs%d' % i for i in range(6)]), (xsB, ['xb%d' % i for i in range(6)])]

            def emit_proj(g2, XS, XK):
                for kind in range(3):
                    slot = load_unit(b, nu(), w_in_d, kind * 1024 + g2 * 256, 256, 16)
                    pr = next_pair()
                    for f in range(2):
                        mm_fm(slot, 16, f, hT, 'hT', pr[f])
                    for f in range(2):
                        shift(kind * 8 + g2 * 2 + f, pr[f], XS[kind * 2 + f], XK[kind * 2 + f])

            def prep_pair(j, sx, xr_, xk_, xv_, kr, kk_, kv):
                T = tq[sx]
                K = ['tq%d_%d' % (sx, q) for q in range(8)]
                sg, cs, eg, eig, alr, kk2, kkn, b32 = T
                k_sg, k_cs, k_eg, k_eig, k_alr, k_kk2, k_kkn, k_b32 = K
                egm, k_egm = cs, k_cs
                rn, k_rn = kk2, k_kk2
                t1, k_t1 = sg, k_sg
                jc = slice(j * 128, (j + 1) * 128)
                if sx == 0:
                    A1, A2, A3 = A[2][:, 0:TB], A[3][:, 0:TB], A[2][:, 0:TB]
                    ak = 'PB1'
                else:
                    A1, A2, A3 = Mb[:, 0:TB], Mb[:, 256:256 + TB], Mb[:, 0:TB]
                    ak = 'Mb'
                tv = sx * 128
                tk = 256 + sx * 256
                P.op('pe', (lambda e: e.matmul(A1, lhsT=wupb[0:64, jc], rhs=lora[0:64, :], start=True, stop=True)),
                     reads=['wupb', 'lora'], writes=[ak])
                P.op('pe', (lambda e: mmk(e, A2, wupb[64:128, jc], lora[64:128, :], 64)),
                     reads=['wupb', 'lora'], writes=[ak])
                P.op('act', (lambda e: e.activation(out=sg, in_=A1, func=AF.Sigmoid, bias=ct['w0c'][:, j:j + 1])),
                     reads=['w0c'], writes=[k_sg, ak])
                P.op('act', (lambda e: e.activation(out=alr, in_=A2, func=AF.Sigmoid, bias=ct['a0c'][:, j:j + 1])),
                     reads=['a0c'], writes=[k_alr, ak])
                P.op('dve', (lambda e: e.tensor_tensor_scan(out=cs, data0=ct['resetm'][:], data1=sg, initial=0.0, op0=ALU.mult, op1=ALU.add)),
                     reads=[k_sg, 'resetm'], writes=[k_cs])
                P.op('act', (lambda e: e.activation(out=eg, in_=cs, func=AF.Exp, scale=-C0)), reads=[k_cs], writes=[k_eg])
                P.op('act', (lambda e: e.activation(out=eig, in_=cs, func=AF.Exp, scale=C0)), reads=[k_cs], writes=[k_eig])
                P.op('dve', (lambda e: e.tensor_tensor(out=t1, in0=cs, in1=sg, op=ALU.subtract)), reads=[k_cs, k_sg], writes=[k_t1])
                P.op('act', (lambda e: e.activation(out=egm, in_=t1, func=AF.Exp, scale=-C0)), reads=[k_t1], writes=[k_egm])
                P.op('dve', (lambda e: e.tensor_copy(out=gC[:, j, :], in_=chunk3(eg)[:, :, 63])), reads=[k_eg], writes=['gC%d' % j])
                P.op('act', (lambda e: e.activation(out=kk2, in_=xk_, func=AF.Square, scale=ct['kkc'][:, j:j + 1])),
                     reads=[kk_, 'kkc'], writes=[k_kk2])
                P.op('pe', (lambda e: e.matmul(A3, lhsT=ct['bones'][:], rhs=kk2, start=True, stop=True)),
                     reads=['bones', k_kk2], writes=[ak])
                P.op('act', (lambda e: e.activation(out=rn, in_=A3, func=AF.Sqrt)), writes=[k_rn, ak])
                P.op('dve', (lambda e: e.tensor_scalar(out=rn, in0=rn, scalar1=1e-12, scalar2=None, op0=ALU.max)), reads=[k_rn], writes=[k_rn])
                P.op('dve', (lambda e: e.reciprocal(out=rn, in_=rn)), reads=[k_rn], writes=[k_rn])
                P.op('dve', (lambda e: e.scalar_tensor_tensor(out=kkn, in0=xk_, scalar=ct['kkc'][:, j:j + 1], in1=rn, op0=ALU.mult, op1=ALU.mult)),
                     reads=[kk_, 'kkc', k_rn], writes=[k_kkn])
                P.op('dve', (lambda e: e.tensor_scalar(out=t1, in0=alr, scalar1=-1.0, scalar2=ct['kac'][:, j:j + 1], op0=ALU.add, op1=ALU.mult)),
                     reads=[k_alr, 'kac'], writes=[k_t1])
                P.op('dve', (lambda e: e.scalar_tensor_tensor(out=t1, in0=t1, scalar=1.0, in1=xk_, op0=ALU.add, op1=ALU.mult)),
                     reads=[k_t1, kk_], writes=[k_t1])
                P.op('dve', (lambda e: e.tensor_tensor(out=rT[:, j, :], in0=xr_, in1=eg, op=ALU.mult)), reads=[kr, k_eg], writes=['rT'])
                P.op('dve', (lambda e: e.scalar_tensor_tensor(out=aT[:, j, :], in0=kkn, scalar=-1.0, in1=egm, op0=ALU.mult, op1=ALU.mult)),
                     reads=[k_kkn, k_egm], writes=['aT'])
                P.op('dve', (lambda e: e.tensor_tensor(out=b32, in0=kkn, in1=alr, op=ALU.mult)), reads=[k_kkn, k_alr], writes=[k_b32])
                P.op('dve', (lambda e: e.tensor_tensor(out=bT[:, j, :], in0=b32, in1=eig, op=ALU.mult)), reads=[k_b32, k_eig], writes=['bT'])
                P.op('dve', (lambda e: e.tensor_tensor(out=kT[:, j, :], in0=t1, in1=eig, op=ALU.mult)), reads=[k_t1, k_eig], writes=['kT'])
                P.op('dve', (lambda e: e.scalar_tensor_tensor(out=kk2, in0=xr_, scalar=ct['rkc'][:, j:j + 1], in1=t1, op0=ALU.mult, op1=ALU.mult)),
                     reads=[kr, 'rkc', k_t1], writes=[k_kk2])
                vb = b32.bitcast(BF16)[:, 0:TB]
                P.op('act', (lambda e: e.copy(out=vb, in_=xv_)), reads=[kv], writes=[k_b32])

                def fnVT(e):
                    ins = None
                    for ci_ in range(2 * NT):
                        for hp in (slice(0, 64), slice(64, 128)):
                            ins = e.transpose(out=tp[hp, tk + ci_ * 64:tk + ci_ * 64 + 64], in_=vb[hp, ci_ * 64:(ci_ + 1) * 64], identity=identb[hp, hp])
                    return ins
                P.op('pe', fnVT, reads=[k_b32, 'identb'], writes=['tp'])
                P.op('act', (lambda e: e.copy(out=vtk[:, j, :, :], in_=tp[:, tk:tk + 2 * NT * 64].rearrange("p (c v) -> p c v", v=64))),
                     writes=['vtk', 'tp'])
                for ti in range(NT):
                    tcs = slice(ti * 128, (ti + 1) * 128)
                    P.op('pe', (lambda e, ti=ti, tcs=tcs: e.matmul(Eb[:, ti * 16 + j * 2:ti * 16 + j * 2 + 2], lhsT=kk2[:, tcs], rhs=ct['bo2'][:], start=True, stop=True)),
                         reads=[k_kk2, 'bo2'], writes=['Eb'])
                    P.op('pe', (lambda e, tcs=tcs: e.transpose(out=tp[:, tv:tv + 128], in_=vb[:, tcs], identity=identb[:])),
                         reads=[k_b32, 'identb'], writes=['tp'])
                    P.op('act', (lambda e, ti=ti: e.copy(out=vtok[:, ti, jc], in_=tp[:, tv:tv + 128])), writes=['vtok', 'tp'])

            for it in range(5):
                P.begin_streams(3)
                if it < 4:
                    P.set_stream(0)
                    st['pb'] = 0
                    emit_proj(it, *XSETS[it % 2])
                    st['pb'] = None
                if it > 0:
                    XS, XK = XSETS[(it - 1) % 2]
                    for jj in range(2):
                        P.set_stream(1 + jj)
                        prep_pair((it - 1) * 2 + jj, jj, XS[jj], XS[2 + jj], XS[4 + jj], XK[jj], XK[2 + jj], XK[4 + jj])
                P.merge_streams()
            P.op('dve', lambda e: e.tensor_copy(out=bon[:].rearrange("p t h -> p (t h)"), in_=Eb[:, 0:NT * 16]), writes=['bon', 'Eb'])

            def aux_ga(i):
                slot = load_unit(b, nu(), w_in_d, 3200 + i * 256, 256, 16)
                pr = next_pair()
                for ti in range(NT):
                    mm_tm(slot, 16, ti, hT, 'hT', pr[ti])
                for ti in range(NT):
                    ai = pr[ti]
                    P.op('act', (lambda e, ai=ai, ti=ti, i=i: e.activation(out=sa[:, ti, i * 256:(i + 1) * 256], in_=A[ai][:, 0:256], func=AF.Silu)),
                         writes=['sa', akey(ai)])

            def aux_q(i):
                slot = load_unit(b, nu(), w_in_d, 4224 + i * 256, 256, 16)
                pr = next_pair()
                for f in range(2):
                    mm_fm(slot, 16, f, hT, 'hT', pr[f])
                for f in range(2):
                    ai = pr[f]
                    P.op('act', (lambda e, ai=ai, i=i, f=f: e.activation(out=qT[:, i * 2 + f, :], in_=A[ai][:, 0:TB], func=AF.Copy, scale=0.125)),
                         writes=['qT', akey(ai)])

            def aux_kd(i):
                slot = load_unit(b, nu(), w_in_d, 5248 + i * 128, 128, 16, dup=True)
                pr = next_pair()
                for f in range(2):
                    mm_fm(slot, 16, f, hT, 'hT', pr[f])
                for f in range(2):
                    ai = pr[f]
                    P.op('dve', (lambda e, ai=ai, i=i, f=f: e.tensor_copy(out=kTd[:, i * 2 + f, 128:128 + TB], in_=A[ai][:, 0:TB])),
                         writes=['kTd', akey(ai)])

            def aux_kv(i):
                slot = load_unit(b, nu(), w_in_d, 5248 + i * 256, 256, 16)
                pr = next_pair()
                for ti in range(NT):
                    mm_tm(slot, 16, ti, hT, 'hT', pr[ti])
                for ti in range(NT):
                    ai = pr[ti]
                    P.op('act', (lambda e, ai=ai, ti=ti, i=i: e.copy(out=kvo[:, ti, i * 256:(i + 1) * 256], in_=A[ai][:, 0:256])),
                         writes=['kvo', akey(ai)])
                    if i == 1:
                        P.op('act', (lambda e, ai=ai, ti=ti: e.copy(out=vat[:, 1 + ti, :], in_=A[ai][:, 0:256])),
                             writes=['vat', akey(ai)])

            def aux_gb(i):
                slot = load_unit(b, nu(), w_in_d, 5760 + i * 256, 256, 16)
                pr = next_pair()
                for ti in range(NT):
                    mm_tm(slot, 16, ti, hT, 'hT', pr[ti])
                for ti in range(NT):
                    ai = pr[ti]
                    P.op('act', (lambda e, ai=ai, ti=ti, i=i: e.activation(out=sbg[:, ti, i * 256:(i + 1) * 256], in_=A[ai][:, 0:256], func=AF.Silu)),
                         writes=['sbg', akey(ai)])

            def aux_carry():
                if 0 < b and not sample:
                    P.op('pool', lambda e: e.tensor_copy(out=kTd[:, :, 0:128], in_=kTd[:, :, TB:TB + 128]), reads=['kTd'], writes=['kTd'])
                    P.op('pool', lambda e: e.tensor_copy(out=vat[:, 0, :], in_=vat[:, NT, :]), reads=['vat'], writes=['vat'])

            AUX = [
                [lambda: aux_ga(0), lambda: aux_ga(1), lambda: aux_ga(2), lambda: aux_ga(3)],
                [lambda: aux_q(0), lambda: aux_q(1), lambda: aux_q(2), lambda: aux_q(3)],
                [aux_carry, lambda: aux_kd(0), lambda: aux_kd(1), lambda: aux_kv(0), lambda: aux_kv(1)],
                [lambda: aux_gb(0), lambda: aux_gb(1), lambda: aux_gb(2), lambda: aux_gb(3)],
            ]

            if DBG['stage'] <= 3:
                continue
            H2 = (slice(0, 64), slice(64, 128))
            for ti in range(NT):
                gt = b * NT + ti
                tcs = slice(ti * 128, (ti + 1) * 128)
                for c in range(2):
                    cp = slice(c * 64, c * 64 + 64)
                    cc = slice(ti * 128 + c * 64, ti * 128 + c * 64 + 64)
                    ci = ti * 2 + c
                    P.begin_streams(3)
                    P.set_stream(2)
                    st['pb0'] = True
                    for task in AUX[ci]:
                        task()
                    st['pb0'] = False
                    for j in range(8):
                        sx = (j % 2) if DBG.get('ss', 1) else 0
                        P.set_stream(sx)
                        kbtok, Mmx, LNS, XUb = kbtokS[sx], MmS[sx], LNSS[sx], XUbS[sx]
                        MC = Mb if sx == 0 else Cb
                        MCk = 'Mb' if sx == 0 else 'Cb'
                        DD = Db[0][:, 0:192] if sx == 0 else Eb[:, 192:384]
                        DDk = 'D0' if sx == 0 else 'Eb'
                        kX, kM, kL = 'X%d' % sx, 'Mm%d' % sx, 'LNS%d_' % sx
                        if sample:
                            seq = ti * 2 + c
                            P.op('sp', (lambda e, seq=seq, j=j: e.dma_start(out=Pf[:, j, :], in_=swkv_d[seq, :, j, :])),
                                 writes=['Pf%d' % j], chan='pst%d' % j)
                            P.op('dve', (lambda e, j=j: e.tensor_copy(out=Pb[:, j, :], in_=Pf[:, j, :])), reads=['Pf%d' % j], writes=['Pb%d' % j])
                        tpo = sx * 128

                        def fnT(e, j=j, cc=cc, tpo=tpo):
                            ins = None
                            for hp in H2:
                                e.transpose(out=tp[hp, tpo:tpo + 64], in_=kT[hp, j, cc], identity=identb[hp, hp])
                                ins = e.transpose(out=tp[hp, tpo + 64:tpo + 128], in_=bT[hp, j, cc], identity=identb[hp, hp])
                            return ins
                        P.op('pe', fnT, reads=['kT', 'bT', 'identb'], writes=['tp'])
                        P.op('act', (lambda e, kbtok=kbtok, tpo=tpo: e.copy(out=kbtok[:, 0:128], in_=tp[:, tpo:tpo + 128])), writes=['kbtok%d' % sx, 'tp'])

                        def fnM(e, j=j, cc=cc, MC=MC):
                            ins = None
                            for hp in H2:
                                e.matmul(MC[hp, 0:64], lhsT=bT[hp, j, cc], rhs=aT[hp, j, cc], start=True, stop=True)
                                e.matmul(MC[hp, 64:128], lhsT=kT[hp, j, cc], rhs=aT[hp, j, cc], start=True, stop=True)
                                e.matmul(MC[hp, 128:192], lhsT=bT[hp, j, cc], rhs=rT[hp, j, cc], start=True, stop=True)
                                e.matmul(MC[hp, 192:256], lhsT=kT[hp, j, cc], rhs=rT[hp, j, cc], start=True, stop=True)
                                ins = e.matmul(MC[hp, 256:320], lhsT=aT[hp, j, cc], rhs=bT[hp, j, cc], start=True, stop=True)
                            return ins
                        P.op('pe', fnM, reads=['aT', 'bT', 'kT', 'rT'], writes=[MCk])
                        P.op('dve', (lambda e, Mmx=Mmx, MC=MC: e.tensor_tensor(out=Mmx[:, 0:320], in0=MC[:, 0:320], in1=ct['MU5'][:], op=ALU.mult)),
                             reads=['MU5'], writes=[kM, MCk])
                        P.op('pool', (lambda e, LNS=LNS, Mmx=Mmx: e.tensor_tensor(out=LNS[0][:, 128:192], in0=Mmx[:, 0:64], in1=I2b[:], op=ALU.add)),
                             reads=[kM, 'I2b'], writes=[kL + '0'])
                        for lvl in range(1, 7):
                            cur, nxt = (lvl - 1) % 2, lvl % 2
                            if lvl == 1:
                                Lc, Nc = Mmx[:, 256:320], Mmx[:, 0:64]
                                rk = [kM, kL + '0']
                            else:
                                Lc, Nc = LNS[cur][:, 0:64], LNS[cur][:, 64:128]
                                rk = [kL + str(cur)]
                            Sc = LNS[cur][:, 128:192]

                            def fnD(e, Lc=Lc, Nc=Nc, Sc=Sc, lvl=lvl, DD=DD):
                                ins = None
                                for hp in H2:
                                    if lvl < 6:
                                        e.matmul(DD[hp, 0:64], lhsT=Nc[hp], rhs=Lc[hp], start=True, stop=True)
                                    if lvl < 5:
                                        e.matmul(DD[hp, 64:128], lhsT=Lc[hp], rhs=Nc[hp], start=True, stop=True)
                                    if lvl == 1:
                                        ins = e.matmul(DD[hp, 128:192], lhsT=I2b[hp], rhs=Sc[hp], start=True, stop=True)
                                    else:
                                        e.matmul(DD[hp, 128:192], lhsT=I2b[hp], rhs=Sc[hp], start=True, stop=False)
                                        ins = e.matmul(DD[hp, 128:192], lhsT=Lc[hp], rhs=Sc[hp], start=False, stop=True)
                                return ins
                            P.op('pe', fnD, reads=rk + ['I2b'], writes=[DDk])
                            lo = 0 if lvl < 6 else 128
                            if (lvl + sx) % 2 == 1:
                                P.op('dve', (lambda e, LNS=LNS, nxt=nxt, lo=lo, DD=DD: e.tensor_copy(out=LNS[nxt][:, lo:192], in_=DD[:, lo:192])),
                                     writes=[kL + str(nxt), DDk])
                            else:
                                P.op('act', (lambda e, LNS=LNS, nxt=nxt, lo=lo, DD=DD: e.copy(out=LNS[nxt][:, lo:192], in_=DD[:, lo:192])),
                                     writes=[kL + str(nxt), DDk])

                        def fnX(e, j=j, cc=cc, ci=ci, MC=MC, Mmx=Mmx):
                            ins = None
                            for hp in H2:
                                e.matmul(MC[hp, 320:384], lhsT=aT[hp, j, cc], rhs=Pb[hp, j, :], start=True, stop=False)
                                ins = e.matmul(MC[hp, 320:384], lhsT=Mmx[hp, 64:128], rhs=vtk[hp, j, ci, :], start=False, stop=True)
                            return ins
                        P.op('pe', fnX, reads=['aT', 'Pb%d' % j, kM, 'vtk'], writes=[MCk])
                        P.op('dve', (lambda e, XUb=XUb, MC=MC: e.tensor_copy(out=XUb[:, 0:64], in_=MC[:, 320:384])), writes=[kX + 'x', MCk])

                        def fnU(e, MC=MC, LNS=LNS, XUb=XUb):
                            ins = None
                            for hp in H2:
                                ins = e.matmul(MC[hp, 384:448], lhsT=LNS[0][hp, 128:192], rhs=XUb[hp, 0:64], start=True, stop=True)
                            return ins
                        P.op('pe', fnU, reads=[kX + 'x', kL + '0'], writes=[MCk])
                        P.op('act', (lambda e, XUb=XUb, MC=MC: e.copy(out=XUb[:, 64:128], in_=MC[:, 384:448])), writes=[kX + 'u', MCk])

                        def fnO(e, j=j, cp=cp, cc=cc, ci=ci, Mmx=Mmx, XUb=XUb):
                            ins = None
                            for hh, hp in enumerate(H2):
                                ob_ = Db[1][cp, j * 64:j * 64 + 64] if hh == 0 else Aacc[1][cp, j * 64:j * 64 + 64]
                                e.matmul(ob_, lhsT=rT[hp, j, cc], rhs=Pb[hp, j, :], start=True, stop=False)
                                e.matmul(ob_, lhsT=Mmx[hp, 128:192], rhs=XUb[hp, 64:128], start=False, stop=False)
                                ins = e.matmul(ob_, lhsT=Mmx[hp, 192:256], rhs=vtk[hp, j, ci, :], start=False, stop=True)
                            return ins
                        P.op('pe', fnO, reads=['rT', 'Pb%d' % j, kM, kX + 'u', 'vtk'], writes=['D1', 'PB1'])

                        def fnP(e, j=j, ci=ci, MC=MC, kbtok=kbtok, XUb=XUb):
                            ins = None
                            for hp in H2:
                                e.matmul(MC[hp, 448:512], lhsT=identf[hp, hp], rhs=Pf[hp, j, :], start=True, stop=False)
                                e.matmul(MC[hp, 448:512], lhsT=kbtok[hp, 64:128], rhs=XUb[hp, 64:128], start=False, stop=False)
                                ins = e.matmul(MC[hp, 448:512], lhsT=kbtok[hp, 0:64], rhs=vtk[hp, j, ci, :], start=False, stop=True)
                            return ins
                        P.op('pe', fnP, reads=['identf', 'Pf%d' % j, 'kbtok%d' % sx, kX + 'u', 'vtk'], writes=[MCk])
                        gcol = gC[:, j, ci:ci + 1]
                        P.op('act', (lambda e, j=j, gcol=gcol, MC=MC: e.activation(out=Pf[:, j, :], in_=MC[:, 448:512], func=AF.Copy, scale=gcol)),
                             reads=['gC%d' % j], writes=['Pf%d' % j, MCk])
                        P.op('dve', (lambda e, j=j, gcol=gcol, MC=MC: e.tensor_scalar(out=Pb[:, j, :], in0=MC[:, 448:512], scalar1=gcol, scalar2=None, op0=ALU.mult)),
                             reads=['gC%d' % j], writes=['Pb%d' % j, MCk])
                        if sample:
                            seq = ti * 2 + c
                            P.op('pool', (lambda e, seq=seq, j=j: e.dma_start(out=wkvs_o[seq, :, j, :], in_=Pf[:, j, :])),
                                 reads=['Pf%d' % j], chan='o_pf%d' % j, cb=out_idx)
                    P.merge_streams()
                y4 = ysb[:].rearrange("p (j h c) -> p j h c", h=2, c=64)
                if DBG.get('oe', 0) == 0:
                    P.op('dve', lambda e: e.tensor_copy(out=y4[:, :, 0, :], in_=Db[1][:, :].rearrange("p (j c) -> p j c", c=64)), writes=['ysb', 'D1'])
                    P.op('act', lambda e: e.copy(out=y4[:, :, 1, :], in_=Aacc[1][:, :].rearrange("p (j c) -> p j c", c=64)), writes=['ysb', 'PB1'])
                else:
                    for j in range(8):
                        P.op('dve', (lambda e, j=j: e.tensor_copy(out=ysb[:, j * 128:j * 128 + 64], in_=Db[1][:, j * 64:j * 64 + 64])), writes=['ysb', 'D1'])
                        P.op('act', (lambda e, j=j: e.copy(out=ysb[:, j * 128 + 64:j * 128 + 128], in_=Aacc[1][:, j * 64:j * 64 + 64])), writes=['ysb', 'PB1'])
                if gt == 15:
                    out_idx.append(P.op('pool', lambda e: e.dma_start(out=wkvp_o, in_=Pf[:]), reads=['Pf%d' % j for j in range(8)], chan='o_pfp'))

                if DBG.get('dump', 0):
                    out_idx.append(P.op('pool', (lambda e, gt=gt: e.dma_start(out=y_o[(gt + 4) * 128:(gt + 5) * 128, 0:1024], in_=ysb[:])), reads=['ysb'], chan='o_dbg'))
                y3 = ysb[:].rearrange("p (h c) -> p h c", c=64)
                q3 = ysq[:].rearrange("p (h c) -> p h c", c=64)
                P.op('dve', lambda e: e.tensor_reduce(out=small[:, 8:24], in_=y3, axis=AX.X, op=ALU.add), reads=['ysb'], writes=['gn_s1'])
                P.op('act', lambda e: e.activation(out=ysq[:], in_=ysb[:], func=AF.Square), reads=['ysb'], writes=['ysq'])
                P.op('dve', lambda e: e.tensor_reduce(out=small[:, 24:40], in_=q3, axis=AX.X, op=ALU.add), reads=['ysq'], writes=['gn_s2'])
                P.op('dve', lambda e: e.tensor_scalar(out=small[:, 40:56], in0=small[:, 8:24], scalar1=1.0 / 64, scalar2=None, op0=ALU.mult),
                     reads=['gn_s1'], writes=['gn_mean'])
                P.op('dve', lambda e: e.tensor_tensor(out=small[:, 8:24], in0=small[:, 40:56], in1=small[:, 40:56], op=ALU.mult),
                     reads=['gn_mean', 'gn_s1'], writes=['gn_s1'])
                P.op('dve', lambda e: e.scalar_tensor_tensor(out=small[:, 24:40], in0=small[:, 24:40], scalar=1.0 / 64, in1=small[:, 8:24], op0=ALU.mult, op1=ALU.subtract),
                     reads=['gn_s2', 'gn_s1'], writes=['gn_s2'])
                P.op('dve', lambda e: e.tensor_scalar(out=small[:, 24:40], in0=small[:, 24:40], scalar1=LNX_EPS, scalar2=None, op0=ALU.add),
                     reads=['gn_s2'], writes=['gn_s2'])
                P.op('act', lambda e: e.activation(out=small[:, 24:40], in_=small[:, 24:40], func=AF.Sqrt), reads=['gn_s2'], writes=['gn_s2'])
                P.op('dve', lambda e: e.reciprocal(out=small[:, 24:40], in_=small[:, 24:40]), reads=['gn_s2'], writes=['gn_s2'])
                P.op('dve', lambda e: e.tensor_tensor(out=y3, in0=y3, in1=small[:, 40:56].unsqueeze(2).to_broadcast([128, 16, 64]), op=ALU.subtract),
                     reads=['ysb', 'gn_mean'], writes=['ysb'])
                P.op('dve', lambda e: e.tensor_tensor(out=y3, in0=y3, in1=small[:, 24:40].unsqueeze(2).to_broadcast([128, 16, 64]), op=ALU.mult),
                     reads=['ysb', 'gn_s2'], writes=['ysb'])
                P.op('dve', lambda e: e.tensor_tensor(out=ysb[:], in0=ysb[:], in1=ct['lnxw'][:], op=ALU.mult), reads=['ysb', 'lnxw'], writes=['ysb'])
                P.op('dve', lambda e: e.tensor_tensor(out=ysb[:], in0=ysb[:], in1=ct['lnxb'][:], op=ALU.add), reads=['ysb', 'lnxb'], writes=['ysb'])
                P.op('dve', (lambda e, ti=ti: e.tensor_tensor(out=q3, in0=vtok[:, ti, :].rearrange("p (h c) -> p h c", c=64),
                                                              in1=bon[:, ti, :].unsqueeze(2).to_broadcast([128, 16, 64]), op=ALU.mult)),
                     reads=['vtok', 'bon', 'ysq'], writes=['ysq'])
                P.op('dve', lambda e: e.tensor_tensor(out=ysb[:], in0=ysb[:], in1=ysq[:], op=ALU.add), reads=['ysb', 'ysq'], writes=['ysb'])
                P.op('dve', (lambda e, ti=ti: e.tensor_tensor(out=yab[:], in0=ysb[:], in1=sa[:, ti, :], op=ALU.mult)), reads=['ysb', 'sa'], writes=['yab'])

                if DBG.get('dump', 0):
                    out_idx.append(P.op('pool', (lambda e, gt=gt: e.dma_start(out=y_o[(gt + 8) * 128:(gt + 9) * 128, 0:1024], in_=yab[:])), reads=['yab'], chan='o_dbg'))
                def fn(e):
                    ins = None
                    for k in range(8):
                        ins = e.transpose(out=tp[:, k * 128:(k + 1) * 128], in_=yab[:, k * 128:(k + 1) * 128], identity=identb[:])
                    return ins
                P.op('pe', fn, reads=['yab', 'identb'], writes=['tp'])
                P.op('act', (lambda e, tcs=tcs: e.copy(out=yaT[:, :, tcs], in_=tp[:, :].rearrange("p (k t) -> p k t", t=128))), writes=['yaT', 'tp'])

            if DBG['stage'] <= 4:
                continue
            for ti in range(NT):
                gt = b * NT + ti
                if gt == 15:
                    out_idx.append(P.op('pool', (lambda e, ti=ti: e.dma_start(out=kp_o, in_=kvo[:, ti, 0:256])), reads=['kvo'], chan='o_kv'))
                    out_idx.append(P.op('pool', (lambda e, ti=ti: e.dma_start(out=vp_o, in_=kvo[:, ti, 256:512])), reads=['kvo'], chan='o_kv'))
                if sample:
                    for c in range(2):
                        seq = ti * 2 + c
                        cp = slice(c * 64, c * 64 + 64)
                        out_idx.append(P.op('pool', (lambda e, ti=ti, seq=seq, cp=cp: e.dma_start(out=ks_o[seq, 64:128, :], in_=kvo[cp, ti, 0:256])), reads=['kvo'], chan='o_kv'))
                        out_idx.append(P.op('pool', (lambda e, ti=ti, seq=seq, cp=cp: e.dma_start(out=vs_o[seq, 64:128, :], in_=kvo[cp, ti, 256:512])), reads=['kvo'], chan='o_kv'))
                        out_idx.append(P.op('pool', (lambda e, seq=seq: e.dma_start(out=ks_o[seq, 0:64, :], in_=ck_raw[seq, 64:128, :])), chan='o_kv'))
                        out_idx.append(P.op('pool', (lambda e, seq=seq: e.dma_start(out=vs_o[seq, 0:64, :], in_=cv_raw[seq, 64:128, :])), chan='o_kv'))

            if DBG['stage'] <= 5:
                continue
            fence(ARENA_PREP, ARENA_FIN)

            def aux_ma(i):
                slot = load_unit(b, nu(), w_in_d, 6784 + i * 256, 256, 16)
                pr = next_pair()
                for f in range(2):
                    mm_fm(slot, 16, f, hT, 'hT', pr[f])
                for f in range(2):
                    ai = pr[f]
                    P.op('act', (lambda e, ai=ai, f=f: e.activation(out=sga[:, f, :], in_=A[ai][:, 0:TB], func=AF.Sigmoid)), writes=['sga', akey(ai)])
                slot = load_unit(b, nu(), p_a_d, i * 256, 256, 8)
                pr = next_pair()
                for f in range(2):
                    mm_fm(slot, 8, f, yaT, 'yaT', pr[f])
                for f in range(2):
                    ai = pr[f]
                    P.op('dve', (lambda e, ai=ai, f=f, i=i: e.tensor_tensor(out=ta_all[:, i * 2 + f, :], in0=sga[:, f, :], in1=A[ai][:, 0:TB], op=ALU.mult)),
                         reads=['sga'], writes=['taall', akey(ai)])
            for ti in range(NT):
                gt = b * NT + ti
                tcs = slice(ti * 128, (ti + 1) * 128)
                nkb = 3 if sample else 2
                nk = nkb * 128
                Dm = ct['DmS'] if sample else (ct['DmP0'] if gt == 0 else ct['DmP'])
                Dk = 'DmS' if sample else ('DmP0' if gt == 0 else 'DmP')
                P.begin_streams(3)
                P.set_stream(2)
                for i_ in range(ti * 4, ti * 4 + 4):
                    aux_ma(i_)
                for h in range(16):
                    sx = (h % 2) if DBG.get('as', 1) else 0
                    P.set_stream(sx)
                    g = h // 4
                    f = h // 2
                    hp = slice((h % 2) * 64, (h % 2) * 64 + 64)
                    SB = Mb if sx == 0 else Cb
                    SBk = 'Mb' if sx == 0 else 'Cb'
                    s_x, e_x, eT_x, sm = s_sbS[sx], e_sbS[sx], eTS[sx], smallS[sx]
                    ks = 'at%d_' % sx
                    tpo = sx * 512

                    def fnS(e, g=g, f=f, hp=hp, ti=ti, tcs=tcs, sample=sample, SB=SB):
                        kb_ = hp.start
                        if not sample:
                            return mmk(e, SB[:, 0:256], qT[hp, f, tcs], kTd[hp, g, ti * 128:ti * 128 + 256], kb_)
                        mmk(e, SB[:, 0:128], qT[hp, f, tcs], kc[hp, ti * 2, g, :], kb_)
                        mmk(e, SB[:, 128:256], qT[hp, f, tcs], kc[hp, ti * 2 + 1, g, :], kb_)
                        return mmk(e, SB[:, 256:384], qT[hp, f, tcs], kTd[hp, g, 128 + ti * 128:256 + ti * 128], kb_)
                    P.op('pe', fnS, reads=['qT', 'kTd', 'kc'], writes=[SBk])
                    P.op('dve', (lambda e, h=h, nk=nk, Dm=Dm, s_x=s_x, SB=SB: e.scalar_tensor_tensor(out=s_x[:, 0:nk], in0=Dm[:, 0:nk], scalar=SLOPES[h], in1=SB[:, 0:nk], op0=ALU.mult, op1=ALU.add)),
                         reads=[Dk], writes=[ks + 's', SBk])
                    P.op('dve', (lambda e, nk=nk, s_x=s_x, sm=sm: e.tensor_reduce(out=sm[:, 0:1], in_=s_x[:, 0:nk], axis=AX.X, op=ALU.max)), reads=[ks + 's'], writes=[ks + 'mx'])
                    P.op('dve', (lambda e, h=h, sm=sm: e.tensor_scalar(out=sm[:, 1:2], in0=sm[:, 0:1], scalar1=ct['sinks'][:, h:h + 1], scalar2=-1.0, op0=ALU.max, op1=ALU.mult)),
                         reads=[ks + 'mx', 'sinks'], writes=[ks + 'negm'])
                    P.op('dve', (lambda e, sm=sm: e.memset(sm[:, 2:3], 0.0)), writes=[ks + 'rs'])
                    P.op('act', (lambda e, nk=nk, s_x=s_x, e_x=e_x, sm=sm: e.activation(out=e_x[:, 0:nk], in_=s_x[:, 0:nk], func=AF.Exp, bias=sm[:, 1:2], accum_out=sm[:, 2:3])),
                         reads=[ks + 's', ks + 'negm', ks + 'rs'], writes=[ks + 'e', ks + 'rs'])
                    P.op('act', (lambda e, h=h, sm=sm: e.activation(out=sm[:, 3:4], in_=sm[:, 1:2], func=AF.Exp, bias=ct['sinks'][:, h:h + 1])),
                         reads=[ks + 'negm', 'sinks'], writes=[ks + 'es'])
                    P.op('dve', (lambda e, sm=sm: e.tensor_tensor(out=sm[:, 3:4], in0=sm[:, 3:4], in1=sm[:, 2:3], op=ALU.add)), reads=[ks + 'es', ks + 'rs'], writes=[ks + 'es'])
                    P.op('dve', (lambda e, h=h, sm=sm: e.reciprocal(out=rden[:, h:h + 1], in_=sm[:, 3:4])), reads=[ks + 'es'], writes=['rden%d' % h])

                    def fnT(e, nkb=nkb, e_x=e_x, tpo=tpo):
                        ins = None
                        for kb in range(nkb):
                            ins = e.transpose(out=tp[:, tpo + kb * 128:tpo + (kb + 1) * 128], in_=e_x[:, kb * 128:(kb + 1) * 128], identity=identb[:])
                        return ins
                    P.op('pe', fnT, reads=[ks + 'e', 'identb'], writes=['tp'])
                    P.op('act', (lambda e, nk=nk, eT_x=eT_x, tpo=tpo: e.copy(out=eT_x[:, 0:nk], in_=tp[:, tpo:tpo + nk])), writes=[ks + 'eT', 'tp'])

                    def fnV(e, g=g, ti=ti, nkb=nkb, sample=sample, SB=SB, eT_x=eT_x):
                        gs = slice(g * 64, g * 64 + 64)
                        po = SB[:, 448:512]
                        if not sample:
                            e.matmul(po, lhsT=eT_x[:, 0:128], rhs=vat[:, ti, gs], start=True, stop=False)
                            return e.matmul(po, lhsT=eT_x[:, 128:256], rhs=vat[:, ti + 1, gs], start=False, stop=True)
                        e.matmul(po, lhsT=eT_x[:, 0:128], rhs=vc[:, ti * 2, gs], start=True, stop=False)
                        e.matmul(po, lhsT=eT_x[:, 128:256], rhs=vc[:, ti * 2 + 1, gs], start=False, stop=False)
                        return e.matmul(po, lhsT=eT_x[:, 256:384], rhs=vat[:, ti + 1, gs], start=False, stop=True)
                    P.op('pe', fnV, reads=[ks + 'eT', 'vat', 'vc'], writes=[SBk])
                    P.op('dve', (lambda e, h=h, SB=SB: e.tensor_scalar(out=ob[:, h * 64:(h + 1) * 64], in0=SB[:, 448:512], scalar1=rden[:, h:h + 1], scalar2=None, op0=ALU.mult)),
                         reads=['rden%d' % h], writes=['ob%d' % h, SBk])
                P.merge_streams()
                if DBG.get('dump', 0):
                    out_idx.append(P.op('pool', (lambda e, gt=gt: e.dma_start(out=y_o[(gt + 4) * 128:(gt + 5) * 128, 1024:2048], in_=ob[:])), reads=['ob'] + ['ob%d' % h for h in range(16)], chan='o_dbg'))
                P.op('dve', (lambda e, ti=ti: e.tensor_tensor(out=yab[:], in0=ob[:], in1=sbg[:, ti, :], op=ALU.mult)), reads=['ob%d' % h for h in range(16)] + ['sbg'], writes=['yab'])

                if DBG.get('dump', 0):
                    out_idx.append(P.op('pool', (lambda e, gt=gt: e.dma_start(out=y_o[(gt + 8) * 128:(gt + 9) * 128, 1024:2048], in_=yab[:])), reads=['yab'], chan='o_dbg'))
                def fn(e):
                    ins = None
                    for k in range(8):
                        ins = e.transpose(out=tp[:, k * 128:(k + 1) * 128], in_=yab[:, k * 128:(k + 1) * 128], identity=identb[:])
                    return ins
                P.op('pe', fn, reads=['yab', 'identb'], writes=['tp'])
                P.op('act', (lambda e, tcs=tcs: e.copy(out=ybT[:, :, tcs], in_=tp[:, :].rearrange("p (k t) -> p k t", t=128))), writes=['ybT', 'tp'])


            if DBG.get('dump', 0):
                out_idx.append(P.op('pool', (lambda e: e.dma_start(out=y_o[12 * 128:13 * 128, :].rearrange('p (k t) -> p k t', t=TB), in_=yaT[:])), reads=['yaT'], chan='o_dbg'))
                out_idx.append(P.op('pool', (lambda e: e.dma_start(out=y_o[13 * 128:14 * 128, :].rearrange('p (k t) -> p k t', t=TB), in_=ybT[:])), reads=['ybT'], chan='o_dbg'))
                out_idx.append(P.op('pool', (lambda e: e.dma_start(out=y_o[14 * 128:15 * 128, :].rearrange('p (k t) -> p k t', t=TB), in_=hT[:, 0:8, :])), reads=['hT'], chan='o_dbg'))
            if DBG['stage'] <= 6:
                continue
            for i in range(8):
                slot = load_unit(b, nu(), w_in_d, 8832 + i * 256, 256, 16)
                pr = next_pair()
                for f in range(2):
                    mm_fm(slot, 16, f, hT, 'hT', pr[f])
                for f in range(2):
                    ai = pr[f]
                    P.op('act', (lambda e, ai=ai, f=f: e.activation(out=sgb[:, f, :], in_=A[ai][:, 0:TB], func=AF.Sigmoid)), writes=['sgb', akey(ai)])
                slot = load_unit(b, nu(), p_b_d, i * 256, 256, 8)
                pr = next_pair()
                for f in range(2):
                    mm_fm(slot, 8, f, ybT, 'ybT', pr[f])
                for f in range(2):
                    ai = pr[f]
                    P.op('dve', (lambda e, ai=ai, f=f: e.tensor_tensor(out=sgb[:, f, :], in0=sgb[:, f, :], in1=A[ai][:, 0:TB], op=ALU.mult)),
                         reads=['sgb'], writes=['sgb', akey(ai)])
                    P.op('pool', (lambda e, i=i, f=f: e.tensor_tensor(out=mergedT[:, i * 2 + f, :], in0=ta_all[:, i * 2 + f, :], in1=sgb[:, f, :], op=ALU.add)),
                         reads=['taall', 'sgb'], writes=['mergedT'])
            fence(['taall'], ['xr'])
            for ti in range(NT):
                gt = b * NT + ti
                P.op('sp', (lambda e, gt=gt, ti=ti: e.dma_start(out=xr[:, ti, :], in_=x_d[gt * 128:(gt + 1) * 128, :])),
                     writes=['xr'], chan='xr')
            for i in range(8):
                slot = load_unit(b, nu(), w_o_d, i * 256, 256, 16)
                pr = next_pair()
                for ti in range(NT):
                    mm_tm(slot, 16, ti, mergedT, 'mergedT', pr[ti])
                for ti in range(NT):
                    ai = pr[ti]
                    P.op('dve', (lambda e, ai=ai, ti=ti, i=i: e.tensor_tensor(out=xr[:, ti, i * 256:(i + 1) * 256], in0=xr[:, ti, i * 256:(i + 1) * 256], in1=A[ai][:, 0:256], op=ALU.add)),
                         reads=['xr'], writes=['xr', akey(ai)])
            for ti in range(NT):
                gt = b * NT + ti
                P.op('dve', lambda e: e.memset(small[:, 0:1], 0.0), writes=['ssq'])
                P.op('act', (lambda e, ti=ti: e.activation(out=hb[:], in_=xr[:, ti, :], func=AF.Square, accum_out=small[:, 0:1])),
                     reads=['xr', 'ssq'], writes=['hb', 'ssq'])
                P.op('dve', lambda e: e.tensor_scalar(out=small[:, 1:2], in0=small[:, 0:1], scalar1=1.0 / D, scalar2=RMS_EPS, op0=ALU.mult, op1=ALU.add),
                     reads=['ssq'], writes=['ms'])
                P.op('act', lambda e: e.activation(out=small[:, 2:3], in_=small[:, 1:2], func=AF.Sqrt), reads=['ms'], writes=['sq'])
                P.op('dve', lambda e: e.reciprocal(out=small[:, 3:4], in_=small[:, 2:3]), reads=['sq'], writes=['rstd'])
                P.op('dve', (lambda e, ti=ti: e.scalar_tensor_tensor(out=xr[:, ti, :], in0=xr[:, ti, :], scalar=small[:, 3:4], in1=ct['gfbc'][:], op0=ALU.mult, op1=ALU.mult)),
                     reads=['xr', 'rstd', 'gfbc'], writes=['xr'])
                out_idx.append(P.op('pool', (lambda e, gt=gt, ti=ti: e.dma_start(out=y_o[gt * 128:(gt + 1) * 128, :], in_=xr[:, ti, :])),
                                    reads=['xr'], writes=['xr_st'], chan='o_y'))
            if b == NBLK - 2:
                out_idx.append(P.op('pool', lambda e: e.dma_start(out=shp_o, in_=plast[:]), reads=['plast'], chan='o_sh'))
            if sample:
                out_idx.append(P.op('pool', lambda e: e.dma_start(out=shs_o, in_=shs[:]), reads=['shs'], chan='o_sh'))
            assert st['u'] <= 80, st['u']

        P.wait_all('pool', out_idx)
        P.emit()
        build.stats = P.stats
    return nc


_CACHE = {}


def _consts():
    c = {}
    c['identf'] = np.eye(128, dtype=np.float32)
    s = np.arange(128)[:, None]
    t = np.arange(128)[None, :]
    same = (s // 64) == (t // 64)
    MUs = (same & (s < t)).astype(np.float32)
    MUi = (same & (s <= t)).astype(np.float32)
    c['MU4'] = np.concatenate([MUs, MUs, MUi, MUi], axis=1)
    c['MLs'] = (same & (t < s)).astype(np.float32)
    c['bones'] = same.astype(np.float32)
    s6 = np.arange(64)[:, None]
    t6 = np.arange(64)[None, :]
    mus = (s6 < t6).astype(np.float32)
    mui = (s6 <= t6).astype(np.float32)
    mls = (t6 < s6).astype(np.float32)
    m5 = np.concatenate([mus, mus, mui, mui, mls], axis=1)
    c['MU5'] = np.concatenate([m5, m5], axis=0)
    c['I2'] = np.concatenate([np.eye(64, dtype=np.float32)] * 2, axis=0)
    bo2 = np.zeros((128, 2), np.float32)
    bo2[:64, 0] = 1
    bo2[64:, 1] = 1
    c['bo2'] = bo2
    rm = np.ones((128, TB), np.float32)
    rm[:, ::64] = 0
    c['resetm'] = rm
    NEG = -1e30
    i = np.arange(128)[:, None]
    k = np.arange(256)[None, :]
    dch = (2 + i // 64) - (k // 64)
    vis = (dch >= 0) & (dch <= 2)
    DmP = np.where(vis, -np.abs(128 + i - k).astype(np.float32), NEG).astype(np.float32)
    c['DmP'] = DmP
    DmP0 = DmP.copy()
    DmP0[:, :128] = NEG
    c['DmP0'] = DmP0
    DmS = np.full((128, 384), NEG, np.float32)
    tt = np.arange(64)[:, None]
    kk = np.arange(128)[None, :]
    t2 = np.arange(64)[None, :]
    for sq in range(2):
        rows = slice(sq * 64, sq * 64 + 64)
        DmS[rows, sq * 128:(sq + 1) * 128] = -(128 + tt - kk).astype(np.float32)
        DmS[rows, 256 + sq * 64:256 + sq * 64 + 64] = -np.abs(tt - t2).astype(np.float32)
    c['DmS'] = DmS
    return c


def kernel(x_prompt, x_sample, state_wkv, state_shift, cache_k, cache_v, g_norm, w_in, mu_shift, w0,
           w_w_up, a0, w_a_up, k_k, k_a, r_k, lnx_w, lnx_b, sinks, p_a, p_b, w_o, g_final):
    f32 = np.float32
    A_ = lambda v: np.ascontiguousarray(np.asarray(v, dtype=f32))
    if 'nc' not in _CACHE:
        _CACHE['nc'] = build()
    nc = _CACHE['nc']
    cst = _consts()
    col = lambda v: A_(np.asarray(v, f32).reshape(-1, 128).T)
    shared = dict(cst)
    shared.update(
        w_in=A_(w_in[0]), p_a=A_(p_a[0]), p_b=A_(p_b[0]), w_o=A_(w_o[0]),
        gbc=A_(np.broadcast_to(np.asarray(g_norm[0], f32)[None, :], (128, D))),
        gfbc=A_(np.broadcast_to(np.asarray(g_final, f32)[None, :], (128, D))),
        lnxw=A_(np.broadcast_to(np.asarray(lnx_w[0], f32)[None, :], (128, 1024))),
        lnxb=A_(np.broadcast_to(np.asarray(lnx_b[0], f32)[None, :], (128, 1024))),
        muT=col(mu_shift[0]), w0c=col(w0[0]), a0c=col(a0[0]), kkc=col(k_k[0]), kac=col(k_a[0]),
        rkc=col(np.asarray(r_k[0], f32).reshape(-1)),
        sinks=A_(np.broadcast_to(np.asarray(sinks[0], f32)[None, :], (128, 16))),
        wup=A_(np.concatenate([np.asarray(w_w_up[0], f32), np.asarray(w_a_up[0], f32)], axis=0)),
    )
    xp = np.asarray(x_prompt, f32)
    xs_ = np.asarray(x_sample, f32)
    swkv = np.asarray(state_wkv[0], f32)
    ssh = np.asarray(state_shift[0], f32)
    ck = np.asarray(cache_k[0], f32)
    cvv = np.asarray(cache_v[0], f32)
    in_maps = []
    for c in range(8):
        sl = slice(4 * c, 4 * c + 4)
        m = dict(shared)
        m['x'] = A_(np.concatenate([xp[c], xs_[sl].reshape(256, D)], axis=0))
        sw = swkv[sl].reshape(4, 8, 2, 64, 64)
        m['swkv'] = A_(sw.transpose(0, 2, 4, 1, 3).reshape(4, 128, 8, 64))
        m['sshT'] = A_(ssh[sl].reshape(4, 25, 128).transpose(2, 1, 0))
        ckc = ck[sl]
        kt_ = ckc.transpose(3, 0, 2, 1)
        m['ckT'] = A_(np.concatenate([kt_, kt_], axis=0))
        m['cv'] = A_(cvv[sl].reshape(4, 128, 256).transpose(1, 0, 2))
        m['ck_raw'] = A_(ckc.reshape(4, 128, 256))
        m['cv_raw'] = A_(cvv[sl].reshape(4, 128, 256))
        in_maps.append(m)
    res = run_bass_kernel_spmd(nc, in_maps, core_ids=list(range(8)))
    R = res.results
    y_prompt = np.stack([R[c]['y'][:2048] for c in range(8)]).astype(f32)
    y_sample = np.concatenate([R[c]['y'][2048:].reshape(4, 64, D) for c in range(8)]).astype(f32)

    def unP(a):
        return a.reshape(2, 64, 8, 64).transpose(2, 0, 3, 1).reshape(16, 64, 64)
    wkv_p = np.stack([unP(R[c]['wkv_p']) for c in range(8)])[None].astype(f32)
    wkv_s = np.stack([unP(R[c]['wkv_s'][s]) for c in range(8) for s in range(4)])[None].astype(f32)
    shift_p = np.stack([R[c]['shift_p'].T.reshape(3200) for c in range(8)])[None].astype(f32)
    shift_s = np.stack([R[c]['shift_s'][:, :, s].T.reshape(3200) for c in range(8) for s in range(4)])[None].astype(f32)
    k_p = np.stack([R[c]['k_p'].reshape(128, 4, 64) for c in range(8)])[None].astype(f32)
    v_p = np.stack([R[c]['v_p'].reshape(128, 4, 64) for c in range(8)])[None].astype(f32)
    k_s = np.concatenate([R[c]['k_s'].reshape(4, 128, 4, 64) for c in range(8)])[None].astype(f32)
    v_s = np.concatenate([R[c]['v_s'].reshape(4, 128, 4, 64) for c in range(8)])[None].astype(f32)
    return (y_prompt, y_sample, wkv_p, shift_p, k_p, v_p, wkv_s, shift_s, k_s, v_s)
```

```python
import numpy as np
from contextlib import ExitStack
import concourse.bass as bass
import concourse.mybir as mybir
from concourse.bass_utils import run_bass_kernel_spmd

F32 = mybir.dt.float32
BF16 = mybir.dt.bfloat16
ALU = mybir.AluOpType
AF = mybir.ActivationFunctionType
AX = mybir.AxisListType

D = 2048
NTILES = 18
NT = 2
TB = NT * 128
NBLK = NTILES // NT
INW = 10880
RMS_EPS = 1e-6
LNX_EPS = 64e-5
C0 = float(np.exp(-0.5))
DBG = dict(nblk=NBLK, stage=99)
SLOPES = [float(2.0 ** (-(h + 1) / 2.0)) for h in range(16)]


class Prog:
    COMPUTE = ('pe', 'act', 'dve', 'pool')

    def __init__(self, nc):
        self.nc = nc
        self.ops = []
        self.last_w = {}
        self.readers = {}
        self.streams = None
        self.cur_stream = None

    def begin_streams(self, n):
        self.streams = [[] for _ in range(n)]
        self.cur_stream = None

    def set_stream(self, i):
        self.cur_stream = i

    def merge_streams(self):
        streams, self.streams, self.cur_stream = self.streams, None, None
        n = max(len(q) for q in streams)
        for k in range(n):
            for q in streams:
                if k < len(q):
                    a, kw, cb = q[k]
                    idx = self.op(*a, **kw)
                    if cb is not None:
                        cb.append(idx)

    def op(self, eng, fn, reads=(), writes=(), chan=None, cb=None):
        if getattr(self, 'cur_stream', None) is not None:
            self.streams[self.cur_stream].append(((eng, fn), dict(reads=reads, writes=writes, chan=chan), cb))
            return -1
        idx = len(self.ops)
        deps = {}
        for k in reads:
            d = self.last_w.get(k)
            if d is not None:
                deps[d] = True
        for k in writes:
            d = self.last_w.get(k)
            if d is not None:
                deps.setdefault(d, False)
            for r in self.readers.get(k, ()):
                deps.setdefault(r, False)
        deps.pop(idx, None)
        self.ops.append(dict(eng=eng, fn=fn, deps=deps, chan=chan))
        for k in reads:
            self.readers.setdefault(k, []).append(idx)
        for k in writes:
            self.last_w[k] = idx
            self.readers[k] = []
        return idx

    def wait_all(self, eng, idxs):
        idx = len(self.ops)
        self.ops.append(dict(eng=eng, fn=None, deps={d: True for d in idxs}, chan=None))
        return idx

    def _need_wait(self, x, d, raw):
        od, ox = self.ops[d], self.ops[x]
        if od['chan'] is not None:
            return True
        if od['eng'] != ox['eng']:
            return True
        if ox['chan'] is not None:
            return True
        if ox['eng'] == 'pe':
            return False
        return True

    def emit(self):
        nc = self.nc
        ops = self.ops
        needed = [False] * len(ops)
        for x, o in enumerate(ops):
            for d, raw in o['deps'].items():
                if self._need_wait(x, d, raw):
                    needed[d] = True
        chans = []
        for o in ops:
            if o['chan'] is not None and o['chan'] not in chans:
                chans.append(o['chan'])
        with ExitStack() as es:
            sems = {}
            for e in self.COMPUTE:
                sems[e] = es.enter_context(nc.semaphore('s_' + e))
            for c in chans:
                sems[('c', c)] = es.enter_context(nc.semaphore('c_' + str(c)))
            cnt = {k: 0 for k in sems}
            ev = [None] * len(ops)
            for x, o in enumerate(ops):
                if o['fn'] is None:
                    continue
                if o['chan'] is not None:
                    k = ('c', o['chan'])
                    cnt[k] += 16
                    ev[x] = (k, cnt[k])
                elif needed[x]:
                    k = o['eng']
                    cnt[k] += 1
                    ev[x] = (k, cnt[k])
            per_eng = {}
            for x, o in enumerate(ops):
                per_eng.setdefault(o['eng'], []).append(x)
            self.stats = {e: len(v) for e, v in per_eng.items()}
            self.stats['sem_max'] = dict((str(k), v) for k, v in cnt.items() if v > 30000)

            def run(e, ename):
                waited = {}
                for x in per_eng.get(ename, ()):
                    o = ops[x]
                    want = {}
                    for d, raw in o['deps'].items():
                        if not self._need_wait(x, d, raw):
                            continue
                        k, v = ev[d]
                        if v > want.get(k, 0):
                            want[k] = v
                    for k, v in want.items():
                        if v > waited.get(k, 0):
                            e.wait_ge(sems[k], v)
                            waited[k] = v
                    if o['fn'] is None:
                        continue
                    ins = o['fn'](e)
                    if ev[x] is not None:
                        k, v = ev[x]
                        ins.then_inc(sems[k], 16 if o['chan'] is not None else 1)

            with nc.Block() as block:
                @block.tensor
                def _(e):
                    run(e, 'pe')

                @block.scalar
                def _(e):
                    run(e, 'act')

                @block.vector
                def _(e):
                    run(e, 'dve')

                @block.gpsimd
                def _(e):
                    run(e, 'pool')

                @block.sync
                def _(e):
                    run(e, 'sp')


def build():
    nc = bass.Bass("TRN2", target_bir_lowering=False)

    def din(name, shape, dt=F32):
        return nc.dram_tensor(name, list(shape), dt, kind="ExternalInput").ap()

    def dout(name, shape, dt=F32):
        return nc.dram_tensor(name, list(shape), dt, kind="ExternalOutput").ap()

    x_d = din("x", [NTILES * 128, D])
    w_in_d = din("w_in", [D, INW])
    p_a_d = din("p_a", [1024, D])
    p_b_d = din("p_b", [1024, D])
    w_o_d = din("w_o", [D, D])
    cnames = dict(gbc=[128, D], gfbc=[128, D], lnxw=[128, 1024], lnxb=[128, 1024], muT=[128, 25],
                  w0c=[128, 8], a0c=[128, 8], kkc=[128, 8], kac=[128, 8], rkc=[128, 8],
                  sinks=[128, 16], identf=[128, 128], MU4=[128, 512], MLs=[128, 128], bones=[128, 128],
                  bo2=[128, 2], resetm=[128, TB], MU5=[128, 320], I2=[128, 64], DmP=[128, 256], DmP0=[128, 256], DmS=[128, 384],
                  sshT=[128, 25, 4])
    cd = {k: din(k, v) for k, v in cnames.items()}
    wup_d = din("wup", [128, 1024])
    swkv_d = din("swkv", [4, 128, 8, 64])
    ckT_d = din("ckT", [128, 4, 4, 128])
    cv_d = din("cv", [128, 4, 256])
    ck_raw = din("ck_raw", [4, 128, 256])
    cv_raw = din("cv_raw", [4, 128, 256])

    y_o = dout("y", [NTILES * 128, D])
    wkvp_o = dout("wkv_p", [128, 8, 64])
    wkvs_o = dout("wkv_s", [4, 128, 8, 64])
    shp_o = dout("shift_p", [128, 25])
    shs_o = dout("shift_s", [128, 25, 4])
    kp_o = dout("k_p", [128, 256])
    vp_o = dout("v_p", [128, 256])
    ks_o = dout("k_s", [4, 128, 256])
    vs_o = dout("v_s", [4, 128, 256])

    NUNITS = 0
    scr = nc.dram_tensor("wscr", [80, 128, 16 * 256], BF16).ap()

    es = ExitStack()
    with es:
        def sb(name, shape, dt=F32):
            return es.enter_context(nc.sbuf_tensor(name, list(shape), dt))

        def ps(name, shape, dt=F32):
            return es.enter_context(nc.psum_tensor(name, list(shape), dt))

        P = Prog(nc)
        ct = {k: sb("c_" + k, v) for k, v in cnames.items()}
        identb = sb("identb", [128, 128], BF16)
        wupb = sb("wupb", [128, 1024], BF16)
        kc = sb("kc", [128, 4, 4, 128], BF16)
        vc = sb("vc", [128, 4, 256], BF16)
        dummy = sb("dummy_t", [128, 8])
        small = sb("small", [128, 64])
        hT = sb("hT", [128, 16, TB], BF16)
        stage = [sb("stage%d" % i, [128, 8, 256]) for i in range(2)]
        wbf = [sb("wbf%d" % i, [128, 16, 256], BF16) for i in range(2)]
        xt = sb("xt", [128, D])
        hb = sb("hb", [128, D], BF16)
        plast = sb("plast", [128, 25])
        shs = sb("shs", [128, 25, 4])
        pT = sb("pT", [128, TB + 1])
        arenaA = sb("arenaA", [128, 23 * TB])
        xs = [arenaA[:, i * TB:(i + 1) * TB] for i in range(6)]
        tq = [[arenaA[:, (6 + s_ * 8 + i) * TB:(7 + s_ * 8 + i) * TB] for i in range(8)] for s_ in range(2)]
        tmpf = {11: arenaA[:, 22 * TB:23 * TB]}
        xr = arenaA[:, 0:NT * D].rearrange("p (t d) -> p t d", d=D)
        ta_all = arenaA[:, 0:16 * TB].rearrange("p (m t) -> p m t", t=TB)
        lora = sb("lora", [128, TB], BF16)
        xsB_t = sb("xsB", [128, 6 * TB])
        xsB = [xsB_t[:, i * TB:(i + 1) * TB] for i in range(6)]
        arenaC = sb("arenaC", [128, 4 * 8 * TB], BF16)
        opT = [arenaC[:, i * 8 * TB:(i + 1) * 8 * TB].rearrange("p (j t) -> p j t", t=TB) for i in range(4)]
        rT, aT, bT, kT = opT
        mergedT = arenaC[:, 0:16 * TB].rearrange("p (j t) -> p j t", t=TB)
        fin32 = arenaC[:, 16 * TB:32 * TB].bitcast(F32)
        sga = fin32[:, 0:2 * TB].rearrange("p (f t) -> p f t", t=TB)
        sgb = fin32[:, 2 * TB:4 * TB].rearrange("p (f t) -> p f t", t=TB)
        ta = fin32[:, 4 * TB:6 * TB].rearrange("p (f t) -> p f t", t=TB)
        gC = sb("gC", [128, 8, 2 * NT])
        vtok = sb("vtok", [128, NT, 1024], BF16)
        vtk = sb("vtk", [128, 8, 2 * NT, 64], BF16)
        I2b = sb("I2b", [128, 64], BF16)
        LNSS = [[sb("LNS%d_%d" % (k, i), [128, 192], BF16) for i in range(2)] for k in range(2)]
        bon = sb("bon", [128, NT, 16])
        kbtokS = [sb("kbtok%d" % i, [128, 128], BF16) for i in range(2)]
        MmS = [sb("Mm%d" % i, [128, 320], BF16) for i in range(2)]
        XUbS = [sb("XUb%d" % i, [128, 128], BF16) for i in range(2)]
        Pf = sb("Pf", [128, 8, 64])
        Pb = sb("Pb", [128, 8, 64], BF16)
        ysb = sb("ysb", [128, 1024])
        ysq = sb("ysq", [128, 1024])
        sa = sb("sa", [128, NT, 1024], BF16)
        sbg = sb("sbg", [128, NT, 1024], BF16)
        yab = sb("yab", [128, 1024], BF16)
        yaT = sb("yaT", [128, 8, TB], BF16)
        ybT = sb("ybT", [128, 8, TB], BF16)
        qT = sb("qT", [128, 8, TB], BF16)
        kTd = sb("kTd", [128, 4, 128 + TB], BF16)
        vat = sb("vat", [128, 1 + NT, 256], BF16)
        kvo = sb("kvo", [128, NT, 512])
        s_sbS = [sb("s_sb%d" % i, [128, 384]) for i in range(2)]
        e_sbS = [sb("e_sb%d" % i, [128, 384], BF16) for i in range(2)]
        eTS = [sb("eT%d" % i, [128, 384], BF16) for i in range(2)]
        smallS = [sb("smallS%d" % i, [128, 8]) for i in range(2)]
        ob = sb("ob", [128, 1024])
        rden = sb("rden", [128, 16])

        Aacc = [ps("A%d" % i, [128, 512]) for i in range(2)]
        A = [Aacc[i // 2][:, (i % 2) * 256:(i % 2) * 256 + 256] for i in range(4)]
        tp = ps("tp", [128, 1024], BF16)
        Mb = ps("Mb", [128, 512])
        Db = [ps("D%d" % i, [128, 512]) for i in range(2)]
        Cb = ps("Cb", [128, 512])
        Eb = ps("Eb", [128, 512])

        cidx = []
        for k in cnames:
            src = cd[k]
            cidx.append(P.op('pool', (lambda e, o=ct[k], s=src: e.dma_start(out=o[:], in_=s)), writes=[k], chan='const'))
        cidx.append(P.op('pool', lambda e: e.dma_start(out=wupb[:], in_=wup_d), writes=['wupb'], chan='const'))
        cidx.append(P.op('pool', lambda e: e.dma_start(out=kc[:], in_=ckT_d), writes=['kc'], chan='const'))
        cidx.append(P.op('pool', lambda e: e.dma_start(out=vc[:], in_=cv_d), writes=['vc'], chan='const'))
        for eng in ('pe', 'act', 'dve', 'pool'):
            P.wait_all(eng, cidx)
        P.op('dve', lambda e: e.tensor_copy(out=identb[:], in_=ct['identf'][:]), reads=['identf'], writes=['identb'])
        P.op('dve', lambda e: e.tensor_copy(out=I2b[:], in_=ct['I2'][:]), reads=['I2'], writes=['I2b'])
        P.op('dve', lambda e: e.memset(Pf[:], 0.0), writes=['Pf%d' % j for j in range(8)])
        P.op('dve', lambda e: e.memset(Pb[:], 0.0), writes=['Pb%d' % j for j in range(8)])
        P.op('dve', lambda e: e.memset(plast[:], 0.0), writes=['plast'])
        P.op('dve', lambda e: e.memset(kTd[:], 0.0), writes=['kTd'])
        P.op('dve', lambda e: e.memset(vat[:], 0.0), writes=['vat'])
        identf = ct['identf']

        st = dict(uid=0, sidx=0, acc=0, cast=0)
        out_idx = []
        ARENA_PREP = (['xs%d' % i for i in range(6)] + ['tq%d_%d' % (s_, i) for s_ in range(2) for i in range(8)] + ['tmp11']
                      + ['rT', 'aT', 'bT', 'kT'])
        ARENA_FIN = ['xr', 'mergedT', 'sga', 'sgb', 'ta', 'taall']

        def fence(after, before):
            P.op('pool', lambda e: e.memset(dummy[0:1, 0:1], 0.0), writes=list(after) + list(before))

        def load_unit(b, u, W, c0, n, KT, dup=False):
            slot = st['uid'] % 2
            st['uid'] += 1
            wk = 'wbf%d' % slot
            ncols = 256 if dup else n
            if b == 0:
                for half in range(KT // 8):
                    si = st['sidx'] % 2
                    st['sidx'] += 1
                    src = W[half * 1024:(half + 1) * 1024, c0:c0 + n].rearrange("(k p) n -> p k n", p=128)
                    P.op('sp', (lambda e, si=si, src=src: e.dma_start(out=stage[si][:, :, 0:n], in_=src)),
                         writes=['stage%d' % si], chan='stg%d' % si)
                    ceng = ('dve', 'act')[st['cast'] % 2]
                    st['cast'] += 1
                    if not dup:
                        dst = wbf[slot][:, half * 8:(half + 1) * 8, 0:n]
                        srcs = stage[si][:, :, 0:n]
                        if ceng == 'act':
                            P.op('act', (lambda e, dst=dst, srcs=srcs: e.copy(out=dst, in_=srcs)),
                                 reads=['stage%d' % si], writes=[wk])
                        else:
                            P.op('dve', (lambda e, dst=dst, srcs=srcs: e.tensor_copy(out=dst, in_=srcs)),
                                 reads=['stage%d' % si], writes=[wk])
                    else:
                        for dd in range(2):
                            dst = wbf[slot][:, half * 8:(half + 1) * 8, :].rearrange(
                                "p k (g d c) -> p k g d c", g=2, d=2)[:, :, :, dd, :]
                            srcs = stage[si][:, :, 0:128].rearrange("p k (g c) -> p k g c", g=2)
                            P.op('pool', (lambda e, dst=dst, srcs=srcs: e.tensor_copy(out=dst, in_=srcs)),
                                 reads=['stage%d' % si], writes=[wk])
                P.op('pool', (lambda e, u=u, slot=slot: e.dma_start(
                    out=scr[u % DBG.get("umod", 80), :, 0:KT * ncols].rearrange("p (k n) -> p k n", n=ncols),
                    in_=wbf[slot][:, 0:KT, 0:ncols])),
                    reads=[wk], writes=['scr%d' % u], chan='wst%d' % slot)
            else:
                P.op('sp', (lambda e, u=u, slot=slot: e.dma_start(
                    out=wbf[slot][:, 0:KT, 0:ncols],
                    in_=scr[u % DBG.get("umod", 80), :, 0:KT * ncols].rearrange("p (k n) -> p k n", n=ncols))),
                    reads=['scr%d' % u], writes=[wk], chan='wld%d' % slot)
            return slot

        def nu():
            st['u'] += 1
            return st['u'] - 1

        def next_pair():
            if st.get('pb0'):
                return (0, 1)
            if st.get('pb') is not None:
                return (2 * st['pb'], 2 * st['pb'] + 1)
            k = st['acc'] % 2
            st['acc'] += 1
            return (2 * k, 2 * k + 1)

        def next_acc():
            return next_pair()[0]

        def akey(ai):
            return 'PB%d' % (ai // 2)

        def mm_fm(slot, KT, f, rhsT, rkey, ai, ncol=TB):
            def fn(e):
                ins = None
                for kt in range(KT):
                    ins = e.matmul(A[ai][:, 0:ncol], lhsT=wbf[slot][:, kt, f * 128:(f + 1) * 128],
                                   rhs=rhsT[:, kt, 0:ncol], start=(kt == 0), stop=(kt == KT - 1))
                return ins
            P.op('pe', fn, reads=['wbf%d' % slot, rkey], writes=[akey(ai)])

        def mm_tm(slot, KT, ti, lhs_tile, lkey, ai, n=256):
            def fn(e):
                ins = None
                for kt in range(KT):
                    ins = e.matmul(A[ai][:, 0:n], lhsT=lhs_tile[:, kt, ti * 128:(ti + 1) * 128],
                                   rhs=wbf[slot][:, kt, 0:n], start=(kt == 0), stop=(kt == KT - 1))
                return ins
            P.op('pe', fn, reads=['wbf%d' % slot, lkey], writes=[akey(ai)])

        def mmk(e, out, lhsT, rhs, kbase):
            if kbase == 0:
                return e.matmul(out, lhsT=lhsT, rhs=rhs, start=True, stop=True)
            e.matmul(out[0:64], lhsT=lhsT[:, 0:64], rhs=rhs, start=True, stop=True)
            return e.matmul(out[64:128], lhsT=lhsT[:, 64:128], rhs=rhs, start=True, stop=True)

        def chunk3(ap):
            return ap.rearrange("p (c t) -> p c t", t=64)

        for b in range(DBG['nblk']):
            sample = (b == NBLK - 1)
            st['u'] = 0
            def emit_S1(bb):
                for ti in range(NT):
                    gt = bb * NT + ti
                    P.op('sp', (lambda e, gt=gt: e.dma_start(out=xt[:], in_=x_d[gt * 128:(gt + 1) * 128, :])),
                         writes=['xt'], chan='xt')
                    P.op('dve', lambda e: e.memset(small[:, 56:57], 0.0), writes=['s1ssq'])
                    P.op('act', lambda e: e.activation(out=hb[:], in_=xt[:], func=AF.Square, accum_out=small[:, 56:57]),
                         reads=['xt', 's1ssq'], writes=['hb', 's1ssq'])
                    P.op('dve', lambda e: e.tensor_scalar(out=small[:, 57:58], in0=small[:, 56:57], scalar1=1.0 / D,
                                                          scalar2=RMS_EPS, op0=ALU.mult, op1=ALU.add),
                         reads=['s1ssq'], writes=['s1ms'])
                    P.op('act', lambda e: e.activation(out=small[:, 58:59], in_=small[:, 57:58], func=AF.Sqrt),
                         reads=['s1ms'], writes=['s1sq'])
                    P.op('dve', lambda e: e.reciprocal(out=small[:, 59:60], in_=small[:, 58:59]), reads=['s1sq'], writes=['s1rstd'])
                    P.op('dve', lambda e: e.scalar_tensor_tensor(out=hb[:], in0=xt[:], scalar=small[:, 59:60],
                                                                 in1=ct['gbc'][:], op0=ALU.mult, op1=ALU.mult),
                         reads=['xt', 's1rstd', 'gbc'], writes=['hb'])
                    for half in range(2):
                        def fn(e, half=half):
                            ins = None
                            for k in range(8):
                                kt = half * 8 + k
                                ins = e.transpose(out=tp[:, k * 128:(k + 1) * 128], in_=hb[:, kt * 128:(kt + 1) * 128],
                                                  identity=identb[:])
                            return ins
                        P.op('pe', fn, reads=['hb', 'identb'], writes=['tp'])
                        dst = hT[:, half * 8:(half + 1) * 8, ti * 128:(ti + 1) * 128]
                        srcv = tp[:, :].rearrange("p (k t) -> p k t", t=128)
                        if half == 0:
                            P.op('act', (lambda e, dst=dst, srcv=srcv: e.copy(out=dst, in_=srcv)), writes=['hT', 'tp'])
                        else:
                            P.op('dve', (lambda e, dst=dst, srcv=srcv: e.tensor_copy(out=dst, in_=srcv)), writes=['hT', 'tp'])

            if b == 0:
                emit_S1(0)

            if DBG['stage'] <= 1:
                continue
            fence(ARENA_FIN, ARENA_PREP)

            def shift(f, ai, xs_ap, xkey):
                P.op('act', (lambda e: e.copy(out=pT[:, 1:TB + 1], in_=A[ai][:, 0:TB])), writes=['pT', akey(ai)])
                if not sample:
                    P.op('dve', (lambda e: e.tensor_copy(out=pT[:, 0:1], in_=plast[:, f:f + 1])), reads=['plast'], writes=['pT'])
                    P.op('dve', (lambda e: e.tensor_copy(out=plast[:, f:f + 1], in_=pT[:, TB:TB + 1])), reads=['pT'], writes=['plast'])
                else:
                    P.op('dve', (lambda e: e.memset(pT[:, 0:1], 0.0)), writes=['pT'])
                    P.op('dve', (lambda e: e.tensor_copy(out=shs[:, f, :], in_=chunk3(pT[:, 1:TB + 1])[:, :, 63])),
                         reads=['pT'], writes=['shs'])
                t0 = tmpf[11]
                P.op('pool', (lambda e: e.tensor_tensor(out=t0, in0=pT[:, 0:TB], in1=pT[:, 1:TB + 1], op=ALU.subtract)),
                     reads=['pT'], writes=['tmp11'])
                if sample:
                    P.op('dve', (lambda e: e.tensor_tensor(out=chunk3(t0)[:, :, 0], in0=ct['sshT'][:, f, :],
                                                           in1=chunk3(pT[:, 1:TB + 1])[:, :, 0], op=ALU.subtract)),
                         reads=['pT', 'sshT', 'tmp11'], writes=['tmp11'])
                P.op('dve', (lambda e: e.scalar_tensor_tensor(out=xs_ap, in0=t0, scalar=ct['muT'][:, f:f + 1],
                                                              in1=pT[:, 1:TB + 1], op0=ALU.mult, op1=ALU.add)),
                     reads=['tmp11', 'pT', 'muT'], writes=[xkey])

            slot = load_unit(b, nu(), w_in_d, 3072, 128, 16)
            ai = next_acc()
            mm_fm(slot, 16, 0, hT, 'hT', ai)
            shift(24, ai, xs[0], 'xs0')
            P.op('act', lambda e: e.activation(out=lora[0:64, :], in_=xs[0][0:64, :], func=AF.Tanh), reads=['xs0'], writes=['lora'])
            P.op('dve', lambda e: e.tensor_copy(out=lora[64:128, :], in_=xs[0][64:128, :]), reads=['xs0'], writes=['lora'])

            XSETS = [(xs, ['xs%d' % i for i in range(6)]), (xsB, ['xb%d' % i for i in range(6)])]

            def emit_proj(g2, XS, XK):
                for kind in range(3):
                    slot = load_unit(b, nu(), w_in_d, kind * 1024 + g2 * 256, 256, 16)
                    pr = next_pair()
                    for f in range(2):
                        mm_fm(slot, 16, f, hT, 'hT', pr[f])
                    for f in range(2):
                        shift(kind * 8 + g2 * 2 + f, pr[f], XS[kind * 2 + f], XK[kind * 2 + f])

            def prep_pair(j, sx, xr_, xk_, xv_, kr, kk_, kv):
                T = tq[sx]
                K = ['tq%d_%d' % (sx, q) for q in range(8)]
                sg, cs, eg, eig, alr, kk2, kkn, b32 = T
                k_sg, k_cs, k_eg, k_eig, k_alr, k_kk2, k_kkn, k_b32 = K
                egm, k_egm = cs, k_cs
                rn, k_rn = kk2, k_kk2
                t1, k_t1 = sg, k_sg
                jc = slice(j * 128, (j + 1) * 128)
                if sx == 0:
                    A1, A2, A3 = A[2][:, 0:TB], A[3][:, 0:TB], A[2][:, 0:TB]
                    ak = 'PB1'
                else:
                    A1, A2, A3 = Mb[:, 0:TB], Mb[:, 256:256 + TB], Mb[:, 0:TB]
                    ak = 'Mb'
                tv = sx * 128
                tk = 256 + sx * 256
                P.op('pe', (lambda e: e.matmul(A1, lhsT=wupb[0:64, jc], rhs=lora[0:64, :], start=True, stop=True)),
                     reads=['wupb', 'lora'], writes=[ak])
                P.op('pe', (lambda e: mmk(e, A2, wupb[64:128, jc], lora[64:128, :], 64)),
                     reads=['wupb', 'lora'], writes=[ak])
                P.op('act', (lambda e: e.activation(out=sg, in_=A1, func=AF.Sigmoid, bias=ct['w0c'][:, j:j + 1])),
                     reads=['w0c'], writes=[k_sg, ak])
                P.op('act', (lambda e: e.activation(out=alr, in_=A2, func=AF.Sigmoid, bias=ct['a0c'][:, j:j + 1])),
                     reads=['a0c'], writes=[k_alr, ak])
                P.op('dve', (lambda e: e.tensor_tensor_scan(out=cs, data0=ct['resetm'][:], data1=sg, initial=0.0, op0=ALU.mult, op1=ALU.add)),
                     reads=[k_sg, 'resetm'], writes=[k_cs])
                P.op('act', (lambda e: e.activation(out=eg, in_=cs, func=AF.Exp, scale=-C0)), reads=[k_cs], writes=[k_eg])
                P.op('act', (lambda e: e.activation(out=eig, in_=cs, func=AF.Exp, scale=C0)), reads=[k_cs], writes=[k_eig])
                P.op('dve', (lambda e: e.tensor_tensor(out=t1, in0=cs, in1=sg, op=ALU.subtract)), reads=[k_cs, k_sg], writes=[k_t1])
                P.op('act', (lambda e: e.activation(out=egm, in_=t1, func=AF.Exp, scale=-C0)), reads=[k_t1], writes=[k_egm])
                P.op('dve', (lambda e: e.tensor_copy(out=gC[:, j, :], in_=chunk3(eg)[:, :, 63])), reads=[k_eg], writes=['gC%d' % j])
                P.op('act', (lambda e: e.activation(out=kk2, in_=xk_, func=AF.Square, scale=ct['kkc'][:, j:j + 1])),
                     reads=[kk_, 'kkc'], writes=[k_kk2])
                P.op('pe', (lambda e: e.matmul(A3, lhsT=ct['bones'][:], rhs=kk2, start=True, stop=True)),
                     reads=['bones', k_kk2], writes=[ak])
                P.op('act', (lambda e: e.activation(out=rn, in_=A3, func=AF.Sqrt)), writes=[k_rn, ak])
                P.op('dve', (lambda e: e.tensor_scalar(out=rn, in0=rn, scalar1=1e-12, scalar2=None, op0=ALU.max)), reads=[k_rn], writes=[k_rn])
                P.op('dve', (lambda e: e.reciprocal(out=rn, in_=rn)), reads=[k_rn], writes=[k_rn])
                P.op('dve', (lambda e: e.scalar_tensor_tensor(out=kkn, in0=xk_, scalar=ct['kkc'][:, j:j + 1], in1=rn, op0=ALU.mult, op1=ALU.mult)),
                     reads=[kk_, 'kkc', k_rn], writes=[k_kkn])
                P.op('dve', (lambda e: e.tensor_scalar(out=t1, in0=alr, scalar1=-1.0, scalar2=ct['kac'][:, j:j + 1], op0=ALU.add, op1=ALU.mult)),
                     reads=[k_alr, 'kac'], writes=[k_t1])
                P.op('dve', (lambda e: e.scalar_tensor_tensor(out=t1, in0=t1, scalar=1.0, in1=xk_, op0=ALU.add, op1=ALU.mult)),
                     reads=[k_t1, kk_], writes=[k_t1])
                P.op('dve', (lambda e: e.tensor_tensor(out=rT[:, j, :], in0=xr_, in1=eg, op=ALU.mult)), reads=[kr, k_eg], writes=['rT'])
                P.op('dve', (lambda e: e.scalar_tensor_tensor(out=aT[:, j, :], in0=kkn, scalar=-1.0, in1=egm, op0=ALU.mult, op1=ALU.mult)),
                     reads=[k_kkn, k_egm], writes=['aT'])
                P.op('dve', (lambda e: e.tensor_tensor(out=b32, in0=kkn, in1=alr, op=ALU.mult)), reads=[k_kkn, k_alr], writes=[k_b32])
                P.op('dve', (lambda e: e.tensor_tensor(out=bT[:, j, :], in0=b32, in1=eig, op=ALU.mult)), reads=[k_b32, k_eig], writes=['bT'])
                P.op('dve', (lambda e: e.tensor_tensor(out=kT[:, j, :], in0=t1, in1=eig, op=ALU.mult)), reads=[k_t1, k_eig], writes=['kT'])
                P.op('dve', (lambda e: e.scalar_tensor_tensor(out=kk2, in0=xr_, scalar=ct['rkc'][:, j:j + 1], in1=t1, op0=ALU.mult, op1=ALU.mult)),
                     reads=[kr, 'rkc', k_t1], writes=[k_kk2])
                vb = b32.bitcast(BF16)[:, 0:TB]
                P.op('act', (lambda e: e.copy(out=vb, in_=xv_)), reads=[kv], writes=[k_b32])

                def fnVT(e):
                    ins = None
                    for ci_ in range(2 * NT):
                        for hp in (slice(0, 64), slice(64, 128)):
                            ins = e.transpose(out=tp[hp, tk + ci_ * 64:tk + ci_ * 64 + 64], in_=vb[hp, ci_ * 64:(ci_ + 1) * 64], identity=identb[hp, hp])
                    return ins
                P.op('pe', fnVT, reads=[k_b32, 'identb'], writes=['tp'])
                P.op('act', (lambda e: e.copy(out=vtk[:, j, :, :], in_=tp[:, tk:tk + 2 * NT * 64].rearrange("p (c v) -> p c v", v=64))),
                     writes=['vtk', 'tp'])
                for ti in range(NT):
                    tcs = slice(ti * 128, (ti + 1) * 128)
                    P.op('pe', (lambda e, ti=ti, tcs=tcs: e.matmul(Eb[:, ti * 16 + j * 2:ti * 16 + j * 2 + 2], lhsT=kk2[:, tcs], rhs=ct['bo2'][:], start=True, stop=True)),
                         reads=[k_kk2, 'bo2'], writes=['Eb'])
                    P.op('pe', (lambda e, tcs=tcs: e.transpose(out=tp[:, tv:tv + 128], in_=vb[:, tcs], identity=identb[:])),
                         reads=[k_b32, 'identb'], writes=['tp'])
                    P.op('act', (lambda e, ti=ti: e.copy(out=vtok[:, ti, jc], in_=tp[:, tv:tv + 128])), writes=['vtok', 'tp'])

            for it in range(5):
                P.begin_streams(3)
                if it < 4:
                    P.set_stream(0)
                    st['pb'] = 0
                    emit_proj(it, *XSETS[it % 2])
                    st['pb'] = None
                if it > 0:
                    XS, XK = XSETS[(it - 1) % 2]
                    for jj in range(2):
                        P.set_stream(1 + jj)
                        prep_pair((it - 1) * 2 + jj, jj, XS[jj], XS[2 + jj], XS[4 + jj], XK[jj], XK[2 + jj], XK[4 + jj])
                P.merge_streams()
            P.op('dve', lambda e: e.tensor_copy(out=bon[:].rearrange("p t h -> p (t h)"), in_=Eb[:, 0:NT * 16]), writes=['bon', 'Eb'])

            def aux_ga(i):
                slot = load_unit(b, nu(), w_in_d, 3200 + i * 256, 256, 16)
                pr = next_pair()
                for ti in range(NT):
                    mm_tm(slot, 16, ti, hT, 'hT', pr[ti])
                for ti in range(NT):
                    ai = pr[ti]
                    P.op('act', (lambda e, ai=ai, ti=ti, i=i: e.activation(out=sa[:, ti, i * 256:(i + 1) * 256], in_=A[ai][:, 0:256], func=AF.Silu)),
                         writes=['sa', akey(ai)])

            def aux_q(i):
                slot = load_unit(b, nu(), w_in_d, 4224 + i * 256, 256, 16)
                pr = next_pair()
                for f in range(2):
                    mm_fm(slot, 16, f, hT, 'hT', pr[f])
                for f in range(2):
                    ai = pr[f]
                    P.op('act', (lambda e, ai=ai, i=i, f=f: e.activation(out=qT[:, i * 2 + f, :], in_=A[ai][:, 0:TB], func=AF.Copy, scale=0.125)),
                         writes=['qT', akey(ai)])

            def aux_kd(i):
                slot = load_unit(b, nu(), w_in_d, 5248 + i * 128, 128, 16, dup=True)
                pr = next_pair()
                for f in range(2):
                    mm_fm(slot, 16, f, hT, 'hT', pr[f])
                for f in range(2):
                    ai = pr[f]
                    P.op('dve', (lambda e, ai=ai, i=i, f=f: e.tensor_copy(out=kTd[:, i * 2 + f, 128:128 + TB], in_=A[ai][:, 0:TB])),
                         writes=['kTd', akey(ai)])

            def aux_kv(i):
                slot = load_unit(b, nu(), w_in_d, 5248 + i * 256, 256, 16)
                pr = next_pair()
                for ti in range(NT):
                    mm_tm(slot, 16, ti, hT, 'hT', pr[ti])
                for ti in range(NT):
                    ai = pr[ti]
                    P.op('act', (lambda e, ai=ai, ti=ti, i=i: e.copy(out=kvo[:, ti, i * 256:(i + 1) * 256], in_=A[ai][:, 0:256])),
                         writes=['kvo', akey(ai)])
                    if i == 1:
                        P.op('act', (lambda e, ai=ai, ti=ti: e.copy(out=vat[:, 1 + ti, :], in_=A[ai][:, 0:256])),
                             writes=['vat', akey(ai)])

            def aux_gb(i):
                slot = load_unit(b, nu(), w_in_d, 5760 + i * 256, 256, 16)
                pr = next_pair()
                for ti in range(NT):
                    mm_tm(slot, 16, ti, hT, 'hT', pr[ti])
                for ti in range(NT):
                    ai = pr[ti]
                    P.op('act', (lambda e, ai=ai, ti=ti, i=i: e.activation(out=sbg[:, ti, i * 256:(i + 1) * 256], in_=A[ai][:, 0:256], func=AF.Silu)),
                         writes=['sbg', akey(ai)])

            def aux_carry():
                if 0 < b and not sample:
                    P.op('pool', lambda e: e.tensor_copy(out=kTd[:, :, 0:128], in_=kTd[:, :, TB:TB + 128]), reads=['kTd'], writes=['kTd'])
                    P.op('pool', lambda e: e.tensor_copy(out=vat[:, 0, :], in_=vat[:, NT, :]), reads=['vat'], writes=['vat'])

            AUX = [
                [lambda: aux_ga(0), lambda: aux_ga(1), lambda: aux_ga(2), lambda: aux_ga(3)],
                [lambda: aux_q(0), lambda: aux_q(1), lambda: aux_q(2), lambda: aux_q(3)],
                [aux_carry, lambda: aux_kd(0), lambda: aux_kd(1), lambda: aux_kv(0), lambda: aux_kv(1)],
                [lambda: aux_gb(0), lambda: aux_gb(1), lambda: aux_gb(2), lambda: aux_gb(3)],
            ]

            if DBG['stage'] <= 3:
                continue
            H2 = (slice(0, 64), slice(64, 128))
            for ti in range(NT):
                gt = b * NT + ti
                tcs = slice(ti * 128, (ti + 1) * 128)
                for c in range(2):
                    cp = slice(c * 64, c * 64 + 64)
                    cc = slice(ti * 128 + c * 64, ti * 128 + c * 64 + 64)
                    ci = ti * 2 + c
                    P.begin_streams(3)
                    P.set_stream(2)
                    st['pb0'] = True
                    for task in AUX[ci]:
                        task()
                    st['pb0'] = False
                    for j in range(8):
                        sx = (j % 2) if DBG.get('ss', 1) else 0
                        P.set_stream(sx)
                        kbtok, Mmx, LNS, XUb = kbtokS[sx], MmS[sx], LNSS[sx], XUbS[sx]
                        MC = Mb if sx == 0 else Cb
                        MCk = 'Mb' if sx == 0 else 'Cb'
                        DD = Db[0][:, 0:192] if sx == 0 else Eb[:, 192:384]
                        DDk = 'D0' if sx == 0 else 'Eb'
                        kX, kM, kL = 'X%d' % sx, 'Mm%d' % sx, 'LNS%d_' % sx
                        if sample:
                            seq = ti * 2 + c
                            P.op('sp', (lambda e, seq=seq, j=j: e.dma_start(out=Pf[:, j, :], in_=swkv_d[seq, :, j, :])),
                                 writes=['Pf%d' % j], chan='pst%d' % j)
                            P.op('dve', (lambda e, j=j: e.tensor_copy(out=Pb[:, j, :], in_=Pf[:, j, :])), reads=['Pf%d' % j], writes=['Pb%d' % j])
                        tpo = sx * 128

                        def fnT(e, j=j, cc=cc, tpo=tpo):
                            ins = None
                            for hp in H2:
                                e.transpose(out=tp[hp, tpo:tpo + 64], in_=kT[hp, j, cc], identity=identb[hp, hp])
                                ins = e.transpose(out=tp[hp, tpo + 64:tpo + 128], in_=bT[hp, j, cc], identity=identb[hp, hp])
                            return ins
                        P.op('pe', fnT, reads=['kT', 'bT', 'identb'], writes=['tp'])
                        P.op('act', (lambda e, kbtok=kbtok, tpo=tpo: e.copy(out=kbtok[:, 0:128], in_=tp[:, tpo:tpo + 128])), writes=['kbtok%d' % sx, 'tp'])

                        def fnM(e, j=j, cc=cc, MC=MC):
                            ins = None
                            for hp in H2:
                                e.matmul(MC[hp, 0:64], lhsT=bT[hp, j, cc], rhs=aT[hp, j, cc], start=True, stop=True)
                                e.matmul(MC[hp, 64:128], lhsT=kT[hp, j, cc], rhs=aT[hp, j, cc], start=True, stop=True)
                                e.matmul(MC[hp, 128:192], lhsT=bT[hp, j, cc], rhs=rT[hp, j, cc], start=True, stop=True)
                                e.matmul(MC[hp, 192:256], lhsT=kT[hp, j, cc], rhs=rT[hp, j, cc], start=True, stop=True)
                                ins = e.matmul(MC[hp, 256:320], lhsT=aT[hp, j, cc], rhs=bT[hp, j, cc], start=True, stop=True)
                            return ins
                        P.op('pe', fnM, reads=['aT', 'bT', 'kT', 'rT'], writes=[MCk])
                        P.op('dve', (lambda e, Mmx=Mmx, MC=MC: e.tensor_tensor(out=Mmx[:, 0:320], in0=MC[:, 0:320], in1=ct['MU5'][:], op=ALU.mult)),
                             reads=['MU5'], writes=[kM, MCk])
                        P.op('pool', (lambda e, LNS=LNS, Mmx=Mmx: e.tensor_tensor(out=LNS[0][:, 128:192], in0=Mmx[:, 0:64], in1=I2b[:], op=ALU.add)),
                             reads=[kM, 'I2b'], writes=[kL + '0'])
                        for lvl in range(1, 7):
                            cur, nxt = (lvl - 1) % 2, lvl % 2
                            if lvl == 1:
                                Lc, Nc = Mmx[:, 256:320], Mmx[:, 0:64]
                                rk = [kM, kL + '0']
                            else:
                                Lc, Nc = LNS[cur][:, 0:64], LNS[cur][:, 64:128]
                                rk = [kL + str(cur)]
                            Sc = LNS[cur][:, 128:192]

                            def fnD(e, Lc=Lc, Nc=Nc, Sc=Sc, lvl=lvl, DD=DD):
                                ins = None
                                for hp in H2:
                                    if lvl < 6:
                                        e.matmul(DD[hp, 0:64], lhsT=Nc[hp], rhs=Lc[hp], start=True, stop=True)
                                    if lvl < 5:
                                        e.matmul(DD[hp, 64:128], lhsT=Lc[hp], rhs=Nc[hp], start=True, stop=True)
                                    if lvl == 1:
                                        ins = e.matmul(DD[hp, 128:192], lhsT=I2b[hp], rhs=Sc[hp], start=True, stop=True)
                                    else:
                                        e.matmul(DD[hp, 128:192], lhsT=I2b[hp], rhs=Sc[hp], start=True, stop=False)
                                        ins = e.matmul(DD[hp, 128:192], lhsT=Lc[hp], rhs=Sc[hp], start=False, stop=True)
                                return ins
                            P.op('pe', fnD, reads=rk + ['I2b'], writes=[DDk])
                            lo = 0 if lvl < 6 else 128
                            if (lvl + sx) % 2 == 1:
                                P.op('dve', (lambda e, LNS=LNS, nxt=nxt, lo=lo, DD=DD: e.tensor_copy(out=LNS[nxt][:, lo:192], in_=DD[:, lo:192])),
                                     writes=[kL + str(nxt), DDk])
                            else:
                                P.op('act', (lambda e, LNS=LNS, nxt=nxt, lo=lo, DD=DD: e.copy(out=LNS[nxt][:, lo:192], in_=DD[:, lo:192])),
                                     writes=[kL + str(nxt), DDk])

                        def fnX(e, j=j, cc=cc, ci=ci, MC=MC, Mmx=Mmx):
                            ins = None
                            for hp in H2:
                                e.matmul(MC[hp, 320:384], lhsT=aT[hp, j, cc], rhs=Pb[hp, j, :], start=True, stop=False)
                                ins = e.matmul(MC[hp, 320:384], lhsT=Mmx[hp, 64:128], rhs=vtk[hp, j, ci, :], start=False, stop=True)
                            return ins
                        P.op('pe', fnX, reads=['aT', 'Pb%d' % j, kM, 'vtk'], writes=[MCk])
                        P.op('dve', (lambda e, XUb=XUb, MC=MC: e.tensor_copy(out=XUb[:, 0:64], in_=MC[:, 320:384])), writes=[kX + 'x', MCk])

                        def fnU(e, MC=MC, LNS=LNS, XUb=XUb):
                            ins = None
                            for hp in H2:
                                ins = e.matmul(MC[hp, 384:448], lhsT=LNS[0][hp, 128:192], rhs=XUb[hp, 0:64], start=True, stop=True)
                            return ins
                        P.op('pe', fnU, reads=[kX + 'x', kL + '0'], writes=[MCk])
                        P.op('act', (lambda e, XUb=XUb, MC=MC: e.copy(out=XUb[:, 64:128], in_=MC[:, 384:448])), writes=[kX + 'u', MCk])

                        def fnO(e, j=j, cp=cp, cc=cc, ci=ci, Mmx=Mmx, XUb=XUb):
                            ins = None
                            for hh, hp in enumerate(H2):
                                ob_ = Db[1][cp, j * 64:j * 64 + 64] if hh == 0 else Aacc[1][cp, j * 64:j * 64 + 64]
                                e.matmul(ob_, lhsT=rT[hp, j, cc], rhs=Pb[hp, j, :], start=True, stop=False)
                                e.matmul(ob_, lhsT=Mmx[hp, 128:192], rhs=XUb[hp, 64:128], start=False, stop=False)
                                ins = e.matmul(ob_, lhsT=Mmx[hp, 192:256], rhs=vtk[hp, j, ci, :], start=False, stop=True)
                            return ins
                        P.op('pe', fnO, reads=['rT', 'Pb%d' % j, kM, kX + 'u', 'vtk'], writes=['D1', 'PB1'])

                        def fnP(e, j=j, ci=ci, MC=MC, kbtok=kbtok, XUb=XUb):
                            ins = None
                            for hp in H2:
                                e.matmul(MC[hp, 448:512], lhsT=identf[hp, hp], rhs=Pf[hp, j, :], start=True, stop=False)
                                e.matmul(MC[hp, 448:512], lhsT=kbtok[hp, 64:128], rhs=XUb[hp, 64:128], start=False, stop=False)
                                ins = e.matmul(MC[hp, 448:512], lhsT=kbtok[hp, 0:64], rhs=vtk[hp, j, ci, :], start=False, stop=True)
                            return ins
                        P.op('pe', fnP, reads=['identf', 'Pf%d' % j, 'kbtok%d' % sx, kX + 'u', 'vtk'], writes=[MCk])
                        gcol = gC[:, j, ci:ci + 1]
                        P.op('act', (lambda e, j=j, gcol=gcol, MC=MC: e.activation(out=Pf[:, j, :], in_=MC[:, 448:512], func=AF.Copy, scale=gcol)),
                             reads=['gC%d' % j], writes=['Pf%d' % j, MCk])
                        P.op('dve', (lambda e, j=j, gcol=gcol, MC=MC: e.tensor_scalar(out=Pb[:, j, :], in0=MC[:, 448:512], scalar1=gcol, scalar2=None, op0=ALU.mult)),
                             reads=['gC%d' % j], writes=['Pb%d' % j, MCk])
                        if sample:
                            seq = ti * 2 + c
                            P.op('pool', (lambda e, seq=seq, j=j: e.dma_start(out=wkvs_o[seq, :, j, :], in_=Pf[:, j, :])),
                                 reads=['Pf%d' % j], chan='o_pf%d' % j, cb=out_idx)
                    P.merge_streams()
                y4 = ysb[:].rearrange("p (j h c) -> p j h c", h=2, c=64)
                if DBG.get('oe', 0) == 0:
                    P.op('dve', lambda e: e.tensor_copy(out=y4[:, :, 0, :], in_=Db[1][:, :].rearrange("p (j c) -> p j c", c=64)), writes=['ysb', 'D1'])
                    P.op('act', lambda e: e.copy(out=y4[:, :, 1, :], in_=Aacc[1][:, :].rearrange("p (j c) -> p j c", c=64)), writes=['ysb', 'PB1'])
                else:
                    for j in range(8):
                        P.op('dve', (lambda e, j=j: e.tensor_copy(out=ysb[:, j * 128:j * 128 + 64], in_=Db[1][:, j * 64:j * 64 + 64])), writes=['ysb', 'D1'])
                        P.op('act', (lambda e, j=j: e.copy(out=ysb[:, j * 128 + 64:j * 128 + 128], in_=Aacc[1][:, j * 64:j * 64 + 64])), writes=['ysb', 'PB1'])
                if gt == 15:
                    out_idx.append(P.op('pool', lambda e: e.dma_start(out=wkvp_o, in_=Pf[:]), reads=['Pf%d' % j for j in range(8)], chan='o_pfp'))

                if DBG.get('dump', 0):
                    out_idx.append(P.op('pool', (lambda e, gt=gt: e.dma_start(out=y_o[(gt + 4) * 128:(gt + 5) * 128, 0:1024], in_=ysb[:])), reads=['ysb'], chan='o_dbg'))
                y3 = ysb[:].rearrange("p (h c) -> p h c", c=64)
                q3 = ysq[:].rearrange("p (h c) -> p h c", c=64)
                P.op('dve', lambda e: e.tensor_reduce(out=small[:, 8:24], in_=y3, axis=AX.X, op=ALU.add), reads=['ysb'], writes=['gn_s1'])
                P.op('act', lambda e: e.activation(out=ysq[:], in_=ysb[:], func=AF.Square), reads=['ysb'], writes=['ysq'])
                P.op('dve', lambda e: e.tensor_reduce(out=small[:, 24:40], in_=q3, axis=AX.X, op=ALU.add), reads=['ysq'], writes=['gn_s2'])
                P.op('dve', lambda e: e.tensor_scalar(out=small[:, 40:56], in0=small[:, 8:24], scalar1=1.0 / 64, scalar2=None, op0=ALU.mult),
                     reads=['gn_s1'], writes=['gn_mean'])
                P.op('dve', lambda e: e.tensor_tensor(out=small[:, 8:24], in0=small[:, 40:56], in1=small[:, 40:56], op=ALU.mult),
                     reads=['gn_mean', 'gn_s1'], writes=['gn_s1'])
                P.op('dve', lambda e: e.scalar_tensor_tensor(out=small[:, 24:40], in0=small[:, 24:40], scalar=1.0 / 64, in1=small[:, 8:24], op0=ALU.mult, op1=ALU.subtract),
                     reads=['gn_s2', 'gn_s1'], writes=['gn_s2'])
                P.op('dve', lambda e: e.tensor_scalar(out=small[:, 24:40], in0=small[:, 24:40], scalar1=LNX_EPS, scalar2=None, op0=ALU.add),
                     reads=['gn_s2'], writes=['gn_s2'])
                P.op('act', lambda e: e.activation(out=small[:, 24:40], in_=small[:, 24:40], func=AF.Sqrt), reads=['gn_s2'], writes=['gn_s2'])
                P.op('dve', lambda e: e.reciprocal(out=small[:, 24:40], in_=small[:, 24:40]), reads=['gn_s2'], writes=['gn_s2'])
                P.op('dve', lambda e: e.tensor_tensor(out=y3, in0=y3, in1=small[:, 40:56].unsqueeze(2).to_broadcast([128, 16, 64]), op=ALU.subtract),
                     reads=['ysb', 'gn_mean'], writes=['ysb'])
                P.op('dve', lambda e: e.tensor_tensor(out=y3, in0=y3, in1=small[:, 24:40].unsqueeze(2).to_broadcast([128, 16, 64]), op=ALU.mult),
                     reads=['ysb', 'gn_s2'], writes=['ysb'])
                P.op('dve', lambda e: e.tensor_tensor(out=ysb[:], in0=ysb[:], in1=ct['lnxw'][:], op=ALU.mult), reads=['ysb', 'lnxw'], writes=['ysb'])
                P.op('dve', lambda e: e.tensor_tensor(out=ysb[:], in0=ysb[:], in1=ct['lnxb'][:], op=ALU.add), reads=['ysb', 'lnxb'], writes=['ysb'])
                P.op('dve', (lambda e, ti=ti: e.tensor_tensor(out=q3, in0=vtok[:, ti, :].rearrange("p (h c) -> p h c", c=64),
                                                              in1=bon[:, ti, :].unsqueeze(2).to_broadcast([128, 16, 64]), op=ALU.mult)),
                     reads=['vtok', 'bon', 'ysq'], writes=['ysq'])
                P.op('dve', lambda e: e.tensor_tensor(out=ysb[:], in0=ysb[:], in1=ysq[:], op=ALU.add), reads=['ysb', 'ysq'], writes=['ysb'])
                P.op('dve', (lambda e, ti=ti: e.tensor_tensor(out=yab[:], in0=ysb[:], in1=sa[:, ti, :], op=ALU.mult)), reads=['ysb', 'sa'], writes=['yab'])

                if DBG.get('dump', 0):
                    out_idx.append(P.op('pool', (lambda e, gt=gt: e.dma_start(out=y_o[(gt + 8) * 128:(gt + 9) * 128, 0:1024], in_=yab[:])), reads=['yab'], chan='o_dbg'))
                def fn(e):
                    ins = None
                    for k in range(8):
                        ins = e.transpose(out=tp[:, k * 128:(k + 1) * 128], in_=yab[:, k * 128:(k + 1) * 128], identity=identb[:])
                    return ins
                P.op('pe', fn, reads=['yab', 'identb'], writes=['tp'])
                P.op('act', (lambda e, tcs=tcs: e.copy(out=yaT[:, :, tcs], in_=tp[:, :].rearrange("p (k t) -> p k t", t=128))), writes=['yaT', 'tp'])

            if DBG['stage'] <= 4:
                continue
            for ti in range(NT):
                gt = b * NT + ti
                if gt == 15:
                    out_idx.append(P.op('pool', (lambda e, ti=ti: e.dma_start(out=kp_o, in_=kvo[:, ti, 0:256])), reads=['kvo'], chan='o_kv'))
                    out_idx.append(P.op('pool', (lambda e, ti=ti: e.dma_start(out=vp_o, in_=kvo[:, ti, 256:512])), reads=['kvo'], chan='o_kv'))
                if sample:
                    for c in range(2):
                        seq = ti * 2 + c
                        cp = slice(c * 64, c * 64 + 64)
                        out_idx.append(P.op('pool', (lambda e, ti=ti, seq=seq, cp=cp: e.dma_start(out=ks_o[seq, 64:128, :], in_=kvo[cp, ti, 0:256])), reads=['kvo'], chan='o_kv'))
                        out_idx.append(P.op('pool', (lambda e, ti=ti, seq=seq, cp=cp: e.dma_start(out=vs_o[seq, 64:128, :], in_=kvo[cp, ti, 256:512])), reads=['kvo'], chan='o_kv'))
                        out_idx.append(P.op('pool', (lambda e, seq=seq: e.dma_start(out=ks_o[seq, 0:64, :], in_=ck_raw[seq, 64:128, :])), chan='o_kv'))
                        out_idx.append(P.op('pool', (lambda e, seq=seq: e.dma_start(out=vs_o[seq, 0:64, :], in_=cv_raw[seq, 64:128, :])), chan='o_kv'))

            if DBG['stage'] <= 5:
                continue
            fence(ARENA_PREP, ARENA_FIN)

            def aux_ma(i):
                slot = load_unit(b, nu(), w_in_d, 6784 + i * 256, 256, 16)
                pr = next_pair()
                for f in range(2):
                    mm_fm(slot, 16, f, hT, 'hT', pr[f])
                for f in range(2):
                    ai = pr[f]
                    P.op('act', (lambda e, ai=ai, f=f: e.activation(out=sga[:, f, :], in_=A[ai][:, 0:TB], func=AF.Sigmoid)), writes=['sga', akey(ai)])
                slot = load_unit(b, nu(), p_a_d, i * 256, 256, 8)
                pr = next_pair()
                for f in range(2):
                    mm_fm(slot, 8, f, yaT, 'yaT', pr[f])
                for f in range(2):
                    ai = pr[f]
                    P.op('dve', (lambda e, ai=ai, f=f, i=i: e.tensor_tensor(out=ta_all[:, i * 2 + f, :], in0=sga[:, f, :], in1=A[ai][:, 0:TB], op=ALU.mult)),
                         reads=['sga'], writes=['taall', akey(ai)])
            for ti in range(NT):
                gt = b * NT + ti
                tcs = slice(ti * 128, (ti + 1) * 128)
                nkb = 3 if sample else 2
                nk = nkb * 128
                Dm = ct['DmS'] if sample else (ct['DmP0'] if gt == 0 else ct['DmP'])
                Dk = 'DmS' if sample else ('DmP0' if gt == 0 else 'DmP')
                P.begin_streams(3)
                P.set_stream(2)
                for i_ in range(ti * 4, ti * 4 + 4):
                    aux_ma(i_)
                for h in range(16):
                    sx = (h % 2) if DBG.get('as', 1) else 0
                    P.set_stream(sx)
                    g = h // 4
                    f = h // 2
                    hp = slice((h % 2) * 64, (h % 2) * 64 + 64)
                    SB = Mb if sx == 0 else Cb
                    SBk = 'Mb' if sx == 0 else 'Cb'
                    s_x, e_x, eT_x, sm = s_sbS[sx], e_sbS[sx], eTS[sx], smallS[sx]
                    ks = 'at%d_' % sx
                    tpo = sx * 512

                    def fnS(e, g=g, f=f, hp=hp, ti=ti, tcs=tcs, sample=sample, SB=SB):
                        kb_ = hp.start
                        if not sample:
                            return mmk(e, SB[:, 0:256], qT[hp, f, tcs], kTd[hp, g, ti * 128:ti * 128 + 256], kb_)
                        mmk(e, SB[:, 0:128], qT[hp, f, tcs], kc[hp, ti * 2, g, :], kb_)
                        mmk(e, SB[:, 128:256], qT[hp, f, tcs], kc[hp, ti * 2 + 1, g, :], kb_)
                        return mmk(e, SB[:, 256:384], qT[hp, f, tcs], kTd[hp, g, 128 + ti * 128:256 + ti * 128], kb_)
                    P.op('pe', fnS, reads=['qT', 'kTd', 'kc'], writes=[SBk])
                    P.op('dve', (lambda e, h=h, nk=nk, Dm=Dm, s_x=s_x, SB=SB: e.scalar_tensor_tensor(out=s_x[:, 0:nk], in0=Dm[:, 0:nk], scalar=SLOPES[h], in1=SB[:, 0:nk], op0=ALU.mult, op1=ALU.add)),
                         reads=[Dk], writes=[ks + 's', SBk])
                    P.op('dve', (lambda e, nk=nk, s_x=s_x, sm=sm: e.tensor_reduce(out=sm[:, 0:1], in_=s_x[:, 0:nk], axis=AX.X, op=ALU.max)), reads=[ks + 's'], writes=[ks + 'mx'])
                    P.op('dve', (lambda e, h=h, sm=sm: e.tensor_scalar(out=sm[:, 1:2], in0=sm[:, 0:1], scalar1=ct['sinks'][:, h:h + 1], scalar2=-1.0, op0=ALU.max, op1=ALU.mult)),
                         reads=[ks + 'mx', 'sinks'], writes=[ks + 'negm'])
                    P.op('dve', (lambda e, sm=sm: e.memset(sm[:, 2:3], 0.0)), writes=[ks + 'rs'])
                    P.op('act', (lambda e, nk=nk, s_x=s_x, e_x=e_x, sm=sm: e.activation(out=e_x[:, 0:nk], in_=s_x[:, 0:nk], func=AF.Exp, bias=sm[:, 1:2], accum_out=sm[:, 2:3])),
                         reads=[ks + 's', ks + 'negm', ks + 'rs'], writes=[ks + 'e', ks + 'rs'])
                    P.op('act', (lambda e, h=h, sm=sm: e.activation(out=sm[:, 3:4], in_=sm[:, 1:2], func=AF.Exp, bias=ct['sinks'][:, h:h + 1])),
                         reads=[ks + 'negm', 'sinks'], writes=[ks + 'es'])
                    P.op('dve', (lambda e, sm=sm: e.tensor_tensor(out=sm[:, 3:4], in0=sm[:, 3:4], in1=sm[:, 2:3], op=ALU.add)), reads=[ks + 'es', ks + 'rs'], writes=[ks + 'es'])
                    P.op('dve', (lambda e, h=h, sm=sm: e.reciprocal(out=rden[:, h:h + 1], in_=sm[:, 3:4])), reads=[ks + 'es'], writes=['rden%d' % h])

                    def fnT(e, nkb=nkb, e_x=e_x, tpo=tpo):
                        ins = None
                        for kb in range(nkb):
                            ins = e.transpose(out=tp[:, tpo + kb * 128:tpo + (kb + 1) * 128], in_=e_x[:, kb * 128:(kb + 1) * 128], identity=identb[:])
                        return ins
                    P.op('pe', fnT, reads=[ks + 'e', 'identb'], writes=['tp'])
                    P.op('act', (lambda e, nk=nk, eT_x=eT_x, tpo=tpo: e.copy(out=eT_x[:, 0:nk], in_=tp[:, tpo:tpo + nk])), writes=[ks + 'eT', 'tp'])

                    def fnV(e, g=g, ti=ti, nkb=nkb, sample=sample, SB=SB, eT_x=eT_x):
                        gs = slice(g * 64, g * 64 + 64)
                        po = SB[:, 448:512]
                        if not sample:
                            e.matmul(po, lhsT=eT_x[:, 0:128], rhs=vat[:, ti, gs], start=True, stop=False)
                            return e.matmul(po, lhsT=eT_x[:, 128:256], rhs=vat[:, ti + 1, gs], start=False, stop=True)
                        e.matmul(po, lhsT=eT_x[:, 0:128], rhs=vc[:, ti * 2, gs], start=True, stop=False)
                        e.matmul(po, lhsT=eT_x[:, 128:256], rhs=vc[:, ti * 2 + 1, gs], start=False, stop=False)
                        return e.matmul(po, lhsT=eT_x[:, 256:384], rhs=vat[:, ti + 1, gs], start=False, stop=True)
                    P.op('pe', fnV, reads=[ks + 'eT', 'vat', 'vc'], writes=[SBk])
                    P.op('dve', (lambda e, h=h, SB=SB: e.tensor_scalar(out=ob[:, h * 64:(h + 1) * 64], in0=SB[:, 448:512], scalar1=rden[:, h:h + 1], scalar2=None, op0=ALU.mult)),
                         reads=['rden%d' % h], writes=['ob%d' % h, SBk])
                P.merge_streams()
                if DBG.get('dump', 0):
                    out_idx.append(P.op('pool', (lambda e, gt=gt: e.dma_start(out=y_o[(gt + 4) * 128:(gt + 5) * 128, 1024:2048], in_=ob[:])), reads=['ob'] + ['ob%d' % h for h in range(16)], chan='o_dbg'))
                P.op('dve', (lambda e, ti=ti: e.tensor_tensor(out=yab[:], in0=ob[:], in1=sbg[:, ti, :], op=ALU.mult)), reads=['ob%d' % h for h in range(16)] + ['sbg'], writes=['yab'])

                if DBG.get('dump', 0):
                    out_idx.append(P.op('pool', (lambda e, gt=gt: e.dma_start(out=y_o[(gt + 8) * 128:(gt + 9) * 128, 1024:2048], in_=yab[:])), reads=['yab'], chan='o_dbg'))
                def fn(e):
                    ins = None
                    for k in range(8):
                        ins = e.transpose(out=tp[:, k * 128:(k + 1) * 128], in_=yab[:, k * 128:(k + 1) * 128], identity=identb[:])
                    return ins
                P.op('pe', fn, reads=['yab', 'identb'], writes=['tp'])
                P.op('act', (lambda e, tcs=tcs: e.copy(out=ybT[:, :, tcs], in_=tp[:, :].rearrange("p (k t) -> p k t", t=128))), writes=['ybT', 'tp'])


            if DBG.get('dump', 0):
                out_idx.append(P.op('pool', (lambda e: e.dma_start(out=y_o[12 * 128:13 * 128, :].rearrange('p (k t) -> p k t', t=TB), in_=yaT[:])), reads=['yaT'], chan='o_dbg'))
                out_idx.append(P.op('pool', (lambda e: e.dma_start(out=y_o[13 * 128:14 * 128, :].rearrange('p (k t) -> p k t', t=TB), in_=ybT[:])), reads=['ybT'], chan='o_dbg'))
                out_idx.append(P.op('pool', (lambda e: e.dma_start(out=y_o[14 * 128:15 * 128, :].rearrange('p (k t) -> p k t', t=TB), in_=hT[:, 0:8, :])), reads=['hT'], chan='o_dbg'))
            if DBG['stage'] <= 6:
                continue
            for i in range(8):
                slot = load_unit(b, nu(), w_in_d, 8832 + i * 256, 256, 16)
                pr = next_pair()
                for f in range(2):
                    mm_fm(slot, 16, f, hT, 'hT', pr[f])
                for f in range(2):
                    ai = pr[f]
                    P.op('act', (lambda e, ai=ai, f=f: e.activation(out=sgb[:, f, :], in_=A[ai][:, 0:TB], func=AF.Sigmoid)), writes=['sgb', akey(ai)])
                slot = load_unit(b, nu(), p_b_d, i * 256, 256, 8)
                pr = next_pair()
                for f in range(2):
                    mm_fm(slot, 8, f, ybT, 'ybT', pr[f])
                for f in range(2):
                    ai = pr[f]
                    P.op('dve', (lambda e, ai=ai, f=f: e.tensor_tensor(out=sgb[:, f, :], in0=sgb[:, f, :], in1=A[ai][:, 0:TB], op=ALU.mult)),
                         reads=['sgb'], writes=['sgb', akey(ai)])
                    P.op('pool', (lambda e, i=i, f=f: e.tensor_tensor(out=mergedT[:, i * 2 + f, :], in0=ta_all[:, i * 2 + f, :], in1=sgb[:, f, :], op=ALU.add)),
                         reads=['taall', 'sgb'], writes=['mergedT'])
            fence(['taall'], ['xr'])
            for ti in range(NT):
                gt = b * NT + ti
                P.op('sp', (lambda e, gt=gt, ti=ti: e.dma_start(out=xr[:, ti, :], in_=x_d[gt * 128:(gt + 1) * 128, :])),
                     writes=['xr'], chan='xr')
            P.begin_streams(2)
            if b + 1 < DBG['nblk']:
                P.set_stream(1)
                emit_S1(b + 1)
            P.set_stream(0)
            for i in range(8):
                slot = load_unit(b, nu(), w_o_d, i * 256, 256, 16)
                pr = next_pair()
                for ti in range(NT):
                    mm_tm(slot, 16, ti, mergedT, 'mergedT', pr[ti])
                for ti in range(NT):
                    ai = pr[ti]
                    P.op('dve', (lambda e, ai=ai, ti=ti, i=i: e.tensor_tensor(out=xr[:, ti, i * 256:(i + 1) * 256], in0=xr[:, ti, i * 256:(i + 1) * 256], in1=A[ai][:, 0:256], op=ALU.add)),
                         reads=['xr'], writes=['xr', akey(ai)])
            for ti in range(NT):
                gt = b * NT + ti
                P.op('dve', lambda e: e.memset(small[:, 0:1], 0.0), writes=['ssq'])
                P.op('act', (lambda e, ti=ti: e.activation(out=sa[:].rearrange("p t c -> p (t c)"), in_=xr[:, ti, :], func=AF.Square, accum_out=small[:, 0:1])),
                     reads=['xr', 'ssq'], writes=['sa', 'ssq'])
                P.op('dve', lambda e: e.tensor_scalar(out=small[:, 1:2], in0=small[:, 0:1], scalar1=1.0 / D, scalar2=RMS_EPS, op0=ALU.mult, op1=ALU.add),
                     reads=['ssq'], writes=['ms'])
                P.op('act', lambda e: e.activation(out=small[:, 2:3], in_=small[:, 1:2], func=AF.Sqrt), reads=['ms'], writes=['sq'])
                P.op('dve', lambda e: e.reciprocal(out=small[:, 3:4], in_=small[:, 2:3]), reads=['sq'], writes=['rstd'])
                P.op('dve', (lambda e, ti=ti: e.scalar_tensor_tensor(out=xr[:, ti, :], in0=xr[:, ti, :], scalar=small[:, 3:4], in1=ct['gfbc'][:], op0=ALU.mult, op1=ALU.mult)),
                     reads=['xr', 'rstd', 'gfbc'], writes=['xr'])
                P.op('pool', (lambda e, gt=gt, ti=ti: e.dma_start(out=y_o[gt * 128:(gt + 1) * 128, :], in_=xr[:, ti, :])),
                     reads=['xr'], writes=['xr_st'], chan='o_y', cb=out_idx)
            P.merge_streams()
            if b == NBLK - 2:
                out_idx.append(P.op('pool', lambda e: e.dma_start(out=shp_o, in_=plast[:]), reads=['plast'], chan='o_sh'))
            if sample:
                out_idx.append(P.op('pool', lambda e: e.dma_start(out=shs_o, in_=shs[:]), reads=['shs'], chan='o_sh'))
            assert st['u'] <= 80, st['u']

        P.wait_all('pool', out_idx)
        P.emit()
        build.stats = P.stats
    return nc


_CACHE = {}


def _consts():
    c = {}
    c['identf'] = np.eye(128, dtype=np.float32)
    s = np.arange(128)[:, None]
    t = np.arange(128)[None, :]
    same = (s // 64) == (t // 64)
    MUs = (same & (s < t)).astype(np.float32)
    MUi = (same & (s <= t)).astype(np.float32)
    c['MU4'] = np.concatenate([MUs, MUs, MUi, MUi], axis=1)
    c['MLs'] = (same & (t < s)).astype(np.float32)
    c['bones'] = same.astype(np.float32)
    s6 = np.arange(64)[:, None]
    t6 = np.arange(64)[None, :]
    mus = (s6 < t6).astype(np.float32)
    mui = (s6 <= t6).astype(np.float32)
    mls = (t6 < s6).astype(np.float32)
    m5 = np.concatenate([mus, mus, mui, mui, mls], axis=1)
    c['MU5'] = np.concatenate([m5, m5], axis=0)
    c['I2'] = np.concatenate([np.eye(64, dtype=np.float32)] * 2, axis=0)
    bo2 = np.zeros((128, 2), np.float32)
    bo2[:64, 0] = 1
    bo2[64:, 1] = 1
    c['bo2'] = bo2
    rm = np.ones((128, TB), np.float32)
    rm[:, ::64] = 0
    c['resetm'] = rm
    NEG = -1e30
    i = np.arange(128)[:, None]
    k = np.arange(256)[None, :]
    dch = (2 + i // 64) - (k // 64)
    vis = (dch >= 0) & (dch <= 2)
    DmP = np.where(vis, -np.abs(128 + i - k).astype(np.float32), NEG).astype(np.float32)
    c['DmP'] = DmP
    DmP0 = DmP.copy()
    DmP0[:, :128] = NEG
    c['DmP0'] = DmP0
    DmS = np.full((128, 384), NEG, np.float32)
    tt = np.arange(64)[:, None]
    kk = np.arange(128)[None, :]
    t2 = np.arange(64)[None, :]
    for sq in range(2):
        rows = slice(sq * 64, sq * 64 + 64)
        DmS[rows, sq * 128:(sq + 1) * 128] = -(128 + tt - kk).astype(np.float32)
        DmS[rows, 256 + sq * 64:256 + sq * 64 + 64] = -np.abs(tt - t2).astype(np.float32)
    c['DmS'] = DmS
    return c


def kernel(x_prompt, x_sample, state_wkv, state_shift, cache_k, cache_v, g_norm, w_in, mu_shift, w0,
           w_w_up, a0, w_a_up, k_k, k_a, r_k, lnx_w, lnx_b, sinks, p_a, p_b, w_o, g_final):
    f32 = np.float32
    A_ = lambda v: np.ascontiguousarray(np.asarray(v, dtype=f32))
    if 'nc' not in _CACHE:
        _CACHE['nc'] = build()
    nc = _CACHE['nc']
    cst = _consts()
    col = lambda v: A_(np.asarray(v, f32).reshape(-1, 128).T)
    shared = dict(cst)
    shared.update(
        w_in=A_(w_in[0]), p_a=A_(p_a[0]), p_b=A_(p_b[0]), w_o=A_(w_o[0]),
        gbc=A_(np.broadcast_to(np.asarray(g_norm[0], f32)[None, :], (128, D))),
        gfbc=A_(np.broadcast_to(np.asarray(g_final, f32)[None, :], (128, D))),
        lnxw=A_(np.broadcast_to(np.asarray(lnx_w[0], f32)[None, :], (128, 1024))),
        lnxb=A_(np.broadcast_to(np.asarray(lnx_b[0], f32)[None, :], (128, 1024))),
        muT=col(mu_shift[0]), w0c=col(w0[0]), a0c=col(a0[0]), kkc=col(k_k[0]), kac=col(k_a[0]),
        rkc=col(np.asarray(r_k[0], f32).reshape(-1)),
        sinks=A_(np.broadcast_to(np.asarray(sinks[0], f32)[None, :], (128, 16))),
        wup=A_(np.concatenate([np.asarray(w_w_up[0], f32), np.asarray(w_a_up[0], f32)], axis=0)),
    )
    xp = np.asarray(x_prompt, f32)
    xs_ = np.asarray(x_sample, f32)
    swkv = np.asarray(state_wkv[0], f32)
    ssh = np.asarray(state_shift[0], f32)
    ck = np.asarray(cache_k[0], f32)
    cvv = np.asarray(cache_v[0], f32)
    in_maps = []
    for c in range(8):
        sl = slice(4 * c, 4 * c + 4)
        m = dict(shared)
        m['x'] = A_(np.concatenate([xp[c], xs_[sl].reshape(256, D)], axis=0))
        sw = swkv[sl].reshape(4, 8, 2, 64, 64)
        m['swkv'] = A_(sw.transpose(0, 2, 4, 1, 3).reshape(4, 128, 8, 64))
        m['sshT'] = A_(ssh[sl].reshape(4, 25, 128).transpose(2, 1, 0))
        ckc = ck[sl]
        kt_ = ckc.transpose(3, 0, 2, 1)
        m['ckT'] = A_(np.concatenate([kt_, kt_], axis=0))
        m['cv'] = A_(cvv[sl].reshape(4, 128, 256).transpose(1, 0, 2))
        m['ck_raw'] = A_(ckc.reshape(4, 128, 256))
        m['cv_raw'] = A_(cvv[sl].reshape(4, 128, 256))
        in_maps.append(m)
    res = run_bass_kernel_spmd(nc, in_maps, core_ids=list(range(8)))
    R = res.results
    y_prompt = np.stack([R[c]['y'][:2048] for c in range(8)]).astype(f32)
    y_sample = np.concatenate([R[c]['y'][2048:].reshape(4, 64, D) for c in range(8)]).astype(f32)

    def unP(a):
        return a.reshape(2, 64, 8, 64).transpose(2, 0, 3, 1).reshape(16, 64, 64)
    wkv_p = np.stack([unP(R[c]['wkv_p']) for c in range(8)])[None].astype(f32)
    wkv_s = np.stack([unP(R[c]['wkv_s'][s]) for c in range(8) for s in range(4)])[None].astype(f32)
    shift_p = np.stack([R[c]['shift_p'].T.reshape(3200) for c in range(8)])[None].astype(f32)
    shift_s = np.stack([R[c]['shift_s'][:, :, s].T.reshape(3200) for c in range(8) for s in range(4)])[None].astype(f32)
    k_p = np.stack([R[c]['k_p'].reshape(128, 4, 64) for c in range(8)])[None].astype(f32)
    v_p = np.stack([R[c]['v_p'].reshape(128, 4, 64) for c in range(8)])[None].astype(f32)
    k_s = np.concatenate([R[c]['k_s'].reshape(4, 128, 4, 64) for c in range(8)])[None].astype(f32)
    v_s = np.concatenate([R[c]['v_s'].reshape(4, 128, 4, 64) for c in range(8)])[None].astype(f32)
    return (y_prompt, y_sample, wkv_p, shift_p, k_p, v_p, wkv_s, shift_s, k_s, v_s)
```

```python
import numpy as np
from contextlib import ExitStack
import concourse.bass as bass
import concourse.mybir as mybir
from concourse.bass_utils import run_bass_kernel_spmd

F32 = mybir.dt.float32
BF16 = mybir.dt.bfloat16
ALU = mybir.AluOpType
AF = mybir.ActivationFunctionType
AX = mybir.AxisListType

D = 2048
NTILES = 18
NT = 2
TB = NT * 128
NBLK = NTILES // NT
INW = 10880
RMS_EPS = 1e-6
LNX_EPS = 64e-5
C0 = float(np.exp(-0.5))
DBG = dict(nblk=NBLK, stage=99)
SLOPES = [float(2.0 ** (-(h + 1) / 2.0)) for h in range(16)]


class Prog:
    COMPUTE = ('pe', 'act', 'dve', 'pool')

    def __init__(self, nc):
        self.nc = nc
        self.ops = []
        self.last_w = {}
        self.readers = {}
        self.streams = None
        self.cur_stream = None

    def begin_streams(self, n):
        self.streams = [[] for _ in range(n)]
        self.cur_stream = None

    def set_stream(self, i):
        self.cur_stream = i

    def capture_start(self):
        assert self.streams is None
        self.streams = [[]]
        self.cur_stream = 0

    def capture_end(self):
        q = self.streams[0]
        self.streams, self.cur_stream = None, None
        return q

    def add_stream(self, q):
        self.streams.append(q)

    def merge_streams(self):
        streams, self.streams, self.cur_stream = self.streams, None, None
        n = max(len(q) for q in streams)
        for k in range(n):
            for q in streams:
                if k < len(q):
                    a, kw, cb = q[k]
                    idx = self.op(*a, **kw)
                    if cb is not None:
                        cb.append(idx)

    def op(self, eng, fn, reads=(), writes=(), chan=None, cb=None):
        if getattr(self, 'cur_stream', None) is not None:
            self.streams[self.cur_stream].append(((eng, fn), dict(reads=reads, writes=writes, chan=chan), cb))
            return -1
        idx = len(self.ops)
        deps = {}
        for k in reads:
            d = self.last_w.get(k)
            if d is not None:
                deps[d] = True
        for k in writes:
            d = self.last_w.get(k)
            if d is not None:
                deps.setdefault(d, False)
            for r in self.readers.get(k, ()):
                deps.setdefault(r, False)
        deps.pop(idx, None)
        self.ops.append(dict(eng=eng, fn=fn, deps=deps, chan=chan))
        for k in reads:
            self.readers.setdefault(k, []).append(idx)
        for k in writes:
            self.last_w[k] = idx
            self.readers[k] = []
        return idx

    def wait_all(self, eng, idxs):
        idx = len(self.ops)
        self.ops.append(dict(eng=eng, fn=None, deps={d: True for d in idxs}, chan=None))
        return idx

    def _need_wait(self, x, d, raw):
        od, ox = self.ops[d], self.ops[x]
        if od['chan'] is not None:
            return True
        if od['eng'] != ox['eng']:
            return True
        if ox['chan'] is not None:
            return True
        if ox['eng'] == 'pe':
            return False
        return True

    def emit(self):
        nc = self.nc
        ops = self.ops
        needed = [False] * len(ops)
        for x, o in enumerate(ops):
            for d, raw in o['deps'].items():
                if self._need_wait(x, d, raw):
                    needed[d] = True
        chans = []
        for o in ops:
            if o['chan'] is not None and o['chan'] not in chans:
                chans.append(o['chan'])
        with ExitStack() as es:
            sems = {}
            for e in self.COMPUTE:
                sems[e] = es.enter_context(nc.semaphore('s_' + e))
            for c in chans:
                sems[('c', c)] = es.enter_context(nc.semaphore('c_' + str(c)))
            cnt = {k: 0 for k in sems}
            ev = [None] * len(ops)
            for x, o in enumerate(ops):
                if o['fn'] is None:
                    continue
                if o['chan'] is not None:
                    k = ('c', o['chan'])
                    cnt[k] += 16
                    ev[x] = (k, cnt[k])
                elif needed[x]:
                    k = o['eng']
                    cnt[k] += 1
                    ev[x] = (k, cnt[k])
            per_eng = {}
            for x, o in enumerate(ops):
                per_eng.setdefault(o['eng'], []).append(x)
            self.stats = {e: len(v) for e, v in per_eng.items()}
            self.stats['sem_max'] = dict((str(k), v) for k, v in cnt.items() if v > 30000)

            def run(e, ename):
                waited = {}
                for x in per_eng.get(ename, ()):
                    o = ops[x]
                    want = {}
                    for d, raw in o['deps'].items():
                        if not self._need_wait(x, d, raw):
                            continue
                        k, v = ev[d]
                        if v > want.get(k, 0):
                            want[k] = v
                    for k, v in want.items():
                        if v > waited.get(k, 0):
                            e.wait_ge(sems[k], v)
                            waited[k] = v
                    if o['fn'] is None:
                        continue
                    ins = o['fn'](e)
                    if ev[x] is not None:
                        k, v = ev[x]
                        ins.then_inc(sems[k], 16 if o['chan'] is not None else 1)

            with nc.Block() as block:
                @block.tensor
                def _(e):
                    run(e, 'pe')

                @block.scalar
                def _(e):
                    run(e, 'act')

                @block.vector
                def _(e):
                    run(e, 'dve')

                @block.gpsimd
                def _(e):
                    run(e, 'pool')

                @block.sync
                def _(e):
                    run(e, 'sp')


def build():
    nc = bass.Bass("TRN2", target_bir_lowering=False)

    def din(name, shape, dt=F32):
        return nc.dram_tensor(name, list(shape), dt, kind="ExternalInput").ap()

    def dout(name, shape, dt=F32):
        return nc.dram_tensor(name, list(shape), dt, kind="ExternalOutput").ap()

    x_d = din("x", [NTILES * 128, D])
    w_in_d = din("w_in", [D, INW])
    p_a_d = din("p_a", [1024, D])
    p_b_d = din("p_b", [1024, D])
    w_o_d = din("w_o", [D, D])
    cnames = dict(gbc=[128, D], gfbc=[128, D], lnxw=[128, 1024], lnxb=[128, 1024], muT=[128, 25],
                  w0c=[128, 8], a0c=[128, 8], kkc=[128, 8], kac=[128, 8], rkc=[128, 8],
                  sinks=[128, 16], identf=[128, 128], MU4=[128, 512], MLs=[128, 128], bones=[128, 128],
                  bo2=[128, 2], resetm=[128, TB], MU5=[128, 320], I2=[128, 64], DmP=[128, 256], DmP0=[128, 256], DmS=[128, 384],
                  sshT=[128, 25, 4])
    cd = {k: din(k, v) for k, v in cnames.items()}
    wup_d = din("wup", [128, 1024])
    swkv_d = din("swkv", [4, 128, 8, 64])
    ckT_d = din("ckT", [128, 4, 4, 128])
    cv_d = din("cv", [128, 4, 256])
    ck_raw = din("ck_raw", [4, 128, 256])
    cv_raw = din("cv_raw", [4, 128, 256])

    y_o = dout("y", [NTILES * 128, D])
    wkvp_o = dout("wkv_p", [128, 8, 64])
    wkvs_o = dout("wkv_s", [4, 128, 8, 64])
    shp_o = dout("shift_p", [128, 25])
    shs_o = dout("shift_s", [128, 25, 4])
    kp_o = dout("k_p", [128, 256])
    vp_o = dout("v_p", [128, 256])
    ks_o = dout("k_s", [4, 128, 256])
    vs_o = dout("v_s", [4, 128, 256])

    NUNITS = 0
    scr = nc.dram_tensor("wscr", [80, 128, 16 * 256], BF16).ap()

    es = ExitStack()
    with es:
        def sb(name, shape, dt=F32):
            return es.enter_context(nc.sbuf_tensor(name, list(shape), dt))

        def ps(name, shape, dt=F32):
            return es.enter_context(nc.psum_tensor(name, list(shape), dt))

        P = Prog(nc)
        ct = {k: sb("c_" + k, v) for k, v in cnames.items()}
        identb = sb("identb", [128, 128], BF16)
        wupb = sb("wupb", [128, 1024], BF16)
        kc = sb("kc", [128, 4, 4, 128], BF16)
        vc = sb("vc", [128, 4, 256], BF16)
        dummy = sb("dummy_t", [128, 8])
        small = sb("small", [128, 64])
        hT = sb("hT", [128, 16, TB], BF16)
        stage = [sb("stage%d" % i, [128, 8, 256]) for i in range(2)]
        wbf = [sb("wbf%d" % i, [128, 16, 256], BF16) for i in range(2)]
        xt = sb("xt", [128, D])
        hb = sb("hb", [128, D], BF16)
        plast = sb("plast", [128, 25])
        shs = sb("shs", [128, 25, 4])
        pT = sb("pT", [128, TB + 1])
        arenaA = sb("arenaA", [128, 23 * TB])
        xs = [arenaA[:, i * TB:(i + 1) * TB] for i in range(6)]
        tq = [[arenaA[:, (6 + s_ * 8 + i) * TB:(7 + s_ * 8 + i) * TB] for i in range(8)] for s_ in range(2)]
        tmpf = {11: arenaA[:, 22 * TB:23 * TB]}
        xr = arenaA[:, 0:NT * D].rearrange("p (t d) -> p t d", d=D)
        ta_all = arenaA[:, 0:16 * TB].rearrange("p (m t) -> p m t", t=TB)
        lora = sb("lora", [128, TB], BF16)
        xsB_t = sb("xsB", [128, 6 * TB])
        xsB = [xsB_t[:, i * TB:(i + 1) * TB] for i in range(6)]
        arenaC = sb("arenaC", [128, 4 * 8 * TB], BF16)
        opT = [arenaC[:, i * 8 * TB:(i + 1) * 8 * TB].rearrange("p (j t) -> p j t", t=TB) for i in range(4)]
        rT, aT, bT, kT = opT
        mergedT = arenaC[:, 0:16 * TB].rearrange("p (j t) -> p j t", t=TB)
        fin32 = arenaC[:, 16 * TB:32 * TB].bitcast(F32)
        sga = fin32[:, 0:2 * TB].rearrange("p (f t) -> p f t", t=TB)
        sgb = fin32[:, 2 * TB:4 * TB].rearrange("p (f t) -> p f t", t=TB)
        ta = fin32[:, 4 * TB:6 * TB].rearrange("p (f t) -> p f t", t=TB)
        gC = sb("gC", [128, 8, 2 * NT])
        vtok = sb("vtok", [128, NT, 1024], BF16)
        vtk = sb("vtk", [128, 8, 2 * NT, 64], BF16)
        I2b = sb("I2b", [128, 64], BF16)
        LNSS = [[sb("LNS%d_%d" % (k, i), [128, 192], BF16) for i in range(2)] for k in range(2)]
        bon = sb("bon", [128, NT, 16])
        kbtokS = [sb("kbtok%d" % i, [128, 128], BF16) for i in range(2)]
        MmS = [sb("Mm%d" % i, [128, 320], BF16) for i in range(2)]
        XUbS = [sb("XUb%d" % i, [128, 128], BF16) for i in range(2)]
        Pf = sb("Pf", [128, 8, 64])
        Pb = sb("Pb", [128, 8, 64], BF16)
        ysb = sb("ysb", [128, 1024])
        ysq = sb("ysq", [128, 1024])
        sa = sb("sa", [128, NT, 1024], BF16)
        sbg = sb("sbg", [128, NT, 1024], BF16)
        yab = sb("yab", [128, 1024], BF16)
        yaT = sb("yaT", [128, 8, TB], BF16)
        ybT = sb("ybT", [128, 8, TB], BF16)
        qT = sb("qT", [128, 8, TB], BF16)
        kTd = sb("kTd", [128, 4, 128 + TB], BF16)
        vat = sb("vat", [128, 1 + NT, 256], BF16)
        kvo = sb("kvo", [128, NT, 512])
        s_sbS = [sb("s_sb%d" % i, [128, 384]) for i in range(2)]
        e_sbS = [sb("e_sb%d" % i, [128, 384], BF16) for i in range(2)]
        eTS = [sb("eT%d" % i, [128, 384], BF16) for i in range(2)]
        smallS = [sb("smallS%d" % i, [128, 8]) for i in range(2)]
        ob = sb("ob", [128, 1024])
        rden = sb("rden", [128, 16])

        Aacc = [ps("A%d" % i, [128, 512]) for i in range(2)]
        A = [Aacc[i // 2][:, (i % 2) * 256:(i % 2) * 256 + 256] for i in range(4)]
        tp = ps("tp", [128, 1024], BF16)
        Mb = ps("Mb", [128, 512])
        Db = [ps("D%d" % i, [128, 512]) for i in range(2)]
        Cb = ps("Cb", [128, 512])
        Eb = ps("Eb", [128, 512])

        cidx = []
        for k in cnames:
            src = cd[k]
            cidx.append(P.op('pool', (lambda e, o=ct[k], s=src: e.dma_start(out=o[:], in_=s)), writes=[k], chan='const'))
        cidx.append(P.op('pool', lambda e: e.dma_start(out=wupb[:], in_=wup_d), writes=['wupb'], chan='const'))
        cidx.append(P.op('pool', lambda e: e.dma_start(out=kc[:], in_=ckT_d), writes=['kc'], chan='const'))
        cidx.append(P.op('pool', lambda e: e.dma_start(out=vc[:], in_=cv_d), writes=['vc'], chan='const'))
        for eng in ('pe', 'act', 'dve', 'pool'):
            P.wait_all(eng, cidx)
        P.op('dve', lambda e: e.tensor_copy(out=identb[:], in_=ct['identf'][:]), reads=['identf'], writes=['identb'])
        P.op('dve', lambda e: e.tensor_copy(out=I2b[:], in_=ct['I2'][:]), reads=['I2'], writes=['I2b'])
        P.op('dve', lambda e: e.memset(Pf[:], 0.0), writes=['Pf%d' % j for j in range(8)])
        P.op('dve', lambda e: e.memset(Pb[:], 0.0), writes=['Pb%d' % j for j in range(8)])
        P.op('dve', lambda e: e.memset(plast[:], 0.0), writes=['plast'])
        P.op('dve', lambda e: e.memset(kTd[:], 0.0), writes=['kTd'])
        P.op('dve', lambda e: e.memset(vat[:], 0.0), writes=['vat'])
        identf = ct['identf']

        st = dict(uid=0, sidx=0, acc=0, cast=0)
        out_idx = []
        ARENA_PREP = (['xs%d' % i for i in range(6)] + ['tq%d_%d' % (s_, i) for s_ in range(2) for i in range(8)] + ['tmp11']
                      + ['rT', 'aT', 'bT', 'kT'])
        ARENA_FIN = ['xr', 'mergedT', 'sga', 'sgb', 'ta', 'taall']

        def fence(after, before):
            P.op('pool', lambda e: e.memset(dummy[0:1, 0:1], 0.0), writes=list(after) + list(before))

        def load_unit(b, u, W, c0, n, KT, dup=False):
            slot = st['uid'] % 2
            st['uid'] += 1
            wk = 'wbf%d' % slot
            ncols = 256 if dup else n
            if b == 0:
                for half in range(KT // 8):
                    si = st['sidx'] % 2
                    st['sidx'] += 1
                    src = W[half * 1024:(half + 1) * 1024, c0:c0 + n].rearrange("(k p) n -> p k n", p=128)
                    P.op('sp', (lambda e, si=si, src=src: e.dma_start(out=stage[si][:, :, 0:n], in_=src)),
                         writes=['stage%d' % si], chan='stg%d' % si)
                    ceng = ('dve', 'act')[st['cast'] % 2]
                    st['cast'] += 1
                    if not dup:
                        dst = wbf[slot][:, half * 8:(half + 1) * 8, 0:n]
                        srcs = stage[si][:, :, 0:n]
                        if ceng == 'act':
                            P.op('act', (lambda e, dst=dst, srcs=srcs: e.copy(out=dst, in_=srcs)),
                                 reads=['stage%d' % si], writes=[wk])
                        else:
                            P.op('dve', (lambda e, dst=dst, srcs=srcs: e.tensor_copy(out=dst, in_=srcs)),
                                 reads=['stage%d' % si], writes=[wk])
                    else:
                        for dd in range(2):
                            dst = wbf[slot][:, half * 8:(half + 1) * 8, :].rearrange(
                                "p k (g d c) -> p k g d c", g=2, d=2)[:, :, :, dd, :]
                            srcs = stage[si][:, :, 0:128].rearrange("p k (g c) -> p k g c", g=2)
                            P.op('pool', (lambda e, dst=dst, srcs=srcs: e.tensor_copy(out=dst, in_=srcs)),
                                 reads=['stage%d' % si], writes=[wk])
                P.op('pool', (lambda e, u=u, slot=slot: e.dma_start(
                    out=scr[u % DBG.get("umod", 80), :, 0:KT * ncols].rearrange("p (k n) -> p k n", n=ncols),
                    in_=wbf[slot][:, 0:KT, 0:ncols])),
                    reads=[wk], writes=['scr%d' % u], chan='wst%d' % slot)
            else:
                P.op('sp', (lambda e, u=u, slot=slot: e.dma_start(
                    out=wbf[slot][:, 0:KT, 0:ncols],
                    in_=scr[u % DBG.get("umod", 80), :, 0:KT * ncols].rearrange("p (k n) -> p k n", n=ncols))),
                    reads=['scr%d' % u], writes=[wk], chan='wld%d' % slot)
            return slot

        def nu():
            st['u'] += 1
            return st['u'] - 1

        def next_pair():
            if st.get('pb0'):
                return (0, 1)
            if st.get('pb') is not None:
                return (2 * st['pb'], 2 * st['pb'] + 1)
            k = st['acc'] % 2
            st['acc'] += 1
            return (2 * k, 2 * k + 1)

        def next_acc():
            return next_pair()[0]

        def akey(ai):
            return 'PB%d' % (ai // 2)

        def mm_fm(slot, KT, f, rhsT, rkey, ai, ncol=TB):
            def fn(e):
                ins = None
                for kt in range(KT):
                    ins = e.matmul(A[ai][:, 0:ncol], lhsT=wbf[slot][:, kt, f * 128:(f + 1) * 128],
                                   rhs=rhsT[:, kt, 0:ncol], start=(kt == 0), stop=(kt == KT - 1))
                return ins
            P.op('pe', fn, reads=['wbf%d' % slot, rkey], writes=[akey(ai)])

        def mm_tm(slot, KT, ti, lhs_tile, lkey, ai, n=256):
            def fn(e):
                ins = None
                for kt in range(KT):
                    ins = e.matmul(A[ai][:, 0:n], lhsT=lhs_tile[:, kt, ti * 128:(ti + 1) * 128],
                                   rhs=wbf[slot][:, kt, 0:n], start=(kt == 0), stop=(kt == KT - 1))
                return ins
            P.op('pe', fn, reads=['wbf%d' % slot, lkey], writes=[akey(ai)])

        def mmk(e, out, lhsT, rhs, kbase):
            if kbase == 0:
                return e.matmul(out, lhsT=lhsT, rhs=rhs, start=True, stop=True)
            e.matmul(out[0:64], lhsT=lhsT[:, 0:64], rhs=rhs, start=True, stop=True)
            return e.matmul(out[64:128], lhsT=lhsT[:, 64:128], rhs=rhs, start=True, stop=True)

        def chunk3(ap):
            return ap.rearrange("p (c t) -> p c t", t=64)

        for b in range(DBG['nblk']):
            sample = (b == NBLK - 1)
            st['u'] = 0
            def emit_S1(bb):
                for ti in range(NT):
                    gt = bb * NT + ti
                    P.op('sp', (lambda e, gt=gt: e.dma_start(out=xt[:], in_=x_d[gt * 128:(gt + 1) * 128, :])),
                         writes=['xt'], chan='xt')
                    P.op('dve', lambda e: e.memset(small[:, 56:57], 0.0), writes=['s1ssq'])
                    P.op('act', lambda e: e.activation(out=hb[:], in_=xt[:], func=AF.Square, accum_out=small[:, 56:57]),
                         reads=['xt', 's1ssq'], writes=['hb', 's1ssq'])
                    P.op('dve', lambda e: e.tensor_scalar(out=small[:, 57:58], in0=small[:, 56:57], scalar1=1.0 / D,
                                                          scalar2=RMS_EPS, op0=ALU.mult, op1=ALU.add),
                         reads=['s1ssq'], writes=['s1ms'])
                    P.op('act', lambda e: e.activation(out=small[:, 58:59], in_=small[:, 57:58], func=AF.Sqrt),
                         reads=['s1ms'], writes=['s1sq'])
                    P.op('dve', lambda e: e.reciprocal(out=small[:, 59:60], in_=small[:, 58:59]), reads=['s1sq'], writes=['s1rstd'])
                    P.op('dve', lambda e: e.scalar_tensor_tensor(out=hb[:], in0=xt[:], scalar=small[:, 59:60],
                                                                 in1=ct['gbc'][:], op0=ALU.mult, op1=ALU.mult),
                         reads=['xt', 's1rstd', 'gbc'], writes=['hb'])
                    for half in range(2):
                        def fn(e, half=half):
                            ins = None
                            for k in range(8):
                                kt = half * 8 + k
                                ins = e.transpose(out=tp[:, k * 128:(k + 1) * 128], in_=hb[:, kt * 128:(kt + 1) * 128],
                                                  identity=identb[:])
                            return ins
                        P.op('pe', fn, reads=['hb', 'identb'], writes=['tp'])
                        dst = hT[:, half * 8:(half + 1) * 8, ti * 128:(ti + 1) * 128]
                        srcv = tp[:, :].rearrange("p (k t) -> p k t", t=128)
                        if half == 0:
                            P.op('act', (lambda e, dst=dst, srcv=srcv: e.copy(out=dst, in_=srcv)), writes=['hT', 'tp'])
                        else:
                            P.op('dve', (lambda e, dst=dst, srcv=srcv: e.tensor_copy(out=dst, in_=srcv)), writes=['hT', 'tp'])

            if b == 0:
                emit_S1(0)

            if DBG['stage'] <= 1:
                continue
            fence(ARENA_FIN, ARENA_PREP)

            def shift(f, ai, xs_ap, xkey):
                P.op('act', (lambda e: e.copy(out=pT[:, 1:TB + 1], in_=A[ai][:, 0:TB])), writes=['pT', akey(ai)])
                if not sample:
                    P.op('dve', (lambda e: e.tensor_copy(out=pT[:, 0:1], in_=plast[:, f:f + 1])), reads=['plast'], writes=['pT'])
                    P.op('dve', (lambda e: e.tensor_copy(out=plast[:, f:f + 1], in_=pT[:, TB:TB + 1])), reads=['pT'], writes=['plast'])
                else:
                    P.op('dve', (lambda e: e.memset(pT[:, 0:1], 0.0)), writes=['pT'])
                    P.op('dve', (lambda e: e.tensor_copy(out=shs[:, f, :], in_=chunk3(pT[:, 1:TB + 1])[:, :, 63])),
                         reads=['pT'], writes=['shs'])
                t0 = tmpf[11]
                P.op('pool', (lambda e: e.tensor_tensor(out=t0, in0=pT[:, 0:TB], in1=pT[:, 1:TB + 1], op=ALU.subtract)),
                     reads=['pT'], writes=['tmp11'])
                if sample:
                    P.op('dve', (lambda e: e.tensor_tensor(out=chunk3(t0)[:, :, 0], in0=ct['sshT'][:, f, :],
                                                           in1=chunk3(pT[:, 1:TB + 1])[:, :, 0], op=ALU.subtract)),
                         reads=['pT', 'sshT', 'tmp11'], writes=['tmp11'])
                P.op('dve', (lambda e: e.scalar_tensor_tensor(out=xs_ap, in0=t0, scalar=ct['muT'][:, f:f + 1],
                                                              in1=pT[:, 1:TB + 1], op0=ALU.mult, op1=ALU.add)),
                     reads=['tmp11', 'pT', 'muT'], writes=[xkey])

            slot = load_unit(b, nu(), w_in_d, 3072, 128, 16)
            ai = next_acc()
            mm_fm(slot, 16, 0, hT, 'hT', ai)
            shift(24, ai, xs[0], 'xs0')
            P.op('act', lambda e: e.activation(out=lora[0:64, :], in_=xs[0][0:64, :], func=AF.Tanh), reads=['xs0'], writes=['lora'])
            P.op('dve', lambda e: e.tensor_copy(out=lora[64:128, :], in_=xs[0][64:128, :]), reads=['xs0'], writes=['lora'])

            XSETS = [(xs, ['xs%d' % i for i in range(6)]), (xsB, ['xb%d' % i for i in range(6)])]

            def emit_proj(g2, XS, XK):
                for kind in range(3):
                    slot = load_unit(b, nu(), w_in_d, kind * 1024 + g2 * 256, 256, 16)
                    pr = next_pair()
                    for f in range(2):
                        mm_fm(slot, 16, f, hT, 'hT', pr[f])
                    for f in range(2):
                        shift(kind * 8 + g2 * 2 + f, pr[f], XS[kind * 2 + f], XK[kind * 2 + f])

            def prep_pair(j, sx, xr_, xk_, xv_, kr, kk_, kv):
                T = tq[sx]
                K = ['tq%d_%d' % (sx, q) for q in range(8)]
                sg, cs, eg, eig, alr, kk2, kkn, b32 = T
                k_sg, k_cs, k_eg, k_eig, k_alr, k_kk2, k_kkn, k_b32 = K
                egm, k_egm = cs, k_cs
                rn, k_rn = kk2, k_kk2
                t1, k_t1 = sg, k_sg
                jc = slice(j * 128, (j + 1) * 128)
                if sx == 0:
                    A1, A2, A3 = A[2][:, 0:TB], A[3][:, 0:TB], A[2][:, 0:TB]
                    ak = 'PB1'
                else:
                    A1, A2, A3 = Mb[:, 0:TB], Mb[:, 256:256 + TB], Mb[:, 0:TB]
                    ak = 'Mb'
                tv = sx * 128
                tk = 256 + sx * 256
                P.op('pe', (lambda e: e.matmul(A1, lhsT=wupb[0:64, jc], rhs=lora[0:64, :], start=True, stop=True)),
                     reads=['wupb', 'lora'], writes=[ak])
                P.op('pe', (lambda e: mmk(e, A2, wupb[64:128, jc], lora[64:128, :], 64)),
                     reads=['wupb', 'lora'], writes=[ak])
                P.op('act', (lambda e: e.activation(out=sg, in_=A1, func=AF.Sigmoid, bias=ct['w0c'][:, j:j + 1])),
                     reads=['w0c'], writes=[k_sg, ak])
                P.op('act', (lambda e: e.activation(out=alr, in_=A2, func=AF.Sigmoid, bias=ct['a0c'][:, j:j + 1])),
                     reads=['a0c'], writes=[k_alr, ak])
                P.op('dve', (lambda e: e.tensor_tensor_scan(out=cs, data0=ct['resetm'][:], data1=sg, initial=0.0, op0=ALU.mult, op1=ALU.add)),
                     reads=[k_sg, 'resetm'], writes=[k_cs])
                P.op('act', (lambda e: e.activation(out=eg, in_=cs, func=AF.Exp, scale=-C0)), reads=[k_cs], writes=[k_eg])
                P.op('act', (lambda e: e.activation(out=eig, in_=cs, func=AF.Exp, scale=C0)), reads=[k_cs], writes=[k_eig])
                P.op('dve', (lambda e: e.tensor_tensor(out=t1, in0=cs, in1=sg, op=ALU.subtract)), reads=[k_cs, k_sg], writes=[k_t1])
                P.op('act', (lambda e: e.activation(out=egm, in_=t1, func=AF.Exp, scale=-C0)), reads=[k_t1], writes=[k_egm])
                P.op('dve', (lambda e: e.tensor_copy(out=gC[:, j, :], in_=chunk3(eg)[:, :, 63])), reads=[k_eg], writes=['gC%d' % j])
                P.op('act', (lambda e: e.activation(out=kk2, in_=xk_, func=AF.Square, scale=ct['kkc'][:, j:j + 1])),
                     reads=[kk_, 'kkc'], writes=[k_kk2])
                P.op('pe', (lambda e: e.matmul(A3, lhsT=ct['bones'][:], rhs=kk2, start=True, stop=True)),
                     reads=['bones', k_kk2], writes=[ak])
                P.op('act', (lambda e: e.activation(out=rn, in_=A3, func=AF.Sqrt)), writes=[k_rn, ak])
                P.op('dve', (lambda e: e.tensor_scalar(out=rn, in0=rn, scalar1=1e-12, scalar2=None, op0=ALU.max)), reads=[k_rn], writes=[k_rn])
                P.op('dve', (lambda e: e.reciprocal(out=rn, in_=rn)), reads=[k_rn], writes=[k_rn])
                P.op('dve', (lambda e: e.scalar_tensor_tensor(out=kkn, in0=xk_, scalar=ct['kkc'][:, j:j + 1], in1=rn, op0=ALU.mult, op1=ALU.mult)),
                     reads=[kk_, 'kkc', k_rn], writes=[k_kkn])
                P.op('dve', (lambda e: e.tensor_scalar(out=t1, in0=alr, scalar1=-1.0, scalar2=ct['kac'][:, j:j + 1], op0=ALU.add, op1=ALU.mult)),
                     reads=[k_alr, 'kac'], writes=[k_t1])
                P.op('dve', (lambda e: e.scalar_tensor_tensor(out=t1, in0=t1, scalar=1.0, in1=xk_, op0=ALU.add, op1=ALU.mult)),
                     reads=[k_t1, kk_], writes=[k_t1])
                P.op('dve', (lambda e: e.tensor_tensor(out=rT[:, j, :], in0=xr_, in1=eg, op=ALU.mult)), reads=[kr, k_eg], writes=['rT'])
                P.op('dve', (lambda e: e.scalar_tensor_tensor(out=aT[:, j, :], in0=kkn, scalar=-1.0, in1=egm, op0=ALU.mult, op1=ALU.mult)),
                     reads=[k_kkn, k_egm], writes=['aT'])
                P.op('dve', (lambda e: e.tensor_tensor(out=b32, in0=kkn, in1=alr, op=ALU.mult)), reads=[k_kkn, k_alr], writes=[k_b32])
                P.op('dve', (lambda e: e.tensor_tensor(out=bT[:, j, :], in0=b32, in1=eig, op=ALU.mult)), reads=[k_b32, k_eig], writes=['bT'])
                P.op('dve', (lambda e: e.tensor_tensor(out=kT[:, j, :], in0=t1, in1=eig, op=ALU.mult)), reads=[k_t1, k_eig], writes=['kT'])
                P.op('dve', (lambda e: e.scalar_tensor_tensor(out=kk2, in0=xr_, scalar=ct['rkc'][:, j:j + 1], in1=t1, op0=ALU.mult, op1=ALU.mult)),
                     reads=[kr, 'rkc', k_t1], writes=[k_kk2])
                vb = b32.bitcast(BF16)[:, 0:TB]
                P.op('act', (lambda e: e.copy(out=vb, in_=xv_)), reads=[kv], writes=[k_b32])

                def fnVT(e):
                    ins = None
                    for ci_ in range(2 * NT):
                        for hp in (slice(0, 64), slice(64, 128)):
                            ins = e.transpose(out=tp[hp, tk + ci_ * 64:tk + ci_ * 64 + 64], in_=vb[hp, ci_ * 64:(ci_ + 1) * 64], identity=identb[hp, hp])
                    return ins
                P.op('pe', fnVT, reads=[k_b32, 'identb'], writes=['tp'])
                P.op('act', (lambda e: e.copy(out=vtk[:, j, :, :], in_=tp[:, tk:tk + 2 * NT * 64].rearrange("p (c v) -> p c v", v=64))),
                     writes=['vtk', 'tp'])
                for ti in range(NT):
                    tcs = slice(ti * 128, (ti + 1) * 128)
                    P.op('pe', (lambda e, ti=ti, tcs=tcs: e.matmul(Eb[:, ti * 16 + j * 2:ti * 16 + j * 2 + 2], lhsT=kk2[:, tcs], rhs=ct['bo2'][:], start=True, stop=True)),
                         reads=[k_kk2, 'bo2'], writes=['Eb'])
                    P.op('pe', (lambda e, tcs=tcs: e.transpose(out=tp[:, tv:tv + 128], in_=vb[:, tcs], identity=identb[:])),
                         reads=[k_b32, 'identb'], writes=['tp'])
                    P.op('act', (lambda e, ti=ti: e.copy(out=vtok[:, ti, jc], in_=tp[:, tv:tv + 128])), writes=['vtok', 'tp'])

            for it in range(5):
                P.begin_streams(3)
                if it < 4:
                    P.set_stream(0)
                    st['pb'] = 0
                    emit_proj(it, *XSETS[it % 2])
                    st['pb'] = None
                if it > 0:
                    XS, XK = XSETS[(it - 1) % 2]
                    for jj in range(2):
                        P.set_stream(1 + jj)
                        prep_pair((it - 1) * 2 + jj, jj, XS[jj], XS[2 + jj], XS[4 + jj], XK[jj], XK[2 + jj], XK[4 + jj])
                P.merge_streams()
            P.op('dve', lambda e: e.tensor_copy(out=bon[:].rearrange("p t h -> p (t h)"), in_=Eb[:, 0:NT * 16]), writes=['bon', 'Eb'])

            def aux_ga(i):
                slot = load_unit(b, nu(), w_in_d, 3200 + i * 256, 256, 16)
                pr = next_pair()
                for ti in range(NT):
                    mm_tm(slot, 16, ti, hT, 'hT', pr[ti])
                for ti in range(NT):
                    ai = pr[ti]
                    P.op('act', (lambda e, ai=ai, ti=ti, i=i: e.activation(out=sa[:, ti, i * 256:(i + 1) * 256], in_=A[ai][:, 0:256], func=AF.Silu)),
                         writes=['sa', akey(ai)])

            def aux_q(i):
                slot = load_unit(b, nu(), w_in_d, 4224 + i * 256, 256, 16)
                pr = next_pair()
                for f in range(2):
                    mm_fm(slot, 16, f, hT, 'hT', pr[f])
                for f in range(2):
                    ai = pr[f]
                    P.op('act', (lambda e, ai=ai, i=i, f=f: e.activation(out=qT[:, i * 2 + f, :], in_=A[ai][:, 0:TB], func=AF.Copy, scale=0.125)),
                         writes=['qT', akey(ai)])

            def aux_kd(i):
                slot = load_unit(b, nu(), w_in_d, 5248 + i * 128, 128, 16, dup=True)
                pr = next_pair()
                for f in range(2):
                    mm_fm(slot, 16, f, hT, 'hT', pr[f])
                for f in range(2):
                    ai = pr[f]
                    P.op('dve', (lambda e, ai=ai, i=i, f=f: e.tensor_copy(out=kTd[:, i * 2 + f, 128:128 + TB], in_=A[ai][:, 0:TB])),
                         writes=['kTd', akey(ai)])

            def aux_kv(i):
                slot = load_unit(b, nu(), w_in_d, 5248 + i * 256, 256, 16)
                pr = next_pair()
                for ti in range(NT):
                    mm_tm(slot, 16, ti, hT, 'hT', pr[ti])
                for ti in range(NT):
                    ai = pr[ti]
                    P.op('act', (lambda e, ai=ai, ti=ti, i=i: e.copy(out=kvo[:, ti, i * 256:(i + 1) * 256], in_=A[ai][:, 0:256])),
                         writes=['kvo', akey(ai)])
                    if i == 1:
                        P.op('act', (lambda e, ai=ai, ti=ti: e.copy(out=vat[:, 1 + ti, :], in_=A[ai][:, 0:256])),
                             writes=['vat', akey(ai)])

            def aux_gb(i):
                slot = load_unit(b, nu(), w_in_d, 5760 + i * 256, 256, 16)
                pr = next_pair()
                for ti in range(NT):
                    mm_tm(slot, 16, ti, hT, 'hT', pr[ti])
                for ti in range(NT):
                    ai = pr[ti]
                    P.op('act', (lambda e, ai=ai, ti=ti, i=i: e.activation(out=sbg[:, ti, i * 256:(i + 1) * 256], in_=A[ai][:, 0:256], func=AF.Silu)),
                         writes=['sbg', akey(ai)])

            def aux_carry():
                if 0 < b and not sample:
                    P.op('pool', lambda e: e.tensor_copy(out=kTd[:, :, 0:128], in_=kTd[:, :, TB:TB + 128]), reads=['kTd'], writes=['kTd'])
                    P.op('pool', lambda e: e.tensor_copy(out=vat[:, 0, :], in_=vat[:, NT, :]), reads=['vat'], writes=['vat'])

            AUX = [
                [lambda: aux_ga(0), lambda: aux_ga(1), lambda: aux_ga(2), lambda: aux_ga(3)],
                [lambda: aux_q(0), lambda: aux_q(1), lambda: aux_q(2), lambda: aux_q(3)],
                [aux_carry, lambda: aux_kd(0), lambda: aux_kd(1), lambda: aux_kv(0), lambda: aux_kv(1)],
                [lambda: aux_gb(0), lambda: aux_gb(1), lambda: aux_gb(2), lambda: aux_gb(3)],
            ]

            if DBG['stage'] <= 3:
                continue
            H2 = (slice(0, 64), slice(64, 128))
            PS_PENDING = []
            PS1_PENDING = []
            for ti in range(NT):
                gt = b * NT + ti
                tcs = slice(ti * 128, (ti + 1) * 128)
                for c in range(2):
                    cp = slice(c * 64, c * 64 + 64)
                    cc = slice(ti * 128 + c * 64, ti * 128 + c * 64 + 64)
                    ci = ti * 2 + c
                    P.begin_streams(3)
                    if ti == 1 and c == 0 and PS_PENDING:
                        P.add_stream(PS_PENDING.pop())
                    P.set_stream(2)
                    st['pb0'] = True
                    for task in AUX[ci]:
                        task()
                    st['pb0'] = False
                    for j in range(8):
                        sx = (j % 2) if DBG.get('ss', 1) else 0
                        P.set_stream(sx)
                        kbtok, Mmx, LNS, XUb = kbtokS[sx], MmS[sx], LNSS[sx], XUbS[sx]
                        MC = Mb if sx == 0 else Cb
                        MCk = 'Mb' if sx == 0 else 'Cb'
                        DD = Db[0][:, 0:192] if sx == 0 else Eb[:, 192:384]
                        DDk = 'D0' if sx == 0 else 'Eb'
                        kX, kM, kL = 'X%d' % sx, 'Mm%d' % sx, 'LNS%d_' % sx
                        if sample:
                            seq = ti * 2 + c
                            P.op('sp', (lambda e, seq=seq, j=j: e.dma_start(out=Pf[:, j, :], in_=swkv_d[seq, :, j, :])),
                                 writes=['Pf%d' % j], chan='pst%d' % j)
                            P.op('dve', (lambda e, j=j: e.tensor_copy(out=Pb[:, j, :], in_=Pf[:, j, :])), reads=['Pf%d' % j], writes=['Pb%d' % j])
                        tpo = sx * 128

                        def fnT(e, j=j, cc=cc, tpo=tpo):
                            ins = None
                            for hp in H2:
                                e.transpose(out=tp[hp, tpo:tpo + 64], in_=kT[hp, j, cc], identity=identb[hp, hp])
                                ins = e.transpose(out=tp[hp, tpo + 64:tpo + 128], in_=bT[hp, j, cc], identity=identb[hp, hp])
                            return ins
                        P.op('pe', fnT, reads=['kT', 'bT', 'identb'], writes=['tp'])
                        P.op('act', (lambda e, kbtok=kbtok, tpo=tpo: e.copy(out=kbtok[:, 0:128], in_=tp[:, tpo:tpo + 128])), writes=['kbtok%d' % sx, 'tp'])

                        def fnM(e, j=j, cc=cc, MC=MC):
                            ins = None
                            for hp in H2:
                                e.matmul(MC[hp, 0:64], lhsT=bT[hp, j, cc], rhs=aT[hp, j, cc], start=True, stop=True)
                                e.matmul(MC[hp, 64:128], lhsT=kT[hp, j, cc], rhs=aT[hp, j, cc], start=True, stop=True)
                                e.matmul(MC[hp, 128:192], lhsT=bT[hp, j, cc], rhs=rT[hp, j, cc], start=True, stop=True)
                                e.matmul(MC[hp, 192:256], lhsT=kT[hp, j, cc], rhs=rT[hp, j, cc], start=True, stop=True)
                                ins = e.matmul(MC[hp, 256:320], lhsT=aT[hp, j, cc], rhs=bT[hp, j, cc], start=True, stop=True)
                            return ins
                        P.op('pe', fnM, reads=['aT', 'bT', 'kT', 'rT'], writes=[MCk])
                        P.op('dve', (lambda e, Mmx=Mmx, MC=MC: e.tensor_tensor(out=Mmx[:, 0:320], in0=MC[:, 0:320], in1=ct['MU5'][:], op=ALU.mult)),
                             reads=['MU5'], writes=[kM, MCk])
                        P.op('pool', (lambda e, LNS=LNS, Mmx=Mmx: e.tensor_tensor(out=LNS[0][:, 128:192], in0=Mmx[:, 0:64], in1=I2b[:], op=ALU.add)),
                             reads=[kM, 'I2b'], writes=[kL + '0'])
                        for lvl in range(1, 7):
                            cur, nxt = (lvl - 1) % 2, lvl % 2
                            if lvl == 1:
                                Lc, Nc = Mmx[:, 256:320], Mmx[:, 0:64]
                                rk = [kM, kL + '0']
                            else:
                                Lc, Nc = LNS[cur][:, 0:64], LNS[cur][:, 64:128]
                                rk = [kL + str(cur)]
                            Sc = LNS[cur][:, 128:192]

                            def fnD(e, Lc=Lc, Nc=Nc, Sc=Sc, lvl=lvl, DD=DD):
                                ins = None
                                for hp in H2:
                                    if lvl < 6:
                                        e.matmul(DD[hp, 0:64], lhsT=Nc[hp], rhs=Lc[hp], start=True, stop=True)
                                    if lvl < 5:
                                        e.matmul(DD[hp, 64:128], lhsT=Lc[hp], rhs=Nc[hp], start=True, stop=True)
                                    if lvl == 1:
                                        ins = e.matmul(DD[hp, 128:192], lhsT=I2b[hp], rhs=Sc[hp], start=True, stop=True)
                                    else:
                                        e.matmul(DD[hp, 128:192], lhsT=I2b[hp], rhs=Sc[hp], start=True, stop=False)
                                        ins = e.matmul(DD[hp, 128:192], lhsT=Lc[hp], rhs=Sc[hp], start=False, stop=True)
                                return ins
                            P.op('pe', fnD, reads=rk + ['I2b'], writes=[DDk])
                            lo = 0 if lvl < 6 else 128
                            if (lvl + sx) % 2 == 1:
                                P.op('dve', (lambda e, LNS=LNS, nxt=nxt, lo=lo, DD=DD: e.tensor_copy(out=LNS[nxt][:, lo:192], in_=DD[:, lo:192])),
                                     writes=[kL + str(nxt), DDk])
                            else:
                                P.op('act', (lambda e, LNS=LNS, nxt=nxt, lo=lo, DD=DD: e.copy(out=LNS[nxt][:, lo:192], in_=DD[:, lo:192])),
                                     writes=[kL + str(nxt), DDk])

                        def fnX(e, j=j, cc=cc, ci=ci, MC=MC, Mmx=Mmx):
                            ins = None
                            for hp in H2:
                                e.matmul(MC[hp, 320:384], lhsT=aT[hp, j, cc], rhs=Pb[hp, j, :], start=True, stop=False)
                                ins = e.matmul(MC[hp, 320:384], lhsT=Mmx[hp, 64:128], rhs=vtk[hp, j, ci, :], start=False, stop=True)
                            return ins
                        P.op('pe', fnX, reads=['aT', 'Pb%d' % j, kM, 'vtk'], writes=[MCk])
                        P.op('dve', (lambda e, XUb=XUb, MC=MC: e.tensor_copy(out=XUb[:, 0:64], in_=MC[:, 320:384])), writes=[kX + 'x', MCk])

                        def fnU(e, MC=MC, LNS=LNS, XUb=XUb):
                            ins = None
                            for hp in H2:
                                ins = e.matmul(MC[hp, 384:448], lhsT=LNS[0][hp, 128:192], rhs=XUb[hp, 0:64], start=True, stop=True)
                            return ins
                        P.op('pe', fnU, reads=[kX + 'x', kL + '0'], writes=[MCk])
                        P.op('act', (lambda e, XUb=XUb, MC=MC: e.copy(out=XUb[:, 64:128], in_=MC[:, 384:448])), writes=[kX + 'u', MCk])

                        def fnO(e, j=j, cp=cp, cc=cc, ci=ci, Mmx=Mmx, XUb=XUb):
                            ins = None
                            for hh, hp in enumerate(H2):
                                ob_ = Db[1][cp, j * 64:j * 64 + 64] if hh == 0 else Aacc[1][cp, j * 64:j * 64 + 64]
                                e.matmul(ob_, lhsT=rT[hp, j, cc], rhs=Pb[hp, j, :], start=True, stop=False)
                                e.matmul(ob_, lhsT=Mmx[hp, 128:192], rhs=XUb[hp, 64:128], start=False, stop=False)
                                ins = e.matmul(ob_, lhsT=Mmx[hp, 192:256], rhs=vtk[hp, j, ci, :], start=False, stop=True)
                            return ins
                        P.op('pe', fnO, reads=['rT', 'Pb%d' % j, kM, kX + 'u', 'vtk'], writes=['D1', 'PB1'])

                        def fnP(e, j=j, ci=ci, MC=MC, kbtok=kbtok, XUb=XUb):
                            ins = None
                            for hp in H2:
                                e.matmul(MC[hp, 448:512], lhsT=identf[hp, hp], rhs=Pf[hp, j, :], start=True, stop=False)
                                e.matmul(MC[hp, 448:512], lhsT=kbtok[hp, 64:128], rhs=XUb[hp, 64:128], start=False, stop=False)
                                ins = e.matmul(MC[hp, 448:512], lhsT=kbtok[hp, 0:64], rhs=vtk[hp, j, ci, :], start=False, stop=True)
                            return ins
                        P.op('pe', fnP, reads=['identf', 'Pf%d' % j, 'kbtok%d' % sx, kX + 'u', 'vtk'], writes=[MCk])
                        gcol = gC[:, j, ci:ci + 1]
                        P.op('act', (lambda e, j=j, gcol=gcol, MC=MC: e.activation(out=Pf[:, j, :], in_=MC[:, 448:512], func=AF.Copy, scale=gcol)),
                             reads=['gC%d' % j], writes=['Pf%d' % j, MCk])
                        P.op('dve', (lambda e, j=j, gcol=gcol, MC=MC: e.tensor_scalar(out=Pb[:, j, :], in0=MC[:, 448:512], scalar1=gcol, scalar2=None, op0=ALU.mult)),
                             reads=['gC%d' % j], writes=['Pb%d' % j, MCk])
                        if sample:
                            seq = ti * 2 + c
                            P.op('pool', (lambda e, seq=seq, j=j: e.dma_start(out=wkvs_o[seq, :, j, :], in_=Pf[:, j, :])),
                                 reads=['Pf%d' % j], chan='o_pf%d' % j, cb=out_idx)
                    P.merge_streams()
                y4 = ysb[:].rearrange("p (j h c) -> p j h c", h=2, c=64)
                if DBG.get('oe', 0) == 0:
                    P.op('dve', lambda e: e.tensor_copy(out=y4[:, :, 0, :], in_=Db[1][:, :].rearrange("p (j c) -> p j c", c=64)), writes=['ysb', 'D1'])
                    P.op('act', lambda e: e.copy(out=y4[:, :, 1, :], in_=Aacc[1][:, :].rearrange("p (j c) -> p j c", c=64)), writes=['ysb', 'PB1'])
                else:
                    for j in range(8):
                        P.op('dve', (lambda e, j=j: e.tensor_copy(out=ysb[:, j * 128:j * 128 + 64], in_=Db[1][:, j * 64:j * 64 + 64])), writes=['ysb', 'D1'])
                        P.op('act', (lambda e, j=j: e.copy(out=ysb[:, j * 128 + 64:j * 128 + 128], in_=Aacc[1][:, j * 64:j * 64 + 64])), writes=['ysb', 'PB1'])
                if gt == 15:
                    out_idx.append(P.op('pool', lambda e: e.dma_start(out=wkvp_o, in_=Pf[:]), reads=['Pf%d' % j for j in range(8)], chan='o_pfp'))

                if DBG.get('pso', 1):
                    P.capture_start()
                if DBG.get('dump', 0):
                    out_idx.append(P.op('pool', (lambda e, gt=gt: e.dma_start(out=y_o[(gt + 4) * 128:(gt + 5) * 128, 0:1024], in_=ysb[:])), reads=['ysb'], chan='o_dbg'))
                y3 = ysb[:].rearrange("p (h c) -> p h c", c=64)
                q3 = ysq[:].rearrange("p (h c) -> p h c", c=64)
                P.op('dve', lambda e: e.tensor_reduce(out=small[:, 8:24], in_=y3, axis=AX.X, op=ALU.add), reads=['ysb'], writes=['gn_s1'])
                P.op('act', lambda e: e.activation(out=ysq[:], in_=ysb[:], func=AF.Square), reads=['ysb'], writes=['ysq'])
                P.op('dve', lambda e: e.tensor_reduce(out=small[:, 24:40], in_=q3, axis=AX.X, op=ALU.add), reads=['ysq'], writes=['gn_s2'])
                P.op('dve', lambda e: e.tensor_scalar(out=small[:, 40:56], in0=small[:, 8:24], scalar1=1.0 / 64, scalar2=None, op0=ALU.mult),
                     reads=['gn_s1'], writes=['gn_mean'])
                P.op('dve', lambda e: e.tensor_tensor(out=small[:, 8:24], in0=small[:, 40:56], in1=small[:, 40:56], op=ALU.mult),
                     reads=['gn_mean', 'gn_s1'], writes=['gn_s1'])
                P.op('dve', lambda e: e.scalar_tensor_tensor(out=small[:, 24:40], in0=small[:, 24:40], scalar=1.0 / 64, in1=small[:, 8:24], op0=ALU.mult, op1=ALU.subtract),
                     reads=['gn_s2', 'gn_s1'], writes=['gn_s2'])
                P.op('dve', lambda e: e.tensor_scalar(out=small[:, 24:40], in0=small[:, 24:40], scalar1=LNX_EPS, scalar2=None, op0=ALU.add),
                     reads=['gn_s2'], writes=['gn_s2'])
                P.op('act', lambda e: e.activation(out=small[:, 24:40], in_=small[:, 24:40], func=AF.Sqrt), reads=['gn_s2'], writes=['gn_s2'])
                P.op('dve', lambda e: e.reciprocal(out=small[:, 24:40], in_=small[:, 24:40]), reads=['gn_s2'], writes=['gn_s2'])
                P.op('dve', lambda e: e.tensor_tensor(out=y3, in0=y3, in1=small[:, 40:56].unsqueeze(2).to_broadcast([128, 16, 64]), op=ALU.subtract),
                     reads=['ysb', 'gn_mean'], writes=['ysb'])
                P.op('dve', lambda e: e.tensor_tensor(out=y3, in0=y3, in1=small[:, 24:40].unsqueeze(2).to_broadcast([128, 16, 64]), op=ALU.mult),
                     reads=['ysb', 'gn_s2'], writes=['ysb'])
                P.op('dve', lambda e: e.tensor_tensor(out=ysb[:], in0=ysb[:], in1=ct['lnxw'][:], op=ALU.mult), reads=['ysb', 'lnxw'], writes=['ysb'])
                P.op('dve', lambda e: e.tensor_tensor(out=ysb[:], in0=ysb[:], in1=ct['lnxb'][:], op=ALU.add), reads=['ysb', 'lnxb'], writes=['ysb'])
                P.op('dve', (lambda e, ti=ti: e.tensor_tensor(out=q3, in0=vtok[:, ti, :].rearrange("p (h c) -> p h c", c=64),
                                                              in1=bon[:, ti, :].unsqueeze(2).to_broadcast([128, 16, 64]), op=ALU.mult)),
                     reads=['vtok', 'bon', 'ysq'], writes=['ysq'])
                P.op('dve', lambda e: e.tensor_tensor(out=ysb[:], in0=ysb[:], in1=ysq[:], op=ALU.add), reads=['ysb', 'ysq'], writes=['ysb'])
                P.op('dve', (lambda e, ti=ti: e.tensor_tensor(out=yab[:], in0=ysb[:], in1=sa[:, ti, :], op=ALU.mult)), reads=['ysb', 'sa'], writes=['yab'])

                if DBG.get('dump', 0):
                    out_idx.append(P.op('pool', (lambda e, gt=gt: e.dma_start(out=y_o[(gt + 8) * 128:(gt + 9) * 128, 0:1024], in_=yab[:])), reads=['yab'], chan='o_dbg'))
                tr_offs = [512, 640, 768, 896] if ti == 0 else [384, 896]
                nr = len(tr_offs)
                for r0 in range(0, 8, nr):
                    def fn(e, r0=r0, tr_offs=tr_offs, nr=nr):
                        ins = None
                        for k in range(nr):
                            ins = e.transpose(out=tp[:, tr_offs[k]:tr_offs[k] + 128], in_=yab[:, (r0 + k) * 128:(r0 + k + 1) * 128], identity=identb[:])
                        return ins
                    P.op('pe', fn, reads=['yab', 'identb'], writes=['tp'])
                    for k in range(nr):
                        P.op('act', (lambda e, tcs=tcs, r0=r0, k=k, off=tr_offs[k]: e.copy(out=yaT[:, r0 + k, tcs], in_=tp[:, off:off + 128])), writes=['yaT', 'tp'])
                if DBG.get('pso', 1):
                    (PS_PENDING if ti == 0 else PS1_PENDING).append(P.capture_end())

            if DBG['stage'] <= 4:
                continue
            for ti in range(NT):
                gt = b * NT + ti
                if gt == 15:
                    out_idx.append(P.op('pool', (lambda e, ti=ti: e.dma_start(out=kp_o, in_=kvo[:, ti, 0:256])), reads=['kvo'], chan='o_kv'))
                    out_idx.append(P.op('pool', (lambda e, ti=ti: e.dma_start(out=vp_o, in_=kvo[:, ti, 256:512])), reads=['kvo'], chan='o_kv'))
                if sample:
                    for c in range(2):
                        seq = ti * 2 + c
                        cp = slice(c * 64, c * 64 + 64)
                        out_idx.append(P.op('pool', (lambda e, ti=ti, seq=seq, cp=cp: e.dma_start(out=ks_o[seq, 64:128, :], in_=kvo[cp, ti, 0:256])), reads=['kvo'], chan='o_kv'))
                        out_idx.append(P.op('pool', (lambda e, ti=ti, seq=seq, cp=cp: e.dma_start(out=vs_o[seq, 64:128, :], in_=kvo[cp, ti, 256:512])), reads=['kvo'], chan='o_kv'))
                        out_idx.append(P.op('pool', (lambda e, seq=seq: e.dma_start(out=ks_o[seq, 0:64, :], in_=ck_raw[seq, 64:128, :])), chan='o_kv'))
                        out_idx.append(P.op('pool', (lambda e, seq=seq: e.dma_start(out=vs_o[seq, 0:64, :], in_=cv_raw[seq, 64:128, :])), chan='o_kv'))

            if DBG['stage'] <= 5:
                continue
            fence(ARENA_PREP, ARENA_FIN)

            def aux_m(i):
                slot = load_unit(b, nu(), w_in_d, 6784 + i * 256, 256, 16)
                pr = next_pair()
                for f in range(2):
                    mm_fm(slot, 16, f, hT, 'hT', pr[f])
                for f in range(2):
                    ai = pr[f]
                    P.op('act', (lambda e, ai=ai, f=f, i=i: e.activation(out=ta_all[:, i * 2 + f, :], in_=A[ai][:, 0:TB], func=AF.Sigmoid)), writes=['taall', akey(ai)])

            def aux_p(i):
                slot = load_unit(b, nu(), p_a_d, i * 256, 256, 8)
                pr = next_pair()
                for f in range(2):
                    mm_fm(slot, 8, f, yaT, 'yaT', pr[f])
                for f in range(2):
                    ai = pr[f]
                    P.op('dve', (lambda e, ai=ai, f=f, i=i: e.tensor_tensor(out=ta_all[:, i * 2 + f, :], in0=ta_all[:, i * 2 + f, :], in1=A[ai][:, 0:TB], op=ALU.mult)),
                         writes=['taall', akey(ai)])

            for ti in range(NT):
                gt = b * NT + ti
                tcs = slice(ti * 128, (ti + 1) * 128)
                nkb = 3 if sample else 2
                nk = nkb * 128
                Dm = ct['DmS'] if sample else (ct['DmP0'] if gt == 0 else ct['DmP'])
                Dk = 'DmS' if sample else ('DmP0' if gt == 0 else 'DmP')
                P.begin_streams(3)
                if ti == 0 and PS1_PENDING:
                    P.add_stream(PS1_PENDING.pop())
                P.set_stream(2)
                for i_ in range(8):
                    (aux_m if ti == 0 else aux_p)(i_)
                for h in range(16):
                    sx = (h % 2) if DBG.get('as', 1) else 0
                    P.set_stream(sx)
                    g = h // 4
                    f = h // 2
                    hp = slice((h % 2) * 64, (h % 2) * 64 + 64)
                    SB = Mb if sx == 0 else Cb
                    SBk = 'Mb' if sx == 0 else 'Cb'
                    s_x, e_x, eT_x, sm = s_sbS[sx], e_sbS[sx], eTS[sx], smallS[sx]
                    ks = 'at%d_' % sx
                    tpo = sx * 512

                    def fnS(e, g=g, f=f, hp=hp, ti=ti, tcs=tcs, sample=sample, SB=SB):
                        kb_ = hp.start
                        if not sample:
                            return mmk(e, SB[:, 0:256], qT[hp, f, tcs], kTd[hp, g, ti * 128:ti * 128 + 256], kb_)
                        mmk(e, SB[:, 0:128], qT[hp, f, tcs], kc[hp, ti * 2, g, :], kb_)
                        mmk(e, SB[:, 128:256], qT[hp, f, tcs], kc[hp, ti * 2 + 1, g, :], kb_)
                        return mmk(e, SB[:, 256:384], qT[hp, f, tcs], kTd[hp, g, 128 + ti * 128:256 + ti * 128], kb_)
                    P.op('pe', fnS, reads=['qT', 'kTd', 'kc'], writes=[SBk])
                    P.op('dve', (lambda e, h=h, nk=nk, Dm=Dm, s_x=s_x, SB=SB: e.scalar_tensor_tensor(out=s_x[:, 0:nk], in0=Dm[:, 0:nk], scalar=SLOPES[h], in1=SB[:, 0:nk], op0=ALU.mult, op1=ALU.add)),
                         reads=[Dk], writes=[ks + 's', SBk])
                    P.op('dve', (lambda e, nk=nk, s_x=s_x, sm=sm: e.tensor_reduce(out=sm[:, 0:1], in_=s_x[:, 0:nk], axis=AX.X, op=ALU.max)), reads=[ks + 's'], writes=[ks + 'mx'])
                    P.op('dve', (lambda e, h=h, sm=sm: e.tensor_scalar(out=sm[:, 1:2], in0=sm[:, 0:1], scalar1=ct['sinks'][:, h:h + 1], scalar2=-1.0, op0=ALU.max, op1=ALU.mult)),
                         reads=[ks + 'mx', 'sinks'], writes=[ks + 'negm'])
                    P.op('dve', (lambda e, sm=sm: e.memset(sm[:, 2:3], 0.0)), writes=[ks + 'rs'])
                    P.op('act', (lambda e, nk=nk, s_x=s_x, e_x=e_x, sm=sm: e.activation(out=e_x[:, 0:nk], in_=s_x[:, 0:nk], func=AF.Exp, bias=sm[:, 1:2], accum_out=sm[:, 2:3])),
                         reads=[ks + 's', ks + 'negm', ks + 'rs'], writes=[ks + 'e', ks + 'rs'])
                    P.op('act', (lambda e, h=h, sm=sm: e.activation(out=sm[:, 3:4], in_=sm[:, 1:2], func=AF.Exp, bias=ct['sinks'][:, h:h + 1])),
                         reads=[ks + 'negm', 'sinks'], writes=[ks + 'es'])
                    P.op('dve', (lambda e, sm=sm: e.tensor_tensor(out=sm[:, 3:4], in0=sm[:, 3:4], in1=sm[:, 2:3], op=ALU.add)), reads=[ks + 'es', ks + 'rs'], writes=[ks + 'es'])
                    P.op('dve', (lambda e, h=h, sm=sm: e.reciprocal(out=rden[:, h:h + 1], in_=sm[:, 3:4])), reads=[ks + 'es'], writes=['rden%d' % h])

                    def fnT(e, nkb=nkb, e_x=e_x, tpo=tpo):
                        ins = None
                        for kb in range(nkb):
                            ins = e.transpose(out=tp[:, tpo + kb * 128:tpo + (kb + 1) * 128], in_=e_x[:, kb * 128:(kb + 1) * 128], identity=identb[:])
                        return ins
                    P.op('pe', fnT, reads=[ks + 'e', 'identb'], writes=['tp'])
                    P.op('act', (lambda e, nk=nk, eT_x=eT_x, tpo=tpo: e.copy(out=eT_x[:, 0:nk], in_=tp[:, tpo:tpo + nk])), writes=[ks + 'eT', 'tp'])

                    def fnV(e, g=g, ti=ti, nkb=nkb, sample=sample, SB=SB, eT_x=eT_x):
                        gs = slice(g * 64, g * 64 + 64)
                        po = SB[:, 448:512]
                        if not sample:
                            e.matmul(po, lhsT=eT_x[:, 0:128], rhs=vat[:, ti, gs], start=True, stop=False)
                            return e.matmul(po, lhsT=eT_x[:, 128:256], rhs=vat[:, ti + 1, gs], start=False, stop=True)
                        e.matmul(po, lhsT=eT_x[:, 0:128], rhs=vc[:, ti * 2, gs], start=True, stop=False)
                        e.matmul(po, lhsT=eT_x[:, 128:256], rhs=vc[:, ti * 2 + 1, gs], start=False, stop=False)
                        return e.matmul(po, lhsT=eT_x[:, 256:384], rhs=vat[:, ti + 1, gs], start=False, stop=True)
                    P.op('pe', fnV, reads=[ks + 'eT', 'vat', 'vc'], writes=[SBk])
                    P.op('dve', (lambda e, h=h, SB=SB: e.tensor_scalar(out=ob[:, h * 64:(h + 1) * 64], in0=SB[:, 448:512], scalar1=rden[:, h:h + 1], scalar2=None, op0=ALU.mult)),
                         reads=['rden%d' % h], writes=['ob%d' % h, SBk])
                P.merge_streams()
                if DBG.get('dump', 0):
                    out_idx.append(P.op('pool', (lambda e, gt=gt: e.dma_start(out=y_o[(gt + 4) * 128:(gt + 5) * 128, 1024:2048], in_=ob[:])), reads=['ob'] + ['ob%d' % h for h in range(16)], chan='o_dbg'))
                P.op('dve', (lambda e, ti=ti: e.tensor_tensor(out=yab[:], in0=ob[:], in1=sbg[:, ti, :], op=ALU.mult)), reads=['ob%d' % h for h in range(16)] + ['sbg'], writes=['yab'])

                if DBG.get('dump', 0):
                    out_idx.append(P.op('pool', (lambda e, gt=gt: e.dma_start(out=y_o[(gt + 8) * 128:(gt + 9) * 128, 1024:2048], in_=yab[:])), reads=['yab'], chan='o_dbg'))
                def fn(e):
                    ins = None
                    for k in range(8):
                        ins = e.transpose(out=tp[:, k * 128:(k + 1) * 128], in_=yab[:, k * 128:(k + 1) * 128], identity=identb[:])
                    return ins
                P.op('pe', fn, reads=['yab', 'identb'], writes=['tp'])
                P.op('act', (lambda e, tcs=tcs: e.copy(out=ybT[:, :, tcs], in_=tp[:, :].rearrange("p (k t) -> p k t", t=128))), writes=['ybT', 'tp'])


            if DBG.get('dump', 0):
                out_idx.append(P.op('pool', (lambda e: e.dma_start(out=y_o[12 * 128:13 * 128, :].rearrange('p (k t) -> p k t', t=TB), in_=yaT[:])), reads=['yaT'], chan='o_dbg'))
                out_idx.append(P.op('pool', (lambda e: e.dma_start(out=y_o[13 * 128:14 * 128, :].rearrange('p (k t) -> p k t', t=TB), in_=ybT[:])), reads=['ybT'], chan='o_dbg'))
                out_idx.append(P.op('pool', (lambda e: e.dma_start(out=y_o[14 * 128:15 * 128, :].rearrange('p (k t) -> p k t', t=TB), in_=hT[:, 0:8, :])), reads=['hT'], chan='o_dbg'))
            if DBG['stage'] <= 6:
                continue
            for i in range(8):
                slot = load_unit(b, nu(), w_in_d, 8832 + i * 256, 256, 16)
                pr = next_pair()
                for f in range(2):
                    mm_fm(slot, 16, f, hT, 'hT', pr[f])
                for f in range(2):
                    ai = pr[f]
                    P.op('act', (lambda e, ai=ai, f=f: e.activation(out=sgb[:, f, :], in_=A[ai][:, 0:TB], func=AF.Sigmoid)), writes=['sgb', akey(ai)])
                slot = load_unit(b, nu(), p_b_d, i * 256, 256, 8)
                pr = next_pair()
                for f in range(2):
                    mm_fm(slot, 8, f, ybT, 'ybT', pr[f])
                for f in range(2):
                    ai = pr[f]
                    P.op('dve', (lambda e, ai=ai, f=f: e.tensor_tensor(out=sgb[:, f, :], in0=sgb[:, f, :], in1=A[ai][:, 0:TB], op=ALU.mult)),
                         reads=['sgb'], writes=['sgb', akey(ai)])
                    P.op('pool', (lambda e, i=i, f=f: e.tensor_tensor(out=mergedT[:, i * 2 + f, :], in0=ta_all[:, i * 2 + f, :], in1=sgb[:, f, :], op=ALU.add)),
                         reads=['taall', 'sgb'], writes=['mergedT'])
            fence(['taall'], ['xr'])
            for ti in range(NT):
                gt = b * NT + ti
                P.op('sp', (lambda e, gt=gt, ti=ti: e.dma_start(out=xr[:, ti, :], in_=x_d[gt * 128:(gt + 1) * 128, :])),
                     writes=['xr'], chan='xr')
            P.begin_streams(2)
            if b + 1 < DBG['nblk']:
                P.set_stream(1)
                emit_S1(b + 1)
            P.set_stream(0)
            for i in range(8):
                slot = load_unit(b, nu(), w_o_d, i * 256, 256, 16)
                pr = next_pair()
                for ti in range(NT):
                    mm_tm(slot, 16, ti, mergedT, 'mergedT', pr[ti])
                for ti in range(NT):
                    ai = pr[ti]
                    P.op('dve', (lambda e, ai=ai, ti=ti, i=i: e.tensor_tensor(out=xr[:, ti, i * 256:(i + 1) * 256], in0=xr[:, ti, i * 256:(i + 1) * 256], in1=A[ai][:, 0:256], op=ALU.add)),
                         reads=['xr'], writes=['xr', akey(ai)])
            for ti in range(NT):
                gt = b * NT + ti
                P.op('dve', lambda e: e.memset(small[:, 0:1], 0.0), writes=['ssq'])
                P.op('act', (lambda e, ti=ti: e.activation(out=sa[:].rearrange("p t c -> p (t c)"), in_=xr[:, ti, :], func=AF.Square, accum_out=small[:, 0:1])),
                     reads=['xr', 'ssq'], writes=['sa', 'ssq'])
                P.op('dve', lambda e: e.tensor_scalar(out=small[:, 1:2], in0=small[:, 0:1], scalar1=1.0 / D, scalar2=RMS_EPS, op0=ALU.mult, op1=ALU.add),
                     reads=['ssq'], writes=['ms'])
                P.op('act', lambda e: e.activation(out=small[:, 2:3], in_=small[:, 1:2], func=AF.Sqrt), reads=['ms'], writes=['sq'])
                P.op('dve', lambda e: e.reciprocal(out=small[:, 3:4], in_=small[:, 2:3]), reads=['sq'], writes=['rstd'])
                P.op('dve', (lambda e, ti=ti: e.scalar_tensor_tensor(out=xr[:, ti, :], in0=xr[:, ti, :], scalar=small[:, 3:4], in1=ct['gfbc'][:], op0=ALU.mult, op1=ALU.mult)),
                     reads=['xr', 'rstd', 'gfbc'], writes=['xr'])
                P.op('pool', (lambda e, gt=gt, ti=ti: e.dma_start(out=y_o[gt * 128:(gt + 1) * 128, :], in_=xr[:, ti, :])),
                     reads=['xr'], writes=['xr_st'], chan='o_y', cb=out_idx)
            P.merge_streams()
            if b == NBLK - 2:
                out_idx.append(P.op('pool', lambda e: e.dma_start(out=shp_o, in_=plast[:]), reads=['plast'], chan='o_sh'))
            if sample:
                out_idx.append(P.op('pool', lambda e: e.dma_start(out=shs_o, in_=shs[:]), reads=['shs'], chan='o_sh'))
            assert st['u'] <= 80, st['u']

        P.wait_all('pool', out_idx)
        P.emit()
        build.stats = P.stats
    return nc


_CACHE = {}


def _consts():
    c = {}
    c['identf'] = np.eye(128, dtype=np.float32)
    s = np.arange(128)[:, None]
    t = np.arange(128)[None, :]
    same = (s // 64) == (t // 64)
    MUs = (same & (s < t)).astype(np.float32)
    MUi = (same & (s <= t)).astype(np.float32)
    c['MU4'] = np.concatenate([MUs, MUs, MUi, MUi], axis=1)
    c['MLs'] = (same & (t < s)).astype(np.float32)
    c['bones'] = same.astype(np.float32)
    s6 = np.arange(64)[:, None]
    t6 = np.arange(64)[None, :]
    mus = (s6 < t6).astype(np.float32)
    mui = (s6 <= t6).astype(np.float32)
    mls = (t6 < s6).astype(np.float32)
    m5 = np.concatenate([mus, mus, mui, mui, mls], axis=1)
    c['MU5'] = np.concatenate([m5, m5], axis=0)
    c['I2'] = np.concatenate([np.eye(64, dtype=np.float32)] * 2, axis=0)
    bo2 = np.zeros((128, 2), np.float32)
    bo2[:64, 0] = 1
    bo2[64:, 1] = 1
    c['bo2'] = bo2
    rm = np.ones((128, TB), np.float32)
    rm[:, ::64] = 0
    c['resetm'] = rm
    NEG = -1e30
    i = np.arange(128)[:, None]
    k = np.arange(256)[None, :]
    dch = (2 + i // 64) - (k // 64)
    vis = (dch >= 0) & (dch <= 2)
    DmP = np.where(vis, -np.abs(128 + i - k).astype(np.float32), NEG).astype(np.float32)
    c['DmP'] = DmP
    DmP0 = DmP.copy()
    DmP0[:, :128] = NEG
    c['DmP0'] = DmP0
    DmS = np.full((128, 384), NEG, np.float32)
    tt = np.arange(64)[:, None]
    kk = np.arange(128)[None, :]
    t2 = np.arange(64)[None, :]
    for sq in range(2):
        rows = slice(sq * 64, sq * 64 + 64)
        DmS[rows, sq * 128:(sq + 1) * 128] = -(128 + tt - kk).astype(np.float32)
        DmS[rows, 256 + sq * 64:256 + sq * 64 + 64] = -np.abs(tt - t2).astype(np.float32)
    c['DmS'] = DmS
    return c


def kernel(x_prompt, x_sample, state_wkv, state_shift, cache_k, cache_v, g_norm, w_in, mu_shift, w0,
           w_w_up, a0, w_a_up, k_k, k_a, r_k, lnx_w, lnx_b, sinks, p_a, p_b, w_o, g_final):
    f32 = np.float32
    A_ = lambda v: np.ascontiguousarray(np.asarray(v, dtype=f32))
    if 'nc' not in _CACHE:
        _CACHE['nc'] = build()
    nc = _CACHE['nc']
    cst = _consts()
    col = lambda v: A_(np.asarray(v, f32).reshape(-1, 128).T)
    shared = dict(cst)
    shared.update(
        w_in=A_(w_in[0]), p_a=A_(p_a[0]), p_b=A_(p_b[0]), w_o=A_(w_o[0]),
        gbc=A_(np.broadcast_to(np.asarray(g_norm[0], f32)[None, :], (128, D))),
        gfbc=A_(np.broadcast_to(np.asarray(g_final, f32)[None, :], (128, D))),
        lnxw=A_(np.broadcast_to(np.asarray(lnx_w[0], f32)[None, :], (128, 1024))),
        lnxb=A_(np.broadcast_to(np.asarray(lnx_b[0], f32)[None, :], (128, 1024))),
        muT=col(mu_shift[0]), w0c=col(w0[0]), a0c=col(a0[0]), kkc=col(k_k[0]), kac=col(k_a[0]),
        rkc=col(np.asarray(r_k[0], f32).reshape(-1)),
        sinks=A_(np.broadcast_to(np.asarray(sinks[0], f32)[None, :], (128, 16))),
        wup=A_(np.concatenate([np.asarray(w_w_up[0], f32), np.asarray(w_a_up[0], f32)], axis=0)),
    )
    xp = np.asarray(x_prompt, f32)
    xs_ = np.asarray(x_sample, f32)
    swkv = np.asarray(state_wkv[0], f32)
    ssh = np.asarray(state_shift[0], f32)
    ck = np.asarray(cache_k[0], f32)
    cvv = np.asarray(cache_v[0], f32)
    in_maps = []
    for c in range(8):
        sl = slice(4 * c, 4 * c + 4)
        m = dict(shared)
        m['x'] = A_(np.concatenate([xp[c], xs_[sl].reshape(256, D)], axis=0))
        sw = swkv[sl].reshape(4, 8, 2, 64, 64)
        m['swkv'] = A_(sw.transpose(0, 2, 4, 1, 3).reshape(4, 128, 8, 64))
        m['sshT'] = A_(ssh[sl].reshape(4, 25, 128).transpose(2, 1, 0))
        ckc = ck[sl]
        kt_ = ckc.transpose(3, 0, 2, 1)
        m['ckT'] = A_(np.concatenate([kt_, kt_], axis=0))
        m['cv'] = A_(cvv[sl].reshape(4, 128, 256).transpose(1, 0, 2))
        m['ck_raw'] = A_(ckc.reshape(4, 128, 256))
        m['cv_raw'] = A_(cvv[sl].reshape(4, 128, 256))
        in_maps.append(m)
    res = run_bass_kernel_spmd(nc, in_maps, core_ids=list(range(8)))
    R = res.results
    y_prompt = np.stack([R[c]['y'][:2048] for c in range(8)]).astype(f32)
    y_sample = np.concatenate([R[c]['y'][2048:].reshape(4, 64, D) for c in range(8)]).astype(f32)

    def unP(a):
        return a.reshape(2, 64, 8, 64).transpose(2, 0, 3, 1).reshape(16, 64, 64)
    wkv_p = np.stack([unP(R[c]['wkv_p']) for c in range(8)])[None].astype(f32)
    wkv_s = np.stack([unP(R[c]['wkv_s'][s]) for c in range(8) for s in range(4)])[None].astype(f32)
    shift_p = np.stack([R[c]['shift_p'].T.reshape(3200) for c in range(8)])[None].astype(f32)
    shift_s = np.stack([R[c]['shift_s'][:, :, s].T.reshape(3200) for c in range(8) for s in range(4)])[None].astype(f32)
    k_p = np.stack([R[c]['k_p'].reshape(128, 4, 64) for c in range(8)])[None].astype(f32)
    v_p = np.stack([R[c]['v_p'].reshape(128, 4, 64) for c in range(8)])[None].astype(f32)
    k_s = np.concatenate([R[c]['k_s'].reshape(4, 128, 4, 64) for c in range(8)])[None].astype(f32)
    v_s = np.concatenate([R[c]['v_s'].reshape(4, 128, 4, 64) for c in range(8)])[None].astype(f32)
    return (y_prompt, y_sample, wkv_p, shift_p, k_p, v_p, wkv_s, shift_s, k_s, v_s)
```

```python
import numpy as np
from contextlib import ExitStack
import concourse.bass as bass
import concourse.mybir as mybir
from concourse.bass_utils import run_bass_kernel_spmd

F32 = mybir.dt.float32
BF16 = mybir.dt.bfloat16
ALU = mybir.AluOpType
AF = mybir.ActivationFunctionType
AX = mybir.AxisListType

D = 2048
NTILES = 18
NT = 2
TB = NT * 128
NBLK = NTILES // NT
INW = 10880
RMS_EPS = 1e-6
LNX_EPS = 64e-5
C0 = float(np.exp(-0.5))
DBG = dict(nblk=NBLK, stage=99)
SLOPES = [float(2.0 ** (-(h + 1) / 2.0)) for h in range(16)]


class Prog:
    COMPUTE = ('pe', 'act', 'dve', 'pool')

    def __init__(self, nc):
        self.nc = nc
        self.ops = []
        self.last_w = {}
        self.readers = {}
        self.streams = None
        self.cur_stream = None

    def begin_streams(self, n):
        self.streams = [[] for _ in range(n)]
        self.cur_stream = None

    def set_stream(self, i):
        self.cur_stream = i

    def capture_start(self):
        assert self.streams is None
        self.streams = [[]]
        self.cur_stream = 0

    def capture_end(self):
        q = self.streams[0]
        self.streams, self.cur_stream = None, None
        return q

    def add_stream(self, q):
        self.streams.append(q)

    def merge_streams(self):
        streams, self.streams, self.cur_stream = self.streams, None, None
        order = []
        for si, q in enumerate(streams):
            L = len(q)
            for k in range(L):
                order.append(((k + 0.5) / L, si, k))
        order.sort()
        for _, si, k in order:
            a, kw, cb = streams[si][k]
            idx = self.op(*a, **kw)
            if cb is not None:
                cb.append(idx)

    def op(self, eng, fn, reads=(), writes=(), chan=None, cb=None):
        if getattr(self, 'cur_stream', None) is not None:
            self.streams[self.cur_stream].append(((eng, fn), dict(reads=reads, writes=writes, chan=chan), cb))
            return -1
        idx = len(self.ops)
        deps = {}
        for k in reads:
            d = self.last_w.get(k)
            if d is not None:
                deps[d] = True
        for k in writes:
            d = self.last_w.get(k)
            if d is not None:
                deps.setdefault(d, False)
            for r in self.readers.get(k, ()):
                deps.setdefault(r, False)
        deps.pop(idx, None)
        self.ops.append(dict(eng=eng, fn=fn, deps=deps, chan=chan))
        for k in reads:
            self.readers.setdefault(k, []).append(idx)
        for k in writes:
            self.last_w[k] = idx
            self.readers[k] = []
        return idx

    def wait_all(self, eng, idxs):
        idx = len(self.ops)
        self.ops.append(dict(eng=eng, fn=None, deps={d: True for d in idxs}, chan=None))
        return idx

    def _need_wait(self, x, d, raw):
        od, ox = self.ops[d], self.ops[x]
        if od['chan'] is not None:
            return True
        if od['eng'] != ox['eng']:
            return True
        if ox['chan'] is not None:
            return True
        if ox['eng'] == 'pe':
            return False
        return True

    def emit(self):
        nc = self.nc
        ops = self.ops
        needed = [False] * len(ops)
        for x, o in enumerate(ops):
            for d, raw in o['deps'].items():
                if self._need_wait(x, d, raw):
                    needed[d] = True
        chans = []
        for o in ops:
            if o['chan'] is not None and o['chan'] not in chans:
                chans.append(o['chan'])
        with ExitStack() as es:
            sems = {}
            for e in self.COMPUTE:
                sems[e] = es.enter_context(nc.semaphore('s_' + e))
            for c in chans:
                sems[('c', c)] = es.enter_context(nc.semaphore('c_' + str(c)))
            cnt = {k: 0 for k in sems}
            ev = [None] * len(ops)
            for x, o in enumerate(ops):
                if o['fn'] is None:
                    continue
                if o['chan'] is not None:
                    k = ('c', o['chan'])
                    cnt[k] += 16
                    ev[x] = (k, cnt[k])
                elif needed[x]:
                    k = o['eng']
                    cnt[k] += 1
                    ev[x] = (k, cnt[k])
            per_eng = {}
            for x, o in enumerate(ops):
                per_eng.setdefault(o['eng'], []).append(x)
            self.stats = {e: len(v) for e, v in per_eng.items()}
            self.stats['sem_max'] = dict((str(k), v) for k, v in cnt.items() if v > 30000)

            def run(e, ename):
                waited = {}
                for x in per_eng.get(ename, ()):
                    o = ops[x]
                    want = {}
                    for d, raw in o['deps'].items():
                        if not self._need_wait(x, d, raw):
                            continue
                        k, v = ev[d]
                        if v > want.get(k, 0):
                            want[k] = v
                    for k, v in want.items():
                        if v > waited.get(k, 0):
                            e.wait_ge(sems[k], v)
                            waited[k] = v
                    if o['fn'] is None:
                        continue
                    ins = o['fn'](e)
                    if ev[x] is not None:
                        k, v = ev[x]
                        ins.then_inc(sems[k], 16 if o['chan'] is not None else 1)

            with nc.Block() as block:
                @block.tensor
                def _(e):
                    run(e, 'pe')

                @block.scalar
                def _(e):
                    run(e, 'act')

                @block.vector
                def _(e):
                    run(e, 'dve')

                @block.gpsimd
                def _(e):
                    run(e, 'pool')

                @block.sync
                def _(e):
                    run(e, 'sp')


def build():
    nc = bass.Bass("TRN2", target_bir_lowering=False)

    def din(name, shape, dt=F32):
        return nc.dram_tensor(name, list(shape), dt, kind="ExternalInput").ap()

    def dout(name, shape, dt=F32):
        return nc.dram_tensor(name, list(shape), dt, kind="ExternalOutput").ap()

    x_d = din("x", [NTILES * 128, D])
    w_in_d = din("w_in", [D, INW])
    p_a_d = din("p_a", [1024, D])
    p_b_d = din("p_b", [1024, D])
    w_o_d = din("w_o", [D, D])
    cnames = dict(gbc=[128, D], gfbc=[128, D], lnxw=[128, 1024], lnxb=[128, 1024], muT=[128, 25],
                  w0c=[128, 8], a0c=[128, 8], kkc=[128, 8], kac=[128, 8], rkc=[128, 8],
                  sinks=[128, 16], identf=[128, 128], MU4=[128, 512], MLs=[128, 128], bones=[128, 128],
                  bo2=[128, 2], resetm=[128, TB], MU5=[128, 320], I2=[128, 64], DmP=[128, 256], DmP0=[128, 256], DmS=[128, 384],
                  sshT=[128, 25, 4])
    cd = {k: din(k, v) for k, v in cnames.items()}
    wup_d = din("wup", [128, 1024])
    swkv_d = din("swkv", [4, 128, 8, 64])
    ckT_d = din("ckT", [128, 4, 4, 128])
    cv_d = din("cv", [128, 4, 256])
    ck_raw = din("ck_raw", [4, 128, 256])
    cv_raw = din("cv_raw", [4, 128, 256])

    y_o = dout("y", [NTILES * 128, D])
    wkvp_o = dout("wkv_p", [128, 8, 64])
    wkvs_o = dout("wkv_s", [4, 128, 8, 64])
    shp_o = dout("shift_p", [128, 25])
    shs_o = dout("shift_s", [128, 25, 4])
    kp_o = dout("k_p", [128, 256])
    vp_o = dout("v_p", [128, 256])
    ks_o = dout("k_s", [4, 128, 256])
    vs_o = dout("v_s", [4, 128, 256])

    NUNITS = 0
    scr = nc.dram_tensor("wscr", [80, 128, 16 * 256], BF16).ap()

    es = ExitStack()
    with es:
        def sb(name, shape, dt=F32):
            return es.enter_context(nc.sbuf_tensor(name, list(shape), dt))

        def ps(name, shape, dt=F32):
            return es.enter_context(nc.psum_tensor(name, list(shape), dt))

        P = Prog(nc)
        ct = {k: sb("c_" + k, v) for k, v in cnames.items()}
        identb = sb("identb", [128, 128], BF16)
        wupb = sb("wupb", [128, 1024], BF16)
        kc = sb("kc", [128, 4, 4, 128], BF16)
        vc = sb("vc", [128, 4, 256], BF16)
        dummy = sb("dummy_t", [128, 8])
        small = sb("small", [128, 64])
        hT = sb("hT", [128, 16, TB], BF16)
        stage = [sb("stage%d" % i, [128, 8, 256]) for i in range(2)]
        wbf = [sb("wbf%d" % i, [128, 16, 256], BF16) for i in range(2)]
        xt = sb("xt", [128, D])
        hb = sb("hb", [128, D], BF16)
        plast = sb("plast", [128, 25])
        shs = sb("shs", [128, 25, 4])
        pT = sb("pT", [128, TB + 1])
        arenaA = sb("arenaA", [128, 23 * TB])
        xs = [arenaA[:, i * TB:(i + 1) * TB] for i in range(6)]
        tq = [[arenaA[:, (6 + s_ * 8 + i) * TB:(7 + s_ * 8 + i) * TB] for i in range(8)] for s_ in range(2)]
        tmpf = {11: arenaA[:, 22 * TB:23 * TB]}
        xr = arenaA[:, 0:NT * D].rearrange("p (t d) -> p t d", d=D)
        ta_all = arenaA[:, 0:16 * TB].rearrange("p (m t) -> p m t", t=TB)
        lora = sb("lora", [128, TB], BF16)
        xsB_t = sb("xsB", [128, 6 * TB])
        xsB = [xsB_t[:, i * TB:(i + 1) * TB] for i in range(6)]
        arenaC = sb("arenaC", [128, 4 * 8 * TB], BF16)
        opT = [arenaC[:, i * 8 * TB:(i + 1) * 8 * TB].rearrange("p (j t) -> p j t", t=TB) for i in range(4)]
        rT, aT, bT, kT = opT
        mergedT = arenaC[:, 0:16 * TB].rearrange("p (j t) -> p j t", t=TB)
        fin32 = arenaC[:, 16 * TB:32 * TB].bitcast(F32)
        sga = fin32[:, 0:2 * TB].rearrange("p (f t) -> p f t", t=TB)
        sgb = fin32[:, 2 * TB:4 * TB].rearrange("p (f t) -> p f t", t=TB)
        ta = fin32[:, 4 * TB:6 * TB].rearrange("p (f t) -> p f t", t=TB)
        gC = sb("gC", [128, 8, 2 * NT])
        vtok = sb("vtok", [128, NT, 1024], BF16)
        vtk = sb("vtk", [128, 8, 2 * NT, 64], BF16)
        I2b = sb("I2b", [128, 64], BF16)
        LNSS = [[sb("LNS%d_%d" % (k, i), [128, 192], BF16) for i in range(2)] for k in range(2)]
        bon = sb("bon", [128, NT, 16])
        kbtokS = [sb("kbtok%d" % i, [128, 128], BF16) for i in range(2)]
        MmS = [sb("Mm%d" % i, [128, 320], BF16) for i in range(2)]
        XUbS = [sb("XUb%d" % i, [128, 128], BF16) for i in range(2)]
        Pf = sb("Pf", [128, 8, 64])
        Pb = sb("Pb", [128, 8, 64], BF16)
        ysb = sb("ysb", [128, 1024])
        ysq = sb("ysq", [128, 1024])
        sa = sb("sa", [128, NT, 1024], BF16)
        sbg = sb("sbg", [128, NT, 1024], BF16)
        yab = sb("yab", [128, 1024], BF16)
        yaT = sb("yaT", [128, 8, TB], BF16)
        ybT = sb("ybT", [128, 8, TB], BF16)
        qT = sb("qT", [128, 8, TB], BF16)
        kTd = sb("kTd", [128, 4, 128 + TB], BF16)
        vat = sb("vat", [128, 1 + NT, 256], BF16)
        kvo = sb("kvo", [128, NT, 512])
        s_sbS = [sb("s_sb%d" % i, [128, 384]) for i in range(2)]
        e_sbS = [sb("e_sb%d" % i, [128, 384], BF16) for i in range(2)]
        eTS = [sb("eT%d" % i, [128, 384], BF16) for i in range(2)]
        smallS = [sb("smallS%d" % i, [128, 8]) for i in range(2)]
        ob = sb("ob", [128, 1024])
        rden = sb("rden", [128, 16])

        Aacc = [ps("A%d" % i, [128, 512]) for i in range(2)]
        A = [Aacc[i // 2][:, (i % 2) * 256:(i % 2) * 256 + 256] for i in range(4)]
        tp = ps("tp", [128, 1024], BF16)
        Mb = ps("Mb", [128, 512])
        Db = [ps("D%d" % i, [128, 512]) for i in range(2)]
        Cb = ps("Cb", [128, 512])
        Eb = ps("Eb", [128, 512])

        cidx = []
        for k in cnames:
            src = cd[k]
            cidx.append(P.op('pool', (lambda e, o=ct[k], s=src: e.dma_start(out=o[:], in_=s)), writes=[k], chan='const'))
        cidx.append(P.op('pool', lambda e: e.dma_start(out=wupb[:], in_=wup_d), writes=['wupb'], chan='const'))
        cidx.append(P.op('pool', lambda e: e.dma_start(out=kc[:], in_=ckT_d), writes=['kc'], chan='const'))
        cidx.append(P.op('pool', lambda e: e.dma_start(out=vc[:], in_=cv_d), writes=['vc'], chan='const'))
        for eng in ('pe', 'act', 'dve', 'pool'):
            P.wait_all(eng, cidx)
        P.op('dve', lambda e: e.tensor_copy(out=identb[:], in_=ct['identf'][:]), reads=['identf'], writes=['identb'])
        P.op('dve', lambda e: e.tensor_copy(out=I2b[:], in_=ct['I2'][:]), reads=['I2'], writes=['I2b'])
        P.op('dve', lambda e: e.memset(Pf[:], 0.0), writes=['Pf%d' % j for j in range(8)])
        P.op('dve', lambda e: e.memset(Pb[:], 0.0), writes=['Pb%d' % j for j in range(8)])
        P.op('dve', lambda e: e.memset(plast[:], 0.0), writes=['plast'])
        P.op('dve', lambda e: e.memset(kTd[:], 0.0), writes=['kTd'])
        P.op('dve', lambda e: e.memset(vat[:], 0.0), writes=['vat'])
        identf = ct['identf']

        st = dict(uid=0, sidx=0, acc=0, cast=0)
        out_idx = []
        ARENA_PREP = (['xs%d' % i for i in range(6)] + ['tq%d_%d' % (s_, i) for s_ in range(2) for i in range(8)] + ['tmp11']
                      + ['rT', 'aT', 'bT', 'kT'])
        ARENA_FIN = ['xr', 'mergedT', 'sga', 'sgb', 'ta', 'taall']

        def fence(after, before):
            P.op('pool', lambda e: e.memset(dummy[0:1, 0:1], 0.0), writes=list(after) + list(before))

        def load_unit(b, u, W, c0, n, KT, dup=False):
            slot = st['uid'] % 2
            st['uid'] += 1
            wk = 'wbf%d' % slot
            ncols = 256 if dup else n
            if b == 0:
                for half in range(KT // 8):
                    si = st['sidx'] % 2
                    st['sidx'] += 1
                    src = W[half * 1024:(half + 1) * 1024, c0:c0 + n].rearrange("(k p) n -> p k n", p=128)
                    P.op('sp', (lambda e, si=si, src=src: e.dma_start(out=stage[si][:, :, 0:n], in_=src)),
                         writes=['stage%d' % si], chan='stg%d' % si)
                    ceng = ('dve', 'act')[st['cast'] % 2]
                    st['cast'] += 1
                    if not dup:
                        dst = wbf[slot][:, half * 8:(half + 1) * 8, 0:n]
                        srcs = stage[si][:, :, 0:n]
                        if ceng == 'act':
                            P.op('act', (lambda e, dst=dst, srcs=srcs: e.copy(out=dst, in_=srcs)),
                                 reads=['stage%d' % si], writes=[wk])
                        else:
                            P.op('dve', (lambda e, dst=dst, srcs=srcs: e.tensor_copy(out=dst, in_=srcs)),
                                 reads=['stage%d' % si], writes=[wk])
                    else:
                        for dd in range(2):
                            dst = wbf[slot][:, half * 8:(half + 1) * 8, :].rearrange(
                                "p k (g d c) -> p k g d c", g=2, d=2)[:, :, :, dd, :]
                            srcs = stage[si][:, :, 0:128].rearrange("p k (g c) -> p k g c", g=2)
                            P.op('pool', (lambda e, dst=dst, srcs=srcs: e.tensor_copy(out=dst, in_=srcs)),
                                 reads=['stage%d' % si], writes=[wk])
                P.op('pool', (lambda e, u=u, slot=slot: e.dma_start(
                    out=scr[u % DBG.get("umod", 80), :, 0:KT * ncols].rearrange("p (k n) -> p k n", n=ncols),
                    in_=wbf[slot][:, 0:KT, 0:ncols])),
                    reads=[wk], writes=['scr%d' % u], chan='wst%d' % slot)
            else:
                P.op('sp', (lambda e, u=u, slot=slot: e.dma_start(
                    out=wbf[slot][:, 0:KT, 0:ncols],
                    in_=scr[u % DBG.get("umod", 80), :, 0:KT * ncols].rearrange("p (k n) -> p k n", n=ncols))),
                    reads=['scr%d' % u], writes=[wk], chan='wld%d' % slot)
            return slot

        def nu():
            st['u'] += 1
            return st['u'] - 1

        def next_pair():
            if st.get('pb0'):
                return (0, 1)
            if st.get('pb') is not None:
                return (2 * st['pb'], 2 * st['pb'] + 1)
            k = st['acc'] % 2
            st['acc'] += 1
            return (2 * k, 2 * k + 1)

        def next_acc():
            return next_pair()[0]

        def akey(ai):
            return 'PB%d' % (ai // 2)

        def mm_fm(slot, KT, f, rhsT, rkey, ai, ncol=TB):
            def fn(e):
                ins = None
                for kt in range(KT):
                    ins = e.matmul(A[ai][:, 0:ncol], lhsT=wbf[slot][:, kt, f * 128:(f + 1) * 128],
                                   rhs=rhsT[:, kt, 0:ncol], start=(kt == 0), stop=(kt == KT - 1))
                return ins
            P.op('pe', fn, reads=['wbf%d' % slot, rkey], writes=[akey(ai)])

        def mm_tm(slot, KT, ti, lhs_tile, lkey, ai, n=256):
            def fn(e):
                ins = None
                for kt in range(KT):
                    ins = e.matmul(A[ai][:, 0:n], lhsT=lhs_tile[:, kt, ti * 128:(ti + 1) * 128],
                                   rhs=wbf[slot][:, kt, 0:n], start=(kt == 0), stop=(kt == KT - 1))
                return ins
            P.op('pe', fn, reads=['wbf%d' % slot, lkey], writes=[akey(ai)])

        def mmk(e, out, lhsT, rhs, kbase):
            if kbase == 0:
                return e.matmul(out, lhsT=lhsT, rhs=rhs, start=True, stop=True)
            e.matmul(out[0:64], lhsT=lhsT[:, 0:64], rhs=rhs, start=True, stop=True)
            return e.matmul(out[64:128], lhsT=lhsT[:, 64:128], rhs=rhs, start=True, stop=True)

        def chunk3(ap):
            return ap.rearrange("p (c t) -> p c t", t=64)

        for b in range(DBG['nblk']):
            sample = (b == NBLK - 1)
            st['u'] = 0
            def emit_S1(bb):
                for ti in range(NT):
                    gt = bb * NT + ti
                    P.op('sp', (lambda e, gt=gt: e.dma_start(out=xt[:], in_=x_d[gt * 128:(gt + 1) * 128, :])),
                         writes=['xt'], chan='xt')
                    P.op('dve', lambda e: e.memset(small[:, 56:57], 0.0), writes=['s1ssq'])
                    P.op('act', lambda e: e.activation(out=hb[:], in_=xt[:], func=AF.Square, accum_out=small[:, 56:57]),
                         reads=['xt', 's1ssq'], writes=['hb', 's1ssq'])
                    P.op('dve', lambda e: e.tensor_scalar(out=small[:, 57:58], in0=small[:, 56:57], scalar1=1.0 / D,
                                                          scalar2=RMS_EPS, op0=ALU.mult, op1=ALU.add),
                         reads=['s1ssq'], writes=['s1ms'])
                    P.op('act', lambda e: e.activation(out=small[:, 58:59], in_=small[:, 57:58], func=AF.Sqrt),
                         reads=['s1ms'], writes=['s1sq'])
                    P.op('dve', lambda e: e.reciprocal(out=small[:, 59:60], in_=small[:, 58:59]), reads=['s1sq'], writes=['s1rstd'])
                    P.op('dve', lambda e: e.scalar_tensor_tensor(out=hb[:], in0=xt[:], scalar=small[:, 59:60],
                                                                 in1=ct['gbc'][:], op0=ALU.mult, op1=ALU.mult),
                         reads=['xt', 's1rstd', 'gbc'], writes=['hb'])
                    for half in range(2):
                        def fn(e, half=half):
                            ins = None
                            for k in range(8):
                                kt = half * 8 + k
                                ins = e.transpose(out=tp[:, k * 128:(k + 1) * 128], in_=hb[:, kt * 128:(kt + 1) * 128],
                                                  identity=identb[:])
                            return ins
                        P.op('pe', fn, reads=['hb', 'identb'], writes=['tp'])
                        dst = hT[:, half * 8:(half + 1) * 8, ti * 128:(ti + 1) * 128]
                        srcv = tp[:, :].rearrange("p (k t) -> p k t", t=128)
                        if half == 0:
                            P.op('act', (lambda e, dst=dst, srcv=srcv: e.copy(out=dst, in_=srcv)), writes=['hT', 'tp'])
                        else:
                            P.op('dve', (lambda e, dst=dst, srcv=srcv: e.tensor_copy(out=dst, in_=srcv)), writes=['hT', 'tp'])

            if b == 0:
                emit_S1(0)

            if DBG['stage'] <= 1:
                continue
            fence(ARENA_FIN, ARENA_PREP)

            def shift(f, ai, xs_ap, xkey):
                P.op('act', (lambda e: e.copy(out=pT[:, 1:TB + 1], in_=A[ai][:, 0:TB])), writes=['pT', akey(ai)])
                if not sample:
                    P.op('dve', (lambda e: e.tensor_copy(out=pT[:, 0:1], in_=plast[:, f:f + 1])), reads=['plast'], writes=['pT'])
                    P.op('dve', (lambda e: e.tensor_copy(out=plast[:, f:f + 1], in_=pT[:, TB:TB + 1])), reads=['pT'], writes=['plast'])
                else:
                    P.op('dve', (lambda e: e.memset(pT[:, 0:1], 0.0)), writes=['pT'])
                    P.op('dve', (lambda e: e.tensor_copy(out=shs[:, f, :], in_=chunk3(pT[:, 1:TB + 1])[:, :, 63])),
                         reads=['pT'], writes=['shs'])
                t0 = tmpf[11]
                P.op('pool', (lambda e: e.tensor_tensor(out=t0, in0=pT[:, 0:TB], in1=pT[:, 1:TB + 1], op=ALU.subtract)),
                     reads=['pT'], writes=['tmp11'])
                if sample:
                    P.op('dve', (lambda e: e.tensor_tensor(out=chunk3(t0)[:, :, 0], in0=ct['sshT'][:, f, :],
                                                           in1=chunk3(pT[:, 1:TB + 1])[:, :, 0], op=ALU.subtract)),
                         reads=['pT', 'sshT', 'tmp11'], writes=['tmp11'])
                P.op('dve', (lambda e: e.scalar_tensor_tensor(out=xs_ap, in0=t0, scalar=ct['muT'][:, f:f + 1],
                                                              in1=pT[:, 1:TB + 1], op0=ALU.mult, op1=ALU.add)),
                     reads=['tmp11', 'pT', 'muT'], writes=[xkey])

            slot = load_unit(b, nu(), w_in_d, 3072, 128, 16)
            ai = next_acc()
            mm_fm(slot, 16, 0, hT, 'hT', ai)
            shift(24, ai, xs[0], 'xs0')
            P.op('act', lambda e: e.activation(out=lora[0:64, :], in_=xs[0][0:64, :], func=AF.Tanh), reads=['xs0'], writes=['lora'])
            P.op('dve', lambda e: e.tensor_copy(out=lora[64:128, :], in_=xs[0][64:128, :]), reads=['xs0'], writes=['lora'])

            XSETS = [(xs, ['xs%d' % i for i in range(6)]), (xsB, ['xb%d' % i for i in range(6)])]

            def emit_proj(g2, XS, XK):
                for kind in range(3):
                    slot = load_unit(b, nu(), w_in_d, kind * 1024 + g2 * 256, 256, 16)
                    pr = next_pair()
                    for f in range(2):
                        mm_fm(slot, 16, f, hT, 'hT', pr[f])
                    for f in range(2):
                        shift(kind * 8 + g2 * 2 + f, pr[f], XS[kind * 2 + f], XK[kind * 2 + f])

            def prep_pair(j, sx, xr_, xk_, xv_, kr, kk_, kv):
                T = tq[sx]
                K = ['tq%d_%d' % (sx, q) for q in range(8)]
                sg, cs, eg, eig, alr, kk2, kkn, b32 = T
                k_sg, k_cs, k_eg, k_eig, k_alr, k_kk2, k_kkn, k_b32 = K
                egm, k_egm = cs, k_cs
                rn, k_rn = kk2, k_kk2
                t1, k_t1 = sg, k_sg
                jc = slice(j * 128, (j + 1) * 128)
                if sx == 0:
                    A1, A2, A3 = A[2][:, 0:TB], A[3][:, 0:TB], A[2][:, 0:TB]
                    ak = 'PB1'
                else:
                    A1, A2, A3 = Mb[:, 0:TB], Mb[:, 256:256 + TB], Mb[:, 0:TB]
                    ak = 'Mb'
                tv = sx * 128
                tk = 256 + sx * 256
                P.op('pe', (lambda e: e.matmul(A1, lhsT=wupb[0:64, jc], rhs=lora[0:64, :], start=True, stop=True)),
                     reads=['wupb', 'lora'], writes=[ak])
                P.op('pe', (lambda e: mmk(e, A2, wupb[64:128, jc], lora[64:128, :], 64)),
                     reads=['wupb', 'lora'], writes=[ak])
                P.op('act', (lambda e: e.activation(out=sg, in_=A1, func=AF.Sigmoid, bias=ct['w0c'][:, j:j + 1])),
                     reads=['w0c'], writes=[k_sg, ak])
                P.op('act', (lambda e: e.activation(out=alr, in_=A2, func=AF.Sigmoid, bias=ct['a0c'][:, j:j + 1])),
                     reads=['a0c'], writes=[k_alr, ak])
                P.op('dve', (lambda e: e.tensor_tensor_scan(out=cs, data0=ct['resetm'][:], data1=sg, initial=0.0, op0=ALU.mult, op1=ALU.add)),
                     reads=[k_sg, 'resetm'], writes=[k_cs])
                P.op('act', (lambda e: e.activation(out=eg, in_=cs, func=AF.Exp, scale=-C0)), reads=[k_cs], writes=[k_eg])
                P.op('act', (lambda e: e.activation(out=eig, in_=cs, func=AF.Exp, scale=C0)), reads=[k_cs], writes=[k_eig])
                P.op('dve', (lambda e: e.tensor_tensor(out=t1, in0=cs, in1=sg, op=ALU.subtract)), reads=[k_cs, k_sg], writes=[k_t1])
                P.op('act', (lambda e: e.activation(out=egm, in_=t1, func=AF.Exp, scale=-C0)), reads=[k_t1], writes=[k_egm])
                P.op('dve', (lambda e: e.tensor_copy(out=gC[:, j, :], in_=chunk3(eg)[:, :, 63])), reads=[k_eg], writes=['gC%d' % j])
                P.op('act', (lambda e: e.activation(out=kk2, in_=xk_, func=AF.Square, scale=ct['kkc'][:, j:j + 1])),
                     reads=[kk_, 'kkc'], writes=[k_kk2])
                P.op('pe', (lambda e: e.matmul(A3, lhsT=ct['bones'][:], rhs=kk2, start=True, stop=True)),
                     reads=['bones', k_kk2], writes=[ak])
                P.op('act', (lambda e: e.activation(out=rn, in_=A3, func=AF.Sqrt)), writes=[k_rn, ak])
                P.op('dve', (lambda e: e.tensor_scalar(out=rn, in0=rn, scalar1=1e-12, scalar2=None, op0=ALU.max)), reads=[k_rn], writes=[k_rn])
                P.op('dve', (lambda e: e.reciprocal(out=rn, in_=rn)), reads=[k_rn], writes=[k_rn])
                P.op('dve', (lambda e: e.scalar_tensor_tensor(out=kkn, in0=xk_, scalar=ct['kkc'][:, j:j + 1], in1=rn, op0=ALU.mult, op1=ALU.mult)),
                     reads=[kk_, 'kkc', k_rn], writes=[k_kkn])
                P.op('dve', (lambda e: e.tensor_scalar(out=t1, in0=alr, scalar1=-1.0, scalar2=ct['kac'][:, j:j + 1], op0=ALU.add, op1=ALU.mult)),
                     reads=[k_alr, 'kac'], writes=[k_t1])
                P.op('dve', (lambda e: e.scalar_tensor_tensor(out=t1, in0=t1, scalar=1.0, in1=xk_, op0=ALU.add, op1=ALU.mult)),
                     reads=[k_t1, kk_], writes=[k_t1])
                P.op('dve', (lambda e: e.tensor_tensor(out=rT[:, j, :], in0=xr_, in1=eg, op=ALU.mult)), reads=[kr, k_eg], writes=['rT'])
                P.op('dve', (lambda e: e.scalar_tensor_tensor(out=aT[:, j, :], in0=kkn, scalar=-1.0, in1=egm, op0=ALU.mult, op1=ALU.mult)),
                     reads=[k_kkn, k_egm], writes=['aT'])
                P.op('dve', (lambda e: e.tensor_tensor(out=b32, in0=kkn, in1=alr, op=ALU.mult)), reads=[k_kkn, k_alr], writes=[k_b32])
                P.op('dve', (lambda e: e.tensor_tensor(out=bT[:, j, :], in0=b32, in1=eig, op=ALU.mult)), reads=[k_b32, k_eig], writes=['bT'])
                P.op('dve', (lambda e: e.tensor_tensor(out=kT[:, j, :], in0=t1, in1=eig, op=ALU.mult)), reads=[k_t1, k_eig], writes=['kT'])
                P.op('dve', (lambda e: e.scalar_tensor_tensor(out=kk2, in0=xr_, scalar=ct['rkc'][:, j:j + 1], in1=t1, op0=ALU.mult, op1=ALU.mult)),
                     reads=[kr, 'rkc', k_t1], writes=[k_kk2])
                vb = b32.bitcast(BF16)[:, 0:TB]
                P.op('act', (lambda e: e.copy(out=vb, in_=xv_)), reads=[kv], writes=[k_b32])

                def fnVT(e):
                    ins = None
                    for ci_ in range(2 * NT):
                        for hp in (slice(0, 64), slice(64, 128)):
                            ins = e.transpose(out=tp[hp, tk + ci_ * 64:tk + ci_ * 64 + 64], in_=vb[hp, ci_ * 64:(ci_ + 1) * 64], identity=identb[hp, hp])
                    return ins
                P.op('pe', fnVT, reads=[k_b32, 'identb'], writes=['tp'])
                P.op('act', (lambda e: e.copy(out=vtk[:, j, :, :], in_=tp[:, tk:tk + 2 * NT * 64].rearrange("p (c v) -> p c v", v=64))),
                     writes=['vtk', 'tp'])
                for ti in range(NT):
                    tcs = slice(ti * 128, (ti + 1) * 128)
                    P.op('pe', (lambda e, ti=ti, tcs=tcs: e.matmul(Eb[:, ti * 16 + j * 2:ti * 16 + j * 2 + 2], lhsT=kk2[:, tcs], rhs=ct['bo2'][:], start=True, stop=True)),
                         reads=[k_kk2, 'bo2'], writes=['Eb'])
                    P.op('pe', (lambda e, tcs=tcs: e.transpose(out=tp[:, tv:tv + 128], in_=vb[:, tcs], identity=identb[:])),
                         reads=[k_b32, 'identb'], writes=['tp'])
                    P.op('act', (lambda e, ti=ti: e.copy(out=vtok[:, ti, jc], in_=tp[:, tv:tv + 128])), writes=['vtok', 'tp'])

            for it in range(5):
                P.begin_streams(3)
                if it < 4:
                    P.set_stream(0)
                    st['pb'] = 0
                    emit_proj(it, *XSETS[it % 2])
                    st['pb'] = None
                if it > 0:
                    XS, XK = XSETS[(it - 1) % 2]
                    for jj in range(2):
                        P.set_stream(1 + jj)
                        prep_pair((it - 1) * 2 + jj, jj, XS[jj], XS[2 + jj], XS[4 + jj], XK[jj], XK[2 + jj], XK[4 + jj])
                P.merge_streams()
            P.op('dve', lambda e: e.tensor_copy(out=bon[:].rearrange("p t h -> p (t h)"), in_=Eb[:, 0:NT * 16]), writes=['bon', 'Eb'])

            def aux_ga(i):
                slot = load_unit(b, nu(), w_in_d, 3200 + i * 256, 256, 16)
                pr = next_pair()
                for ti in range(NT):
                    mm_tm(slot, 16, ti, hT, 'hT', pr[ti])
                for ti in range(NT):
                    ai = pr[ti]
                    P.op('act', (lambda e, ai=ai, ti=ti, i=i: e.activation(out=sa[:, ti, i * 256:(i + 1) * 256], in_=A[ai][:, 0:256], func=AF.Silu)),
                         writes=['sa', akey(ai)])

            def aux_q(i):
                slot = load_unit(b, nu(), w_in_d, 4224 + i * 256, 256, 16)
                pr = next_pair()
                for f in range(2):
                    mm_fm(slot, 16, f, hT, 'hT', pr[f])
                for f in range(2):
                    ai = pr[f]
                    P.op('act', (lambda e, ai=ai, i=i, f=f: e.activation(out=qT[:, i * 2 + f, :], in_=A[ai][:, 0:TB], func=AF.Copy, scale=0.125)),
                         writes=['qT', akey(ai)])

            def aux_kd(i):
                slot = load_unit(b, nu(), w_in_d, 5248 + i * 128, 128, 16, dup=True)
                pr = next_pair()
                for f in range(2):
                    mm_fm(slot, 16, f, hT, 'hT', pr[f])
                for f in range(2):
                    ai = pr[f]
                    P.op('dve', (lambda e, ai=ai, i=i, f=f: e.tensor_copy(out=kTd[:, i * 2 + f, 128:128 + TB], in_=A[ai][:, 0:TB])),
                         writes=['kTd', akey(ai)])

            def aux_kv(i):
                slot = load_unit(b, nu(), w_in_d, 5248 + i * 256, 256, 16)
                pr = next_pair()
                for ti in range(NT):
                    mm_tm(slot, 16, ti, hT, 'hT', pr[ti])
                for ti in range(NT):
                    ai = pr[ti]
                    P.op('act', (lambda e, ai=ai, ti=ti, i=i: e.copy(out=kvo[:, ti, i * 256:(i + 1) * 256], in_=A[ai][:, 0:256])),
                         writes=['kvo', akey(ai)])
                    if i == 1:
                        P.op('act', (lambda e, ai=ai, ti=ti: e.copy(out=vat[:, 1 + ti, :], in_=A[ai][:, 0:256])),
                             writes=['vat', akey(ai)])

            def aux_gb(i):
                slot = load_unit(b, nu(), w_in_d, 5760 + i * 256, 256, 16)
                pr = next_pair()
                for ti in range(NT):
                    mm_tm(slot, 16, ti, hT, 'hT', pr[ti])
                for ti in range(NT):
                    ai = pr[ti]
                    P.op('act', (lambda e, ai=ai, ti=ti, i=i: e.activation(out=sbg[:, ti, i * 256:(i + 1) * 256], in_=A[ai][:, 0:256], func=AF.Silu)),
                         writes=['sbg', akey(ai)])

            def aux_carry():
                if 0 < b and not sample:
                    P.op('pool', lambda e: e.tensor_copy(out=kTd[:, :, 0:128], in_=kTd[:, :, TB:TB + 128]), reads=['kTd'], writes=['kTd'])
                    P.op('pool', lambda e: e.tensor_copy(out=vat[:, 0, :], in_=vat[:, NT, :]), reads=['vat'], writes=['vat'])

            AUX = [
                [lambda: aux_ga(0), lambda: aux_ga(1), lambda: aux_ga(2), lambda: aux_ga(3)],
                [lambda: aux_q(0), lambda: aux_q(1), lambda: aux_q(2), lambda: aux_q(3)],
                [aux_carry, lambda: aux_kd(0), lambda: aux_kd(1), lambda: aux_kv(0), lambda: aux_kv(1)],
                [lambda: aux_gb(0), lambda: aux_gb(1), lambda: aux_gb(2), lambda: aux_gb(3)],
            ]

            if DBG['stage'] <= 3:
                continue
            H2 = (slice(0, 64), slice(64, 128))
            PS_PENDING = []
            PS1_PENDING = []
            for ti in range(NT):
                gt = b * NT + ti
                tcs = slice(ti * 128, (ti + 1) * 128)
                for c in range(2):
                    cp = slice(c * 64, c * 64 + 64)
                    cc = slice(ti * 128 + c * 64, ti * 128 + c * 64 + 64)
                    ci = ti * 2 + c
                    P.begin_streams(3)
                    if ti == 1 and c == 0 and PS_PENDING:
                        P.add_stream(PS_PENDING.pop())
                    P.set_stream(2)
                    st['pb0'] = True
                    for task in AUX[ci]:
                        task()
                    st['pb0'] = False
                    for j in range(8):
                        sx = (j % 2) if DBG.get('ss', 1) else 0
                        P.set_stream(sx)
                        kbtok, Mmx, LNS, XUb = kbtokS[sx], MmS[sx], LNSS[sx], XUbS[sx]
                        MC = Mb if sx == 0 else Cb
                        MCk = 'Mb' if sx == 0 else 'Cb'
                        DD = Db[0][:, 0:192] if sx == 0 else Eb[:, 192:384]
                        DDk = 'D0' if sx == 0 else 'Eb'
                        kX, kM, kL = 'X%d' % sx, 'Mm%d' % sx, 'LNS%d_' % sx
                        if sample:
                            seq = ti * 2 + c
                            P.op('sp', (lambda e, seq=seq, j=j: e.dma_start(out=Pf[:, j, :], in_=swkv_d[seq, :, j, :])),
                                 writes=['Pf%d' % j], chan='pst%d' % j)
                            P.op('dve', (lambda e, j=j: e.tensor_copy(out=Pb[:, j, :], in_=Pf[:, j, :])), reads=['Pf%d' % j], writes=['Pb%d' % j])
                        tpo = sx * 128

                        def fnT(e, j=j, cc=cc, tpo=tpo):
                            ins = None
                            for hp in H2:
                                e.transpose(out=tp[hp, tpo:tpo + 64], in_=kT[hp, j, cc], identity=identb[hp, hp])
                                ins = e.transpose(out=tp[hp, tpo + 64:tpo + 128], in_=bT[hp, j, cc], identity=identb[hp, hp])
                            return ins
                        P.op('pe', fnT, reads=['kT', 'bT', 'identb'], writes=['tp'])
                        P.op('act', (lambda e, kbtok=kbtok, tpo=tpo: e.copy(out=kbtok[:, 0:128], in_=tp[:, tpo:tpo + 128])), writes=['kbtok%d' % sx, 'tp'])

                        def fnM(e, j=j, cc=cc, MC=MC):
                            ins = None
                            for hp in H2:
                                e.matmul(MC[hp, 0:64], lhsT=bT[hp, j, cc], rhs=aT[hp, j, cc], start=True, stop=True)
                                e.matmul(MC[hp, 64:128], lhsT=kT[hp, j, cc], rhs=aT[hp, j, cc], start=True, stop=True)
                                e.matmul(MC[hp, 128:192], lhsT=bT[hp, j, cc], rhs=rT[hp, j, cc], start=True, stop=True)
                                e.matmul(MC[hp, 192:256], lhsT=kT[hp, j, cc], rhs=rT[hp, j, cc], start=True, stop=True)
                                ins = e.matmul(MC[hp, 256:320], lhsT=aT[hp, j, cc], rhs=bT[hp, j, cc], start=True, stop=True)
                            return ins
                        P.op('pe', fnM, reads=['aT', 'bT', 'kT', 'rT'], writes=[MCk])
                        P.op('dve', (lambda e, Mmx=Mmx, MC=MC: e.tensor_tensor(out=Mmx[:, 0:320], in0=MC[:, 0:320], in1=ct['MU5'][:], op=ALU.mult)),
                             reads=['MU5'], writes=[kM, MCk])
                        P.op('pool', (lambda e, LNS=LNS, Mmx=Mmx: e.tensor_tensor(out=LNS[0][:, 128:192], in0=Mmx[:, 0:64], in1=I2b[:], op=ALU.add)),
                             reads=[kM, 'I2b'], writes=[kL + '0'])
                        for lvl in range(1, 7):
                            cur, nxt = (lvl - 1) % 2, lvl % 2
                            if lvl == 1:
                                Lc, Nc = Mmx[:, 256:320], Mmx[:, 0:64]
                                rk = [kM, kL + '0']
                            else:
                                Lc, Nc = LNS[cur][:, 0:64], LNS[cur][:, 64:128]
                                rk = [kL + str(cur)]
                            Sc = LNS[cur][:, 128:192]

                            def fnD(e, Lc=Lc, Nc=Nc, Sc=Sc, lvl=lvl, DD=DD):
                                ins = None
                                for hp in H2:
                                    if lvl < 6:
                                        e.matmul(DD[hp, 0:64], lhsT=Nc[hp], rhs=Lc[hp], start=True, stop=True)
                                    if lvl < 5:
                                        e.matmul(DD[hp, 64:128], lhsT=Lc[hp], rhs=Nc[hp], start=True, stop=True)
                                    if lvl == 1:
                                        ins = e.matmul(DD[hp, 128:192], lhsT=I2b[hp], rhs=Sc[hp], start=True, stop=True)
                                    else:
                                        e.matmul(DD[hp, 128:192], lhsT=I2b[hp], rhs=Sc[hp], start=True, stop=False)
                                        ins = e.matmul(DD[hp, 128:192], lhsT=Lc[hp], rhs=Sc[hp], start=False, stop=True)
                                return ins
                            P.op('pe', fnD, reads=rk + ['I2b'], writes=[DDk])
                            lo = 0 if lvl < 6 else 128
                            if (lvl + sx) % 2 == 1:
                                P.op('dve', (lambda e, LNS=LNS, nxt=nxt, lo=lo, DD=DD: e.tensor_copy(out=LNS[nxt][:, lo:192], in_=DD[:, lo:192])),
                                     writes=[kL + str(nxt), DDk])
                            else:
                                P.op('act', (lambda e, LNS=LNS, nxt=nxt, lo=lo, DD=DD: e.copy(out=LNS[nxt][:, lo:192], in_=DD[:, lo:192])),
                                     writes=[kL + str(nxt), DDk])

                        def fnX(e, j=j, cc=cc, ci=ci, MC=MC, Mmx=Mmx):
                            ins = None
                            for hp in H2:
                                e.matmul(MC[hp, 320:384], lhsT=aT[hp, j, cc], rhs=Pb[hp, j, :], start=True, stop=False)
                                ins = e.matmul(MC[hp, 320:384], lhsT=Mmx[hp, 64:128], rhs=vtk[hp, j, ci, :], start=False, stop=True)
                            return ins
                        P.op('pe', fnX, reads=['aT', 'Pb%d' % j, kM, 'vtk'], writes=[MCk])
                        P.op('dve', (lambda e, XUb=XUb, MC=MC: e.tensor_copy(out=XUb[:, 0:64], in_=MC[:, 320:384])), writes=[kX + 'x', MCk])

                        def fnU(e, MC=MC, LNS=LNS, XUb=XUb):
                            ins = None
                            for hp in H2:
                                ins = e.matmul(MC[hp, 384:448], lhsT=LNS[0][hp, 128:192], rhs=XUb[hp, 0:64], start=True, stop=True)
                            return ins
                        P.op('pe', fnU, reads=[kX + 'x', kL + '0'], writes=[MCk])
                        P.op('act', (lambda e, XUb=XUb, MC=MC: e.copy(out=XUb[:, 64:128], in_=MC[:, 384:448])), writes=[kX + 'u', MCk])

                        def fnO(e, j=j, cp=cp, cc=cc, ci=ci, Mmx=Mmx, XUb=XUb):
                            ins = None
                            for hh, hp in enumerate(H2):
                                ob_ = Db[1][cp, j * 64:j * 64 + 64] if hh == 0 else Aacc[1][cp, j * 64:j * 64 + 64]
                                e.matmul(ob_, lhsT=rT[hp, j, cc], rhs=Pb[hp, j, :], start=True, stop=False)
                                e.matmul(ob_, lhsT=Mmx[hp, 128:192], rhs=XUb[hp, 64:128], start=False, stop=False)
                                ins = e.matmul(ob_, lhsT=Mmx[hp, 192:256], rhs=vtk[hp, j, ci, :], start=False, stop=True)
                            return ins
                        P.op('pe', fnO, reads=['rT', 'Pb%d' % j, kM, kX + 'u', 'vtk'], writes=['D1', 'PB1'])

                        def fnP(e, j=j, ci=ci, MC=MC, kbtok=kbtok, XUb=XUb):
                            ins = None
                            for hp in H2:
                                e.matmul(MC[hp, 448:512], lhsT=identf[hp, hp], rhs=Pf[hp, j, :], start=True, stop=False)
                                e.matmul(MC[hp, 448:512], lhsT=kbtok[hp, 64:128], rhs=XUb[hp, 64:128], start=False, stop=False)
                                ins = e.matmul(MC[hp, 448:512], lhsT=kbtok[hp, 0:64], rhs=vtk[hp, j, ci, :], start=False, stop=True)
                            return ins
                        P.op('pe', fnP, reads=['identf', 'Pf%d' % j, 'kbtok%d' % sx, kX + 'u', 'vtk'], writes=[MCk])
                        gcol = gC[:, j, ci:ci + 1]
                        P.op('act', (lambda e, j=j, gcol=gcol, MC=MC: e.activation(out=Pf[:, j, :], in_=MC[:, 448:512], func=AF.Copy, scale=gcol)),
                             reads=['gC%d' % j], writes=['Pf%d' % j, MCk])
                        P.op('dve', (lambda e, j=j, gcol=gcol, MC=MC: e.tensor_scalar(out=Pb[:, j, :], in0=MC[:, 448:512], scalar1=gcol, scalar2=None, op0=ALU.mult)),
                             reads=['gC%d' % j], writes=['Pb%d' % j, MCk])
                        if sample:
                            seq = ti * 2 + c
                            P.op('pool', (lambda e, seq=seq, j=j: e.dma_start(out=wkvs_o[seq, :, j, :], in_=Pf[:, j, :])),
                                 reads=['Pf%d' % j], chan='o_pf%d' % j, cb=out_idx)
                    P.merge_streams()
                y4 = ysb[:].rearrange("p (j h c) -> p j h c", h=2, c=64)
                if DBG.get('oe', 0) == 0:
                    P.op('dve', lambda e: e.tensor_copy(out=y4[:, :, 0, :], in_=Db[1][:, :].rearrange("p (j c) -> p j c", c=64)), writes=['ysb', 'D1'])
                    P.op('act', lambda e: e.copy(out=y4[:, :, 1, :], in_=Aacc[1][:, :].rearrange("p (j c) -> p j c", c=64)), writes=['ysb', 'PB1'])
                else:
                    for j in range(8):
                        P.op('dve', (lambda e, j=j: e.tensor_copy(out=ysb[:, j * 128:j * 128 + 64], in_=Db[1][:, j * 64:j * 64 + 64])), writes=['ysb', 'D1'])
                        P.op('act', (lambda e, j=j: e.copy(out=ysb[:, j * 128 + 64:j * 128 + 128], in_=Aacc[1][:, j * 64:j * 64 + 64])), writes=['ysb', 'PB1'])
                if gt == 15:
                    out_idx.append(P.op('pool', lambda e: e.dma_start(out=wkvp_o, in_=Pf[:]), reads=['Pf%d' % j for j in range(8)], chan='o_pfp'))

                if DBG.get('pso', 1):
                    P.capture_start()
                if DBG.get('dump', 0):
                    out_idx.append(P.op('pool', (lambda e, gt=gt: e.dma_start(out=y_o[(gt + 4) * 128:(gt + 5) * 128, 0:1024], in_=ysb[:])), reads=['ysb'], chan='o_dbg'))
                y3 = ysb[:].rearrange("p (h c) -> p h c", c=64)
                q3 = ysq[:].rearrange("p (h c) -> p h c", c=64)
                P.op('dve', lambda e: e.tensor_reduce(out=small[:, 8:24], in_=y3, axis=AX.X, op=ALU.add), reads=['ysb'], writes=['gn_s1'])
                P.op('act', lambda e: e.activation(out=ysq[:], in_=ysb[:], func=AF.Square), reads=['ysb'], writes=['ysq'])
                P.op('dve', lambda e: e.tensor_reduce(out=small[:, 24:40], in_=q3, axis=AX.X, op=ALU.add), reads=['ysq'], writes=['gn_s2'])
                P.op('dve', lambda e: e.tensor_scalar(out=small[:, 40:56], in0=small[:, 8:24], scalar1=1.0 / 64, scalar2=None, op0=ALU.mult),
                     reads=['gn_s1'], writes=['gn_mean'])
                P.op('dve', lambda e: e.tensor_tensor(out=small[:, 8:24], in0=small[:, 40:56], in1=small[:, 40:56], op=ALU.mult),
                     reads=['gn_mean', 'gn_s1'], writes=['gn_s1'])
                P.op('dve', lambda e: e.scalar_tensor_tensor(out=small[:, 24:40], in0=small[:, 24:40], scalar=1.0 / 64, in1=small[:, 8:24], op0=ALU.mult, op1=ALU.subtract),
                     reads=['gn_s2', 'gn_s1'], writes=['gn_s2'])
                P.op('dve', lambda e: e.tensor_scalar(out=small[:, 24:40], in0=small[:, 24:40], scalar1=LNX_EPS, scalar2=None, op0=ALU.add),
                     reads=['gn_s2'], writes=['gn_s2'])
                P.op('act', lambda e: e.activation(out=small[:, 24:40], in_=small[:, 24:40], func=AF.Sqrt), reads=['gn_s2'], writes=['gn_s2'])
                P.op('dve', lambda e: e.reciprocal(out=small[:, 24:40], in_=small[:, 24:40]), reads=['gn_s2'], writes=['gn_s2'])
                P.op('dve', lambda e: e.tensor_tensor(out=y3, in0=y3, in1=small[:, 40:56].unsqueeze(2).to_broadcast([128, 16, 64]), op=ALU.subtract),
                     reads=['ysb', 'gn_mean'], writes=['ysb'])
                P.op('dve', lambda e: e.tensor_tensor(out=y3, in0=y3, in1=small[:, 24:40].unsqueeze(2).to_broadcast([128, 16, 64]), op=ALU.mult),
                     reads=['ysb', 'gn_s2'], writes=['ysb'])
                P.op('dve', lambda e: e.tensor_tensor(out=ysb[:], in0=ysb[:], in1=ct['lnxw'][:], op=ALU.mult), reads=['ysb', 'lnxw'], writes=['ysb'])
                P.op('dve', lambda e: e.tensor_tensor(out=ysb[:], in0=ysb[:], in1=ct['lnxb'][:], op=ALU.add), reads=['ysb', 'lnxb'], writes=['ysb'])
                P.op('dve', (lambda e, ti=ti: e.tensor_tensor(out=q3, in0=vtok[:, ti, :].rearrange("p (h c) -> p h c", c=64),
                                                              in1=bon[:, ti, :].unsqueeze(2).to_broadcast([128, 16, 64]), op=ALU.mult)),
                     reads=['vtok', 'bon', 'ysq'], writes=['ysq'])
                P.op('dve', lambda e: e.tensor_tensor(out=ysb[:], in0=ysb[:], in1=ysq[:], op=ALU.add), reads=['ysb', 'ysq'], writes=['ysb'])
                P.op('dve', (lambda e, ti=ti: e.tensor_tensor(out=yab[:], in0=ysb[:], in1=sa[:, ti, :], op=ALU.mult)), reads=['ysb', 'sa'], writes=['yab'])

                if DBG.get('dump', 0):
                    out_idx.append(P.op('pool', (lambda e, gt=gt: e.dma_start(out=y_o[(gt + 8) * 128:(gt + 9) * 128, 0:1024], in_=yab[:])), reads=['yab'], chan='o_dbg'))
                tr_offs = [512, 640, 768, 896] if ti == 0 else [384, 896]
                nr = len(tr_offs)
                for r0 in range(0, 8, nr):
                    def fn(e, r0=r0, tr_offs=tr_offs, nr=nr):
                        ins = None
                        for k in range(nr):
                            ins = e.transpose(out=tp[:, tr_offs[k]:tr_offs[k] + 128], in_=yab[:, (r0 + k) * 128:(r0 + k + 1) * 128], identity=identb[:])
                        return ins
                    P.op('pe', fn, reads=['yab', 'identb'], writes=['tp'])
                    for k in range(nr):
                        P.op('act', (lambda e, tcs=tcs, r0=r0, k=k, off=tr_offs[k]: e.copy(out=yaT[:, r0 + k, tcs], in_=tp[:, off:off + 128])), writes=['yaT', 'tp'])
                if DBG.get('pso', 1):
                    (PS_PENDING if ti == 0 else PS1_PENDING).append(P.capture_end())

            if DBG['stage'] <= 4:
                continue
            for ti in range(NT):
                gt = b * NT + ti
                if gt == 15:
                    out_idx.append(P.op('pool', (lambda e, ti=ti: e.dma_start(out=kp_o, in_=kvo[:, ti, 0:256])), reads=['kvo'], chan='o_kv'))
                    out_idx.append(P.op('pool', (lambda e, ti=ti: e.dma_start(out=vp_o, in_=kvo[:, ti, 256:512])), reads=['kvo'], chan='o_kv'))
                if sample:
                    for c in range(2):
                        seq = ti * 2 + c
                        cp = slice(c * 64, c * 64 + 64)
                        out_idx.append(P.op('pool', (lambda e, ti=ti, seq=seq, cp=cp: e.dma_start(out=ks_o[seq, 64:128, :], in_=kvo[cp, ti, 0:256])), reads=['kvo'], chan='o_kv'))
                        out_idx.append(P.op('pool', (lambda e, ti=ti, seq=seq, cp=cp: e.dma_start(out=vs_o[seq, 64:128, :], in_=kvo[cp, ti, 256:512])), reads=['kvo'], chan='o_kv'))
                        out_idx.append(P.op('pool', (lambda e, seq=seq: e.dma_start(out=ks_o[seq, 0:64, :], in_=ck_raw[seq, 64:128, :])), chan='o_kv'))
                        out_idx.append(P.op('pool', (lambda e, seq=seq: e.dma_start(out=vs_o[seq, 0:64, :], in_=cv_raw[seq, 64:128, :])), chan='o_kv'))

            if DBG['stage'] <= 5:
                continue
            fence(ARENA_PREP, ARENA_FIN)

            def aux_m(i):
                slot = load_unit(b, nu(), w_in_d, 6784 + i * 256, 256, 16)
                pr = next_pair()
                for f in range(2):
                    mm_fm(slot, 16, f, hT, 'hT', pr[f])
                for f in range(2):
                    ai = pr[f]
                    P.op('act', (lambda e, ai=ai, f=f, i=i: e.activation(out=ta_all[:, i * 2 + f, :], in_=A[ai][:, 0:TB], func=AF.Sigmoid)), writes=['taall', akey(ai)])

            def aux_p(i):
                slot = load_unit(b, nu(), p_a_d, i * 256, 256, 8)
                pr = next_pair()
                for f in range(2):
                    mm_fm(slot, 8, f, yaT, 'yaT', pr[f])
                for f in range(2):
                    ai = pr[f]
                    P.op('dve', (lambda e, ai=ai, f=f, i=i: e.tensor_tensor(out=ta_all[:, i * 2 + f, :], in0=ta_all[:, i * 2 + f, :], in1=A[ai][:, 0:TB], op=ALU.mult)),
                         writes=['taall', akey(ai)])

            for ti in range(NT):
                gt = b * NT + ti
                tcs = slice(ti * 128, (ti + 1) * 128)
                nkb = 3 if sample else 2
                nk = nkb * 128
                Dm = ct['DmS'] if sample else (ct['DmP0'] if gt == 0 else ct['DmP'])
                Dk = 'DmS' if sample else ('DmP0' if gt == 0 else 'DmP')
                P.begin_streams(3)
                if ti == 0 and PS1_PENDING:
                    P.add_stream(PS1_PENDING.pop())
                P.set_stream(2)
                for i_ in range(8):
                    (aux_m if ti == 0 else aux_p)(i_)
                for h in range(16):
                    sx = (h % 2) if DBG.get('as', 1) else 0
                    P.set_stream(sx)
                    g = h // 4
                    f = h // 2
                    hp = slice((h % 2) * 64, (h % 2) * 64 + 64)
                    SB = Mb if sx == 0 else Cb
                    SBk = 'Mb' if sx == 0 else 'Cb'
                    s_x, e_x, eT_x, sm = s_sbS[sx], e_sbS[sx], eTS[sx], smallS[sx]
                    ks = 'at%d_' % sx
                    tpo = sx * 512

                    def fnS(e, g=g, f=f, hp=hp, ti=ti, tcs=tcs, sample=sample, SB=SB):
                        kb_ = hp.start
                        if not sample:
                            return mmk(e, SB[:, 0:256], qT[hp, f, tcs], kTd[hp, g, ti * 128:ti * 128 + 256], kb_)
                        mmk(e, SB[:, 0:128], qT[hp, f, tcs], kc[hp, ti * 2, g, :], kb_)
                        mmk(e, SB[:, 128:256], qT[hp, f, tcs], kc[hp, ti * 2 + 1, g, :], kb_)
                        return mmk(e, SB[:, 256:384], qT[hp, f, tcs], kTd[hp, g, 128 + ti * 128:256 + ti * 128], kb_)
                    P.op('pe', fnS, reads=['qT', 'kTd', 'kc'], writes=[SBk])
                    P.op('dve', (lambda e, h=h, nk=nk, Dm=Dm, s_x=s_x, SB=SB: e.scalar_tensor_tensor(out=s_x[:, 0:nk], in0=Dm[:, 0:nk], scalar=SLOPES[h], in1=SB[:, 0:nk], op0=ALU.mult, op1=ALU.add)),
                         reads=[Dk], writes=[ks + 's', SBk])
                    P.op('dve', (lambda e, nk=nk, s_x=s_x, sm=sm: e.tensor_reduce(out=sm[:, 0:1], in_=s_x[:, 0:nk], axis=AX.X, op=ALU.max)), reads=[ks + 's'], writes=[ks + 'mx'])
                    P.op('dve', (lambda e, h=h, sm=sm: e.tensor_scalar(out=sm[:, 1:2], in0=sm[:, 0:1], scalar1=ct['sinks'][:, h:h + 1], scalar2=-1.0, op0=ALU.max, op1=ALU.mult)),
                         reads=[ks + 'mx', 'sinks'], writes=[ks + 'negm'])
                    P.op('dve', (lambda e, sm=sm: e.memset(sm[:, 2:3], 0.0)), writes=[ks + 'rs'])
                    P.op('act', (lambda e, nk=nk, s_x=s_x, e_x=e_x, sm=sm: e.activation(out=e_x[:, 0:nk], in_=s_x[:, 0:nk], func=AF.Exp, bias=sm[:, 1:2], accum_out=sm[:, 2:3])),
                         reads=[ks + 's', ks + 'negm', ks + 'rs'], writes=[ks + 'e', ks + 'rs'])
                    P.op('act', (lambda e, h=h, sm=sm: e.activation(out=sm[:, 3:4], in_=sm[:, 1:2], func=AF.Exp, bias=ct['sinks'][:, h:h + 1])),
                         reads=[ks + 'negm', 'sinks'], writes=[ks + 'es'])
                    P.op('dve', (lambda e, sm=sm: e.tensor_tensor(out=sm[:, 3:4], in0=sm[:, 3:4], in1=sm[:, 2:3], op=ALU.add)), reads=[ks + 'es', ks + 'rs'], writes=[ks + 'es'])
                    P.op('dve', (lambda e, h=h, sm=sm: e.reciprocal(out=rden[:, h:h + 1], in_=sm[:, 3:4])), reads=[ks + 'es'], writes=['rden%d' % h])

                    def fnT(e, nkb=nkb, e_x=e_x, tpo=tpo):
                        ins = None
                        for kb in range(nkb):
                            ins = e.transpose(out=tp[:, tpo + kb * 128:tpo + (kb + 1) * 128], in_=e_x[:, kb * 128:(kb + 1) * 128], identity=identb[:])
                        return ins
                    P.op('pe', fnT, reads=[ks + 'e', 'identb'], writes=['tp'])
                    P.op('act', (lambda e, nk=nk, eT_x=eT_x, tpo=tpo: e.copy(out=eT_x[:, 0:nk], in_=tp[:, tpo:tpo + nk])), writes=[ks + 'eT', 'tp'])

                    def fnV(e, g=g, ti=ti, nkb=nkb, sample=sample, SB=SB, eT_x=eT_x):
                        gs = slice(g * 64, g * 64 + 64)
                        po = SB[:, 448:512]
                        if not sample:
                            e.matmul(po, lhsT=eT_x[:, 0:128], rhs=vat[:, ti, gs], start=True, stop=False)
                            return e.matmul(po, lhsT=eT_x[:, 128:256], rhs=vat[:, ti + 1, gs], start=False, stop=True)
                        e.matmul(po, lhsT=eT_x[:, 0:128], rhs=vc[:, ti * 2, gs], start=True, stop=False)
                        e.matmul(po, lhsT=eT_x[:, 128:256], rhs=vc[:, ti * 2 + 1, gs], start=False, stop=False)
                        return e.matmul(po, lhsT=eT_x[:, 256:384], rhs=vat[:, ti + 1, gs], start=False, stop=True)
                    P.op('pe', fnV, reads=[ks + 'eT', 'vat', 'vc'], writes=[SBk])
                    P.op('dve', (lambda e, h=h, SB=SB: e.tensor_scalar(out=ob[:, h * 64:(h + 1) * 64], in0=SB[:, 448:512], scalar1=rden[:, h:h + 1], scalar2=None, op0=ALU.mult)),
                         reads=['rden%d' % h], writes=['ob%d' % h, SBk])
                P.merge_streams()
                if DBG.get('dump', 0):
                    out_idx.append(P.op('pool', (lambda e, gt=gt: e.dma_start(out=y_o[(gt + 4) * 128:(gt + 5) * 128, 1024:2048], in_=ob[:])), reads=['ob'] + ['ob%d' % h for h in range(16)], chan='o_dbg'))
                P.op('dve', (lambda e, ti=ti: e.tensor_tensor(out=yab[:], in0=ob[:], in1=sbg[:, ti, :], op=ALU.mult)), reads=['ob%d' % h for h in range(16)] + ['sbg'], writes=['yab'])

                if DBG.get('dump', 0):
                    out_idx.append(P.op('pool', (lambda e, gt=gt: e.dma_start(out=y_o[(gt + 8) * 128:(gt + 9) * 128, 1024:2048], in_=yab[:])), reads=['yab'], chan='o_dbg'))
                def fn(e):
                    ins = None
                    for k in range(8):
                        ins = e.transpose(out=tp[:, k * 128:(k + 1) * 128], in_=yab[:, k * 128:(k + 1) * 128], identity=identb[:])
                    return ins
                P.op('pe', fn, reads=['yab', 'identb'], writes=['tp'])
                P.op('act', (lambda e, tcs=tcs: e.copy(out=ybT[:, :, tcs], in_=tp[:, :].rearrange("p (k t) -> p k t", t=128))), writes=['ybT', 'tp'])


            if DBG.get('dump', 0):
                out_idx.append(P.op('pool', (lambda e: e.dma_start(out=y_o[12 * 128:13 * 128, :].rearrange('p (k t) -> p k t', t=TB), in_=yaT[:])), reads=['yaT'], chan='o_dbg'))
                out_idx.append(P.op('pool', (lambda e: e.dma_start(out=y_o[13 * 128:14 * 128, :].rearrange('p (k t) -> p k t', t=TB), in_=ybT[:])), reads=['ybT'], chan='o_dbg'))
                out_idx.append(P.op('pool', (lambda e: e.dma_start(out=y_o[14 * 128:15 * 128, :].rearrange('p (k t) -> p k t', t=TB), in_=hT[:, 0:8, :])), reads=['hT'], chan='o_dbg'))
            if DBG['stage'] <= 6:
                continue
            for i in range(8):
                slot = load_unit(b, nu(), w_in_d, 8832 + i * 256, 256, 16)
                pr = next_pair()
                for f in range(2):
                    mm_fm(slot, 16, f, hT, 'hT', pr[f])
                for f in range(2):
                    ai = pr[f]
                    P.op('act', (lambda e, ai=ai, f=f: e.activation(out=sgb[:, f, :], in_=A[ai][:, 0:TB], func=AF.Sigmoid)), writes=['sgb', akey(ai)])
                slot = load_unit(b, nu(), p_b_d, i * 256, 256, 8)
                pr = next_pair()
                for f in range(2):
                    mm_fm(slot, 8, f, ybT, 'ybT', pr[f])
                for f in range(2):
                    ai = pr[f]
                    P.op('dve', (lambda e, ai=ai, f=f: e.tensor_tensor(out=sgb[:, f, :], in0=sgb[:, f, :], in1=A[ai][:, 0:TB], op=ALU.mult)),
                         reads=['sgb'], writes=['sgb', akey(ai)])
                    P.op('pool', (lambda e, i=i, f=f: e.tensor_tensor(out=mergedT[:, i * 2 + f, :], in0=ta_all[:, i * 2 + f, :], in1=sgb[:, f, :], op=ALU.add)),
                         reads=['taall', 'sgb'], writes=['mergedT'])
            fence(['taall'], ['xr'])
            for ti in range(NT):
                gt = b * NT + ti
                P.op('sp', (lambda e, gt=gt, ti=ti: e.dma_start(out=xr[:, ti, :], in_=x_d[gt * 128:(gt + 1) * 128, :])),
                     writes=['xr'], chan='xr')
            P.begin_streams(2)
            if b + 1 < DBG['nblk']:
                P.set_stream(1)
                emit_S1(b + 1)
            P.set_stream(0)
            for i in range(8):
                slot = load_unit(b, nu(), w_o_d, i * 256, 256, 16)
                pr = next_pair()
                for ti in range(NT):
                    mm_tm(slot, 16, ti, mergedT, 'mergedT', pr[ti])
                for ti in range(NT):
                    ai = pr[ti]
                    P.op('dve', (lambda e, ai=ai, ti=ti, i=i: e.tensor_tensor(out=xr[:, ti, i * 256:(i + 1) * 256], in0=xr[:, ti, i * 256:(i + 1) * 256], in1=A[ai][:, 0:256], op=ALU.add)),
                         reads=['xr'], writes=['xr', akey(ai)])
            for ti in range(NT):
                gt = b * NT + ti
                P.op('dve', lambda e: e.memset(small[:, 0:1], 0.0), writes=['ssq'])
                P.op('act', (lambda e, ti=ti: e.activation(out=sa[:].rearrange("p t c -> p (t c)"), in_=xr[:, ti, :], func=AF.Square, accum_out=small[:, 0:1])),
                     reads=['xr', 'ssq'], writes=['sa', 'ssq'])
                P.op('dve', lambda e: e.tensor_scalar(out=small[:, 1:2], in0=small[:, 0:1], scalar1=1.0 / D, scalar2=RMS_EPS, op0=ALU.mult, op1=ALU.add),
                     reads=['ssq'], writes=['ms'])
                P.op('act', lambda e: e.activation(out=small[:, 2:3], in_=small[:, 1:2], func=AF.Sqrt), reads=['ms'], writes=['sq'])
                P.op('dve', lambda e: e.reciprocal(out=small[:, 3:4], in_=small[:, 2:3]), reads=['sq'], writes=['rstd'])
                P.op('dve', (lambda e, ti=ti: e.scalar_tensor_tensor(out=xr[:, ti, :], in0=xr[:, ti, :], scalar=small[:, 3:4], in1=ct['gfbc'][:], op0=ALU.mult, op1=ALU.mult)),
                     reads=['xr', 'rstd', 'gfbc'], writes=['xr'])
                P.op('pool', (lambda e, gt=gt, ti=ti: e.dma_start(out=y_o[gt * 128:(gt + 1) * 128, :], in_=xr[:, ti, :])),
                     reads=['xr'], writes=['xr_st'], chan='o_y', cb=out_idx)
            P.merge_streams()
            if b == NBLK - 2:
                out_idx.append(P.op('pool', lambda e: e.dma_start(out=shp_o, in_=plast[:]), reads=['plast'], chan='o_sh'))
            if sample:
                out_idx.append(P.op('pool', lambda e: e.dma_start(out=shs_o, in_=shs[:]), reads=['shs'], chan='o_sh'))
            assert st['u'] <= 80, st['u']

        P.wait_all('pool', out_idx)
        P.emit()
        build.stats = P.stats
    return nc


_CACHE = {}


def _consts():
    c = {}
    c['identf'] = np.eye(128, dtype=np.float32)
    s = np.arange(128)[:, None]
    t = np.arange(128)[None, :]
    same = (s // 64) == (t // 64)
    MUs = (same & (s < t)).astype(np.float32)
    MUi = (same & (s <= t)).astype(np.float32)
    c['MU4'] = np.concatenate([MUs, MUs, MUi, MUi], axis=1)
    c['MLs'] = (same & (t < s)).astype(np.float32)
    c['bones'] = same.astype(np.float32)
    s6 = np.arange(64)[:, None]
    t6 = np.arange(64)[None, :]
    mus = (s6 < t6).astype(np.float32)
    mui = (s6 <= t6).astype(np.float32)
    mls = (t6 < s6).astype(np.float32)
    m5 = np.concatenate([mus, mus, mui, mui, mls], axis=1)
    c['MU5'] = np.concatenate([m5, m5], axis=0)
    c['I2'] = np.concatenate([np.eye(64, dtype=np.float32)] * 2, axis=0)
    bo2 = np.zeros((128, 2), np.float32)
    bo2[:64, 0] = 1
    bo2[64:, 1] = 1
    c['bo2'] = bo2
    rm = np.ones((128, TB), np.float32)
    rm[:, ::64] = 0
    c['resetm'] = rm
    NEG = -1e30
    i = np.arange(128)[:, None]
    k = np.arange(256)[None, :]
    dch = (2 + i // 64) - (k // 64)
    vis = (dch >= 0) & (dch <= 2)
    DmP = np.where(vis, -np.abs(128 + i - k).astype(np.float32), NEG).astype(np.float32)
    c['DmP'] = DmP
    DmP0 = DmP.copy()
    DmP0[:, :128] = NEG
    c['DmP0'] = DmP0
    DmS = np.full((128, 384), NEG, np.float32)
    tt = np.arange(64)[:, None]
    kk = np.arange(128)[None, :]
    t2 = np.arange(64)[None, :]
    for sq in range(2):
        rows = slice(sq * 64, sq * 64 + 64)
        DmS[rows, sq * 128:(sq + 1) * 128] = -(128 + tt - kk).astype(np.float32)
        DmS[rows, 256 + sq * 64:256 + sq * 64 + 64] = -np.abs(tt - t2).astype(np.float32)
    c['DmS'] = DmS
    return c


def kernel(x_prompt, x_sample, state_wkv, state_shift, cache_k, cache_v, g_norm, w_in, mu_shift, w0,
           w_w_up, a0, w_a_up, k_k, k_a, r_k, lnx_w, lnx_b, sinks, p_a, p_b, w_o, g_final):
    f32 = np.float32
    A_ = lambda v: np.ascontiguousarray(np.asarray(v, dtype=f32))
    if 'nc' not in _CACHE:
        _CACHE['nc'] = build()
    nc = _CACHE['nc']
    cst = _consts()
    col = lambda v: A_(np.asarray(v, f32).reshape(-1, 128).T)
    shared = dict(cst)
    shared.update(
        w_in=A_(w_in[0]), p_a=A_(p_a[0]), p_b=A_(p_b[0]), w_o=A_(w_o[0]),
        gbc=A_(np.broadcast_to(np.asarray(g_norm[0], f32)[None, :], (128, D))),
        gfbc=A_(np.broadcast_to(np.asarray(g_final, f32)[None, :], (128, D))),
        lnxw=A_(np.broadcast_to(np.asarray(lnx_w[0], f32)[None, :], (128, 1024))),
        lnxb=A_(np.broadcast_to(np.asarray(lnx_b[0], f32)[None, :], (128, 1024))),
        muT=col(mu_shift[0]), w0c=col(w0[0]), a0c=col(a0[0]), kkc=col(k_k[0]), kac=col(k_a[0]),
        rkc=col(np.asarray(r_k[0], f32).reshape(-1)),
        sinks=A_(np.broadcast_to(np.asarray(sinks[0], f32)[None, :], (128, 16))),
        wup=A_(np.concatenate([np.asarray(w_w_up[0], f32), np.asarray(w_a_up[0], f32)], axis=0)),
    )
    xp = np.asarray(x_prompt, f32)
    xs_ = np.asarray(x_sample, f32)
    swkv = np.asarray(state_wkv[0], f32)
    ssh = np.asarray(state_shift[0], f32)
    ck = np.asarray(cache_k[0], f32)
    cvv = np.asarray(cache_v[0], f32)
    in_maps = []
    for c in range(8):
        sl = slice(4 * c, 4 * c + 4)
        m = dict(shared)
        m['x'] = A_(np.concatenate([xp[c], xs_[sl].reshape(256, D)], axis=0))
        sw = swkv[sl].reshape(4, 8, 2, 64, 64)
        m['swkv'] = A_(sw.transpose(0, 2, 4, 1, 3).reshape(4, 128, 8, 64))
        m['sshT'] = A_(ssh[sl].reshape(4, 25, 128).transpose(2, 1, 0))
        ckc = ck[sl]
        kt_ = ckc.transpose(3, 0, 2, 1)
        m['ckT'] = A_(np.concatenate([kt_, kt_], axis=0))
        m['cv'] = A_(cvv[sl].reshape(4, 128, 256).transpose(1, 0, 2))
        m['ck_raw'] = A_(ckc.reshape(4, 128, 256))
        m['cv_raw'] = A_(cvv[sl].reshape(4, 128, 256))
        in_maps.append(m)
    res = run_bass_kernel_spmd(nc, in_maps, core_ids=list(range(8)))
    R = res.results
    y_prompt = np.stack([R[c]['y'][:2048] for c in range(8)]).astype(f32)
    y_sample = np.concatenate([R[c]['y'][2048:].reshape(4, 64, D) for c in range(8)]).astype(f32)

    def unP(a):
        return a.reshape(2, 64, 8, 64).transpose(2, 0, 3, 1).reshape(16, 64, 64)
    wkv_p = np.stack([unP(R[c]['wkv_p']) for c in range(8)])[None].astype(f32)
    wkv_s = np.stack([unP(R[c]['wkv_s'][s]) for c in range(8) for s in range(4)])[None].astype(f32)
    shift_p = np.stack([R[c]['shift_p'].T.reshape(3200) for c in range(8)])[None].astype(f32)
    shift_s = np.stack([R[c]['shift_s'][:, :, s].T.reshape(3200) for c in range(8) for s in range(4)])[None].astype(f32)
    k_p = np.stack([R[c]['k_p'].reshape(128, 4, 64) for c in range(8)])[None].astype(f32)
    v_p = np.stack([R[c]['v_p'].reshape(128, 4, 64) for c in range(8)])[None].astype(f32)
    k_s = np.concatenate([R[c]['k_s'].reshape(4, 128, 4, 64) for c in range(8)])[None].astype(f32)
    v_s = np.concatenate([R[c]['v_s'].reshape(4, 128, 4, 64) for c in range(8)])[None].astype(f32)
    return (y_prompt, y_sample, wkv_p, shift_p, k_p, v_p, wkv_s, shift_s, k_s, v_s)
```

```python
import numpy as np
from contextlib import ExitStack
import concourse.bass as bass
import concourse.mybir as mybir
from concourse.bass_utils import run_bass_kernel_spmd

F32 = mybir.dt.float32
BF16 = mybir.dt.bfloat16
ALU = mybir.AluOpType
AF = mybir.ActivationFunctionType
AX = mybir.AxisListType

D = 2048
NTILES = 18
NT = 2
TB = NT * 128
NBLK = NTILES // NT
INW = 10880
RMS_EPS = 1e-6
LNX_EPS = 64e-5
C0 = float(np.exp(-0.5))
DBG = dict(nblk=NBLK, stage=99)
SLOPES = [float(2.0 ** (-(h + 1) / 2.0)) for h in range(16)]


class Prog:
    COMPUTE = ('pe', 'act', 'dve', 'pool')

    def __init__(self, nc):
        self.nc = nc
        self.ops = []
        self.last_w = {}
        self.readers = {}
        self.streams = None
        self.cur_stream = None

    def begin_streams(self, n):
        self.streams = [[] for _ in range(n)]
        self.cur_stream = None

    def set_stream(self, i):
        self.cur_stream = i

    def capture_start(self):
        assert self.streams is None
        self.streams = [[]]
        self.cur_stream = 0

    def capture_end(self):
        q = self.streams[0]
        self.streams, self.cur_stream = None, None
        return q

    def add_stream(self, q):
        self.streams.append(q)

    def merge_streams(self):
        streams, self.streams, self.cur_stream = self.streams, None, None
        order = []
        for si, q in enumerate(streams):
            L = len(q)
            for k in range(L):
                order.append(((k + 0.5) / L, si, k))
        order.sort()
        for _, si, k in order:
            a, kw, cb = streams[si][k]
            idx = self.op(*a, **kw)
            if cb is not None:
                cb.append(idx)

    def op(self, eng, fn, reads=(), writes=(), chan=None, cb=None):
        if getattr(self, 'cur_stream', None) is not None:
            self.streams[self.cur_stream].append(((eng, fn), dict(reads=reads, writes=writes, chan=chan), cb))
            return -1
        idx = len(self.ops)
        deps = {}
        for k in reads:
            d = self.last_w.get(k)
            if d is not None:
                deps[d] = True
        for k in writes:
            d = self.last_w.get(k)
            if d is not None:
                deps.setdefault(d, False)
            for r in self.readers.get(k, ()):
                deps.setdefault(r, False)
        deps.pop(idx, None)
        self.ops.append(dict(eng=eng, fn=fn, deps=deps, chan=chan))
        for k in reads:
            self.readers.setdefault(k, []).append(idx)
        for k in writes:
            self.last_w[k] = idx
            self.readers[k] = []
        return idx

    def wait_all(self, eng, idxs):
        idx = len(self.ops)
        self.ops.append(dict(eng=eng, fn=None, deps={d: True for d in idxs}, chan=None))
        return idx

    def _need_wait(self, x, d, raw):
        od, ox = self.ops[d], self.ops[x]
        if od['chan'] is not None:
            return True
        if od['eng'] != ox['eng']:
            return True
        if ox['chan'] is not None:
            return True
        if ox['eng'] == 'pe':
            return False
        return True

    def emit(self):
        nc = self.nc
        ops = self.ops
        needed = [False] * len(ops)
        for x, o in enumerate(ops):
            for d, raw in o['deps'].items():
                if self._need_wait(x, d, raw):
                    needed[d] = True
        chans = []
        for o in ops:
            if o['chan'] is not None and o['chan'] not in chans:
                chans.append(o['chan'])
        with ExitStack() as es:
            sems = {}
            for e in self.COMPUTE:
                sems[e] = es.enter_context(nc.semaphore('s_' + e))
            for c in chans:
                sems[('c', c)] = es.enter_context(nc.semaphore('c_' + str(c)))
            cnt = {k: 0 for k in sems}
            ev = [None] * len(ops)
            for x, o in enumerate(ops):
                if o['fn'] is None:
                    continue
                if o['chan'] is not None:
                    k = ('c', o['chan'])
                    cnt[k] += 16
                    ev[x] = (k, cnt[k])
                elif needed[x]:
                    k = o['eng']
                    cnt[k] += 1
                    ev[x] = (k, cnt[k])
            per_eng = {}
            for x, o in enumerate(ops):
                per_eng.setdefault(o['eng'], []).append(x)
            self.stats = {e: len(v) for e, v in per_eng.items()}
            self.stats['sem_max'] = dict((str(k), v) for k, v in cnt.items() if v > 30000)

            def run(e, ename):
                waited = {}
                for x in per_eng.get(ename, ()):
                    o = ops[x]
                    want = {}
                    for d, raw in o['deps'].items():
                        if not self._need_wait(x, d, raw):
                            continue
                        k, v = ev[d]
                        if v > want.get(k, 0):
                            want[k] = v
                    for k, v in want.items():
                        if v > waited.get(k, 0):
                            e.wait_ge(sems[k], v)
                            waited[k] = v
                    if o['fn'] is None:
                        continue
                    ins = o['fn'](e)
                    if ev[x] is not None:
                        k, v = ev[x]
                        ins.then_inc(sems[k], 16 if o['chan'] is not None else 1)

            with nc.Block() as block:
                @block.tensor
                def _(e):
                    run(e, 'pe')

                @block.scalar
                def _(e):
                    run(e, 'act')

                @block.vector
                def _(e):
                    run(e, 'dve')

                @block.gpsimd
                def _(e):
                    run(e, 'pool')

                @block.sync
                def _(e):
                    run(e, 'sp')


def build():
    nc = bass.Bass("TRN2", target_bir_lowering=False)

    def din(name, shape, dt=F32):
        return nc.dram_tensor(name, list(shape), dt, kind="ExternalInput").ap()

    def dout(name, shape, dt=F32):
        return nc.dram_tensor(name, list(shape), dt, kind="ExternalOutput").ap()

    x_d = din("x", [NTILES * 128, D])
    w_in_d = din("w_in", [D, INW])
    p_a_d = din("p_a", [1024, D])
    p_b_d = din("p_b", [1024, D])
    w_o_d = din("w_o", [D, D])
    cnames = dict(gbc=[128, D], gfbc=[128, D], lnxw=[128, 1024], lnxb=[128, 1024], muT=[128, 25],
                  w0c=[128, 8], a0c=[128, 8], kkc=[128, 8], kac=[128, 8], rkc=[128, 8],
                  sinks=[128, 16], identf=[128, 128], MU4=[128, 512], MLs=[128, 128], bones=[128, 128],
                  bo2=[128, 2], resetm=[128, TB], MU5=[128, 320], I2=[128, 64], DmP=[128, 256], DmP0=[128, 256], DmS=[128, 384],
                  sshT=[128, 25, 4])
    cd = {k: din(k, v) for k, v in cnames.items()}
    wup_d = din("wup", [128, 1024])
    swkv_d = din("swkv", [4, 128, 8, 64])
    ckT_d = din("ckT", [128, 4, 4, 128])
    cv_d = din("cv", [128, 4, 256])
    ck_raw = din("ck_raw", [4, 128, 256])
    cv_raw = din("cv_raw", [4, 128, 256])

    y_o = dout("y", [NTILES * 128, D])
    wkvp_o = dout("wkv_p", [128, 8, 64])
    wkvs_o = dout("wkv_s", [4, 128, 8, 64])
    shp_o = dout("shift_p", [128, 25])
    shs_o = dout("shift_s", [128, 25, 4])
    kp_o = dout("k_p", [128, 256])
    vp_o = dout("v_p", [128, 256])
    ks_o = dout("k_s", [4, 128, 256])
    vs_o = dout("v_s", [4, 128, 256])

    NUNITS = 0
    scr = nc.dram_tensor("wscr", [80, 128, 16 * 256], BF16).ap()

    es = ExitStack()
    with es:
        def sb(name, shape, dt=F32):
            return es.enter_context(nc.sbuf_tensor(name, list(shape), dt))

        def ps(name, shape, dt=F32):
            return es.enter_context(nc.psum_tensor(name, list(shape), dt))

        P = Prog(nc)
        ct = {k: sb("c_" + k, v) for k, v in cnames.items()}
        identb = sb("identb", [128, 128], BF16)
        wupb = sb("wupb", [128, 1024], BF16)
        kc = sb("kc", [128, 4, 4, 128], BF16)
        vc = sb("vc", [128, 4, 256], BF16)
        dummy = sb("dummy_t", [128, 8])
        small = sb("small", [128, 64])
        hT = sb("hT", [128, 16, TB], BF16)
        stage = [sb("stage%d" % i, [128, 8, 256]) for i in range(2)]
        wbf = [sb("wbf%d" % i, [128, 16, 256], BF16) for i in range(2)]
        wbf = wbf + [stage[i][:].rearrange("p k n -> p (k n)").bitcast(BF16).rearrange("p (k n) -> p k n", n=256) for i in range(2)]
        WK = ['wbf0', 'wbf1', 'stage0', 'stage1']
        xt = sb("xt", [128, D])
        hb = sb("hb", [128, D], BF16)
        plast = sb("plast", [128, 25])
        shs = sb("shs", [128, 25, 4])
        pT = sb("pT", [128, TB + 1])
        arenaA = sb("arenaA", [128, 23 * TB])
        xs = [arenaA[:, i * TB:(i + 1) * TB] for i in range(6)]
        tq = [[arenaA[:, (6 + s_ * 8 + i) * TB:(7 + s_ * 8 + i) * TB] for i in range(8)] for s_ in range(2)]
        tmpf = {11: arenaA[:, 22 * TB:23 * TB]}
        xr = arenaA[:, 0:NT * D].rearrange("p (t d) -> p t d", d=D)
        ta_all = arenaA[:, 0:16 * TB].rearrange("p (m t) -> p m t", t=TB)
        lora = sb("lora", [128, TB], BF16)
        xsB_t = sb("xsB", [128, 6 * TB])
        xsB = [xsB_t[:, i * TB:(i + 1) * TB] for i in range(6)]
        arenaC = sb("arenaC", [128, 4 * 8 * TB], BF16)
        opT = [arenaC[:, i * 8 * TB:(i + 1) * 8 * TB].rearrange("p (j t) -> p j t", t=TB) for i in range(4)]
        rT, aT, bT, kT = opT
        mergedT = arenaC[:, 0:16 * TB].rearrange("p (j t) -> p j t", t=TB)
        fin32 = arenaC[:, 16 * TB:32 * TB].bitcast(F32)
        sga = fin32[:, 0:2 * TB].rearrange("p (f t) -> p f t", t=TB)
        sgb = fin32[:, 2 * TB:4 * TB].rearrange("p (f t) -> p f t", t=TB)
        ta = fin32[:, 4 * TB:6 * TB].rearrange("p (f t) -> p f t", t=TB)
        gC = sb("gC", [128, 8, 2 * NT])
        vtok = sb("vtok", [128, NT, 1024], BF16)
        vtk = sb("vtk", [128, 8, 2 * NT, 64], BF16)
        I2b = sb("I2b", [128, 64], BF16)
        LNSS = [[sb("LNS%d_%d" % (k, i), [128, 192], BF16) for i in range(2)] for k in range(2)]
        bon = sb("bon", [128, NT, 16])
        kbtokS = [sb("kbtok%d" % i, [128, 128], BF16) for i in range(2)]
        MmS = [sb("Mm%d" % i, [128, 320], BF16) for i in range(2)]
        XUbS = [sb("XUb%d" % i, [128, 128], BF16) for i in range(2)]
        Pf = sb("Pf", [128, 8, 64])
        Pb = sb("Pb", [128, 8, 64], BF16)
        ysb = sb("ysb", [128, 1024])
        ysq = sb("ysq", [128, 1024])
        sa = sb("sa", [128, NT, 1024], BF16)
        sbg = sb("sbg", [128, NT, 1024], BF16)
        yab = sb("yab", [128, 1024], BF16)
        yaT = sb("yaT", [128, 8, TB], BF16)
        ybT = sb("ybT", [128, 8, TB], BF16)
        qT = sb("qT", [128, 8, TB], BF16)
        kTd = sb("kTd", [128, 4, 128 + TB], BF16)
        vat = sb("vat", [128, 1 + NT, 256], BF16)
        kvo = sb("kvo", [128, NT, 512])
        s_sbS = [sb("s_sb%d" % i, [128, 384]) for i in range(2)]
        e_sbS = [sb("e_sb%d" % i, [128, 384], BF16) for i in range(2)]
        eTS = [sb("eT%d" % i, [128, 384], BF16) for i in range(2)]
        smallS = [sb("smallS%d" % i, [128, 8]) for i in range(2)]
        ob = sb("ob", [128, 1024])
        rden = sb("rden", [128, 16])

        Aacc = [ps("A%d" % i, [128, 512]) for i in range(2)]
        A = [Aacc[i // 2][:, (i % 2) * 256:(i % 2) * 256 + 256] for i in range(4)]
        tp = ps("tp", [128, 1024], BF16)
        Mb = ps("Mb", [128, 512])
        Db = [ps("D%d" % i, [128, 512]) for i in range(2)]
        Cb = ps("Cb", [128, 512])
        Eb = ps("Eb", [128, 512])

        cidx = []
        for k in cnames:
            src = cd[k]
            cidx.append(P.op('pool', (lambda e, o=ct[k], s=src: e.dma_start(out=o[:], in_=s)), writes=[k], chan='const'))
        cidx.append(P.op('pool', lambda e: e.dma_start(out=wupb[:], in_=wup_d), writes=['wupb'], chan='const'))
        cidx.append(P.op('pool', lambda e: e.dma_start(out=kc[:], in_=ckT_d), writes=['kc'], chan='const'))
        cidx.append(P.op('pool', lambda e: e.dma_start(out=vc[:], in_=cv_d), writes=['vc'], chan='const'))
        for eng in ('pe', 'act', 'dve', 'pool'):
            P.wait_all(eng, cidx)
        P.op('dve', lambda e: e.tensor_copy(out=identb[:], in_=ct['identf'][:]), reads=['identf'], writes=['identb'])
        P.op('dve', lambda e: e.tensor_copy(out=I2b[:], in_=ct['I2'][:]), reads=['I2'], writes=['I2b'])
        P.op('dve', lambda e: e.memset(Pf[:], 0.0), writes=['Pf%d' % j for j in range(8)])
        P.op('dve', lambda e: e.memset(Pb[:], 0.0), writes=['Pb%d' % j for j in range(8)])
        P.op('dve', lambda e: e.memset(plast[:], 0.0), writes=['plast'])
        P.op('dve', lambda e: e.memset(kTd[:], 0.0), writes=['kTd'])
        P.op('dve', lambda e: e.memset(vat[:], 0.0), writes=['vat'])
        identf = ct['identf']

        st = dict(uid=0, sidx=0, acc=0, cast=0)
        out_idx = []
        ARENA_PREP = (['xs%d' % i for i in range(6)] + ['tq%d_%d' % (s_, i) for s_ in range(2) for i in range(8)] + ['tmp11']
                      + ['rT', 'aT', 'bT', 'kT'])
        ARENA_FIN = ['xr', 'mergedT', 'sga', 'sgb', 'ta', 'taall']

        def fence(after, before):
            P.op('pool', lambda e: e.memset(dummy[0:1, 0:1], 0.0), writes=list(after) + list(before))

        def load_unit(b, u, W, c0, n, KT, dup=False):
            slot = st['uid'] % (2 if b == 0 else 4)
            st['uid'] += 1
            wk = WK[slot]
            ncols = 256 if dup else n
            if b == 0:
                for half in range(KT // 8):
                    si = st['sidx'] % 2
                    st['sidx'] += 1
                    src = W[half * 1024:(half + 1) * 1024, c0:c0 + n].rearrange("(k p) n -> p k n", p=128)
                    P.op('sp', (lambda e, si=si, src=src: e.dma_start(out=stage[si][:, :, 0:n], in_=src)),
                         writes=['stage%d' % si], chan='stg%d' % si)
                    ceng = ('dve', 'act')[st['cast'] % 2]
                    st['cast'] += 1
                    if not dup:
                        dst = wbf[slot][:, half * 8:(half + 1) * 8, 0:n]
                        srcs = stage[si][:, :, 0:n]
                        if ceng == 'act':
                            P.op('act', (lambda e, dst=dst, srcs=srcs: e.copy(out=dst, in_=srcs)),
                                 reads=['stage%d' % si], writes=[wk])
                        else:
                            P.op('dve', (lambda e, dst=dst, srcs=srcs: e.tensor_copy(out=dst, in_=srcs)),
                                 reads=['stage%d' % si], writes=[wk])
                    else:
                        for dd in range(2):
                            dst = wbf[slot][:, half * 8:(half + 1) * 8, :].rearrange(
                                "p k (g d c) -> p k g d c", g=2, d=2)[:, :, :, dd, :]
                            srcs = stage[si][:, :, 0:128].rearrange("p k (g c) -> p k g c", g=2)
                            P.op('pool', (lambda e, dst=dst, srcs=srcs: e.tensor_copy(out=dst, in_=srcs)),
                                 reads=['stage%d' % si], writes=[wk])
                P.op('pool', (lambda e, u=u, slot=slot: e.dma_start(
                    out=scr[u % DBG.get("umod", 80), :, 0:KT * ncols].rearrange("p (k n) -> p k n", n=ncols),
                    in_=wbf[slot][:, 0:KT, 0:ncols])),
                    reads=[wk], writes=['scr%d' % u], chan='wst%d' % slot)
            else:
                P.op('sp', (lambda e, u=u, slot=slot: e.dma_start(
                    out=wbf[slot][:, 0:KT, 0:ncols],
                    in_=scr[u % DBG.get("umod", 80), :, 0:KT * ncols].rearrange("p (k n) -> p k n", n=ncols))),
                    reads=['scr%d' % u], writes=[wk], chan='wld%d' % slot)
            return slot

        def nu():
            st['u'] += 1
            return st['u'] - 1

        def next_pair():
            if st.get('pb0'):
                return (0, 1)
            if st.get('pb') is not None:
                return (2 * st['pb'], 2 * st['pb'] + 1)
            k = st['acc'] % 2
            st['acc'] += 1
            return (2 * k, 2 * k + 1)

        def next_acc():
            return next_pair()[0]

        def akey(ai):
            return 'PB%d' % (ai // 2)

        def mm_fm(slot, KT, f, rhsT, rkey, ai, ncol=TB):
            def fn(e):
                ins = None
                for kt in range(KT):
                    ins = e.matmul(A[ai][:, 0:ncol], lhsT=wbf[slot][:, kt, f * 128:(f + 1) * 128],
                                   rhs=rhsT[:, kt, 0:ncol], start=(kt == 0), stop=(kt == KT - 1))
                return ins
            P.op('pe', fn, reads=[WK[slot], rkey], writes=[akey(ai)])

        def mm_tm(slot, KT, ti, lhs_tile, lkey, ai, n=256):
            def fn(e):
                ins = None
                for kt in range(KT):
                    ins = e.matmul(A[ai][:, 0:n], lhsT=lhs_tile[:, kt, ti * 128:(ti + 1) * 128],
                                   rhs=wbf[slot][:, kt, 0:n], start=(kt == 0), stop=(kt == KT - 1))
                return ins
            P.op('pe', fn, reads=[WK[slot], lkey], writes=[akey(ai)])

        def mmk(e, out, lhsT, rhs, kbase):
            if kbase == 0:
                return e.matmul(out, lhsT=lhsT, rhs=rhs, start=True, stop=True)
            e.matmul(out[0:64], lhsT=lhsT[:, 0:64], rhs=rhs, start=True, stop=True)
            return e.matmul(out[64:128], lhsT=lhsT[:, 64:128], rhs=rhs, start=True, stop=True)

        def chunk3(ap):
            return ap.rearrange("p (c t) -> p c t", t=64)

        for b in range(DBG['nblk']):
            sample = (b == NBLK - 1)
            st['u'] = 0
            def emit_S1(bb):
                for ti in range(NT):
                    gt = bb * NT + ti
                    P.op('sp', (lambda e, gt=gt: e.dma_start(out=xt[:], in_=x_d[gt * 128:(gt + 1) * 128, :])),
                         writes=['xt'], chan='xt')
                    P.op('dve', lambda e: e.memset(small[:, 56:57], 0.0), writes=['s1ssq'])
                    P.op('act', lambda e: e.activation(out=hb[:], in_=xt[:], func=AF.Square, accum_out=small[:, 56:57]),
                         reads=['xt', 's1ssq'], writes=['hb', 's1ssq'])
                    P.op('dve', lambda e: e.tensor_scalar(out=small[:, 57:58], in0=small[:, 56:57], scalar1=1.0 / D,
                                                          scalar2=RMS_EPS, op0=ALU.mult, op1=ALU.add),
                         reads=['s1ssq'], writes=['s1ms'])
                    P.op('act', lambda e: e.activation(out=small[:, 58:59], in_=small[:, 57:58], func=AF.Sqrt),
                         reads=['s1ms'], writes=['s1sq'])
                    P.op('dve', lambda e: e.reciprocal(out=small[:, 59:60], in_=small[:, 58:59]), reads=['s1sq'], writes=['s1rstd'])
                    P.op('dve', lambda e: e.scalar_tensor_tensor(out=hb[:], in0=xt[:], scalar=small[:, 59:60],
                                                                 in1=ct['gbc'][:], op0=ALU.mult, op1=ALU.mult),
                         reads=['xt', 's1rstd', 'gbc'], writes=['hb'])
                    for half in range(2):
                        def fn(e, half=half):
                            ins = None
                            for k in range(8):
                                kt = half * 8 + k
                                ins = e.transpose(out=tp[:, k * 128:(k + 1) * 128], in_=hb[:, kt * 128:(kt + 1) * 128],
                                                  identity=identb[:])
                            return ins
                        P.op('pe', fn, reads=['hb', 'identb'], writes=['tp'])
                        dst = hT[:, half * 8:(half + 1) * 8, ti * 128:(ti + 1) * 128]
                        srcv = tp[:, :].rearrange("p (k t) -> p k t", t=128)
                        if half == 0:
                            P.op('act', (lambda e, dst=dst, srcv=srcv: e.copy(out=dst, in_=srcv)), writes=['hT', 'tp'])
                        else:
                            P.op('dve', (lambda e, dst=dst, srcv=srcv: e.tensor_copy(out=dst, in_=srcv)), writes=['hT', 'tp'])

            if b == 0:
                emit_S1(0)

            if DBG['stage'] <= 1:
                continue
            fence(ARENA_FIN, ARENA_PREP)

            def shift(f, ai, xs_ap, xkey):
                P.op('act', (lambda e: e.copy(out=pT[:, 1:TB + 1], in_=A[ai][:, 0:TB])), writes=['pT', akey(ai)])
                if not sample:
                    P.op('dve', (lambda e: e.tensor_copy(out=pT[:, 0:1], in_=plast[:, f:f + 1])), reads=['plast'], writes=['pT'])
                    P.op('dve', (lambda e: e.tensor_copy(out=plast[:, f:f + 1], in_=pT[:, TB:TB + 1])), reads=['pT'], writes=['plast'])
                else:
                    P.op('dve', (lambda e: e.memset(pT[:, 0:1], 0.0)), writes=['pT'])
                    P.op('dve', (lambda e: e.tensor_copy(out=shs[:, f, :], in_=chunk3(pT[:, 1:TB + 1])[:, :, 63])),
                         reads=['pT'], writes=['shs'])
                t0 = tmpf[11]
                P.op('pool', (lambda e: e.tensor_tensor(out=t0, in0=pT[:, 0:TB], in1=pT[:, 1:TB + 1], op=ALU.subtract)),
                     reads=['pT'], writes=['tmp11'])
                if sample:
                    P.op('dve', (lambda e: e.tensor_tensor(out=chunk3(t0)[:, :, 0], in0=ct['sshT'][:, f, :],
                                                           in1=chunk3(pT[:, 1:TB + 1])[:, :, 0], op=ALU.subtract)),
                         reads=['pT', 'sshT', 'tmp11'], writes=['tmp11'])
                P.op('dve', (lambda e: e.scalar_tensor_tensor(out=xs_ap, in0=t0, scalar=ct['muT'][:, f:f + 1],
                                                              in1=pT[:, 1:TB + 1], op0=ALU.mult, op1=ALU.add)),
                     reads=['tmp11', 'pT', 'muT'], writes=[xkey])

            slot = load_unit(b, nu(), w_in_d, 3072, 128, 16)
            ai = next_acc()
            mm_fm(slot, 16, 0, hT, 'hT', ai)
            shift(24, ai, xs[0], 'xs0')
            P.op('act', lambda e: e.activation(out=lora[0:64, :], in_=xs[0][0:64, :], func=AF.Tanh), reads=['xs0'], writes=['lora'])
            P.op('dve', lambda e: e.tensor_copy(out=lora[64:128, :], in_=xs[0][64:128, :]), reads=['xs0'], writes=['lora'])

            XSETS = [(xs, ['xs%d' % i for i in range(6)]), (xsB, ['xb%d' % i for i in range(6)])]

            def emit_proj(g2, XS, XK):
                for kind in range(3):
                    slot = load_unit(b, nu(), w_in_d, kind * 1024 + g2 * 256, 256, 16)
                    pr = next_pair()
                    for f in range(2):
                        mm_fm(slot, 16, f, hT, 'hT', pr[f])
                    for f in range(2):
                        shift(kind * 8 + g2 * 2 + f, pr[f], XS[kind * 2 + f], XK[kind * 2 + f])

            def prep_pair(j, sx, xr_, xk_, xv_, kr, kk_, kv):
                T = tq[sx]
                K = ['tq%d_%d' % (sx, q) for q in range(8)]
                sg, cs, eg, eig, alr, kk2, kkn, b32 = T
                k_sg, k_cs, k_eg, k_eig, k_alr, k_kk2, k_kkn, k_b32 = K
                egm, k_egm = cs, k_cs
                rn, k_rn = kk2, k_kk2
                t1, k_t1 = sg, k_sg
                jc = slice(j * 128, (j + 1) * 128)
                if sx == 0:
                    A1, A2, A3 = A[2][:, 0:TB], A[3][:, 0:TB], A[2][:, 0:TB]
                    ak = 'PB1'
                else:
                    A1, A2, A3 = Mb[:, 0:TB], Mb[:, 256:256 + TB], Mb[:, 0:TB]
                    ak = 'Mb'
                tv = sx * 128
                tk = 256 + sx * 256
                P.op('pe', (lambda e: e.matmul(A1, lhsT=wupb[0:64, jc], rhs=lora[0:64, :], start=True, stop=True)),
                     reads=['wupb', 'lora'], writes=[ak])
                P.op('pe', (lambda e: mmk(e, A2, wupb[64:128, jc], lora[64:128, :], 64)),
                     reads=['wupb', 'lora'], writes=[ak])
                P.op('act', (lambda e: e.activation(out=sg, in_=A1, func=AF.Sigmoid, bias=ct['w0c'][:, j:j + 1])),
                     reads=['w0c'], writes=[k_sg, ak])
                P.op('act', (lambda e: e.activation(out=alr, in_=A2, func=AF.Sigmoid, bias=ct['a0c'][:, j:j + 1])),
                     reads=['a0c'], writes=[k_alr, ak])
                P.op('dve', (lambda e: e.tensor_tensor_scan(out=cs, data0=ct['resetm'][:], data1=sg, initial=0.0, op0=ALU.mult, op1=ALU.add)),
                     reads=[k_sg, 'resetm'], writes=[k_cs])
                P.op('act', (lambda e: e.activation(out=eg, in_=cs, func=AF.Exp, scale=-C0)), reads=[k_cs], writes=[k_eg])
                P.op('act', (lambda e: e.activation(out=eig, in_=cs, func=AF.Exp, scale=C0)), reads=[k_cs], writes=[k_eig])
                P.op('dve', (lambda e: e.tensor_tensor(out=t1, in0=cs, in1=sg, op=ALU.subtract)), reads=[k_cs, k_sg], writes=[k_t1])
                P.op('act', (lambda e: e.activation(out=egm, in_=t1, func=AF.Exp, scale=-C0)), reads=[k_t1], writes=[k_egm])
                P.op('dve', (lambda e: e.tensor_copy(out=gC[:, j, :], in_=chunk3(eg)[:, :, 63])), reads=[k_eg], writes=['gC%d' % j])
                P.op('act', (lambda e: e.activation(out=kk2, in_=xk_, func=AF.Square, scale=ct['kkc'][:, j:j + 1])),
                     reads=[kk_, 'kkc'], writes=[k_kk2])
                P.op('pe', (lambda e: e.matmul(A3, lhsT=ct['bones'][:], rhs=kk2, start=True, stop=True)),
                     reads=['bones', k_kk2], writes=[ak])
                P.op('act', (lambda e: e.activation(out=rn, in_=A3, func=AF.Sqrt)), writes=[k_rn, ak])
                P.op('dve', (lambda e: e.tensor_scalar(out=rn, in0=rn, scalar1=1e-12, scalar2=None, op0=ALU.max)), reads=[k_rn], writes=[k_rn])
                P.op('dve', (lambda e: e.reciprocal(out=rn, in_=rn)), reads=[k_rn], writes=[k_rn])
                P.op('dve', (lambda e: e.scalar_tensor_tensor(out=kkn, in0=xk_, scalar=ct['kkc'][:, j:j + 1], in1=rn, op0=ALU.mult, op1=ALU.mult)),
                     reads=[kk_, 'kkc', k_rn], writes=[k_kkn])
                P.op('dve', (lambda e: e.tensor_scalar(out=t1, in0=alr, scalar1=-1.0, scalar2=ct['kac'][:, j:j + 1], op0=ALU.add, op1=ALU.mult)),
                     reads=[k_alr, 'kac'], writes=[k_t1])
                P.op('dve', (lambda e: e.scalar_tensor_tensor(out=t1, in0=t1, scalar=1.0, in1=xk_, op0=ALU.add, op1=ALU.mult)),
                     reads=[k_t1, kk_], writes=[k_t1])
                P.op('dve', (lambda e: e.tensor_tensor(out=rT[:, j, :], in0=xr_, in1=eg, op=ALU.mult)), reads=[kr, k_eg], writes=['rT'])
                P.op('dve', (lambda e: e.scalar_tensor_tensor(out=aT[:, j, :], in0=kkn, scalar=-1.0, in1=egm, op0=ALU.mult, op1=ALU.mult)),
                     reads=[k_kkn, k_egm], writes=['aT'])
                P.op('dve', (lambda e: e.tensor_tensor(out=b32, in0=kkn, in1=alr, op=ALU.mult)), reads=[k_kkn, k_alr], writes=[k_b32])
                P.op('dve', (lambda e: e.tensor_tensor(out=bT[:, j, :], in0=b32, in1=eig, op=ALU.mult)), reads=[k_b32, k_eig], writes=['bT'])
                P.op('dve', (lambda e: e.tensor_tensor(out=kT[:, j, :], in0=t1, in1=eig, op=ALU.mult)), reads=[k_t1, k_eig], writes=['kT'])
                P.op('dve', (lambda e: e.scalar_tensor_tensor(out=kk2, in0=xr_, scalar=ct['rkc'][:, j:j + 1], in1=t1, op0=ALU.mult, op1=ALU.mult)),
                     reads=[kr, 'rkc', k_t1], writes=[k_kk2])
                vb = b32.bitcast(BF16)[:, 0:TB]
                P.op('act', (lambda e: e.copy(out=vb, in_=xv_)), reads=[kv], writes=[k_b32])

                def fnVT(e):
                    ins = None
                    for ci_ in range(2 * NT):
                        for hp in (slice(0, 64), slice(64, 128)):
                            ins = e.transpose(out=tp[hp, tk + ci_ * 64:tk + ci_ * 64 + 64], in_=vb[hp, ci_ * 64:(ci_ + 1) * 64], identity=identb[hp, hp])
                    return ins
                P.op('pe', fnVT, reads=[k_b32, 'identb'], writes=['tp'])
                P.op('act', (lambda e: e.copy(out=vtk[:, j, :, :], in_=tp[:, tk:tk + 2 * NT * 64].rearrange("p (c v) -> p c v", v=64))),
                     writes=['vtk', 'tp'])
                for ti in range(NT):
                    tcs = slice(ti * 128, (ti + 1) * 128)
                    P.op('pe', (lambda e, ti=ti, tcs=tcs: e.matmul(Eb[:, ti * 16 + j * 2:ti * 16 + j * 2 + 2], lhsT=kk2[:, tcs], rhs=ct['bo2'][:], start=True, stop=True)),
                         reads=[k_kk2, 'bo2'], writes=['Eb'])
                    P.op('pe', (lambda e, tcs=tcs: e.transpose(out=tp[:, tv:tv + 128], in_=vb[:, tcs], identity=identb[:])),
                         reads=[k_b32, 'identb'], writes=['tp'])
                    P.op('act', (lambda e, ti=ti: e.copy(out=vtok[:, ti, jc], in_=tp[:, tv:tv + 128])), writes=['vtok', 'tp'])

            for it in range(5):
                P.begin_streams(3)
                if it < 4:
                    P.set_stream(0)
                    st['pb'] = 0
                    emit_proj(it, *XSETS[it % 2])
                    st['pb'] = None
                if it > 0:
                    XS, XK = XSETS[(it - 1) % 2]
                    for jj in range(2):
                        P.set_stream(1 + jj)
                        prep_pair((it - 1) * 2 + jj, jj, XS[jj], XS[2 + jj], XS[4 + jj], XK[jj], XK[2 + jj], XK[4 + jj])
                P.merge_streams()
            P.op('dve', lambda e: e.tensor_copy(out=bon[:].rearrange("p t h -> p (t h)"), in_=Eb[:, 0:NT * 16]), writes=['bon', 'Eb'])

            def aux_ga(i):
                slot = load_unit(b, nu(), w_in_d, 3200 + i * 256, 256, 16)
                pr = next_pair()
                for ti in range(NT):
                    mm_tm(slot, 16, ti, hT, 'hT', pr[ti])
                for ti in range(NT):
                    ai = pr[ti]
                    P.op('act', (lambda e, ai=ai, ti=ti, i=i: e.activation(out=sa[:, ti, i * 256:(i + 1) * 256], in_=A[ai][:, 0:256], func=AF.Silu)),
                         writes=['sa', akey(ai)])

            def aux_q(i):
                slot = load_unit(b, nu(), w_in_d, 4224 + i * 256, 256, 16)
                pr = next_pair()
                for f in range(2):
                    mm_fm(slot, 16, f, hT, 'hT', pr[f])
                for f in range(2):
                    ai = pr[f]
                    P.op('act', (lambda e, ai=ai, i=i, f=f: e.activation(out=qT[:, i * 2 + f, :], in_=A[ai][:, 0:TB], func=AF.Copy, scale=0.125)),
                         writes=['qT', akey(ai)])

            def aux_kd(i):
                slot = load_unit(b, nu(), w_in_d, 5248 + i * 128, 128, 16, dup=True)
                pr = next_pair()
                for f in range(2):
                    mm_fm(slot, 16, f, hT, 'hT', pr[f])
                for f in range(2):
                    ai = pr[f]
                    P.op('dve', (lambda e, ai=ai, i=i, f=f: e.tensor_copy(out=kTd[:, i * 2 + f, 128:128 + TB], in_=A[ai][:, 0:TB])),
                         writes=['kTd', akey(ai)])

            def aux_kv(i):
                slot = load_unit(b, nu(), w_in_d, 5248 + i * 256, 256, 16)
                pr = next_pair()
                for ti in range(NT):
                    mm_tm(slot, 16, ti, hT, 'hT', pr[ti])
                for ti in range(NT):
                    ai = pr[ti]
                    P.op('act', (lambda e, ai=ai, ti=ti, i=i: e.copy(out=kvo[:, ti, i * 256:(i + 1) * 256], in_=A[ai][:, 0:256])),
                         writes=['kvo', akey(ai)])
                    if i == 1:
                        P.op('act', (lambda e, ai=ai, ti=ti: e.copy(out=vat[:, 1 + ti, :], in_=A[ai][:, 0:256])),
                             writes=['vat', akey(ai)])

            def aux_gb(i):
                slot = load_unit(b, nu(), w_in_d, 5760 + i * 256, 256, 16)
                pr = next_pair()
                for ti in range(NT):
                    mm_tm(slot, 16, ti, hT, 'hT', pr[ti])
                for ti in range(NT):
                    ai = pr[ti]
                    P.op('act', (lambda e, ai=ai, ti=ti, i=i: e.activation(out=sbg[:, ti, i * 256:(i + 1) * 256], in_=A[ai][:, 0:256], func=AF.Silu)),
                         writes=['sbg', akey(ai)])

            def aux_carry():
                if 0 < b and not sample:
                    P.op('pool', lambda e: e.tensor_copy(out=kTd[:, :, 0:128], in_=kTd[:, :, TB:TB + 128]), reads=['kTd'], writes=['kTd'])
                    P.op('pool', lambda e: e.tensor_copy(out=vat[:, 0, :], in_=vat[:, NT, :]), reads=['vat'], writes=['vat'])

            AUX = [
                [lambda: aux_ga(0), lambda: aux_ga(1), lambda: aux_ga(2), lambda: aux_ga(3)],
                [lambda: aux_q(0), lambda: aux_q(1), lambda: aux_q(2), lambda: aux_q(3)],
                [aux_carry, lambda: aux_kd(0), lambda: aux_kd(1), lambda: aux_kv(0), lambda: aux_kv(1)],
                [lambda: aux_gb(0), lambda: aux_gb(1), lambda: aux_gb(2), lambda: aux_gb(3)],
            ]

            if DBG['stage'] <= 3:
                continue
            H2 = (slice(0, 64), slice(64, 128))
            PS_PENDING = []
            PS1_PENDING = []
            for ti in range(NT):
                gt = b * NT + ti
                tcs = slice(ti * 128, (ti + 1) * 128)
                for c in range(2):
                    cp = slice(c * 64, c * 64 + 64)
                    cc = slice(ti * 128 + c * 64, ti * 128 + c * 64 + 64)
                    ci = ti * 2 + c
                    P.begin_streams(3)
                    if ti == 1 and c == 0 and PS_PENDING:
                        P.add_stream(PS_PENDING.pop())
                    P.set_stream(2)
                    st['pb0'] = True
                    for task in AUX[ci]:
                        task()
                    st['pb0'] = False
                    for j in range(8):
                        sx = (j % 2) if DBG.get('ss', 1) else 0
                        P.set_stream(sx)
                        kbtok, Mmx, LNS, XUb = kbtokS[sx], MmS[sx], LNSS[sx], XUbS[sx]
                        MC = Mb if sx == 0 else Cb
                        MCk = 'Mb' if sx == 0 else 'Cb'
                        DD = Db[0][:, 0:192] if sx == 0 else Eb[:, 192:384]
                        DDk = 'D0' if sx == 0 else 'Eb'
                        kX, kM, kL = 'X%d' % sx, 'Mm%d' % sx, 'LNS%d_' % sx
                        if sample:
                            seq = ti * 2 + c
                            P.op('sp', (lambda e, seq=seq, j=j: e.dma_start(out=Pf[:, j, :], in_=swkv_d[seq, :, j, :])),
                                 writes=['Pf%d' % j], chan='pst%d' % j)
                            P.op('dve', (lambda e, j=j: e.tensor_copy(out=Pb[:, j, :], in_=Pf[:, j, :])), reads=['Pf%d' % j], writes=['Pb%d' % j])
                        tpo = sx * 128

                        def fnT(e, j=j, cc=cc, tpo=tpo):
                            ins = None
                            for hp in H2:
                                e.transpose(out=tp[hp, tpo:tpo + 64], in_=kT[hp, j, cc], identity=identb[hp, hp])
                                ins = e.transpose(out=tp[hp, tpo + 64:tpo + 128], in_=bT[hp, j, cc], identity=identb[hp, hp])
                            return ins
                        P.op('pe', fnT, reads=['kT', 'bT', 'identb'], writes=['tp'])
                        P.op('act', (lambda e, kbtok=kbtok, tpo=tpo: e.copy(out=kbtok[:, 0:128], in_=tp[:, tpo:tpo + 128])), writes=['kbtok%d' % sx, 'tp'])

                        def fnM(e, j=j, cc=cc, MC=MC):
                            ins = None
                            for hp in H2:
                                e.matmul(MC[hp, 0:64], lhsT=bT[hp, j, cc], rhs=aT[hp, j, cc], start=True, stop=True)
                                e.matmul(MC[hp, 64:128], lhsT=kT[hp, j, cc], rhs=aT[hp, j, cc], start=True, stop=True)
                                e.matmul(MC[hp, 128:192], lhsT=bT[hp, j, cc], rhs=rT[hp, j, cc], start=True, stop=True)
                                e.matmul(MC[hp, 192:256], lhsT=kT[hp, j, cc], rhs=rT[hp, j, cc], start=True, stop=True)
                                ins = e.matmul(MC[hp, 256:320], lhsT=aT[hp, j, cc], rhs=bT[hp, j, cc], start=True, stop=True)
                            return ins
                        P.op('pe', fnM, reads=['aT', 'bT', 'kT', 'rT'], writes=[MCk])
                        P.op('dve', (lambda e, Mmx=Mmx, MC=MC: e.tensor_tensor(out=Mmx[:, 0:320], in0=MC[:, 0:320], in1=ct['MU5'][:], op=ALU.mult)),
                             reads=['MU5'], writes=[kM, MCk])
                        P.op('pool', (lambda e, LNS=LNS, Mmx=Mmx: e.tensor_tensor(out=LNS[0][:, 128:192], in0=Mmx[:, 0:64], in1=I2b[:], op=ALU.add)),
                             reads=[kM, 'I2b'], writes=[kL + '0'])
                        for lvl in range(1, 7):
                            cur, nxt = (lvl - 1) % 2, lvl % 2
                            if lvl == 1:
                                Lc, Nc = Mmx[:, 256:320], Mmx[:, 0:64]
                                rk = [kM, kL + '0']
                            else:
                                Lc, Nc = LNS[cur][:, 0:64], LNS[cur][:, 64:128]
                                rk = [kL + str(cur)]
                            Sc = LNS[cur][:, 128:192]

                            def fnD(e, Lc=Lc, Nc=Nc, Sc=Sc, lvl=lvl, DD=DD):
                                ins = None
                                for hp in H2:
                                    if lvl < 6:
                                        e.matmul(DD[hp, 0:64], lhsT=Nc[hp], rhs=Lc[hp], start=True, stop=True)
                                    if lvl < 5:
                                        e.matmul(DD[hp, 64:128], lhsT=Lc[hp], rhs=Nc[hp], start=True, stop=True)
                                    if lvl == 1:
                                        ins = e.matmul(DD[hp, 128:192], lhsT=I2b[hp], rhs=Sc[hp], start=True, stop=True)
                                    else:
                                        e.matmul(DD[hp, 128:192], lhsT=I2b[hp], rhs=Sc[hp], start=True, stop=False)
                                        ins = e.matmul(DD[hp, 128:192], lhsT=Lc[hp], rhs=Sc[hp], start=False, stop=True)
                                return ins
                            P.op('pe', fnD, reads=rk + ['I2b'], writes=[DDk])
                            lo = 0 if lvl < 6 else 128
                            if (lvl + sx) % 2 == 1:
                                P.op('dve', (lambda e, LNS=LNS, nxt=nxt, lo=lo, DD=DD: e.tensor_copy(out=LNS[nxt][:, lo:192], in_=DD[:, lo:192])),
                                     writes=[kL + str(nxt), DDk])
                            else:
                                P.op('act', (lambda e, LNS=LNS, nxt=nxt, lo=lo, DD=DD: e.copy(out=LNS[nxt][:, lo:192], in_=DD[:, lo:192])),
                                     writes=[kL + str(nxt), DDk])

                        def fnX(e, j=j, cc=cc, ci=ci, MC=MC, Mmx=Mmx):
                            ins = None
                            for hp in H2:
                                e.matmul(MC[hp, 320:384], lhsT=aT[hp, j, cc], rhs=Pb[hp, j, :], start=True, stop=False)
                                ins = e.matmul(MC[hp, 320:384], lhsT=Mmx[hp, 64:128], rhs=vtk[hp, j, ci, :], start=False, stop=True)
                            return ins
                        P.op('pe', fnX, reads=['aT', 'Pb%d' % j, kM, 'vtk'], writes=[MCk])
                        P.op('dve', (lambda e, XUb=XUb, MC=MC: e.tensor_copy(out=XUb[:, 0:64], in_=MC[:, 320:384])), writes=[kX + 'x', MCk])

                        def fnU(e, MC=MC, LNS=LNS, XUb=XUb):
                            ins = None
                            for hp in H2:
                                ins = e.matmul(MC[hp, 384:448], lhsT=LNS[0][hp, 128:192], rhs=XUb[hp, 0:64], start=True, stop=True)
                            return ins
                        P.op('pe', fnU, reads=[kX + 'x', kL + '0'], writes=[MCk])
                        P.op('act', (lambda e, XUb=XUb, MC=MC: e.copy(out=XUb[:, 64:128], in_=MC[:, 384:448])), writes=[kX + 'u', MCk])

                        def fnO(e, j=j, cp=cp, cc=cc, ci=ci, Mmx=Mmx, XUb=XUb):
                            ins = None
                            for hh, hp in enumerate(H2):
                                ob_ = Db[1][cp, j * 64:j * 64 + 64] if hh == 0 else Aacc[1][cp, j * 64:j * 64 + 64]
                                e.matmul(ob_, lhsT=rT[hp, j, cc], rhs=Pb[hp, j, :], start=True, stop=False)
                                e.matmul(ob_, lhsT=Mmx[hp, 128:192], rhs=XUb[hp, 64:128], start=False, stop=False)
                                ins = e.matmul(ob_, lhsT=Mmx[hp, 192:256], rhs=vtk[hp, j, ci, :], start=False, stop=True)
                            return ins
                        P.op('pe', fnO, reads=['rT', 'Pb%d' % j, kM, kX + 'u', 'vtk'], writes=['D1', 'PB1'])

                        def fnP(e, j=j, ci=ci, MC=MC, kbtok=kbtok, XUb=XUb):
                            ins = None
                            for hp in H2:
                                e.matmul(MC[hp, 448:512], lhsT=identf[hp, hp], rhs=Pf[hp, j, :], start=True, stop=False)
                                e.matmul(MC[hp, 448:512], lhsT=kbtok[hp, 64:128], rhs=XUb[hp, 64:128], start=False, stop=False)
                                ins = e.matmul(MC[hp, 448:512], lhsT=kbtok[hp, 0:64], rhs=vtk[hp, j, ci, :], start=False, stop=True)
                            return ins
                        P.op('pe', fnP, reads=['identf', 'Pf%d' % j, 'kbtok%d' % sx, kX + 'u', 'vtk'], writes=[MCk])
                        gcol = gC[:, j, ci:ci + 1]
                        P.op('act', (lambda e, j=j, gcol=gcol, MC=MC: e.activation(out=Pf[:, j, :], in_=MC[:, 448:512], func=AF.Copy, scale=gcol)),
                             reads=['gC%d' % j], writes=['Pf%d' % j, MCk])
                        P.op('dve', (lambda e, j=j, gcol=gcol, MC=MC: e.tensor_scalar(out=Pb[:, j, :], in0=MC[:, 448:512], scalar1=gcol, scalar2=None, op0=ALU.mult)),
                             reads=['gC%d' % j], writes=['Pb%d' % j, MCk])
                        if sample:
                            seq = ti * 2 + c
                            P.op('pool', (lambda e, seq=seq, j=j: e.dma_start(out=wkvs_o[seq, :, j, :], in_=Pf[:, j, :])),
                                 reads=['Pf%d' % j], chan='o_pf%d' % j, cb=out_idx)
                    P.merge_streams()
                y4 = ysb[:].rearrange("p (j h c) -> p j h c", h=2, c=64)
                if DBG.get('oe', 0) == 0:
                    P.op('dve', lambda e: e.tensor_copy(out=y4[:, :, 0, :], in_=Db[1][:, :].rearrange("p (j c) -> p j c", c=64)), writes=['ysb', 'D1'])
                    P.op('act', lambda e: e.copy(out=y4[:, :, 1, :], in_=Aacc[1][:, :].rearrange("p (j c) -> p j c", c=64)), writes=['ysb', 'PB1'])
                else:
                    for j in range(8):
                        P.op('dve', (lambda e, j=j: e.tensor_copy(out=ysb[:, j * 128:j * 128 + 64], in_=Db[1][:, j * 64:j * 64 + 64])), writes=['ysb', 'D1'])
                        P.op('act', (lambda e, j=j: e.copy(out=ysb[:, j * 128 + 64:j * 128 + 128], in_=Aacc[1][:, j * 64:j * 64 + 64])), writes=['ysb', 'PB1'])
                if gt == 15:
                    out_idx.append(P.op('pool', lambda e: e.dma_start(out=wkvp_o, in_=Pf[:]), reads=['Pf%d' % j for j in range(8)], chan='o_pfp'))

                if DBG.get('pso', 1):
                    P.capture_start()
                if DBG.get('dump', 0):
                    out_idx.append(P.op('pool', (lambda e, gt=gt: e.dma_start(out=y_o[(gt + 4) * 128:(gt + 5) * 128, 0:1024], in_=ysb[:])), reads=['ysb'], chan='o_dbg'))
                y3 = ysb[:].rearrange("p (h c) -> p h c", c=64)
                q3 = ysq[:].rearrange("p (h c) -> p h c", c=64)
                P.op('dve', lambda e: e.tensor_reduce(out=small[:, 8:24], in_=y3, axis=AX.X, op=ALU.add), reads=['ysb'], writes=['gn_s1'])
                P.op('act', lambda e: e.activation(out=ysq[:], in_=ysb[:], func=AF.Square), reads=['ysb'], writes=['ysq'])
                P.op('dve', lambda e: e.tensor_reduce(out=small[:, 24:40], in_=q3, axis=AX.X, op=ALU.add), reads=['ysq'], writes=['gn_s2'])
                P.op('dve', lambda e: e.tensor_scalar(out=small[:, 40:56], in0=small[:, 8:24], scalar1=1.0 / 64, scalar2=None, op0=ALU.mult),
                     reads=['gn_s1'], writes=['gn_mean'])
                P.op('dve', lambda e: e.tensor_tensor(out=small[:, 8:24], in0=small[:, 40:56], in1=small[:, 40:56], op=ALU.mult),
                     reads=['gn_mean', 'gn_s1'], writes=['gn_s1'])
                P.op('dve', lambda e: e.scalar_tensor_tensor(out=small[:, 24:40], in0=small[:, 24:40], scalar=1.0 / 64, in1=small[:, 8:24], op0=ALU.mult, op1=ALU.subtract),
                     reads=['gn_s2', 'gn_s1'], writes=['gn_s2'])
                P.op('dve', lambda e: e.tensor_scalar(out=small[:, 24:40], in0=small[:, 24:40], scalar1=LNX_EPS, scalar2=None, op0=ALU.add),
                     reads=['gn_s2'], writes=['gn_s2'])
                P.op('act', lambda e: e.activation(out=small[:, 24:40], in_=small[:, 24:40], func=AF.Sqrt), reads=['gn_s2'], writes=['gn_s2'])
                P.op('dve', lambda e: e.reciprocal(out=small[:, 24:40], in_=small[:, 24:40]), reads=['gn_s2'], writes=['gn_s2'])
                P.op('dve', lambda e: e.tensor_tensor(out=y3, in0=y3, in1=small[:, 40:56].unsqueeze(2).to_broadcast([128, 16, 64]), op=ALU.subtract),
                     reads=['ysb', 'gn_mean'], writes=['ysb'])
                P.op('dve', lambda e: e.tensor_tensor(out=y3, in0=y3, in1=small[:, 24:40].unsqueeze(2).to_broadcast([128, 16, 64]), op=ALU.mult),
                     reads=['ysb', 'gn_s2'], writes=['ysb'])
                P.op('dve', lambda e: e.tensor_tensor(out=ysb[:], in0=ysb[:], in1=ct['lnxw'][:], op=ALU.mult), reads=['ysb', 'lnxw'], writes=['ysb'])
                P.op('dve', lambda e: e.tensor_tensor(out=ysb[:], in0=ysb[:], in1=ct['lnxb'][:], op=ALU.add), reads=['ysb', 'lnxb'], writes=['ysb'])
                P.op('dve', (lambda e, ti=ti: e.tensor_tensor(out=q3, in0=vtok[:, ti, :].rearrange("p (h c) -> p h c", c=64),
                                                              in1=bon[:, ti, :].unsqueeze(2).to_broadcast([128, 16, 64]), op=ALU.mult)),
                     reads=['vtok', 'bon', 'ysq'], writes=['ysq'])
                P.op('dve', lambda e: e.tensor_tensor(out=ysb[:], in0=ysb[:], in1=ysq[:], op=ALU.add), reads=['ysb', 'ysq'], writes=['ysb'])
                P.op('dve', (lambda e, ti=ti: e.tensor_tensor(out=yab[:], in0=ysb[:], in1=sa[:, ti, :], op=ALU.mult)), reads=['ysb', 'sa'], writes=['yab'])

                if DBG.get('dump', 0):
                    out_idx.append(P.op('pool', (lambda e, gt=gt: e.dma_start(out=y_o[(gt + 8) * 128:(gt + 9) * 128, 0:1024], in_=yab[:])), reads=['yab'], chan='o_dbg'))
                tr_offs = [512, 640, 768, 896] if ti == 0 else [384, 896]
                nr = len(tr_offs)
                for r0 in range(0, 8, nr):
                    def fn(e, r0=r0, tr_offs=tr_offs, nr=nr):
                        ins = None
                        for k in range(nr):
                            ins = e.transpose(out=tp[:, tr_offs[k]:tr_offs[k] + 128], in_=yab[:, (r0 + k) * 128:(r0 + k + 1) * 128], identity=identb[:])
                        return ins
                    P.op('pe', fn, reads=['yab', 'identb'], writes=['tp'])
                    for k in range(nr):
                        P.op('act', (lambda e, tcs=tcs, r0=r0, k=k, off=tr_offs[k]: e.copy(out=yaT[:, r0 + k, tcs], in_=tp[:, off:off + 128])), writes=['yaT', 'tp'])
                if DBG.get('pso', 1):
                    (PS_PENDING if ti == 0 else PS1_PENDING).append(P.capture_end())

            if DBG['stage'] <= 4:
                continue
            for ti in range(NT):
                gt = b * NT + ti
                if gt == 15:
                    out_idx.append(P.op('pool', (lambda e, ti=ti: e.dma_start(out=kp_o, in_=kvo[:, ti, 0:256])), reads=['kvo'], chan='o_kv'))
                    out_idx.append(P.op('pool', (lambda e, ti=ti: e.dma_start(out=vp_o, in_=kvo[:, ti, 256:512])), reads=['kvo'], chan='o_kv'))
                if sample:
                    for c in range(2):
                        seq = ti * 2 + c
                        cp = slice(c * 64, c * 64 + 64)
                        out_idx.append(P.op('pool', (lambda e, ti=ti, seq=seq, cp=cp: e.dma_start(out=ks_o[seq, 64:128, :], in_=kvo[cp, ti, 0:256])), reads=['kvo'], chan='o_kv'))
                        out_idx.append(P.op('pool', (lambda e, ti=ti, seq=seq, cp=cp: e.dma_start(out=vs_o[seq, 64:128, :], in_=kvo[cp, ti, 256:512])), reads=['kvo'], chan='o_kv'))
                        out_idx.append(P.op('pool', (lambda e, seq=seq: e.dma_start(out=ks_o[seq, 0:64, :], in_=ck_raw[seq, 64:128, :])), chan='o_kv'))
                        out_idx.append(P.op('pool', (lambda e, seq=seq: e.dma_start(out=vs_o[seq, 0:64, :], in_=cv_raw[seq, 64:128, :])), chan='o_kv'))

            if DBG['stage'] <= 5:
                continue
            fence(ARENA_PREP, ARENA_FIN)

            def aux_m(i):
                slot = load_unit(b, nu(), w_in_d, 6784 + i * 256, 256, 16)
                pr = next_pair()
                for f in range(2):
                    mm_fm(slot, 16, f, hT, 'hT', pr[f])
                for f in range(2):
                    ai = pr[f]
                    P.op('act', (lambda e, ai=ai, f=f, i=i: e.activation(out=ta_all[:, i * 2 + f, :], in_=A[ai][:, 0:TB], func=AF.Sigmoid)), writes=['taall', akey(ai)])

            def aux_p(i):
                slot = load_unit(b, nu(), p_a_d, i * 256, 256, 8)
                pr = next_pair()
                for f in range(2):
                    mm_fm(slot, 8, f, yaT, 'yaT', pr[f])
                for f in range(2):
                    ai = pr[f]
                    P.op('dve', (lambda e, ai=ai, f=f, i=i: e.tensor_tensor(out=ta_all[:, i * 2 + f, :], in0=ta_all[:, i * 2 + f, :], in1=A[ai][:, 0:TB], op=ALU.mult)),
                         writes=['taall', akey(ai)])

            for ti in range(NT):
                gt = b * NT + ti
                tcs = slice(ti * 128, (ti + 1) * 128)
                nkb = 3 if sample else 2
                nk = nkb * 128
                Dm = ct['DmS'] if sample else (ct['DmP0'] if gt == 0 else ct['DmP'])
                Dk = 'DmS' if sample else ('DmP0' if gt == 0 else 'DmP')
                P.begin_streams(3)
                if ti == 0 and PS1_PENDING:
                    P.add_stream(PS1_PENDING.pop())
                P.set_stream(2)
                for i_ in range(8):
                    (aux_m if ti == 0 else aux_p)(i_)
                for h in range(16):
                    sx = (h % 2) if DBG.get('as', 1) else 0
                    P.set_stream(sx)
                    g = h // 4
                    f = h // 2
                    hp = slice((h % 2) * 64, (h % 2) * 64 + 64)
                    SB = Mb if sx == 0 else Cb
                    SBk = 'Mb' if sx == 0 else 'Cb'
                    s_x, e_x, eT_x, sm = s_sbS[sx], e_sbS[sx], eTS[sx], smallS[sx]
                    ks = 'at%d_' % sx
                    tpo = sx * 512

                    def fnS(e, g=g, f=f, hp=hp, ti=ti, tcs=tcs, sample=sample, SB=SB):
                        kb_ = hp.start
                        if not sample:
                            return mmk(e, SB[:, 0:256], qT[hp, f, tcs], kTd[hp, g, ti * 128:ti * 128 + 256], kb_)
                        mmk(e, SB[:, 0:128], qT[hp, f, tcs], kc[hp, ti * 2, g, :], kb_)
                        mmk(e, SB[:, 128:256], qT[hp, f, tcs], kc[hp, ti * 2 + 1, g, :], kb_)
                        return mmk(e, SB[:, 256:384], qT[hp, f, tcs], kTd[hp, g, 128 + ti * 128:256 + ti * 128], kb_)
                    P.op('pe', fnS, reads=['qT', 'kTd', 'kc'], writes=[SBk])
                    P.op('dve', (lambda e, h=h, nk=nk, Dm=Dm, s_x=s_x, SB=SB: e.scalar_tensor_tensor(out=s_x[:, 0:nk], in0=Dm[:, 0:nk], scalar=SLOPES[h], in1=SB[:, 0:nk], op0=ALU.mult, op1=ALU.add)),
                         reads=[Dk], writes=[ks + 's', SBk])
                    P.op('dve', (lambda e, nk=nk, s_x=s_x, sm=sm: e.tensor_reduce(out=sm[:, 0:1], in_=s_x[:, 0:nk], axis=AX.X, op=ALU.max)), reads=[ks + 's'], writes=[ks + 'mx'])
                    P.op('dve', (lambda e, h=h, sm=sm: e.tensor_scalar(out=sm[:, 1:2], in0=sm[:, 0:1], scalar1=ct['sinks'][:, h:h + 1], scalar2=-1.0, op0=ALU.max, op1=ALU.mult)),
                         reads=[ks + 'mx', 'sinks'], writes=[ks + 'negm'])
                    P.op('dve', (lambda e, sm=sm: e.memset(sm[:, 2:3], 0.0)), writes=[ks + 'rs'])
                    P.op('act', (lambda e, nk=nk, s_x=s_x, e_x=e_x, sm=sm: e.activation(out=e_x[:, 0:nk], in_=s_x[:, 0:nk], func=AF.Exp, bias=sm[:, 1:2], accum_out=sm[:, 2:3])),
                         reads=[ks + 's', ks + 'negm', ks + 'rs'], writes=[ks + 'e', ks + 'rs'])
                    P.op('act', (lambda e, h=h, sm=sm: e.activation(out=sm[:, 3:4], in_=sm[:, 1:2], func=AF.Exp, bias=ct['sinks'][:, h:h + 1])),
                         reads=[ks + 'negm', 'sinks'], writes=[ks + 'es'])
                    P.op('dve', (lambda e, sm=sm: e.tensor_tensor(out=sm[:, 3:4], in0=sm[:, 3:4], in1=sm[:, 2:3], op=ALU.add)), reads=[ks + 'es', ks + 'rs'], writes=[ks + 'es'])
                    P.op('dve', (lambda e, h=h, sm=sm: e.reciprocal(out=rden[:, h:h + 1], in_=sm[:, 3:4])), reads=[ks + 'es'], writes=['rden%d' % h])

                    def fnT(e, nkb=nkb, e_x=e_x, tpo=tpo):
                        ins = None
                        for kb in range(nkb):
                            ins = e.transpose(out=tp[:, tpo + kb * 128:tpo + (kb + 1) * 128], in_=e_x[:, kb * 128:(kb + 1) * 128], identity=identb[:])
                        return ins
                    P.op('pe', fnT, reads=[ks + 'e', 'identb'], writes=['tp'])
                    P.op('act', (lambda e, nk=nk, eT_x=eT_x, tpo=tpo: e.copy(out=eT_x[:, 0:nk], in_=tp[:, tpo:tpo + nk])), writes=[ks + 'eT', 'tp'])

                    def fnV(e, g=g, ti=ti, nkb=nkb, sample=sample, SB=SB, eT_x=eT_x):
                        gs = slice(g * 64, g * 64 + 64)
                        po = SB[:, 448:512]
                        if not sample:
                            e.matmul(po, lhsT=eT_x[:, 0:128], rhs=vat[:, ti, gs], start=True, stop=False)
                            return e.matmul(po, lhsT=eT_x[:, 128:256], rhs=vat[:, ti + 1, gs], start=False, stop=True)
                        e.matmul(po, lhsT=eT_x[:, 0:128], rhs=vc[:, ti * 2, gs], start=True, stop=False)
                        e.matmul(po, lhsT=eT_x[:, 128:256], rhs=vc[:, ti * 2 + 1, gs], start=False, stop=False)
                        return e.matmul(po, lhsT=eT_x[:, 256:384], rhs=vat[:, ti + 1, gs], start=False, stop=True)
                    P.op('pe', fnV, reads=[ks + 'eT', 'vat', 'vc'], writes=[SBk])
                    P.op('dve', (lambda e, h=h, SB=SB: e.tensor_scalar(out=ob[:, h * 64:(h + 1) * 64], in0=SB[:, 448:512], scalar1=rden[:, h:h + 1], scalar2=None, op0=ALU.mult)),
                         reads=['rden%d' % h], writes=['ob%d' % h, SBk])
                P.merge_streams()
                if DBG.get('dump', 0):
                    out_idx.append(P.op('pool', (lambda e, gt=gt: e.dma_start(out=y_o[(gt + 4) * 128:(gt + 5) * 128, 1024:2048], in_=ob[:])), reads=['ob'] + ['ob%d' % h for h in range(16)], chan='o_dbg'))
                P.op('dve', (lambda e, ti=ti: e.tensor_tensor(out=yab[:], in0=ob[:], in1=sbg[:, ti, :], op=ALU.mult)), reads=['ob%d' % h for h in range(16)] + ['sbg'], writes=['yab'])

                if DBG.get('dump', 0):
                    out_idx.append(P.op('pool', (lambda e, gt=gt: e.dma_start(out=y_o[(gt + 8) * 128:(gt + 9) * 128, 1024:2048], in_=yab[:])), reads=['yab'], chan='o_dbg'))
                def fn(e):
                    ins = None
                    for k in range(8):
                        ins = e.transpose(out=tp[:, k * 128:(k + 1) * 128], in_=yab[:, k * 128:(k + 1) * 128], identity=identb[:])
                    return ins
                P.op('pe', fn, reads=['yab', 'identb'], writes=['tp'])
                P.op('act', (lambda e, tcs=tcs: e.copy(out=ybT[:, :, tcs], in_=tp[:, :].rearrange("p (k t) -> p k t", t=128))), writes=['ybT', 'tp'])


            if DBG.get('dump', 0):
                out_idx.append(P.op('pool', (lambda e: e.dma_start(out=y_o[12 * 128:13 * 128, :].rearrange('p (k t) -> p k t', t=TB), in_=yaT[:])), reads=['yaT'], chan='o_dbg'))
                out_idx.append(P.op('pool', (lambda e: e.dma_start(out=y_o[13 * 128:14 * 128, :].rearrange('p (k t) -> p k t', t=TB), in_=ybT[:])), reads=['ybT'], chan='o_dbg'))
                out_idx.append(P.op('pool', (lambda e: e.dma_start(out=y_o[14 * 128:15 * 128, :].rearrange('p (k t) -> p k t', t=TB), in_=hT[:, 0:8, :])), reads=['hT'], chan='o_dbg'))
            if DBG['stage'] <= 6:
                continue
            for i in range(8):
                slot = load_unit(b, nu(), w_in_d, 8832 + i * 256, 256, 16)
                pr = next_pair()
                for f in range(2):
                    mm_fm(slot, 16, f, hT, 'hT', pr[f])
                for f in range(2):
                    ai = pr[f]
                    P.op('act', (lambda e, ai=ai, f=f: e.activation(out=sgb[:, f, :], in_=A[ai][:, 0:TB], func=AF.Sigmoid)), writes=['sgb', akey(ai)])
                slot = load_unit(b, nu(), p_b_d, i * 256, 256, 8)
                pr = next_pair()
                for f in range(2):
                    mm_fm(slot, 8, f, ybT, 'ybT', pr[f])
                for f in range(2):
                    ai = pr[f]
                    P.op('dve', (lambda e, ai=ai, f=f: e.tensor_tensor(out=sgb[:, f, :], in0=sgb[:, f, :], in1=A[ai][:, 0:TB], op=ALU.mult)),
                         reads=['sgb'], writes=['sgb', akey(ai)])
                    P.op('pool', (lambda e, i=i, f=f: e.tensor_tensor(out=mergedT[:, i * 2 + f, :], in0=ta_all[:, i * 2 + f, :], in1=sgb[:, f, :], op=ALU.add)),
                         reads=['taall', 'sgb'], writes=['mergedT'])
            fence(['taall'], ['xr'])
            for ti in range(NT):
                gt = b * NT + ti
                P.op('sp', (lambda e, gt=gt, ti=ti: e.dma_start(out=xr[:, ti, :], in_=x_d[gt * 128:(gt + 1) * 128, :])),
                     writes=['xr'], chan='xr')
            P.begin_streams(2)
            if b + 1 < DBG['nblk']:
                P.set_stream(1)
                emit_S1(b + 1)
            P.set_stream(0)
            for i in range(8):
                slot = load_unit(b, nu(), w_o_d, i * 256, 256, 16)
                pr = next_pair()
                for ti in range(NT):
                    mm_tm(slot, 16, ti, mergedT, 'mergedT', pr[ti])
                for ti in range(NT):
                    ai = pr[ti]
                    P.op('dve', (lambda e, ai=ai, ti=ti, i=i: e.tensor_tensor(out=xr[:, ti, i * 256:(i + 1) * 256], in0=xr[:, ti, i * 256:(i + 1) * 256], in1=A[ai][:, 0:256], op=ALU.add)),
                         reads=['xr'], writes=['xr', akey(ai)])
            for ti in range(NT):
                gt = b * NT + ti
                P.op('dve', lambda e: e.memset(small[:, 0:1], 0.0), writes=['ssq'])
                P.op('act', (lambda e, ti=ti: e.activation(out=sa[:].rearrange("p t c -> p (t c)"), in_=xr[:, ti, :], func=AF.Square, accum_out=small[:, 0:1])),
                     reads=['xr', 'ssq'], writes=['sa', 'ssq'])
                P.op('dve', lambda e: e.tensor_scalar(out=small[:, 1:2], in0=small[:, 0:1], scalar1=1.0 / D, scalar2=RMS_EPS, op0=ALU.mult, op1=ALU.add),
                     reads=['ssq'], writes=['ms'])
                P.op('act', lambda e: e.activation(out=small[:, 2:3], in_=small[:, 1:2], func=AF.Sqrt), reads=['ms'], writes=['sq'])
                P.op('dve', lambda e: e.reciprocal(out=small[:, 3:4], in_=small[:, 2:3]), reads=['sq'], writes=['rstd'])
                P.op('dve', (lambda e, ti=ti: e.scalar_tensor_tensor(out=xr[:, ti, :], in0=xr[:, ti, :], scalar=small[:, 3:4], in1=ct['gfbc'][:], op0=ALU.mult, op1=ALU.mult)),
                     reads=['xr', 'rstd', 'gfbc'], writes=['xr'])
                P.op('pool', (lambda e, gt=gt, ti=ti: e.dma_start(out=y_o[gt * 128:(gt + 1) * 128, :], in_=xr[:, ti, :])),
                     reads=['xr'], writes=['xr_st'], chan='o_y', cb=out_idx)
            P.merge_streams()
            if b == NBLK - 2:
                out_idx.append(P.op('pool', lambda e: e.dma_start(out=shp_o, in_=plast[:]), reads=['plast'], chan='o_sh'))
            if sample:
                out_idx.append(P.op('pool', lambda e: e.dma_start(out=shs_o, in_=shs[:]), reads=['shs'], chan='o_sh'))
            assert st['u'] <= 80, st['u']

        P.wait_all('pool', out_idx)
        P.emit()
        build.stats = P.stats
    return nc


_CACHE = {}


def _consts():
    c = {}
    c['identf'] = np.eye(128, dtype=np.float32)
    s = np.arange(128)[:, None]
    t = np.arange(128)[None, :]
    same = (s // 64) == (t // 64)
    MUs = (same & (s < t)).astype(np.float32)
    MUi = (same & (s <= t)).astype(np.float32)
    c['MU4'] = np.concatenate([MUs, MUs, MUi, MUi], axis=1)
    c['MLs'] = (same & (t < s)).astype(np.float32)
    c['bones'] = same.astype(np.float32)
    s6 = np.arange(64)[:, None]
    t6 = np.arange(64)[None, :]
    mus = (s6 < t6).astype(np.float32)
    mui = (s6 <= t6).astype(np.float32)
    mls = (t6 < s6).astype(np.float32)
    m5 = np.concatenate([mus, mus, mui, mui, mls], axis=1)
    c['MU5'] = np.concatenate([m5, m5], axis=0)
    c['I2'] = np.concatenate([np.eye(64, dtype=np.float32)] * 2, axis=0)
    bo2 = np.zeros((128, 2), np.float32)
    bo2[:64, 0] = 1
    bo2[64:, 1] = 1
    c['bo2'] = bo2
    rm = np.ones((128, TB), np.float32)
    rm[:, ::64] = 0
    c['resetm'] = rm
    NEG = -1e30
    i = np.arange(128)[:, None]
    k = np.arange(256)[None, :]
    dch = (2 + i // 64) - (k // 64)
    vis = (dch >= 0) & (dch <= 2)
    DmP = np.where(vis, -np.abs(128 + i - k).astype(np.float32), NEG).astype(np.float32)
    c['DmP'] = DmP
    DmP0 = DmP.copy()
    DmP0[:, :128] = NEG
    c['DmP0'] = DmP0
    DmS = np.full((128, 384), NEG, np.float32)
    tt = np.arange(64)[:, None]
    kk = np.arange(128)[None, :]
    t2 = np.arange(64)[None, :]
    for sq in range(2):
        rows = slice(sq * 64, sq * 64 + 64)
        DmS[rows, sq * 128:(sq + 1) * 128] = -(128 + tt - kk).astype(np.float32)
        DmS[rows, 256 + sq * 64:256 + sq * 64 + 64] = -np.abs(tt - t2).astype(np.float32)
    c['DmS'] = DmS
    return c


def kernel(x_prompt, x_sample, state_wkv, state_shift, cache_k, cache_v, g_norm, w_in, mu_shift, w0,
           w_w_up, a0, w_a_up, k_k, k_a, r_k, lnx_w, lnx_b, sinks, p_a, p_b, w_o, g_final):
    f32 = np.float32
    A_ = lambda v: np.ascontiguousarray(np.asarray(v, dtype=f32))
    if 'nc' not in _CACHE:
        _CACHE['nc'] = build()
    nc = _CACHE['nc']
    cst = _consts()
    col = lambda v: A_(np.asarray(v, f32).reshape(-1, 128).T)
    shared = dict(cst)
    shared.update(
        w_in=A_(w_in[0]), p_a=A_(p_a[0]), p_b=A_(p_b[0]), w_o=A_(w_o[0]),
        gbc=A_(np.broadcast_to(np.asarray(g_norm[0], f32)[None, :], (128, D))),
        gfbc=A_(np.broadcast_to(np.asarray(g_final, f32)[None, :], (128, D))),
        lnxw=A_(np.broadcast_to(np.asarray(lnx_w[0], f32)[None, :], (128, 1024))),
        lnxb=A_(np.broadcast_to(np.asarray(lnx_b[0], f32)[None, :], (128, 1024))),
        muT=col(mu_shift[0]), w0c=col(w0[0]), a0c=col(a0[0]), kkc=col(k_k[0]), kac=col(k_a[0]),
        rkc=col(np.asarray(r_k[0], f32).reshape(-1)),
        sinks=A_(np.broadcast_to(np.asarray(sinks[0], f32)[None, :], (128, 16))),
        wup=A_(np.concatenate([np.asarray(w_w_up[0], f32), np.asarray(w_a_up[0], f32)], axis=0)),
    )
    xp = np.asarray(x_prompt, f32)
    xs_ = np.asarray(x_sample, f32)
    swkv = np.asarray(state_wkv[0], f32)
    ssh = np.asarray(state_shift[0], f32)
    ck = np.asarray(cache_k[0], f32)
    cvv = np.asarray(cache_v[0], f32)
    in_maps = []
    for c in range(8):
        sl = slice(4 * c, 4 * c + 4)
        m = dict(shared)
        m['x'] = A_(np.concatenate([xp[c], xs_[sl].reshape(256, D)], axis=0))
        sw = swkv[sl].reshape(4, 8, 2, 64, 64)
        m['swkv'] = A_(sw.transpose(0, 2, 4, 1, 3).reshape(4, 128, 8, 64))
        m['sshT'] = A_(ssh[sl].reshape(4, 25, 128).transpose(2, 1, 0))
        ckc = ck[sl]
        kt_ = ckc.transpose(3, 0, 2, 1)
        m['ckT'] = A_(np.concatenate([kt_, kt_], axis=0))
        m['cv'] = A_(cvv[sl].reshape(4, 128, 256).transpose(1, 0, 2))
        m['ck_raw'] = A_(ckc.reshape(4, 128, 256))
        m['cv_raw'] = A_(cvv[sl].reshape(4, 128, 256))
        in_maps.append(m)
    res = run_bass_kernel_spmd(nc, in_maps, core_ids=list(range(8)))
    R = res.results
    y_prompt = np.stack([R[c]['y'][:2048] for c in range(8)]).astype(f32)
    y_sample = np.concatenate([R[c]['y'][2048:].reshape(4, 64, D) for c in range(8)]).astype(f32)

    def unP(a):
        return a.reshape(2, 64, 8, 64).transpose(2, 0, 3, 1).reshape(16, 64, 64)
    wkv_p = np.stack([unP(R[c]['wkv_p']) for c in range(8)])[None].astype(f32)
    wkv_s = np.stack([unP(R[c]['wkv_s'][s]) for c in range(8) for s in range(4)])[None].astype(f32)
    shift_p = np.stack([R[c]['shift_p'].T.reshape(3200) for c in range(8)])[None].astype(f32)
    shift_s = np.stack([R[c]['shift_s'][:, :, s].T.reshape(3200) for c in range(8) for s in range(4)])[None].astype(f32)
    k_p = np.stack([R[c]['k_p'].reshape(128, 4, 64) for c in range(8)])[None].astype(f32)
    v_p = np.stack([R[c]['v_p'].reshape(128, 4, 64) for c in range(8)])[None].astype(f32)
    k_s = np.concatenate([R[c]['k_s'].reshape(4, 128, 4, 64) for c in range(8)])[None].astype(f32)
    v_s = np.concatenate([R[c]['v_s'].reshape(4, 128, 4, 64) for c in range(8)])[None].astype(f32)
    return (y_prompt, y_sample, wkv_p, shift_p, k_p, v_p, wkv_s, shift_s, k_s, v_s)
```

```python
import numpy as np
from contextlib import ExitStack
import concourse.bass as bass
import concourse.mybir as mybir
from concourse.bass_utils import run_bass_kernel_spmd

F32 = mybir.dt.float32
BF16 = mybir.dt.bfloat16
ALU = mybir.AluOpType
AF = mybir.ActivationFunctionType
AX = mybir.AxisListType

D = 2048
NTILES = 18
NT = 2
TB = NT * 128
NBLK = NTILES // NT
INW = 10880
RMS_EPS = 1e-6
LNX_EPS = 64e-5
C0 = float(np.exp(-0.5))
DBG = dict(nblk=NBLK, stage=99)
SLOPES = [float(2.0 ** (-(h + 1) / 2.0)) for h in range(16)]


class Prog:
    COMPUTE = ('pe', 'act', 'dve', 'pool')

    def __init__(self, nc):
        self.nc = nc
        self.ops = []
        self.last_w = {}
        self.readers = {}
        self.streams = None
        self.cur_stream = None

    def begin_streams(self, n):
        self.streams = [[] for _ in range(n)]
        self.cur_stream = None

    def set_stream(self, i):
        self.cur_stream = i

    def capture_start(self):
        assert self.streams is None
        self.streams = [[]]
        self.cur_stream = 0

    def capture_end(self):
        q = self.streams[0]
        self.streams, self.cur_stream = None, None
        return q

    def add_stream(self, q):
        self.streams.append(q)

    def merge_streams(self):
        streams, self.streams, self.cur_stream = self.streams, None, None
        order = []
        for si, q in enumerate(streams):
            L = len(q)
            for k in range(L):
                order.append(((k + 0.5) / L, si, k))
        order.sort()
        for _, si, k in order:
            a, kw, cb = streams[si][k]
            idx = self.op(*a, **kw)
            if cb is not None:
                cb.append(idx)

    def op(self, eng, fn, reads=(), writes=(), chan=None, cb=None):
        if getattr(self, 'cur_stream', None) is not None:
            self.streams[self.cur_stream].append(((eng, fn), dict(reads=reads, writes=writes, chan=chan), cb))
            return -1
        idx = len(self.ops)
        deps = {}
        for k in reads:
            d = self.last_w.get(k)
            if d is not None:
                deps[d] = True
        for k in writes:
            d = self.last_w.get(k)
            if d is not None:
                deps.setdefault(d, False)
            for r in self.readers.get(k, ()):
                deps.setdefault(r, False)
        deps.pop(idx, None)
        self.ops.append(dict(eng=eng, fn=fn, deps=deps, chan=chan))
        for k in reads:
            self.readers.setdefault(k, []).append(idx)
        for k in writes:
            self.last_w[k] = idx
            self.readers[k] = []
        return idx

    def wait_all(self, eng, idxs):
        idx = len(self.ops)
        self.ops.append(dict(eng=eng, fn=None, deps={d: True for d in idxs}, chan=None))
        return idx

    def _need_wait(self, x, d, raw):
        od, ox = self.ops[d], self.ops[x]
        if od['chan'] is not None:
            return True
        if od['eng'] != ox['eng']:
            return True
        if ox['chan'] is not None:
            return True
        if ox['eng'] == 'pe':
            return False
        return True

    def emit(self):
        nc = self.nc
        ops = self.ops
        needed = [False] * len(ops)
        for x, o in enumerate(ops):
            for d, raw in o['deps'].items():
                if self._need_wait(x, d, raw):
                    needed[d] = True
        chans = []
        for o in ops:
            if o['chan'] is not None and o['chan'] not in chans:
                chans.append(o['chan'])
        with ExitStack() as es:
            sems = {}
            for e in self.COMPUTE:
                sems[e] = es.enter_context(nc.semaphore('s_' + e))
            for c in chans:
                sems[('c', c)] = es.enter_context(nc.semaphore('c_' + str(c)))
            cnt = {k: 0 for k in sems}
            ev = [None] * len(ops)
            for x, o in enumerate(ops):
                if o['fn'] is None:
                    continue
                if o['chan'] is not None:
                    k = ('c', o['chan'])
                    cnt[k] += 16
                    ev[x] = (k, cnt[k])
                elif needed[x]:
                    k = o['eng']
                    cnt[k] += 1
                    ev[x] = (k, cnt[k])
            per_eng = {}
            for x, o in enumerate(ops):
                per_eng.setdefault(o['eng'], []).append(x)
            self.stats = {e: len(v) for e, v in per_eng.items()}
            self.stats['sem_max'] = dict((str(k), v) for k, v in cnt.items() if v > 30000)

            def run(e, ename):
                waited = {}
                for x in per_eng.get(ename, ()):
                    o = ops[x]
                    want = {}
                    for d, raw in o['deps'].items():
                        if not self._need_wait(x, d, raw):
                            continue
                        k, v = ev[d]
                        if v > want.get(k, 0):
                            want[k] = v
                    for k, v in want.items():
                        if v > waited.get(k, 0):
                            e.wait_ge(sems[k], v)
                            waited[k] = v
                    if o['fn'] is None:
                        continue
                    ins = o['fn'](e)
                    if ev[x] is not None:
                        k, v = ev[x]
                        ins.then_inc(sems[k], 16 if o['chan'] is not None else 1)

            with nc.Block() as block:
                @block.tensor
                def _(e):
                    run(e, 'pe')

                @block.scalar
                def _(e):
                    run(e, 'act')

                @block.vector
                def _(e):
                    run(e, 'dve')

                @block.gpsimd
                def _(e):
                    run(e, 'pool')

                @block.sync
                def _(e):
                    run(e, 'sp')


def build():
    nc = bass.Bass("TRN2", target_bir_lowering=False)

    def din(name, shape, dt=F32):
        return nc.dram_tensor(name, list(shape), dt, kind="ExternalInput").ap()

    def dout(name, shape, dt=F32):
        return nc.dram_tensor(name, list(shape), dt, kind="ExternalOutput").ap()

    x_d = din("x", [NTILES * 128, D])
    w_in_d = din("w_in", [D, INW])
    p_a_d = din("p_a", [1024, D])
    p_b_d = din("p_b", [1024, D])
    w_o_d = din("w_o", [D, D])
    cnames = dict(gbc=[128, D], gfbc=[128, D], lnxw=[128, 1024], lnxb=[128, 1024], muT=[128, 25],
                  w0c=[128, 8], a0c=[128, 8], kkc=[128, 8], kac=[128, 8], rkc=[128, 8],
                  sinks=[128, 16], identf=[128, 128], MU4=[128, 512], MLs=[128, 128], bones=[128, 128],
                  bo2=[128, 2], resetm=[128, TB], MU5=[128, 320], I2=[128, 64], DmP=[128, 256], DmP0=[128, 256], DmS=[128, 384],
                  sshT=[128, 25, 4])
    cd = {k: din(k, v) for k, v in cnames.items()}
    wup_d = din("wup", [128, 1024])
    swkv_d = din("swkv", [4, 128, 8, 64])
    ckT_d = din("ckT", [128, 4, 4, 128])
    cv_d = din("cv", [128, 4, 256])
    ck_raw = din("ck_raw", [4, 128, 256])
    cv_raw = din("cv_raw", [4, 128, 256])

    y_o = dout("y", [NTILES * 128, D])
    wkvp_o = dout("wkv_p", [128, 8, 64])
    wkvs_o = dout("wkv_s", [4, 128, 8, 64])
    shp_o = dout("shift_p", [128, 25])
    shs_o = dout("shift_s", [128, 25, 4])
    kp_o = dout("k_p", [128, 256])
    vp_o = dout("v_p", [128, 256])
    ks_o = dout("k_s", [4, 128, 256])
    vs_o = dout("v_s", [4, 128, 256])

    NUNITS = 0
    scr = nc.dram_tensor("wscr", [80, 128, 16 * 256], BF16).ap()

    es = ExitStack()
    with es:
        def sb(name, shape, dt=F32):
            return es.enter_context(nc.sbuf_tensor(name, list(shape), dt))

        def ps(name, shape, dt=F32):
            return es.enter_context(nc.psum_tensor(name, list(shape), dt))

        P = Prog(nc)
        ct = {k: sb("c_" + k, v) for k, v in cnames.items()}
        identb = sb("identb", [128, 128], BF16)
        wupb = sb("wupb", [128, 1024], BF16)
        kc = sb("kc", [128, 4, 4, 128], BF16)
        vc = sb("vc", [128, 4, 256], BF16)
        dummy = sb("dummy_t", [128, 8])
        small = sb("small", [128, 64])
        hT = sb("hT", [128, 16, TB], BF16)
        stage = [sb("stage%d" % i, [128, 8, 256]) for i in range(2)]
        wbf = [sb("wbf%d" % i, [128, 16, 256], BF16) for i in range(2)]
        wbf = wbf + [stage[i][:].rearrange("p k n -> p (k n)").bitcast(BF16).rearrange("p (k n) -> p k n", n=256) for i in range(2)]
        WK = ['wbf0', 'wbf1', 'stage0', 'stage1']
        xt = sb("xt", [128, D])
        hb = sb("hb", [128, D], BF16)
        plast = sb("plast", [128, 25])
        shs = sb("shs", [128, 25, 4])
        pT = sb("pT", [128, TB + 1])
        arenaA = sb("arenaA", [128, 23 * TB])
        xs = [arenaA[:, i * TB:(i + 1) * TB] for i in range(6)]
        tq = [[arenaA[:, (6 + s_ * 8 + i) * TB:(7 + s_ * 8 + i) * TB] for i in range(8)] for s_ in range(2)]
        tmpf = {11: arenaA[:, 22 * TB:23 * TB]}
        xr = arenaA[:, 0:NT * D].rearrange("p (t d) -> p t d", d=D)
        ta_all = arenaA[:, 0:16 * TB].rearrange("p (m t) -> p m t", t=TB)
        lora = sb("lora", [128, TB], BF16)
        xsB_t = sb("xsB", [128, 6 * TB])
        xsB = [xsB_t[:, i * TB:(i + 1) * TB] for i in range(6)]
        arenaC = sb("arenaC", [128, 4 * 8 * TB], BF16)
        opT = [arenaC[:, i * 8 * TB:(i + 1) * 8 * TB].rearrange("p (j t) -> p j t", t=TB) for i in range(4)]
        rT, aT, bT, kT = opT
        mergedT = arenaC[:, 0:16 * TB].rearrange("p (j t) -> p j t", t=TB)
        fin32 = arenaC[:, 16 * TB:32 * TB].bitcast(F32)
        sga = fin32[:, 0:2 * TB].rearrange("p (f t) -> p f t", t=TB)
        sgb = fin32[:, 2 * TB:4 * TB].rearrange("p (f t) -> p f t", t=TB)
        ta = fin32[:, 4 * TB:6 * TB].rearrange("p (f t) -> p f t", t=TB)
        gC = sb("gC", [128, 8, 2 * NT])
        vtok = sb("vtok", [128, NT, 1024], BF16)
        vtk = sb("vtk", [128, 8, 2 * NT, 64], BF16)
        I2b = sb("I2b", [128, 64], BF16)
        LNSS = [[sb("LNS%d_%d" % (k, i), [128, 192], BF16) for i in range(2)] for k in range(2)]
        bon = sb("bon", [128, NT, 16])
        kbtokS = [sb("kbtok%d" % i, [128, 128], BF16) for i in range(2)]
        MmS = [sb("Mm%d" % i, [128, 320], BF16) for i in range(2)]
        XUbS = [sb("XUb%d" % i, [128, 128], BF16) for i in range(2)]
        Pf = sb("Pf", [128, 8, 64])
        Pb = sb("Pb", [128, 8, 64], BF16)
        ysb = sb("ysb", [128, 1024])
        ysq = sb("ysq", [128, 1024])
        sa = sb("sa", [128, NT, 1024], BF16)
        sbg = sb("sbg", [128, NT, 1024], BF16)
        yab = sb("yab", [128, 1024], BF16)
        yaT = sb("yaT", [128, 8, TB], BF16)
        ybT = sb("ybT", [128, 8, TB], BF16)
        qT = sb("qT", [128, 8, TB], BF16)
        kTd = sb("kTd", [128, 4, 128 + TB], BF16)
        vat = sb("vat", [128, 1 + NT, 256], BF16)
        kvo = sb("kvo", [128, NT, 512])
        s_sbS = [sb("s_sb%d" % i, [128, 384]) for i in range(2)]
        e_sbS = [sb("e_sb%d" % i, [128, 384], BF16) for i in range(2)]
        eTS = [sb("eT%d" % i, [128, 384], BF16) for i in range(2)]
        smallS = [sb("smallS%d" % i, [128, 8]) for i in range(2)]
        ob = sb("ob", [128, 1024])
        rden = sb("rden", [128, 16])

        Aacc = [ps("A%d" % i, [128, 512]) for i in range(2)]
        A = [Aacc[i // 2][:, (i % 2) * 256:(i % 2) * 256 + 256] for i in range(4)]
        tp = ps("tp", [128, 1024], BF16)
        Mb = ps("Mb", [128, 512])
        Db = [ps("D%d" % i, [128, 512]) for i in range(2)]
        Cb = ps("Cb", [128, 512])
        Eb = ps("Eb", [128, 512])

        cidx = []
        for k in cnames:
            src = cd[k]
            cidx.append(P.op('pool', (lambda e, o=ct[k], s=src: e.dma_start(out=o[:], in_=s)), writes=[k], chan='const'))
        cidx.append(P.op('pool', lambda e: e.dma_start(out=wupb[:], in_=wup_d), writes=['wupb'], chan='const'))
        cidx.append(P.op('pool', lambda e: e.dma_start(out=kc[:], in_=ckT_d), writes=['kc'], chan='const'))
        cidx.append(P.op('pool', lambda e: e.dma_start(out=vc[:], in_=cv_d), writes=['vc'], chan='const'))
        for eng in ('pe', 'act', 'dve', 'pool'):
            P.wait_all(eng, cidx)
        P.op('dve', lambda e: e.tensor_copy(out=identb[:], in_=ct['identf'][:]), reads=['identf'], writes=['identb'])
        P.op('dve', lambda e: e.tensor_copy(out=I2b[:], in_=ct['I2'][:]), reads=['I2'], writes=['I2b'])
        P.op('dve', lambda e: e.memset(Pf[:], 0.0), writes=['Pf%d' % j for j in range(8)])
        P.op('dve', lambda e: e.memset(Pb[:], 0.0), writes=['Pb%d' % j for j in range(8)])
        P.op('dve', lambda e: e.memset(plast[:], 0.0), writes=['plast'])
        P.op('dve', lambda e: e.memset(kTd[:], 0.0), writes=['kTd'])
        P.op('dve', lambda e: e.memset(vat[:], 0.0), writes=['vat'])
        identf = ct['identf']

        st = dict(uid=0, sidx=0, acc=0, cast=0)
        out_idx = []
        ARENA_PREP = (['xs%d' % i for i in range(6)] + ['tq%d_%d' % (s_, i) for s_ in range(2) for i in range(8)] + ['tmp11']
                      + ['rT', 'aT', 'bT', 'kT'])
        ARENA_FIN = ['xr', 'mergedT', 'sga', 'sgb', 'ta', 'taall']

        def fence(after, before):
            P.op('pool', lambda e: e.memset(dummy[0:1, 0:1], 0.0), writes=list(after) + list(before))

        def load_unit(b, u, W, c0, n, KT, dup=False):
            slot = st['uid'] % (2 if b == 0 else 4)
            st['uid'] += 1
            wk = WK[slot]
            ncols = 256 if dup else n
            if b == 0:
                for half in range(KT // 8):
                    si = st['sidx'] % 2
                    st['sidx'] += 1
                    src = W[half * 1024:(half + 1) * 1024, c0:c0 + n].rearrange("(k p) n -> p k n", p=128)
                    P.op('sp', (lambda e, si=si, src=src: e.dma_start(out=stage[si][:, :, 0:n], in_=src)),
                         writes=['stage%d' % si], chan='stg%d' % si)
                    ceng = ('dve', 'act')[st['cast'] % 2]
                    st['cast'] += 1
                    if not dup:
                        dst = wbf[slot][:, half * 8:(half + 1) * 8, 0:n]
                        srcs = stage[si][:, :, 0:n]
                        if ceng == 'act':
                            P.op('act', (lambda e, dst=dst, srcs=srcs: e.copy(out=dst, in_=srcs)),
                                 reads=['stage%d' % si], writes=[wk])
                        else:
                            P.op('dve', (lambda e, dst=dst, srcs=srcs: e.tensor_copy(out=dst, in_=srcs)),
                                 reads=['stage%d' % si], writes=[wk])
                    else:
                        for dd in range(2):
                            dst = wbf[slot][:, half * 8:(half + 1) * 8, :].rearrange(
                                "p k (g d c) -> p k g d c", g=2, d=2)[:, :, :, dd, :]
                            srcs = stage[si][:, :, 0:128].rearrange("p k (g c) -> p k g c", g=2)
                            P.op('pool', (lambda e, dst=dst, srcs=srcs: e.tensor_copy(out=dst, in_=srcs)),
                                 reads=['stage%d' % si], writes=[wk])
                P.op('pool', (lambda e, u=u, slot=slot: e.dma_start(
                    out=scr[u % DBG.get("umod", 80), :, 0:KT * ncols].rearrange("p (k n) -> p k n", n=ncols),
                    in_=wbf[slot][:, 0:KT, 0:ncols])),
                    reads=[wk], writes=['scr%d' % u], chan='wst%d' % slot)
            else:
                P.op('sp', (lambda e, u=u, slot=slot: e.dma_start(
                    out=wbf[slot][:, 0:KT, 0:ncols],
                    in_=scr[u % DBG.get("umod", 80), :, 0:KT * ncols].rearrange("p (k n) -> p k n", n=ncols))),
                    reads=['scr%d' % u], writes=[wk], chan='wld%d' % slot)
            return slot

        def nu():
            st['u'] += 1
            return st['u'] - 1

        def next_pair():
            if st.get('pb0'):
                return (0, 1)
            if st.get('pb') is not None:
                return (2 * st['pb'], 2 * st['pb'] + 1)
            k = st['acc'] % 2
            st['acc'] += 1
            return (2 * k, 2 * k + 1)

        def next_acc():
            return next_pair()[0]

        def akey(ai):
            return 'PB%d' % (ai // 2)

        def mm_fm(slot, KT, f, rhsT, rkey, ai, ncol=TB):
            def fn(e):
                ins = None
                for kt in range(KT):
                    ins = e.matmul(A[ai][:, 0:ncol], lhsT=wbf[slot][:, kt, f * 128:(f + 1) * 128],
                                   rhs=rhsT[:, kt, 0:ncol], start=(kt == 0), stop=(kt == KT - 1))
                return ins
            P.op('pe', fn, reads=[WK[slot], rkey], writes=[akey(ai)])

        def mm_tm(slot, KT, ti, lhs_tile, lkey, ai, n=256):
            def fn(e):
                ins = None
                for kt in range(KT):
                    ins = e.matmul(A[ai][:, 0:n], lhsT=lhs_tile[:, kt, ti * 128:(ti + 1) * 128],
                                   rhs=wbf[slot][:, kt, 0:n], start=(kt == 0), stop=(kt == KT - 1))
                return ins
            P.op('pe', fn, reads=[WK[slot], lkey], writes=[akey(ai)])

        def mmk(e, out, lhsT, rhs, kbase):
            if kbase == 0:
                return e.matmul(out, lhsT=lhsT, rhs=rhs, start=True, stop=True)
            e.matmul(out[0:64], lhsT=lhsT[:, 0:64], rhs=rhs, start=True, stop=True)
            return e.matmul(out[64:128], lhsT=lhsT[:, 64:128], rhs=rhs, start=True, stop=True)

        def chunk3(ap):
            return ap.rearrange("p (c t) -> p c t", t=64)

        for b in range(DBG['nblk']):
            sample = (b == NBLK - 1)
            st['u'] = 0
            def emit_S1(bb):
                for ti in range(NT):
                    gt = bb * NT + ti
                    P.op('sp', (lambda e, gt=gt: e.dma_start(out=xt[:], in_=x_d[gt * 128:(gt + 1) * 128, :])),
                         writes=['xt'], chan='xt')
                    P.op('dve', lambda e: e.memset(small[:, 56:57], 0.0), writes=['s1ssq'])
                    P.op('act', lambda e: e.activation(out=hb[:], in_=xt[:], func=AF.Square, accum_out=small[:, 56:57]),
                         reads=['xt', 's1ssq'], writes=['hb', 's1ssq'])
                    P.op('dve', lambda e: e.tensor_scalar(out=small[:, 57:58], in0=small[:, 56:57], scalar1=1.0 / D,
                                                          scalar2=RMS_EPS, op0=ALU.mult, op1=ALU.add),
                         reads=['s1ssq'], writes=['s1ms'])
                    P.op('act', lambda e: e.activation(out=small[:, 58:59], in_=small[:, 57:58], func=AF.Sqrt),
                         reads=['s1ms'], writes=['s1sq'])
                    P.op('dve', lambda e: e.reciprocal(out=small[:, 59:60], in_=small[:, 58:59]), reads=['s1sq'], writes=['s1rstd'])
                    P.op('dve', lambda e: e.scalar_tensor_tensor(out=hb[:], in0=xt[:], scalar=small[:, 59:60],
                                                                 in1=ct['gbc'][:], op0=ALU.mult, op1=ALU.mult),
                         reads=['xt', 's1rstd', 'gbc'], writes=['hb'])
                    for half in range(2):
                        def fn(e, half=half):
                            ins = None
                            for k in range(8):
                                kt = half * 8 + k
                                ins = e.transpose(out=tp[:, k * 128:(k + 1) * 128], in_=hb[:, kt * 128:(kt + 1) * 128],
                                                  identity=identb[:])
                            return ins
                        P.op('pe', fn, reads=['hb', 'identb'], writes=['tp'])
                        dst = hT[:, half * 8:(half + 1) * 8, ti * 128:(ti + 1) * 128]
                        srcv = tp[:, :].rearrange("p (k t) -> p k t", t=128)
                        if half == 0:
                            P.op('act', (lambda e, dst=dst, srcv=srcv: e.copy(out=dst, in_=srcv)), writes=['hT', 'tp'])
                        else:
                            P.op('dve', (lambda e, dst=dst, srcv=srcv: e.tensor_copy(out=dst, in_=srcv)), writes=['hT', 'tp'])

            if b == 0:
                emit_S1(0)

            if DBG['stage'] <= 1:
                continue
            fence(ARENA_FIN, ARENA_PREP)

            def shift(f, ai, xs_ap, xkey):
                P.op('act', (lambda e: e.copy(out=pT[:, 1:TB + 1], in_=A[ai][:, 0:TB])), writes=['pT', akey(ai)])
                if not sample:
                    P.op('dve', (lambda e: e.tensor_copy(out=pT[:, 0:1], in_=plast[:, f:f + 1])), reads=['plast'], writes=['pT'])
                    P.op('dve', (lambda e: e.tensor_copy(out=plast[:, f:f + 1], in_=pT[:, TB:TB + 1])), reads=['pT'], writes=['plast'])
                else:
                    P.op('dve', (lambda e: e.memset(pT[:, 0:1], 0.0)), writes=['pT'])
                    P.op('dve', (lambda e: e.tensor_copy(out=shs[:, f, :], in_=chunk3(pT[:, 1:TB + 1])[:, :, 63])),
                         reads=['pT'], writes=['shs'])
                t0 = tmpf[11]
                P.op('pool', (lambda e: e.tensor_tensor(out=t0, in0=pT[:, 0:TB], in1=pT[:, 1:TB + 1], op=ALU.subtract)),
                     reads=['pT'], writes=['tmp11'])
                if sample:
                    P.op('dve', (lambda e: e.tensor_tensor(out=chunk3(t0)[:, :, 0], in0=ct['sshT'][:, f, :],
                                                           in1=chunk3(pT[:, 1:TB + 1])[:, :, 0], op=ALU.subtract)),
                         reads=['pT', 'sshT', 'tmp11'], writes=['tmp11'])
                P.op('dve', (lambda e: e.scalar_tensor_tensor(out=xs_ap, in0=t0, scalar=ct['muT'][:, f:f + 1],
                                                              in1=pT[:, 1:TB + 1], op0=ALU.mult, op1=ALU.add)),
                     reads=['tmp11', 'pT', 'muT'], writes=[xkey])

            slot = load_unit(b, nu(), w_in_d, 3072, 128, 16)
            ai = next_acc()
            mm_fm(slot, 16, 0, hT, 'hT', ai)
            shift(24, ai, xs[0], 'xs0')
            P.op('act', lambda e: e.activation(out=lora[0:64, :], in_=xs[0][0:64, :], func=AF.Tanh), reads=['xs0'], writes=['lora'])
            P.op('dve', lambda e: e.tensor_copy(out=lora[64:128, :], in_=xs[0][64:128, :]), reads=['xs0'], writes=['lora'])

            XSETS = [(xs, ['xs%d' % i for i in range(6)]), (xsB, ['xb%d' % i for i in range(6)])]

            def emit_proj(g2, XS, XK):
                for kind in range(3):
                    slot = load_unit(b, nu(), w_in_d, kind * 1024 + g2 * 256, 256, 16)
                    pr = next_pair()
                    for f in range(2):
                        mm_fm(slot, 16, f, hT, 'hT', pr[f])
                    for f in range(2):
                        shift(kind * 8 + g2 * 2 + f, pr[f], XS[kind * 2 + f], XK[kind * 2 + f])

            def prep_pair(j, sx, xr_, xk_, xv_, kr, kk_, kv):
                T = tq[sx]
                K = ['tq%d_%d' % (sx, q) for q in range(8)]
                sg, cs, eg, eig, alr, kk2, kkn, b32 = T
                k_sg, k_cs, k_eg, k_eig, k_alr, k_kk2, k_kkn, k_b32 = K
                egm, k_egm = cs, k_cs
                rn, k_rn = kk2, k_kk2
                t1, k_t1 = sg, k_sg
                jc = slice(j * 128, (j + 1) * 128)
                if sx == 0:
                    A1, A2, A3 = A[2][:, 0:TB], A[3][:, 0:TB], A[2][:, 0:TB]
                    ak = 'PB1'
                else:
                    A1, A2, A3 = Mb[:, 0:TB], Mb[:, 256:256 + TB], Mb[:, 0:TB]
                    ak = 'Mb'
                tv = sx * 128
                tk = 256 + sx * 256
                P.op('pe', (lambda e: e.matmul(A1, lhsT=wupb[0:64, jc], rhs=lora[0:64, :], start=True, stop=True)),
                     reads=['wupb', 'lora'], writes=[ak])
                P.op('pe', (lambda e: mmk(e, A2, wupb[64:128, jc], lora[64:128, :], 64)),
                     reads=['wupb', 'lora'], writes=[ak])
                P.op('act', (lambda e: e.activation(out=sg, in_=A1, func=AF.Sigmoid, bias=ct['w0c'][:, j:j + 1])),
                     reads=['w0c'], writes=[k_sg, ak])
                P.op('act', (lambda e: e.activation(out=alr, in_=A2, func=AF.Sigmoid, bias=ct['a0c'][:, j:j + 1])),
                     reads=['a0c'], writes=[k_alr, ak])
                P.op('dve', (lambda e: e.tensor_tensor_scan(out=cs, data0=ct['resetm'][:], data1=sg, initial=0.0, op0=ALU.mult, op1=ALU.add)),
                     reads=[k_sg, 'resetm'], writes=[k_cs])
                P.op('act', (lambda e: e.activation(out=eg, in_=cs, func=AF.Exp, scale=-C0)), reads=[k_cs], writes=[k_eg])
                P.op('act', (lambda e: e.activation(out=eig, in_=cs, func=AF.Exp, scale=C0)), reads=[k_cs], writes=[k_eig])
                P.op('dve', (lambda e: e.tensor_tensor(out=t1, in0=cs, in1=sg, op=ALU.subtract)), reads=[k_cs, k_sg], writes=[k_t1])
                P.op('act', (lambda e: e.activation(out=egm, in_=t1, func=AF.Exp, scale=-C0)), reads=[k_t1], writes=[k_egm])
                P.op('dve', (lambda e: e.tensor_copy(out=gC[:, j, :], in_=chunk3(eg)[:, :, 63])), reads=[k_eg], writes=['gC%d' % j])
                P.op('act', (lambda e: e.activation(out=kk2, in_=xk_, func=AF.Square, scale=ct['kkc'][:, j:j + 1])),
                     reads=[kk_, 'kkc'], writes=[k_kk2])
                P.op('pe', (lambda e: e.matmul(A3, lhsT=ct['bones'][:], rhs=kk2, start=True, stop=True)),
                     reads=['bones', k_kk2], writes=[ak])
                P.op('act', (lambda e: e.activation(out=rn, in_=A3, func=AF.Sqrt)), writes=[k_rn, ak])
                P.op('dve', (lambda e: e.tensor_scalar(out=rn, in0=rn, scalar1=1e-12, scalar2=None, op0=ALU.max)), reads=[k_rn], writes=[k_rn])
                P.op('dve', (lambda e: e.reciprocal(out=rn, in_=rn)), reads=[k_rn], writes=[k_rn])
                P.op('dve', (lambda e: e.scalar_tensor_tensor(out=kkn, in0=xk_, scalar=ct['kkc'][:, j:j + 1], in1=rn, op0=ALU.mult, op1=ALU.mult)),
                     reads=[kk_, 'kkc', k_rn], writes=[k_kkn])
                P.op('dve', (lambda e: e.tensor_scalar(out=t1, in0=alr, scalar1=-1.0, scalar2=ct['kac'][:, j:j + 1], op0=ALU.add, op1=ALU.mult)),
                     reads=[k_alr, 'kac'], writes=[k_t1])
                P.op('dve', (lambda e: e.scalar_tensor_tensor(out=t1, in0=t1, scalar=1.0, in1=xk_, op0=ALU.add, op1=ALU.mult)),
                     reads=[k_t1, kk_], writes=[k_t1])
                P.op('dve', (lambda e: e.tensor_tensor(out=rT[:, j, :], in0=xr_, in1=eg, op=ALU.mult)), reads=[kr, k_eg], writes=['rT'])
                P.op('dve', (lambda e: e.scalar_tensor_tensor(out=aT[:, j, :], in0=kkn, scalar=-1.0, in1=egm, op0=ALU.mult, op1=ALU.mult)),
                     reads=[k_kkn, k_egm], writes=['aT'])
                P.op('dve', (lambda e: e.tensor_tensor(out=b32, in0=kkn, in1=alr, op=ALU.mult)), reads=[k_kkn, k_alr], writes=[k_b32])
                P.op('dve', (lambda e: e.tensor_tensor(out=bT[:, j, :], in0=b32, in1=eig, op=ALU.mult)), reads=[k_b32, k_eig], writes=['bT'])
                P.op('dve', (lambda e: e.tensor_tensor(out=kT[:, j, :], in0=t1, in1=eig, op=ALU.mult)), reads=[k_t1, k_eig], writes=['kT'])
                P.op('dve', (lambda e: e.scalar_tensor_tensor(out=kk2, in0=xr_, scalar=ct['rkc'][:, j:j + 1], in1=t1, op0=ALU.mult, op1=ALU.mult)),
                     reads=[kr, 'rkc', k_t1], writes=[k_kk2])
                vb = b32.bitcast(BF16)[:, 0:TB]
                P.op('act', (lambda e: e.copy(out=vb, in_=xv_)), reads=[kv], writes=[k_b32])

                def fnVT(e):
                    ins = None
                    for ci_ in range(2 * NT):
                        for hp in (slice(0, 64), slice(64, 128)):
                            ins = e.transpose(out=tp[hp, tk + ci_ * 64:tk + ci_ * 64 + 64], in_=vb[hp, ci_ * 64:(ci_ + 1) * 64], identity=identb[hp, hp])
                    return ins
                P.op('pe', fnVT, reads=[k_b32, 'identb'], writes=['tp'])
                P.op('act', (lambda e: e.copy(out=vtk[:, j, :, :], in_=tp[:, tk:tk + 2 * NT * 64].rearrange("p (c v) -> p c v", v=64))),
                     writes=['vtk', 'tp'])
                for ti in range(NT):
                    tcs = slice(ti * 128, (ti + 1) * 128)
                    P.op('pe', (lambda e, ti=ti, tcs=tcs: e.matmul(Eb[:, ti * 16 + j * 2:ti * 16 + j * 2 + 2], lhsT=kk2[:, tcs], rhs=ct['bo2'][:], start=True, stop=True)),
                         reads=[k_kk2, 'bo2'], writes=['Eb'])
                    P.op('pe', (lambda e, tcs=tcs: e.transpose(out=tp[:, tv:tv + 128], in_=vb[:, tcs], identity=identb[:])),
                         reads=[k_b32, 'identb'], writes=['tp'])
                    P.op('act', (lambda e, ti=ti: e.copy(out=vtok[:, ti, jc], in_=tp[:, tv:tv + 128])), writes=['vtok', 'tp'])

            for it in range(5):
                P.begin_streams(3)
                if it < 4:
                    P.set_stream(0)
                    st['pb'] = 0
                    emit_proj(it, *XSETS[it % 2])
                    st['pb'] = None
                if it > 0:
                    XS, XK = XSETS[(it - 1) % 2]
                    for jj in range(2):
                        P.set_stream(1 + jj)
                        prep_pair((it - 1) * 2 + jj, jj, XS[jj], XS[2 + jj], XS[4 + jj], XK[jj], XK[2 + jj], XK[4 + jj])
                P.merge_streams()
            P.op('dve', lambda e: e.tensor_copy(out=bon[:].rearrange("p t h -> p (t h)"), in_=Eb[:, 0:NT * 16]), writes=['bon', 'Eb'])

            def aux_ga(i):
                slot = load_unit(b, nu(), w_in_d, 3200 + i * 256, 256, 16)
                pr = next_pair()
                for ti in range(NT):
                    mm_tm(slot, 16, ti, hT, 'hT', pr[ti])
                for ti in range(NT):
                    ai = pr[ti]
                    P.op('act', (lambda e, ai=ai, ti=ti, i=i: e.activation(out=sa[:, ti, i * 256:(i + 1) * 256], in_=A[ai][:, 0:256], func=AF.Silu)),
                         writes=['sa', akey(ai)])

            def aux_q(i):
                slot = load_unit(b, nu(), w_in_d, 4224 + i * 256, 256, 16)
                pr = next_pair()
                for f in range(2):
                    mm_fm(slot, 16, f, hT, 'hT', pr[f])
                for f in range(2):
                    ai = pr[f]
                    P.op('act', (lambda e, ai=ai, i=i, f=f: e.activation(out=qT[:, i * 2 + f, :], in_=A[ai][:, 0:TB], func=AF.Copy, scale=0.125)),
                         writes=['qT', akey(ai)])

            def aux_kd(i):
                slot = load_unit(b, nu(), w_in_d, 5248 + i * 128, 128, 16, dup=True)
                pr = next_pair()
                for f in range(2):
                    mm_fm(slot, 16, f, hT, 'hT', pr[f])
                for f in range(2):
                    ai = pr[f]
                    P.op('dve', (lambda e, ai=ai, i=i, f=f: e.tensor_copy(out=kTd[:, i * 2 + f, 128:128 + TB], in_=A[ai][:, 0:TB])),
                         writes=['kTd', akey(ai)])

            def aux_kv(i):
                slot = load_unit(b, nu(), w_in_d, 5248 + i * 256, 256, 16)
                pr = next_pair()
                for ti in range(NT):
                    mm_tm(slot, 16, ti, hT, 'hT', pr[ti])
                for ti in range(NT):
                    ai = pr[ti]
                    P.op('act', (lambda e, ai=ai, ti=ti, i=i: e.copy(out=kvo[:, ti, i * 256:(i + 1) * 256], in_=A[ai][:, 0:256])),
                         writes=['kvo', akey(ai)])
                    if i == 1:
                        P.op('act', (lambda e, ai=ai, ti=ti: e.copy(out=vat[:, 1 + ti, :], in_=A[ai][:, 0:256])),
                             writes=['vat', akey(ai)])

            def aux_gb(i):
                slot = load_unit(b, nu(), w_in_d, 5760 + i * 256, 256, 16)
                pr = next_pair()
                for ti in range(NT):
                    mm_tm(slot, 16, ti, hT, 'hT', pr[ti])
                for ti in range(NT):
                    ai = pr[ti]
                    P.op('act', (lambda e, ai=ai, ti=ti, i=i: e.activation(out=sbg[:, ti, i * 256:(i + 1) * 256], in_=A[ai][:, 0:256], func=AF.Silu)),
                         writes=['sbg', akey(ai)])

            def aux_carry():
                if 0 < b and not sample:
                    P.op('pool', lambda e: e.tensor_copy(out=kTd[:, :, 0:128], in_=kTd[:, :, TB:TB + 128]), reads=['kTd'], writes=['kTd'])
                    P.op('pool', lambda e: e.tensor_copy(out=vat[:, 0, :], in_=vat[:, NT, :]), reads=['vat'], writes=['vat'])

            AUX = [
                [lambda: aux_ga(0), lambda: aux_ga(1), lambda: aux_ga(2), lambda: aux_ga(3)],
                [lambda: aux_q(0), lambda: aux_q(1), lambda: aux_q(2), lambda: aux_q(3)],
                [aux_carry, lambda: aux_kd(0), lambda: aux_kd(1), lambda: aux_kv(0), lambda: aux_kv(1)],
                [lambda: aux_gb(0), lambda: aux_gb(1), lambda: aux_gb(2), lambda: aux_gb(3)],
            ]

            if DBG['stage'] <= 3:
                continue
            H2 = (slice(0, 64), slice(64, 128))
            PS_PENDING = []
            PS1_PENDING = []
            for ti in range(NT):
                gt = b * NT + ti
                tcs = slice(ti * 128, (ti + 1) * 128)
                for c in range(2):
                    cp = slice(c * 64, c * 64 + 64)
                    cc = slice(ti * 128 + c * 64, ti * 128 + c * 64 + 64)
                    ci = ti * 2 + c
                    P.begin_streams(3)
                    if ti == 1 and c == 0 and PS_PENDING:
                        P.add_stream(PS_PENDING.pop())
                    P.set_stream(2)
                    st['pb0'] = True
                    for task in AUX[ci]:
                        task()
                    st['pb0'] = False
                    for j in range(8):
                        sx = (j % 2) if DBG.get('ss', 1) else 0
                        P.set_stream(sx)
                        kbtok, Mmx, LNS, XUb = kbtokS[sx], MmS[sx], LNSS[sx], XUbS[sx]
                        MC = Mb if sx == 0 else Cb
                        MCk = 'Mb' if sx == 0 else 'Cb'
                        DD = Db[0][:, 0:192] if sx == 0 else Eb[:, 192:384]
                        DDk = 'D0' if sx == 0 else 'Eb'
                        kX, kM, kL = 'X%d' % sx, 'Mm%d' % sx, 'LNS%d_' % sx
                        if sample:
                            seq = ti * 2 + c
                            P.op('sp', (lambda e, seq=seq, j=j: e.dma_start(out=Pf[:, j, :], in_=swkv_d[seq, :, j, :])),
                                 writes=['Pf%d' % j], chan='pst%d' % j)
                            P.op('dve', (lambda e, j=j: e.tensor_copy(out=Pb[:, j, :], in_=Pf[:, j, :])), reads=['Pf%d' % j], writes=['Pb%d' % j])
                        tpo = sx * 128

                        def fnT(e, j=j, cc=cc, tpo=tpo):
                            ins = None
                            for hp in H2:
                                e.transpose(out=tp[hp, tpo:tpo + 64], in_=kT[hp, j, cc], identity=identb[hp, hp])
                                ins = e.transpose(out=tp[hp, tpo + 64:tpo + 128], in_=bT[hp, j, cc], identity=identb[hp, hp])
                            return ins
                        P.op('pe', fnT, reads=['kT', 'bT', 'identb'], writes=['tp'])
                        P.op('act', (lambda e, kbtok=kbtok, tpo=tpo: e.copy(out=kbtok[:, 0:128], in_=tp[:, tpo:tpo + 128])), writes=['kbtok%d' % sx, 'tp'])

                        def fnM(e, j=j, cc=cc, MC=MC):
                            ins = None
                            for hp in H2:
                                e.matmul(MC[hp, 0:64], lhsT=bT[hp, j, cc], rhs=aT[hp, j, cc], start=True, stop=True)
                                e.matmul(MC[hp, 64:128], lhsT=kT[hp, j, cc], rhs=aT[hp, j, cc], start=True, stop=True)
                                e.matmul(MC[hp, 128:192], lhsT=bT[hp, j, cc], rhs=rT[hp, j, cc], start=True, stop=True)
                                e.matmul(MC[hp, 192:256], lhsT=kT[hp, j, cc], rhs=rT[hp, j, cc], start=True, stop=True)
                                ins = e.matmul(MC[hp, 256:320], lhsT=aT[hp, j, cc], rhs=bT[hp, j, cc], start=True, stop=True)
                            return ins
                        P.op('pe', fnM, reads=['aT', 'bT', 'kT', 'rT'], writes=[MCk])
                        P.op('dve', (lambda e, Mmx=Mmx, MC=MC: e.tensor_tensor(out=Mmx[:, 0:320], in0=MC[:, 0:320], in1=ct['MU5'][:], op=ALU.mult)),
                             reads=['MU5'], writes=[kM, MCk])
                        P.op('pool', (lambda e, LNS=LNS, Mmx=Mmx: e.tensor_tensor(out=LNS[0][:, 128:192], in0=Mmx[:, 0:64], in1=I2b[:], op=ALU.add)),
                             reads=[kM, 'I2b'], writes=[kL + '0'])
                        for lvl in range(1, 7):
                            cur, nxt = (lvl - 1) % 2, lvl % 2
                            if lvl == 1:
                                Lc, Nc = Mmx[:, 256:320], Mmx[:, 0:64]
                                rk = [kM, kL + '0']
                            else:
                                Lc, Nc = LNS[cur][:, 0:64], LNS[cur][:, 64:128]
                                rk = [kL + str(cur)]
                            Sc = LNS[cur][:, 128:192]

                            def fnD(e, Lc=Lc, Nc=Nc, Sc=Sc, lvl=lvl, DD=DD):
                                ins = None
                                for hp in H2:
                                    if lvl < 6:
                                        e.matmul(DD[hp, 0:64], lhsT=Nc[hp], rhs=Lc[hp], start=True, stop=True)
                                    if lvl < 5:
                                        e.matmul(DD[hp, 64:128], lhsT=Lc[hp], rhs=Nc[hp], start=True, stop=True)
                                    if lvl == 1:
                                        ins = e.matmul(DD[hp, 128:192], lhsT=I2b[hp], rhs=Sc[hp], start=True, stop=True)
                                    else:
                                        e.matmul(DD[hp, 128:192], lhsT=I2b[hp], rhs=Sc[hp], start=True, stop=False)
                                        ins = e.matmul(DD[hp, 128:192], lhsT=Lc[hp], rhs=Sc[hp], start=False, stop=True)
                                return ins
                            P.op('pe', fnD, reads=rk + ['I2b'], writes=[DDk])
                            lo = 0 if lvl < 6 else 128
                            if (lvl + sx) % 2 == 1:
                                P.op('dve', (lambda e, LNS=LNS, nxt=nxt, lo=lo, DD=DD: e.tensor_copy(out=LNS[nxt][:, lo:192], in_=DD[:, lo:192])),
                                     writes=[kL + str(nxt), DDk])
                            else:
                                P.op('act', (lambda e, LNS=LNS, nxt=nxt, lo=lo, DD=DD: e.copy(out=LNS[nxt][:, lo:192], in_=DD[:, lo:192])),
                                     writes=[kL + str(nxt), DDk])

                        def fnX(e, j=j, cc=cc, ci=ci, MC=MC, Mmx=Mmx):
                            ins = None
                            for hp in H2:
                                e.matmul(MC[hp, 320:384], lhsT=aT[hp, j, cc], rhs=Pb[hp, j, :], start=True, stop=False)
                                ins = e.matmul(MC[hp, 320:384], lhsT=Mmx[hp, 64:128], rhs=vtk[hp, j, ci, :], start=False, stop=True)
                            return ins
                        P.op('pe', fnX, reads=['aT', 'Pb%d' % j, kM, 'vtk'], writes=[MCk])
                        P.op('dve', (lambda e, XUb=XUb, MC=MC: e.tensor_copy(out=XUb[:, 0:64], in_=MC[:, 320:384])), writes=[kX + 'x', MCk])

                        def fnU(e, MC=MC, LNS=LNS, XUb=XUb):
                            ins = None
                            for hp in H2:
                                ins = e.matmul(MC[hp, 384:448], lhsT=LNS[0][hp, 128:192], rhs=XUb[hp, 0:64], start=True, stop=True)
                            return ins
                        P.op('pe', fnU, reads=[kX + 'x', kL + '0'], writes=[MCk])
                        P.op('act', (lambda e, XUb=XUb, MC=MC: e.copy(out=XUb[:, 64:128], in_=MC[:, 384:448])), writes=[kX + 'u', MCk])

                        def fnO(e, j=j, cp=cp, cc=cc, ci=ci, Mmx=Mmx, XUb=XUb):
                            ins = None
                            for hh, hp in enumerate(H2):
                                ob_ = Db[1][cp, j * 64:j * 64 + 64] if hh == 0 else Aacc[1][cp, j * 64:j * 64 + 64]
                                e.matmul(ob_, lhsT=rT[hp, j, cc], rhs=Pb[hp, j, :], start=True, stop=False)
                                e.matmul(ob_, lhsT=Mmx[hp, 128:192], rhs=XUb[hp, 64:128], start=False, stop=False)
                                ins = e.matmul(ob_, lhsT=Mmx[hp, 192:256], rhs=vtk[hp, j, ci, :], start=False, stop=True)
                            return ins
                        P.op('pe', fnO, reads=['rT', 'Pb%d' % j, kM, kX + 'u', 'vtk'], writes=['D1', 'PB1'])

                        def fnP(e, j=j, ci=ci, MC=MC, kbtok=kbtok, XUb=XUb):
                            ins = None
                            for hp in H2:
                                e.matmul(MC[hp, 448:512], lhsT=identf[hp, hp], rhs=Pf[hp, j, :], start=True, stop=False)
                                e.matmul(MC[hp, 448:512], lhsT=kbtok[hp, 64:128], rhs=XUb[hp, 64:128], start=False, stop=False)
                                ins = e.matmul(MC[hp, 448:512], lhsT=kbtok[hp, 0:64], rhs=vtk[hp, j, ci, :], start=False, stop=True)
                            return ins
                        P.op('pe', fnP, reads=['identf', 'Pf%d' % j, 'kbtok%d' % sx, kX + 'u', 'vtk'], writes=[MCk])
                        gcol = gC[:, j, ci:ci + 1]
                        P.op('act', (lambda e, j=j, gcol=gcol, MC=MC: e.activation(out=Pf[:, j, :], in_=MC[:, 448:512], func=AF.Copy, scale=gcol)),
                             reads=['gC%d' % j], writes=['Pf%d' % j, MCk])
                        P.op('dve', (lambda e, j=j, gcol=gcol, MC=MC: e.tensor_scalar(out=Pb[:, j, :], in0=MC[:, 448:512], scalar1=gcol, scalar2=None, op0=ALU.mult)),
                             reads=['gC%d' % j], writes=['Pb%d' % j, MCk])
                        if sample:
                            seq = ti * 2 + c
                            P.op('pool', (lambda e, seq=seq, j=j: e.dma_start(out=wkvs_o[seq, :, j, :], in_=Pf[:, j, :])),
                                 reads=['Pf%d' % j], chan='o_pf%d' % j, cb=out_idx)
                    P.merge_streams()
                y4 = ysb[:].rearrange("p (j h c) -> p j h c", h=2, c=64)
                if DBG.get('oe', 0) == 0:
                    P.op('dve', lambda e: e.tensor_copy(out=y4[:, :, 0, :], in_=Db[1][:, :].rearrange("p (j c) -> p j c", c=64)), writes=['ysb', 'D1'])
                    P.op('act', lambda e: e.copy(out=y4[:, :, 1, :], in_=Aacc[1][:, :].rearrange("p (j c) -> p j c", c=64)), writes=['ysb', 'PB1'])
                else:
                    for j in range(8):
                        P.op('dve', (lambda e, j=j: e.tensor_copy(out=ysb[:, j * 128:j * 128 + 64], in_=Db[1][:, j * 64:j * 64 + 64])), writes=['ysb', 'D1'])
                        P.op('act', (lambda e, j=j: e.copy(out=ysb[:, j * 128 + 64:j * 128 + 128], in_=Aacc[1][:, j * 64:j * 64 + 64])), writes=['ysb', 'PB1'])
                if gt == 15:
                    out_idx.append(P.op('pool', lambda e: e.dma_start(out=wkvp_o, in_=Pf[:]), reads=['Pf%d' % j for j in range(8)], chan='o_pfp'))

                if DBG.get('pso', 1):
                    P.capture_start()
                if DBG.get('dump', 0):
                    out_idx.append(P.op('pool', (lambda e, gt=gt: e.dma_start(out=y_o[(gt + 4) * 128:(gt + 5) * 128, 0:1024], in_=ysb[:])), reads=['ysb'], chan='o_dbg'))
                y3 = ysb[:].rearrange("p (h c) -> p h c", c=64)
                q3 = ysq[:].rearrange("p (h c) -> p h c", c=64)
                P.op('dve', lambda e: e.tensor_reduce(out=small[:, 8:24], in_=y3, axis=AX.X, op=ALU.add), reads=['ysb'], writes=['gn_s1'])
                P.op('act', lambda e: e.activation(out=ysq[:], in_=ysb[:], func=AF.Square), reads=['ysb'], writes=['ysq'])
                P.op('dve', lambda e: e.tensor_reduce(out=small[:, 24:40], in_=q3, axis=AX.X, op=ALU.add), reads=['ysq'], writes=['gn_s2'])
                P.op('dve', lambda e: e.tensor_scalar(out=small[:, 40:56], in0=small[:, 8:24], scalar1=1.0 / 64, scalar2=None, op0=ALU.mult),
                     reads=['gn_s1'], writes=['gn_mean'])
                P.op('dve', lambda e: e.tensor_tensor(out=small[:, 8:24], in0=small[:, 40:56], in1=small[:, 40:56], op=ALU.mult),
                     reads=['gn_mean', 'gn_s1'], writes=['gn_s1'])
                P.op('dve', lambda e: e.scalar_tensor_tensor(out=small[:, 24:40], in0=small[:, 24:40], scalar=1.0 / 64, in1=small[:, 8:24], op0=ALU.mult, op1=ALU.subtract),
                     reads=['gn_s2', 'gn_s1'], writes=['gn_s2'])
                P.op('dve', lambda e: e.tensor_scalar(out=small[:, 24:40], in0=small[:, 24:40], scalar1=LNX_EPS, scalar2=None, op0=ALU.add),
                     reads=['gn_s2'], writes=['gn_s2'])
                P.op('act', lambda e: e.activation(out=small[:, 24:40], in_=small[:, 24:40], func=AF.Sqrt), reads=['gn_s2'], writes=['gn_s2'])
                P.op('dve', lambda e: e.reciprocal(out=small[:, 24:40], in_=small[:, 24:40]), reads=['gn_s2'], writes=['gn_s2'])
                P.op('dve', lambda e: e.tensor_tensor(out=y3, in0=y3, in1=small[:, 40:56].unsqueeze(2).to_broadcast([128, 16, 64]), op=ALU.subtract),
                     reads=['ysb', 'gn_mean'], writes=['ysb'])
                P.op('dve', lambda e: e.tensor_tensor(out=y3, in0=y3, in1=small[:, 24:40].unsqueeze(2).to_broadcast([128, 16, 64]), op=ALU.mult),
                     reads=['ysb', 'gn_s2'], writes=['ysb'])
                P.op('dve', lambda e: e.tensor_tensor(out=ysb[:], in0=ysb[:], in1=ct['lnxw'][:], op=ALU.mult), reads=['ysb', 'lnxw'], writes=['ysb'])
                P.op('dve', lambda e: e.tensor_tensor(out=ysb[:], in0=ysb[:], in1=ct['lnxb'][:], op=ALU.add), reads=['ysb', 'lnxb'], writes=['ysb'])
                P.op('dve', (lambda e, ti=ti: e.tensor_tensor(out=q3, in0=vtok[:, ti, :].rearrange("p (h c) -> p h c", c=64),
                                                              in1=bon[:, ti, :].unsqueeze(2).to_broadcast([128, 16, 64]), op=ALU.mult)),
                     reads=['vtok', 'bon', 'ysq'], writes=['ysq'])
                P.op('dve', lambda e: e.tensor_tensor(out=ysb[:], in0=ysb[:], in1=ysq[:], op=ALU.add), reads=['ysb', 'ysq'], writes=['ysb'])
                P.op('dve', (lambda e, ti=ti: e.tensor_tensor(out=yab[:], in0=ysb[:], in1=sa[:, ti, :], op=ALU.mult)), reads=['ysb', 'sa'], writes=['yab'])

                if DBG.get('dump', 0):
                    out_idx.append(P.op('pool', (lambda e, gt=gt: e.dma_start(out=y_o[(gt + 8) * 128:(gt + 9) * 128, 0:1024], in_=yab[:])), reads=['yab'], chan='o_dbg'))
                tr_offs = [512, 640, 768, 896] if ti == 0 else [384, 896]
                nr = len(tr_offs)
                for r0 in range(0, 8, nr):
                    def fn(e, r0=r0, tr_offs=tr_offs, nr=nr):
                        ins = None
                        for k in range(nr):
                            ins = e.transpose(out=tp[:, tr_offs[k]:tr_offs[k] + 128], in_=yab[:, (r0 + k) * 128:(r0 + k + 1) * 128], identity=identb[:])
                        return ins
                    P.op('pe', fn, reads=['yab', 'identb'], writes=['tp'])
                    for k in range(nr):
                        P.op('act', (lambda e, tcs=tcs, r0=r0, k=k, off=tr_offs[k]: e.copy(out=yaT[:, r0 + k, tcs], in_=tp[:, off:off + 128])), writes=['yaT', 'tp'])
                if DBG.get('pso', 1):
                    (PS_PENDING if ti == 0 else PS1_PENDING).append(P.capture_end())

            if DBG['stage'] <= 4:
                continue
            for ti in range(NT):
                gt = b * NT + ti
                if gt == 15:
                    out_idx.append(P.op('pool', (lambda e, ti=ti: e.dma_start(out=kp_o, in_=kvo[:, ti, 0:256])), reads=['kvo'], chan='o_kv'))
                    out_idx.append(P.op('pool', (lambda e, ti=ti: e.dma_start(out=vp_o, in_=kvo[:, ti, 256:512])), reads=['kvo'], chan='o_kv'))
                if sample:
                    for c in range(2):
                        seq = ti * 2 + c
                        cp = slice(c * 64, c * 64 + 64)
                        out_idx.append(P.op('pool', (lambda e, ti=ti, seq=seq, cp=cp: e.dma_start(out=ks_o[seq, 64:128, :], in_=kvo[cp, ti, 0:256])), reads=['kvo'], chan='o_kv'))
                        out_idx.append(P.op('pool', (lambda e, ti=ti, seq=seq, cp=cp: e.dma_start(out=vs_o[seq, 64:128, :], in_=kvo[cp, ti, 256:512])), reads=['kvo'], chan='o_kv'))
                        out_idx.append(P.op('pool', (lambda e, seq=seq: e.dma_start(out=ks_o[seq, 0:64, :], in_=ck_raw[seq, 64:128, :])), chan='o_kv'))
                        out_idx.append(P.op('pool', (lambda e, seq=seq: e.dma_start(out=vs_o[seq, 0:64, :], in_=cv_raw[seq, 64:128, :])), chan='o_kv'))

            if DBG['stage'] <= 5:
                continue
            fence(ARENA_PREP, ARENA_FIN)

            def aux_m(i):
                slot = load_unit(b, nu(), w_in_d, 6784 + i * 256, 256, 16)
                pr = next_pair()
                for f in range(2):
                    mm_fm(slot, 16, f, hT, 'hT', pr[f])
                for f in range(2):
                    ai = pr[f]
                    P.op('act', (lambda e, ai=ai, f=f, i=i: e.activation(out=ta_all[:, i * 2 + f, :], in_=A[ai][:, 0:TB], func=AF.Tanh, scale=0.5)), writes=['taall', akey(ai)])
                    P.op('dve', (lambda e, f=f, i=i: e.tensor_scalar(out=ta_all[:, i * 2 + f, :], in0=ta_all[:, i * 2 + f, :], scalar1=0.5, scalar2=0.5, op0=ALU.mult, op1=ALU.add)),
                         writes=['taall'])

            def aux_p(i):
                slot = load_unit(b, nu(), p_a_d, i * 256, 256, 8)
                pr = next_pair()
                for f in range(2):
                    mm_fm(slot, 8, f, yaT, 'yaT', pr[f])
                for f in range(2):
                    ai = pr[f]
                    P.op('dve', (lambda e, ai=ai, f=f, i=i: e.tensor_tensor(out=ta_all[:, i * 2 + f, :], in0=ta_all[:, i * 2 + f, :], in1=A[ai][:, 0:TB], op=ALU.mult)),
                         writes=['taall', akey(ai)])

            for ti in range(NT):
                gt = b * NT + ti
                tcs = slice(ti * 128, (ti + 1) * 128)
                nkb = 3 if sample else 2
                nk = nkb * 128
                Dm = ct['DmS'] if sample else (ct['DmP0'] if gt == 0 else ct['DmP'])
                Dk = 'DmS' if sample else ('DmP0' if gt == 0 else 'DmP')
                P.begin_streams(3)
                if ti == 0 and PS1_PENDING:
                    P.add_stream(PS1_PENDING.pop())
                P.set_stream(2)
                for i_ in range(8):
                    (aux_m if ti == 0 else aux_p)(i_)
                for h in range(16):
                    sx = (h % 2) if DBG.get('as', 1) else 0
                    P.set_stream(sx)
                    g = h // 4
                    f = h // 2
                    hp = slice((h % 2) * 64, (h % 2) * 64 + 64)
                    SB = Mb if sx == 0 else Cb
                    SBk = 'Mb' if sx == 0 else 'Cb'
                    s_x, e_x, eT_x, sm = s_sbS[sx], e_sbS[sx], eTS[sx], smallS[sx]
                    ks = 'at%d_' % sx
                    tpo = sx * 512

                    def fnS(e, g=g, f=f, hp=hp, ti=ti, tcs=tcs, sample=sample, SB=SB):
                        kb_ = hp.start
                        if not sample:
                            return mmk(e, SB[:, 0:256], qT[hp, f, tcs], kTd[hp, g, ti * 128:ti * 128 + 256], kb_)
                        mmk(e, SB[:, 0:128], qT[hp, f, tcs], kc[hp, ti * 2, g, :], kb_)
                        mmk(e, SB[:, 128:256], qT[hp, f, tcs], kc[hp, ti * 2 + 1, g, :], kb_)
                        return mmk(e, SB[:, 256:384], qT[hp, f, tcs], kTd[hp, g, 128 + ti * 128:256 + ti * 128], kb_)
                    P.op('pe', fnS, reads=['qT', 'kTd', 'kc'], writes=[SBk])
                    P.op('dve', (lambda e, h=h, nk=nk, Dm=Dm, s_x=s_x, SB=SB: e.scalar_tensor_tensor(out=s_x[:, 0:nk], in0=Dm[:, 0:nk], scalar=SLOPES[h], in1=SB[:, 0:nk], op0=ALU.mult, op1=ALU.add)),
                         reads=[Dk], writes=[ks + 's', SBk])
                    P.op('dve', (lambda e, nk=nk, s_x=s_x, sm=sm: e.tensor_reduce(out=sm[:, 0:1], in_=s_x[:, 0:nk], axis=AX.X, op=ALU.max)), reads=[ks + 's'], writes=[ks + 'mx'])
                    P.op('dve', (lambda e, h=h, sm=sm: e.tensor_scalar(out=sm[:, 1:2], in0=sm[:, 0:1], scalar1=ct['sinks'][:, h:h + 1], scalar2=-1.0, op0=ALU.max, op1=ALU.mult)),
                         reads=[ks + 'mx', 'sinks'], writes=[ks + 'negm'])
                    P.op('dve', (lambda e, sm=sm: e.memset(sm[:, 2:3], 0.0)), writes=[ks + 'rs'])
                    P.op('act', (lambda e, nk=nk, s_x=s_x, e_x=e_x, sm=sm: e.activation(out=e_x[:, 0:nk], in_=s_x[:, 0:nk], func=AF.Exp, bias=sm[:, 1:2], accum_out=sm[:, 2:3])),
                         reads=[ks + 's', ks + 'negm', ks + 'rs'], writes=[ks + 'e', ks + 'rs'])
                    P.op('act', (lambda e, h=h, sm=sm: e.activation(out=sm[:, 3:4], in_=sm[:, 1:2], func=AF.Exp, bias=ct['sinks'][:, h:h + 1])),
                         reads=[ks + 'negm', 'sinks'], writes=[ks + 'es'])
                    P.op('dve', (lambda e, sm=sm: e.tensor_tensor(out=sm[:, 3:4], in0=sm[:, 3:4], in1=sm[:, 2:3], op=ALU.add)), reads=[ks + 'es', ks + 'rs'], writes=[ks + 'es'])
                    P.op('dve', (lambda e, h=h, sm=sm: e.reciprocal(out=rden[:, h:h + 1], in_=sm[:, 3:4])), reads=[ks + 'es'], writes=['rden%d' % h])

                    def fnT(e, nkb=nkb, e_x=e_x, tpo=tpo):
                        ins = None
                        for kb in range(nkb):
                            ins = e.transpose(out=tp[:, tpo + kb * 128:tpo + (kb + 1) * 128], in_=e_x[:, kb * 128:(kb + 1) * 128], identity=identb[:])
                        return ins
                    P.op('pe', fnT, reads=[ks + 'e', 'identb'], writes=['tp'])
                    P.op('act', (lambda e, nk=nk, eT_x=eT_x, tpo=tpo: e.copy(out=eT_x[:, 0:nk], in_=tp[:, tpo:tpo + nk])), writes=[ks + 'eT', 'tp'])

                    def fnV(e, g=g, ti=ti, nkb=nkb, sample=sample, SB=SB, eT_x=eT_x):
                        gs = slice(g * 64, g * 64 + 64)
                        po = SB[:, 448:512]
                        if not sample:
                            e.matmul(po, lhsT=eT_x[:, 0:128], rhs=vat[:, ti, gs], start=True, stop=False)
                            return e.matmul(po, lhsT=eT_x[:, 128:256], rhs=vat[:, ti + 1, gs], start=False, stop=True)
                        e.matmul(po, lhsT=eT_x[:, 0:128], rhs=vc[:, ti * 2, gs], start=True, stop=False)
                        e.matmul(po, lhsT=eT_x[:, 128:256], rhs=vc[:, ti * 2 + 1, gs], start=False, stop=False)
                        return e.matmul(po, lhsT=eT_x[:, 256:384], rhs=vat[:, ti + 1, gs], start=False, stop=True)
                    P.op('pe', fnV, reads=[ks + 'eT', 'vat', 'vc'], writes=[SBk])
                    P.op('dve', (lambda e, h=h, SB=SB: e.tensor_scalar(out=ob[:, h * 64:(h + 1) * 64], in0=SB[:, 448:512], scalar1=rden[:, h:h + 1], scalar2=None, op0=ALU.mult)),
                         reads=['rden%d' % h], writes=['ob%d' % h, SBk])
                P.merge_streams()
                if DBG.get('dump', 0):
                    out_idx.append(P.op('pool', (lambda e, gt=gt: e.dma_start(out=y_o[(gt + 4) * 128:(gt + 5) * 128, 1024:2048], in_=ob[:])), reads=['ob'] + ['ob%d' % h for h in range(16)], chan='o_dbg'))
                P.op('dve', (lambda e, ti=ti: e.tensor_tensor(out=yab[:], in0=ob[:], in1=sbg[:, ti, :], op=ALU.mult)), reads=['ob%d' % h for h in range(16)] + ['sbg'], writes=['yab'])

                if DBG.get('dump', 0):
                    out_idx.append(P.op('pool', (lambda e, gt=gt: e.dma_start(out=y_o[(gt + 8) * 128:(gt + 9) * 128, 1024:2048], in_=yab[:])), reads=['yab'], chan='o_dbg'))
                def fn(e):
                    ins = None
                    for k in range(8):
                        ins = e.transpose(out=tp[:, k * 128:(k + 1) * 128], in_=yab[:, k * 128:(k + 1) * 128], identity=identb[:])
                    return ins
                P.op('pe', fn, reads=['yab', 'identb'], writes=['tp'])
                P.op('act', (lambda e, tcs=tcs: e.copy(out=ybT[:, :, tcs], in_=tp[:, :].rearrange("p (k t) -> p k t", t=128))), writes=['ybT', 'tp'])


            if DBG.get('dump', 0):
                out_idx.append(P.op('pool', (lambda e: e.dma_start(out=y_o[12 * 128:13 * 128, :].rearrange('p (k t) -> p k t', t=TB), in_=yaT[:])), reads=['yaT'], chan='o_dbg'))
                out_idx.append(P.op('pool', (lambda e: e.dma_start(out=y_o[13 * 128:14 * 128, :].rearrange('p (k t) -> p k t', t=TB), in_=ybT[:])), reads=['ybT'], chan='o_dbg'))
                out_idx.append(P.op('pool', (lambda e: e.dma_start(out=y_o[14 * 128:15 * 128, :].rearrange('p (k t) -> p k t', t=TB), in_=hT[:, 0:8, :])), reads=['hT'], chan='o_dbg'))
            if DBG['stage'] <= 6:
                continue
            for i in range(8):
                slot = load_unit(b, nu(), w_in_d, 8832 + i * 256, 256, 16)
                pr = next_pair()
                for f in range(2):
                    mm_fm(slot, 16, f, hT, 'hT', pr[f])
                for f in range(2):
                    ai = pr[f]
                    P.op('act', (lambda e, ai=ai, f=f: e.activation(out=sgb[:, f, :], in_=A[ai][:, 0:TB], func=AF.Sigmoid)), writes=['sgb', akey(ai)])
                slot = load_unit(b, nu(), p_b_d, i * 256, 256, 8)
                pr = next_pair()
                for f in range(2):
                    mm_fm(slot, 8, f, ybT, 'ybT', pr[f])
                for f in range(2):
                    ai = pr[f]
                    P.op('dve', (lambda e, ai=ai, f=f: e.tensor_tensor(out=sgb[:, f, :], in0=sgb[:, f, :], in1=A[ai][:, 0:TB], op=ALU.mult)),
                         reads=['sgb'], writes=['sgb', akey(ai)])
                    P.op('pool', (lambda e, i=i, f=f: e.tensor_tensor(out=mergedT[:, i * 2 + f, :], in0=ta_all[:, i * 2 + f, :], in1=sgb[:, f, :], op=ALU.add)),
                         reads=['taall', 'sgb'], writes=['mergedT'])
            fence(['taall'], ['xr'])
            for ti in range(NT):
                gt = b * NT + ti
                P.op('sp', (lambda e, gt=gt, ti=ti: e.dma_start(out=xr[:, ti, :], in_=x_d[gt * 128:(gt + 1) * 128, :])),
                     writes=['xr'], chan='xr')
            P.begin_streams(2)
            if b + 1 < DBG['nblk']:
                P.set_stream(1)
                emit_S1(b + 1)
            P.set_stream(0)
            for i in range(8):
                slot = load_unit(b, nu(), w_o_d, i * 256, 256, 16)
                pr = next_pair()
                for ti in range(NT):
                    mm_tm(slot, 16, ti, mergedT, 'mergedT', pr[ti])
                for ti in range(NT):
                    ai = pr[ti]
                    P.op('dve', (lambda e, ai=ai, ti=ti, i=i: e.tensor_tensor(out=xr[:, ti, i * 256:(i + 1) * 256], in0=xr[:, ti, i * 256:(i + 1) * 256], in1=A[ai][:, 0:256], op=ALU.add)),
                         reads=['xr'], writes=['xr', akey(ai)])
            for ti in range(NT):
                gt = b * NT + ti
                P.op('dve', lambda e: e.memset(small[:, 0:1], 0.0), writes=['ssq'])
                P.op('act', (lambda e, ti=ti: e.activation(out=sa[:].rearrange("p t c -> p (t c)"), in_=xr[:, ti, :], func=AF.Square, accum_out=small[:, 0:1])),
                     reads=['xr', 'ssq'], writes=['sa', 'ssq'])
                P.op('dve', lambda e: e.tensor_scalar(out=small[:, 1:2], in0=small[:, 0:1], scalar1=1.0 / D, scalar2=RMS_EPS, op0=ALU.mult, op1=ALU.add),
                     reads=['ssq'], writes=['ms'])
                P.op('act', lambda e: e.activation(out=small[:, 2:3], in_=small[:, 1:2], func=AF.Sqrt), reads=['ms'], writes=['sq'])
                P.op('dve', lambda e: e.reciprocal(out=small[:, 3:4], in_=small[:, 2:3]), reads=['sq'], writes=['rstd'])
                P.op('dve', (lambda e, ti=ti: e.scalar_tensor_tensor(out=xr[:, ti, :], in0=xr[:, ti, :], scalar=small[:, 3:4], in1=ct['gfbc'][:], op0=ALU.mult, op1=ALU.mult)),
                     reads=['xr', 'rstd', 'gfbc'], writes=['xr'])
                P.op('pool', (lambda e, gt=gt, ti=ti: e.dma_start(out=y_o[gt * 128:(gt + 1) * 128, :], in_=xr[:, ti, :])),
                     reads=['xr'], writes=['xr_st'], chan='o_y', cb=out_idx)
            P.merge_streams()
            if b == NBLK - 2:
                out_idx.append(P.op('pool', lambda e: e.dma_start(out=shp_o, in_=plast[:]), reads=['plast'], chan='o_sh'))
            if sample:
                out_idx.append(P.op('pool', lambda e: e.dma_start(out=shs_o, in_=shs[:]), reads=['shs'], chan='o_sh'))
            assert st['u'] <= 80, st['u']

        P.wait_all('pool', out_idx)
        P.emit()
        build.stats = P.stats
    return nc


_CACHE = {}


def _consts():
    c = {}
    c['identf'] = np.eye(128, dtype=np.float32)
    s = np.arange(128)[:, None]
    t = np.arange(128)[None, :]
    same = (s // 64) == (t // 64)
    MUs = (same & (s < t)).astype(np.float32)
    MUi = (same & (s <= t)).astype(np.float32)
    c['MU4'] = np.concatenate([MUs, MUs, MUi, MUi], axis=1)
    c['MLs'] = (same & (t < s)).astype(np.float32)
    c['bones'] = same.astype(np.float32)
    s6 = np.arange(64)[:, None]
    t6 = np.arange(64)[None, :]
    mus = (s6 < t6).astype(np.float32)
    mui = (s6 <= t6).astype(np.float32)
    mls = (t6 < s6).astype(np.float32)
    m5 = np.concatenate([mus, mus, mui, mui, mls], axis=1)
    c['MU5'] = np.concatenate([m5, m5], axis=0)
    c['I2'] = np.concatenate([np.eye(64, dtype=np.float32)] * 2, axis=0)
    bo2 = np.zeros((128, 2), np.float32)
    bo2[:64, 0] = 1
    bo2[64:, 1] = 1
    c['bo2'] = bo2
    rm = np.ones((128, TB), np.float32)
    rm[:, ::64] = 0
    c['resetm'] = rm
    NEG = -1e30
    i = np.arange(128)[:, None]
    k = np.arange(256)[None, :]
    dch = (2 + i // 64) - (k // 64)
    vis = (dch >= 0) & (dch <= 2)
    DmP = np.where(vis, -np.abs(128 + i - k).astype(np.float32), NEG).astype(np.float32)
    c['DmP'] = DmP
    DmP0 = DmP.copy()
    DmP0[:, :128] = NEG
    c['DmP0'] = DmP0
    DmS = np.full((128, 384), NEG, np.float32)
    tt = np.arange(64)[:, None]
    kk = np.arange(128)[None, :]
    t2 = np.arange(64)[None, :]
    for sq in range(2):
        rows = slice(sq * 64, sq * 64 + 64)
        DmS[rows, sq * 128:(sq + 1) * 128] = -(128 + tt - kk).astype(np.float32)
        DmS[rows, 256 + sq * 64:256 + sq * 64 + 64] = -np.abs(tt - t2).astype(np.float32)
    c['DmS'] = DmS
    return c


def kernel(x_prompt, x_sample, state_wkv, state_shift, cache_k, cache_v, g_norm, w_in, mu_shift, w0,
           w_w_up, a0, w_a_up, k_k, k_a, r_k, lnx_w, lnx_b, sinks, p_a, p_b, w_o, g_final):
    f32 = np.float32
    A_ = lambda v: np.ascontiguousarray(np.asarray(v, dtype=f32))
    if 'nc' not in _CACHE:
        _CACHE['nc'] = build()
    nc = _CACHE['nc']
    cst = _consts()
    col = lambda v: A_(np.asarray(v, f32).reshape(-1, 128).T)
    shared = dict(cst)
    shared.update(
        w_in=A_(w_in[0]), p_a=A_(p_a[0]), p_b=A_(p_b[0]), w_o=A_(w_o[0]),
        gbc=A_(np.broadcast_to(np.asarray(g_norm[0], f32)[None, :], (128, D))),
        gfbc=A_(np.broadcast_to(np.asarray(g_final, f32)[None, :], (128, D))),
        lnxw=A_(np.broadcast_to(np.asarray(lnx_w[0], f32)[None, :], (128, 1024))),
        lnxb=A_(np.broadcast_to(np.asarray(lnx_b[0], f32)[None, :], (128, 1024))),
        muT=col(mu_shift[0]), w0c=col(w0[0]), a0c=col(a0[0]), kkc=col(k_k[0]), kac=col(k_a[0]),
        rkc=col(np.asarray(r_k[0], f32).reshape(-1)),
        sinks=A_(np.broadcast_to(np.asarray(sinks[0], f32)[None, :], (128, 16))),
        wup=A_(np.concatenate([np.asarray(w_w_up[0], f32), np.asarray(w_a_up[0], f32)], axis=0)),
    )
    xp = np.asarray(x_prompt, f32)
    xs_ = np.asarray(x_sample, f32)
    swkv = np.asarray(state_wkv[0], f32)
    ssh = np.asarray(state_shift[0], f32)
    ck = np.asarray(cache_k[0], f32)
    cvv = np.asarray(cache_v[0], f32)
    in_maps = []
    for c in range(8):
        sl = slice(4 * c, 4 * c + 4)
        m = dict(shared)
        m['x'] = A_(np.concatenate([xp[c], xs_[sl].reshape(256, D)], axis=0))
        sw = swkv[sl].reshape(4, 8, 2, 64, 64)
        m['swkv'] = A_(sw.transpose(0, 2, 4, 1, 3).reshape(4, 128, 8, 64))
        m['sshT'] = A_(ssh[sl].reshape(4, 25, 128).transpose(2, 1, 0))
        ckc = ck[sl]
        kt_ = ckc.transpose(3, 0, 2, 1)
        m['ckT'] = A_(np.concatenate([kt_, kt_], axis=0))
        m['cv'] = A_(cvv[sl].reshape(4, 128, 256).transpose(1, 0, 2))
        m['ck_raw'] = A_(ckc.reshape(4, 128, 256))
        m['cv_raw'] = A_(cvv[sl].reshape(4, 128, 256))
        in_maps.append(m)
    res = run_bass_kernel_spmd(nc, in_maps, core_ids=list(range(8)))
    R = res.results
    y_prompt = np.stack([R[c]['y'][:2048] for c in range(8)]).astype(f32)
    y_sample = np.concatenate([R[c]['y'][2048:].reshape(4, 64, D) for c in range(8)]).astype(f32)

    def unP(a):
        return a.reshape(2, 64, 8, 64).transpose(2, 0, 3, 1).reshape(16, 64, 64)
    wkv_p = np.stack([unP(R[c]['wkv_p']) for c in range(8)])[None].astype(f32)
    wkv_s = np.stack([unP(R[c]['wkv_s'][s]) for c in range(8) for s in range(4)])[None].astype(f32)
    shift_p = np.stack([R[c]['shift_p'].T.reshape(3200) for c in range(8)])[None].astype(f32)
    shift_s = np.stack([R[c]['shift_s'][:, :, s].T.reshape(3200) for c in range(8) for s in range(4)])[None].astype(f32)
    k_p = np.stack([R[c]['k_p'].reshape(128, 4, 64) for c in range(8)])[None].astype(f32)
    v_p = np.stack([R[c]['v_p'].reshape(128, 4, 64) for c in range(8)])[None].astype(f32)
    k_s = np.concatenate([R[c]['k_s'].reshape(4, 128, 4, 64) for c in range(8)])[None].astype(f32)
    v_s = np.concatenate([R[c]['v_s'].reshape(4, 128, 4, 64) for c in range(8)])[None].astype(f32)
    return (y_prompt, y_sample, wkv_p, shift_p, k_p, v_p, wkv_s, shift_s, k_s, v_s)
```
